# Optimizing a Trainium2 kernel written in Bass

```python
import math
import jax
import jax.numpy as jnp
from jax import lax
import numpy as np

D_MODEL = 2048
BATCH = 4
SEQ = 4096
DEPTH = 4

GRID_W = 64
CTX_LEN = 256
N_MIXERS = 3
N_HYENA = (DEPTH + 2) // 3
N_HGRN = (DEPTH + 1) // 3
N_GLA = DEPTH // 3
N_MOD = 6
FFN_HIDDEN = 256 * (-(-8 * D_MODEL // (3 * 256)))
RMS_EPS = 1e-6
CHUNK = 64

HYENA_ORDER = 2
FILTER_BANDS = 16
FILTER_EMB = 1 + 2 * FILTER_BANDS
FILTER_HIDDEN = 64
HYENA_DECAY_TARGET = 1e-2
HYENA_SHORT_DECAY_PCT = 0.3
HYENA_LONG_DECAY_PCT = 1.5
HYENA_DECAY_MIN = math.log(HYENA_DECAY_TARGET) / HYENA_LONG_DECAY_PCT
HYENA_DECAY_MAX = math.log(HYENA_DECAY_TARGET) / HYENA_SHORT_DECAY_PCT

HGRN_EXPAND = 128
HGRN_HEADS = D_MODEL // HGRN_EXPAND
HGRN_IN_DIM = 5 * D_MODEL

GLA_HEADS = 4
GLA_KEY_DIM = D_MODEL // 2
GLA_VAL_DIM = D_MODEL
GLA_DK = GLA_KEY_DIM // GLA_HEADS
GLA_GATE_RANK = 16
GLA_GATE_NORM = 16.0
GLA_IN_DIM = 2 * GLA_KEY_DIM + 2 * GLA_VAL_DIM + 2 * GLA_GATE_RANK

kernel_name = "hybrid_hyena_hgrn2_gla_prefix_dit"


def rms_norm(x, g):
    xf = x.astype(jnp.float32)
    y = xf * lax.rsqrt(jnp.mean(xf * xf, axis=-1, keepdims=True) + RMS_EPS)
    return (y * g.astype(jnp.float32)).astype(x.dtype)


def head_rms_norm(o, g, n_heads):
    b, l, d = o.shape
    y = rms_norm(o.reshape(b, l, n_heads, d // n_heads), g.reshape(n_heads, d // n_heads))
    return y.reshape(b, l, d)


def modulate(h, shift, scale):
    return h * (1.0 + scale) + shift


def to_heads(a, n_heads):
    b, l, d = a.shape
    return a.reshape(b, l, n_heads, d // n_heads).transpose(0, 2, 1, 3)


def from_heads(a):
    b, h, l, d = a.shape
    return a.transpose(0, 2, 1, 3).reshape(b, l, h * d)


def swiglu(h, w_in, w_out):
    gate, up = jnp.split(h @ w_in, 2, axis=-1)
    return (jax.nn.silu(gate) * up) @ w_out


def short_conv3(u, w, n_rows):
    b, l, ch = u.shape
    r = u.reshape(b, n_rows, l // n_rows, ch)
    p = jnp.pad(r, ((0, 0), (0, 0), (1, 1), (0, 0)))
    y = w[0] * p[:, :, :-2] + w[1] * r + w[2] * p[:, :, 2:]
    return y.reshape(b, l, ch)


def hyena_filters(seq_len, fw1, fb1, ffreq, fw2, fb2, fwout):
    f32 = jnp.float32
    d = fwout.shape[-1] // (2 * HYENA_ORDER)
    pos = jnp.arange(seq_len, dtype=f32)
    t = pos / max(seq_len - 1, 1)
    bands = jnp.arange(1, FILTER_BANDS + 1, dtype=f32)
    ang = (2.0 * math.pi / seq_len) * pos[:, None] * bands[None, :]
    z = jnp.concatenate([t[:, None], jnp.cos(ang), -jnp.sin(ang)], axis=-1)
    hid = jnp.sin(ffreq[0].astype(f32) * (z @ fw1.astype(f32) + fb1.astype(f32)))
    hid = jnp.sin(ffreq[1].astype(f32) * (hid @ fw2.astype(f32) + fb2.astype(f32)))
    h = (hid @ fwout.astype(f32)).reshape(seq_len, 2, HYENA_ORDER, d)
    deltas = jnp.abs(jnp.linspace(HYENA_DECAY_MIN, HYENA_DECAY_MAX, d, dtype=f32))
    h = h * jnp.exp(-t[:, None] * deltas[None, :])[:, None, None, :]
    full = jnp.concatenate(
        [h[:, 0], jnp.zeros((1, HYENA_ORDER, d), f32), h[: seq_len - 1, 1][::-1]], axis=0)
    full = full / jnp.sum(jnp.abs(full), axis=0, keepdims=True)
    return jnp.fft.rfft(full, axis=0)


def fft_long_conv(u, h_freq, skip):
    l = u.shape[1]
    uf = u.astype(jnp.float32)
    y = jnp.fft.irfft(jnp.fft.rfft(uf, n=2 * l, axis=1) * h_freq[None], n=2 * l, axis=1)[:, :l]
    return (y + uf * skip.astype(jnp.float32)).astype(u.dtype)


def hyena_mix(h, n_rows, w_in, conv_w, fw1, fb1, ffreq, fw2, fb2, fwout, fskip, w_out):
    l = h.shape[1]
    v, x1, x2 = jnp.split(short_conv3(h @ w_in, conv_w, n_rows), 3, axis=-1)
    h_freq = hyena_filters(l, fw1, fb1, ffreq, fw2, fb2, fwout)
    z = x1 * fft_long_conv(v, h_freq[:, 0], fskip[0])
    z = x2 * fft_long_conv(z, h_freq[:, 1], fskip[1])
    return z @ w_out


def chunk_gla(q, k, v, g, s0):
    out_dtype = v.dtype
    q, k, v, g = (a.astype(jnp.float32) for a in (q, k, v, g))
    b, nh, l, _ = q.shape
    dv = v.shape[-1]
    n = l // CHUNK

    def to_chunks(a):
        return jnp.moveaxis(a.reshape(b, nh, n, CHUNK, a.shape[-1]), 2, 0)

    mask = jnp.tril(jnp.ones((CHUNK, CHUNK), bool))[:, :, None]

    def step(s, inp):
        qi, ki, vi, gi = inp
        bcum = jnp.cumsum(gi, axis=2)
        diff = bcum[:, :, :, None, :] - bcum[:, :, None, :, :]
        decay = jnp.exp(jnp.where(mask, diff, -jnp.inf))
        att = jnp.einsum('bhtd,bhsd,bhtsd->bhts', qi, ki, decay)
        o = (jnp.einsum('bhts,bhsv->bhtv', att, vi)
             + jnp.einsum('bhtd,bhdv->bhtv', qi * jnp.exp(bcum), s))
        btot = bcum[:, :, -1:, :]
        s_new = (jnp.exp(btot[:, :, 0, :])[..., None] * s
                 + jnp.einsum('bhsd,bhsv->bhdv', ki * jnp.exp(btot - bcum), vi))
        return s_new, o

    s_fin, oc = lax.scan(step, s0, tuple(to_chunks(a) for a in (q, k, v, g)))
    o = jnp.moveaxis(oc, 0, 2).reshape(b, nh, l, dv)
    return o.astype(out_dtype), s_fin


def prefix_scan(ctx_in, lat_in, reverse):
    if reverse:
        ctx_in = tuple(jnp.flip(a, axis=2) for a in ctx_in)
        lat_in = tuple(jnp.flip(a, axis=2) for a in lat_in)
    b, nh, _, dk = ctx_in[0].shape
    s0 = jnp.zeros((b, nh, dk, ctx_in[2].shape[-1]), jnp.float32)
    o_ctx, s_ctx = chunk_gla(*ctx_in, s0)
    o_lat, _ = chunk_gla(*lat_in, s_ctx)
    if reverse:
        o_ctx, o_lat = jnp.flip(o_ctx, axis=2), jnp.flip(o_lat, axis=2)
    return o_ctx, o_lat


def bidir_recurrence(ctx_dirs, lat_dirs):
    oc_f, ol_f = prefix_scan(ctx_dirs[0], lat_dirs[0], False)
    oc_b, ol_b = prefix_scan(ctx_dirs[1], lat_dirs[1], True)
    return from_heads(oc_f + oc_b), from_heads(ol_f + ol_b)


def hgrn2_inputs(h, w_in, lb):
    q, i_in, gate, f_fwd, f_bwd = jnp.split(h @ w_in, 5, axis=-1)
    q = to_heads(jax.nn.silu(q), HGRN_HEADS)
    v = to_heads(i_in, HGRN_HEADS)
    dirs = []
    for f_raw, lb_d in ((f_fwd, lb[0]), (f_bwd, lb[1])):
        log_f = jnp.logaddexp(jnp.log(lb_d),
                              jnp.log1p(-lb_d) + jax.nn.log_sigmoid(f_raw.astype(jnp.float32)))
        dirs.append((q, to_heads(-jnp.expm1(log_f), HGRN_HEADS), v, to_heads(log_f, HGRN_HEADS)))
    return dirs, gate


def hgrn2_mix(h_ctx, h_lat, layer_idx, w_in, lb_logits, onorm_g, w_out):
    lb_cum = jnp.cumsum(jax.nn.softmax(lb_logits.astype(jnp.float32), axis=1), axis=1)
    lb = lb_cum[:, layer_idx] - lb_cum[:, 0]
    ctx_dirs, g_ctx = hgrn2_inputs(h_ctx, w_in, lb)
    lat_dirs, g_lat = hgrn2_inputs(h_lat, w_in, lb)
    o_ctx, o_lat = bidir_recurrence(ctx_dirs, lat_dirs)
    y_ctx = (head_rms_norm(o_ctx, onorm_g, HGRN_HEADS) * jax.nn.silu(g_ctx)) @ w_out
    y_lat = (head_rms_norm(o_lat, onorm_g, HGRN_HEADS) * jax.nn.silu(g_lat)) @ w_out
    return y_ctx, y_lat


def gla_inputs(h, w_in, w_up, b_up):
    k0 = GLA_KEY_DIM
    v0 = 2 * GLA_KEY_DIM
    g0 = v0 + GLA_VAL_DIM
    a0 = g0 + GLA_VAL_DIM
    q, k, v, gate, a_f, a_b = jnp.split(h @ w_in, [k0, v0, g0, a0, a0 + GLA_GATE_RANK], axis=-1)
    q = to_heads(q * GLA_DK ** -0.5, GLA_HEADS)
    k = to_heads(k, GLA_HEADS)
    v = to_heads(v, GLA_HEADS)
    dirs = []
    for dr, a in enumerate((a_f, a_b)):
        log_a = jax.nn.log_sigmoid((a @ w_up[dr] + b_up[dr]).astype(jnp.float32)) / GLA_GATE_NORM
        dirs.append((q, k, v, to_heads(log_a, GLA_HEADS)))
    return dirs, gate


def gla_mix(h_ctx, h_lat, w_in, w_up, b_up, onorm_g, w_out):
    ctx_dirs, g_ctx = gla_inputs(h_ctx, w_in, w_up, b_up)
    lat_dirs, g_lat = gla_inputs(h_lat, w_in, w_up, b_up)
    o_ctx, o_lat = bidir_recurrence(ctx_dirs, lat_dirs)
    y_ctx = (head_rms_norm(o_ctx, onorm_g, GLA_HEADS) * jax.nn.silu(g_ctx)) @ w_out
    y_lat = (head_rms_norm(o_lat, onorm_g, GLA_HEADS) * jax.nn.silu(g_lat)) @ w_out
    return y_ctx, y_lat


def setup_inputs(seed: int = 0) -> dict:
    key = jax.random.key(seed)
    keys = iter(jax.random.split(key, 40))

    def nrm(shape, scale):
        return jax.random.normal(next(keys), shape, jnp.float32) * scale

    d = D_MODEL
    return {
        'x': nrm((BATCH, SEQ, d), 1.0),
        'c': nrm((BATCH, d), 1.0),
        'ctx': nrm((BATCH, CTX_LEN, d), 1.0),
        'c_ctx': nrm((d,), 1.0),
        'w_mod': nrm((DEPTH, d, N_MOD * d), 0.5 * d ** -0.5),
        'b_mod': nrm((DEPTH, N_MOD * d), 0.01),
        'norm1_g': 1.0 + nrm((DEPTH, d), 0.01),
        'norm2_g': 1.0 + nrm((DEPTH, d), 0.01),
        'w_ffn_in': nrm((DEPTH, d, 2 * FFN_HIDDEN), d ** -0.5),
        'w_ffn_out': nrm((DEPTH, FFN_HIDDEN, d), FFN_HIDDEN ** -0.5),
        'final_g': 1.0 + nrm((d,), 0.01),
        'hy_w_in': nrm((N_HYENA, d, 3 * d), d ** -0.5),
        'hy_conv_w': nrm((N_HYENA, 3, 3 * d), 3 ** -0.5),
        'hy_fw1': nrm((N_HYENA, FILTER_EMB, FILTER_HIDDEN), FILTER_EMB ** -0.5),
        'hy_fb1': nrm((N_HYENA, FILTER_HIDDEN), 0.1),
        'hy_ffreq': 1.0 + nrm((N_HYENA, 2, FILTER_HIDDEN), 0.1),
        'hy_fw2': nrm((N_HYENA, FILTER_HIDDEN, FILTER_HIDDEN), FILTER_HIDDEN ** -0.5),
        'hy_fb2': nrm((N_HYENA, FILTER_HIDDEN), 0.1),
        'hy_fwout': nrm((N_HYENA, FILTER_HIDDEN, 2 * HYENA_ORDER * d), FILTER_HIDDEN ** -0.5),
        'hy_fskip': nrm((N_HYENA, HYENA_ORDER, d), 0.1),
        'hy_w_out': nrm((N_HYENA, d, d), d ** -0.5),
        'hg_w_in': nrm((N_HGRN, d, HGRN_IN_DIM), d ** -0.5),
        'hg_lb_logits': nrm((2, DEPTH, d), 0.1),
        'hg_onorm_g': 1.0 + nrm((N_HGRN, d), 0.01),
        'hg_w_out': nrm((N_HGRN, d, d), d ** -0.5),
        'gla_w_in': nrm((N_GLA, d, GLA_IN_DIM), d ** -0.5),
        'gla_w_up': nrm((N_GLA, 2, GLA_GATE_RANK, GLA_KEY_DIM), GLA_GATE_RANK ** -0.5),
        'gla_b_up': nrm((N_GLA, 2, GLA_KEY_DIM), 0.1),
        'gla_onorm_g': 1.0 + nrm((N_GLA, GLA_VAL_DIM), 0.01),
        'gla_w_out': nrm((N_GLA, GLA_VAL_DIM, d), GLA_VAL_DIM ** -0.5),
    }


def reference(x, c, ctx, c_ctx, w_mod, b_mod, norm1_g, norm2_g, w_ffn_in, w_ffn_out, final_g,
              hy_w_in, hy_conv_w, hy_fw1, hy_fb1, hy_ffreq, hy_fw2, hy_fb2, hy_fwout, hy_fskip,
              hy_w_out, hg_w_in, hg_lb_logits, hg_onorm_g, hg_w_out,
              gla_w_in, gla_w_up, gla_b_up, gla_onorm_g, gla_w_out):
    rows = x.shape[1] // GRID_W
    sc = jax.nn.silu(c)[:, None, :]
    sc_ctx = jax.nn.silu(c_ctx)[None, None, :]
    for i in range(DEPTH):
        last = i == DEPTH - 1
        kind, j = i % N_MIXERS, i // N_MIXERS
        mod_lat = jnp.split(sc @ w_mod[i] + b_mod[i], N_MOD, axis=-1)
        mod_ctx = jnp.split(sc_ctx @ w_mod[i] + b_mod[i], N_MOD, axis=-1)
        h_lat = modulate(rms_norm(x, norm1_g[i]), mod_lat[0], mod_lat[1])
        need_ctx = (not last) or kind != 0
        h_ctx = modulate(rms_norm(ctx, norm1_g[i]), mod_ctx[0], mod_ctx[1]) if need_ctx else None
        if kind == 0:
            hy = (hy_w_in[j], hy_conv_w[j], hy_fw1[j], hy_fb1[j], hy_ffreq[j], hy_fw2[j],
                  hy_fb2[j], hy_fwout[j], hy_fskip[j], hy_w_out[j])
            y_lat = hyena_mix(h_lat, rows, *hy)
            y_ctx = hyena_mix(h_ctx, 1, *hy) if need_ctx else None
        elif kind == 1:
            y_ctx, y_lat = hgrn2_mix(h_ctx, h_lat, i, hg_w_in[j], hg_lb_logits,
                                     hg_onorm_g[j], hg_w_out[j])
        else:
            y_ctx, y_lat = gla_mix(h_ctx, h_lat, gla_w_in[j], gla_w_up[j], gla_b_up[j],
                                   gla_onorm_g[j], gla_w_out[j])
        x = x + mod_lat[2] * y_lat
        x = x + mod_lat[5] * swiglu(modulate(rms_norm(x, norm2_g[i]), mod_lat[3], mod_lat[4]),
                                    w_ffn_in[i], w_ffn_out[i])
        if not last:
            ctx = ctx + mod_ctx[2] * y_ctx
            ctx = ctx + mod_ctx[5] * swiglu(
                modulate(rms_norm(ctx, norm2_g[i]), mod_ctx[3], mod_ctx[4]),
                w_ffn_in[i], w_ffn_out[i])
    return rms_norm(x, final_g)
```

```python
import math
from contextlib import ExitStack
import numpy as np
import ml_dtypes
import concourse.bass as bass
import concourse.mybir as mybir
from concourse.bass_utils import run_bass_kernel_spmd

F32 = mybir.dt.float32
BF16 = mybir.dt.bfloat16
ALU = mybir.AluOpType
AF = mybir.ActivationFunctionType
P = 128
D = 2048
KC = 16
TL = 4096
TCX = 256
TT = TL + TCX
FF = 5632
NU = TT // P
DEPTH = 4
EPS = 1e-6
DMIN = math.log(1e-2) / 1.5
DMAX = math.log(1e-2) / 0.3
WELE = 8704
SUPER = [(0, TCX, 1)] + [(TCX + i * 1024, 1024, 0) for i in range(4)]


class Buf:
    __slots__ = ("w", "r")

    def __init__(self):
        self.w = None
        self.r = {}


class Tile:
    def __init__(self, h, nparts=1):
        self.h = h
        self.parts = [Buf() for _ in range(nparts)]
        self.sem = None
        self.cnt = 0

    def __getitem__(self, k):
        return self.h[k]

    @property
    def b(self):
        return self.parts[0]

    @property
    def all(self):
        return self.parts


class KB:
    def __init__(self, nc, es):
        self.nc = nc
        self.es = es
        self.eng = {"pe": nc.tensor, "act": nc.scalar, "dve": nc.vector, "pool": nc.gpsimd, "sp": nc.sync}
        self.sem = {}
        self.cnt = {}
        for e in ("pe", "act", "dve", "pool"):
            self.sem[e] = es.enter_context(nc.semaphore("s_" + e))
            self.cnt[e] = 0
        self.waited = {e: {} for e in self.eng}
        self.tiles = []
        self.nm = 0
        self.sempool = []
        self.bsem = es.enter_context(nc.semaphore("s_bar"))
        self.bcnt = 0

    def name(self, p):
        self.nm += 1
        return "%s%d" % (p, self.nm)

    def tile(self, shape, dt, nparts=1, es=None):
        h = (es or self.es).enter_context(self.nc.sbuf_tensor(self.name("t"), list(shape), dt))
        t = Tile(h, nparts)
        if self.sempool:
            t.sem, t.cnt, t.key = self.sempool.pop()
        else:
            t.sem = self.es.enter_context(self.nc.semaphore(self.name("d")))
            t.key = self.name("k")
        self.tiles.append(t)
        if es is not None and hasattr(es, "mine"):
            es.mine.append(t)
        return t

    def psum(self, shape, dt):
        h = self.es.enter_context(self.nc.psum_tensor(self.name("p"), list(shape), dt))
        return Tile(h, 1)

    def _wait(self, eng, ev):
        if ev is None:
            return
        sem, val, key = ev
        if eng == "pe" and key == "pe":
            return
        if self.waited[eng].get(key, 0) >= val:
            return
        self.waited[eng][key] = val
        self.eng[eng].wait_ge(sem, val)

    def _deps(self, eng, r, w):
        for b in r:
            self._wait(eng, b.w)
        for b in w:
            self._wait(eng, b.w)
            for ev in list(b.r.values()):
                self._wait(eng, ev)

    def _mark(self, ev, r, w):
        for b in r:
            b.r[ev[2]] = ev
        for b in w:
            b.w = ev
            b.r = {}

    def op(self, eng, fn, r=(), w=(), inc=True):
        self._deps(eng, r, w)
        ins = fn(self.eng[eng])
        if inc:
            self.cnt[eng] += 1
            ins.then_inc(self.sem[eng], 1)
            ev = (self.sem[eng], self.cnt[eng], eng)
        else:
            ev = (self.sem[eng], self.cnt[eng] + 1, eng)
        self._mark(ev, r, w)
        return ins

    def dma(self, q, out, in_, tile, r=(), w=()):
        self._deps(q, r, w)
        ins = self.eng[q].dma_start(out=out, in_=in_)
        tile.cnt += 16
        ins.then_inc(tile.sem, 16)
        ev = (tile.sem, tile.cnt, tile.key)
        self._mark(ev, r, w)

    def barrier(self):
        sp = self.eng["sp"]
        for e in ("pe", "act", "dve", "pool"):
            if self.cnt[e] > 0:
                self._wait("sp", (self.sem[e], self.cnt[e], e))
        for t in self.tiles:
            if t.cnt > 0:
                self._wait("sp", (t.sem, t.cnt, t.key))
        self.bcnt += 1
        ins = sp.nop()
        ins.then_inc(self.bsem, 1)
        for e in ("pe", "act", "dve", "pool"):
            self.eng[e].wait_ge(self.bsem, self.bcnt)
        for e in self.eng:
            for e2 in ("pe", "act", "dve", "pool"):
                self.waited[e][e2] = self.cnt[e2]
            for t in self.tiles:
                self.waited[e][t.key] = t.cnt

    def scope(self):
        kb = self

        class _S(ExitStack):
            def __exit__(s, *a):
                if a[0] is None:
                    kb.barrier()
                    kb.tiles = [t for t in kb.tiles if t not in s.mine]
                    for t in s.mine:
                        kb.sempool.append((t.sem, t.cnt, t.key))
                return ExitStack.__exit__(s, *a)
        st = _S()
        st.mine = []
        return st


def build(dbg=False):
    nc = bass.Bass("TRN2", target_bir_lowering=False)
    es = ExitStack()
    kb = KB(nc, es)

    def din(name, shape, dt=F32):
        return nc.dram_tensor(name, list(shape), dt, kind="ExternalInput").ap()

    def dint(name, shape, dt=F32):
        return nc.dram_tensor(name, list(shape), dt, kind="Internal").ap()

    XIN = din("xin", [D, TT])
    SCIN = din("scin", [P, KC * 2])
    VECS = din("vecs", [P, NV])
    w_mod = din("w_mod", [DEPTH, D, 6 * D])
    w_ffn_in = din("w_ffn_in", [DEPTH, D, 2 * FF])
    w_ffn_out = din("w_ffn_out", [DEPTH, FF, D])
    hy_w_in = din("hy_w_in", [2, D, 3 * D])
    hy_w_out = din("hy_w_out", [2, D, D])
    hy_fw1 = din("hy_fw1", [2, 33, 64])
    hy_fw2 = din("hy_fw2", [2, 64, 64])
    hy_fwout = din("hy_fwout", [2, 64, 4 * D])
    fskip = din("fskip", [2, P, 2 * D])
    hg_w_in = din("hg_w_in", [1, D, 5 * D])
    hg_w_out = din("hg_w_out", [1, D, D])
    gla_w_in = din("gla_w_in", [1, D, 6176])
    gla_w_up = din("gla_w_up", [1, 2, 16, 1024])
    gla_w_out = din("gla_w_out", [1, D, D])
    C_ident = din("c_ident", [P, P], BF16)
    C_ones = din("c_ones", [P, 2 * P], BF16)
    C_tri = din("c_tri", [P, 2 * P])
    C_smask = din("c_smask", [P, 1024])
    C_negd = din("c_negd", [P, D])
    C_smask64 = din("c_smask64", [P, 1024])
    CL = {}
    for L in (TL, TCX):
        tc_, kf = L // P, (L // P) + 1
        CL[L] = dict(
            zT=din("c_zT%d" % L, [33, L]), tn=din("c_tn%d" % L, [P, tc_]),
            Ff=din("c_Ff%d" % L, [kf, P, tc_ * 256], BF16),
            FF=din("c_FF%d" % L, [2 * kf, P, 2 * tc_ * P], BF16),
            G=din("c_G%d" % L, [tc_, P, 2 * kf * P], BF16),
            HS=(nc.dram_tensor("hs%d" % L, [2, 2 * kf, P, D], F32, kind="ExternalOutput").ap() if dbg else dint("hs%d" % L, [2, 2 * kf, P, D])), TC=tc_, KF=kf)
    OUT = nc.dram_tensor("out", [D, TL], F32, kind="ExternalOutput").ap()
    XR = (nc.dram_tensor("xr", [D, TT], F32, kind="ExternalOutput").ap() if dbg else dint("xr", [D, TT]))
    ZT = None
    VT = (nc.dram_tensor("vt", [3, TT, D], BF16, kind="ExternalOutput").ap() if dbg else dint("vt", [3, TT, D], BF16))
    ZT = (nc.dram_tensor("zt", [D, TT], BF16, kind="ExternalOutput").ap() if dbg else dint("zt", [D, TT], BF16))
    QT = (nc.dram_tensor("qt", [2, D, TT], BF16, kind="ExternalOutput").ap() if dbg else dint("qt", [2, D, TT], BF16))
    KTs = (nc.dram_tensor("kts", [2, D, TT], BF16, kind="ExternalOutput").ap() if dbg else dint("kts", [2, D, TT], BF16))
    KHs = (nc.dram_tensor("khs", [2, D, TT], BF16, kind="ExternalOutput").ap() if dbg else dint("khs", [2, D, TT], BF16))
    EX = (nc.dram_tensor("ex", [2, D, NU], F32, kind="ExternalOutput").ap() if dbg else dint("ex", [2, D, NU]))
    EX64 = dint("ex64", [2, D, TT // 64])
    GATE = (nc.dram_tensor("gate", [D, TT], BF16, kind="ExternalOutput").ap() if dbg else dint("gate", [D, TT], BF16))
    XRb = [Buf() for _ in SUPER]
    SCR = Buf()

    vecs = kb.tile([P, NV], F32)
    ident = kb.tile([P, P], BF16)
    ones = kb.tile([P, 2 * P], BF16)
    tri = kb.tile([P, 2 * P], F32)
    smask = kb.tile([P, 1024], F32)
    smask64 = kb.tile([P, 1024], F32)
    scT = kb.tile([P, KC * 2], BF16)
    MOD = kb.tile([P, 96 * 2], F32)
    MA = kb.tile([P, 2 * KC * 2], F32)
    wbuf = [kb.tile([P, WELE], BF16) for _ in range(2)]
    wsel = [0]
    PS = [kb.psum([P, 512], F32) for _ in range(6)]
    PSB = [kb.psum([P, 1024], BF16) for _ in range(2)]
    pssel = [0]
    psbsel = [0]

    def nps():
        pssel[0] = (pssel[0] + 1) % 6
        return PS[pssel[0]]

    def npsb():
        psbsel[0] = (psbsel[0] + 1) % 2
        return PSB[psbsel[0]]

    def V(name, i=0, n=1):
        o = VOFF[name] + i
        return vecs[:, o:o + n]

    for (t, src) in ((vecs, VECS), (ident, C_ident), (ones, C_ones), (tri, C_tri), (smask, C_smask), (smask64, C_smask64)):
        kb.dma("sp", t[:], src, t, w=t.all)
    sc32 = kb.tile([P, KC * 2], F32)
    kb.dma("sp", sc32[:], SCIN, sc32, w=sc32.all)
    kb.op("act", lambda e: e.activation(out=scT[:], in_=sc32[:], func=AF.Silu), r=sc32.all, w=scT.all)

    def load_w(W, KCn, ranges, cast=True, pre=None):
        wsel[0] ^= 1
        wt = wbuf[wsel[0]]
        ntot = sum(n for _, n in ranges)
        assert KCn * ntot <= WELE
        view = wt[:, 0:KCn * ntot].rearrange("p (k n) -> p k n", n=ntot)
        off = 0
        for (c0, n) in ranges:
            if pre is not None:
                src = pre
            else:
                src = W[:, c0:c0 + n].rearrange("(k p) n -> p k n", p=P)
            kb.dma("pool" if cast else "sp", view[:, :, off:off + n], src, wt, w=wt.all)
            off += n
        return wt, view

    def linear(xT, xbufs, KCn, W, blocks, ttiles, epi, mparts=None):
        for blk in blocks:
            wt, view = load_w(W, KCn, blk["ranges"], pre=blk.get("pre"), cast=blk.get("cast", True))
            for (off, m, tag) in blk["chunks"]:
                pst = []
                for (t0, tn) in ttiles:
                    ps = nps()
                    for kc in range(KCn):
                        kb.op("pe", lambda e, ps=ps, kc=kc, off=off, m=m, t0=t0, tn=tn: e.matmul(
                            ps[0:m, 0:tn], lhsT=view[:, kc, off:off + m], rhs=xT[:, kc, t0:t0 + tn],
                            start=(kc == 0), stop=(kc == KCn - 1)),
                            r=list(wt.all) + list(xbufs), w=ps.all, inc=(kc == KCn - 1))
                    pst.append(ps)
                epi(tag, pst)

    def std_blocks(c0, ncols, bw=512, tag0=0):
        blocks = []
        t = tag0
        for b0 in range(c0, c0 + ncols, bw):
            n = min(bw, c0 + ncols - b0)
            ch = []
            for o in range(0, n, P):
                ch.append((o, min(P, n - o), t))
                t += 1
            blocks.append(dict(ranges=[(b0, n)], chunks=ch))
        return blocks

    def pass_mod(i):
        def epi(tag, pst):
            ps = pst[0]
            kb.op("act", lambda e: e.activation(out=MOD[:, tag * 2:tag * 2 + 2], in_=ps[:, 0:2], func=AF.Identity,
                                                bias=V("bmod", i * 96 + tag), scale=1.0), r=ps.all, w=MOD.all)
        sv = scT[:].rearrange("p (k c) -> p k c", c=2)
        linear(sv, scT.all, KC, w_mod[i], std_blocks(0, 6 * D), [(0, 2)], epi)
        mv = MOD[:].rearrange("p (s k c) -> p s k c", s=6, c=2)
        mav = MA[:].rearrange("p (n k c) -> p n k c", n=2, c=2)
        for n, (sidx, gname) in enumerate(((1, "n1g"), (4, "n2g"))):
            for col in range(2):
                kb.op("dve", lambda e, n=n, sidx=sidx, col=col, gname=gname: e.scalar_tensor_tensor(
                    out=mav[:, n, :, col], in0=mv[:, sidx, :, col], scalar=1.0, in1=V(gname, i * KC, KC),
                    op0=ALU.add, op1=ALU.mult), r=MOD.all, w=MA.all)
        return mv, mav

    def norm_mod(es2, si, mv, mav, n_idx, shift_idx, hT):
        tok0, n, col = SUPER[si]
        xs = kb.tile([P, KC * 256], F32, es=es2)
        sq = kb.tile([P, KC * 256], BF16, es=es2)
        rstd = kb.tile([P, 256], F32, es=es2)
        tmp = kb.tile([P, 256], F32, es=es2)
        xv = xs[:].rearrange("p (k t) -> p k t", t=256)
        sv = sq[:].rearrange("p (k t) -> p k t", t=256)
        hv = hT[:, 0:KC * n].rearrange("p (k t) -> p k t", k=KC)
        for s0 in range(0, n, 256):
            kb.dma("sp", xv, XR[:, tok0 + s0:tok0 + s0 + 256].rearrange("(k p) t -> p k t", p=P), xs,
                   r=[XRb[si]], w=xs.all)
            kb.op("act", lambda e: e.activation(out=sq[:], in_=xs[:], func=AF.Square), r=xs.all, w=sq.all)
            ps = nps()
            for kc in range(KC):
                kb.op("pe", lambda e, kc=kc: e.matmul(ps[:, 0:256], lhsT=ones[:, 0:P], rhs=sv[:, kc, :],
                                                      start=(kc == 0), stop=(kc == KC - 1)),
                      r=list(sq.all) + list(ones.all), w=ps.all, inc=(kc == KC - 1))
            kb.op("act", lambda e: e.activation(out=rstd[:], in_=ps[:, 0:256], func=AF.Sqrt, bias=V("eps"), scale=1.0 / D),
                  r=ps.all, w=rstd.all)
            kb.op("dve", lambda e: e.reciprocal(out=rstd[:], in_=rstd[:]), r=rstd.all, w=rstd.all)
            for kc in range(KC):
                kb.op("dve", lambda e, kc=kc: e.tensor_tensor(out=tmp[:], in0=xv[:, kc, :], in1=rstd[:], op=ALU.mult),
                      r=list(xs.all) + list(rstd.all), w=tmp.all)
                kb.op("act", lambda e, kc=kc: e.activation(out=hv[:, kc, s0:s0 + 256], in_=tmp[:], func=AF.Identity,
                                                           bias=mv[:, shift_idx, kc, col:col + 1],
                                                           scale=mav[:, n_idx, kc, col:col + 1]),
                      r=list(tmp.all) + list(MOD.all) + list(MA.all), w=hT.all)

    def ttiles_of(n):
        return [(t, min(512, n - t)) for t in range(0, n, 512)]

    def resid_epi(es2, si, mv, gate_idx):
        tok0, n, col = SUPER[si]
        xo = [kb.tile([P, 1024], F32, es=es2) for _ in range(2)]
        sel = [0]

        def epi(tag, pst):
            sel[0] ^= 1
            x = xo[sel[0]]
            kb.dma("sp", x[:, 0:n], XR[tag * P:(tag + 1) * P, tok0:tok0 + n], x, r=[XRb[si]], w=x.all)
            for ti, (t0, tn) in enumerate(ttiles_of(n)):
                ps = pst[ti]
                kb.op("dve", lambda e, ps=ps, t0=t0, tn=tn: e.scalar_tensor_tensor(
                    out=x[:, t0:t0 + tn], in0=ps[:, 0:tn], scalar=mv[:, gate_idx, tag, col:col + 1], in1=x[:, t0:t0 + tn],
                    op0=ALU.mult, op1=ALU.add), r=list(ps.all) + list(x.all) + list(MOD.all), w=x.all)
            kb.dma("sp", XR[tag * P:(tag + 1) * P, tok0:tok0 + n], x[:, 0:n], x, r=x.all, w=[XRb[si]])
        return epi

    def pass_out_ffn(i, si, mv, mav, Wout):
        tok0, n, col = SUPER[si]
        with kb.scope() as es2:
            zt = kb.tile([P, KC * 1024], BF16, es=es2)
            zv = zt[:, 0:KC * n].rearrange("p (k t) -> p k t", k=KC)
            kb.dma("sp", zv, ZT[:, tok0:tok0 + n].rearrange("(k p) t -> p k t", p=P), zt, r=[SCR], w=zt.all)
            linear(zv, zt.all, KC, Wout, std_blocks(0, D), ttiles_of(n), resid_epi(es2, si, mv, 2))
        with kb.scope() as es2:
            hT = kb.tile([P, KC * 1024], BF16, es=es2)
            hid = kb.tile([P, 44 * 1024], BF16, es=es2)
            sg = [kb.tile([P, 512], F32, es=es2) for _ in range(2)]
            sgs = [0]
            with kb.scope() as es3:
                norm_mod(es3, si, mv, mav, 1, 3, hT)
            hv = hT[:, 0:KC * n].rearrange("p (k t) -> p k t", k=KC)
            hidv = hid[:, 0:44 * n].rearrange("p (k t) -> p k t", k=44)
            pend = {}

            def epi_in(tag, pst):
                j, isup = tag
                if not isup:
                    pend[j] = pst
                    return
                gp = pend.pop(j)
                for ti, (t0, tn) in enumerate(ttiles_of(n)):
                    sgs[0] ^= 1
                    s = sg[sgs[0]]
                    kb.op("act", lambda e, s=s, g=gp[ti], tn=tn: e.activation(out=s[:, 0:tn], in_=g[:, 0:tn], func=AF.Silu),
                          r=gp[ti].all, w=s.all)
                    kb.op("dve", lambda e, s=s, u=pst[ti], t0=t0, tn=tn: e.tensor_tensor(
                        out=hidv[:, j, t0:t0 + tn], in0=s[:, 0:tn], in1=u[:, 0:tn], op=ALU.mult),
                        r=list(s.all) + list(pst[ti].all), w=hid.all)
            blocks = []
            for j in range(44):
                blocks.append(dict(ranges=[(j * P, P), (FF + j * P, P)], chunks=[(0, P, (j, 0)), (P, P, (j, 1))]))
            linear(hv, hT.all, KC, w_ffn_in[i], blocks, ttiles_of(n), epi_in)
            linear(hidv, hid.all, 44, w_ffn_out[i], std_blocks(0, D, bw=P), ttiles_of(n), resid_epi(es2, si, mv, 5))

    def fm_to_tm(es2, src_tile, n, dst, tok0, c0, tsb):
        nt = n // P
        psb = npsb()
        pv = psb[:, 0:nt * P].rearrange("p (a b) -> p a b", b=P)
        for tt in range(nt):
            kb.op("pe", lambda e, tt=tt: e.transpose(out=pv[:, tt, :], in_=src_tile[:, tt * P:(tt + 1) * P], identity=ident[:]),
                  r=list(src_tile.all) + list(ident.all), w=psb.all, inc=(tt == nt - 1))
        tv = tsb[:, 0:nt * P].rearrange("p (a b) -> p a b", b=P)
        kb.op("act", lambda e: e.copy(out=tsb[:, 0:nt * P], in_=psb[:, 0:nt * P]), r=psb.all, w=tsb.all)
        kb.dma("sp", dst[tok0:tok0 + n, c0:c0 + P].rearrange("(a p) c -> p a c", p=P), tv, tsb, r=tsb.all, w=[SCR])

    def hyena_pass_a(i, j, si, mv, mav):
        tok0, n, col = SUPER[si]
        rowlen = 256 if si == 0 else 64
        with kb.scope() as es2:
            hT = kb.tile([P, KC * 1024], BF16, es=es2)
            with kb.scope() as es3:
                norm_mod(es3, si, mv, mav, 0, 0, hT)
            hv = hT[:, 0:KC * n].rearrange("p (k t) -> p k t", k=KC)
            pc = kb.tile([P, 1024], F32, es=es2)
            o1 = kb.tile([P, 1024], F32, es=es2)
            ob = kb.tile([P, 1024], BF16, es=es2)
            tsb = kb.tile([P, 1024], BF16, es=es2)

            def epi(tag, pst):
                for ti, (t0, tn) in enumerate(ttiles_of(n)):
                    ps = pst[ti]
                    kb.op("act", lambda e, ps=ps, t0=t0, tn=tn: e.copy(out=pc[:, t0:t0 + tn], in_=ps[:, 0:tn]), r=ps.all, w=pc.all)
                    kb.op("act", lambda e, ps=ps, t0=t0, tn=tn: e.activation(
                        out=o1[:, t0:t0 + tn], in_=ps[:, 0:tn], func=AF.Identity, bias=0.0,
                        scale=V("cw", (j * 3 + 1) * 48 + tag)), r=ps.all, w=o1.all)
                o1v = o1[:, 0:n].rearrange("p (a b) -> p a b", b=rowlen)
                pcv = pc[:, 0:n].rearrange("p (a b) -> p a b", b=rowlen)
                kb.op("dve", lambda e: e.scalar_tensor_tensor(
                    out=o1v[:, :, 1:rowlen], in0=pcv[:, :, 0:rowlen - 1], scalar=V("cw", (j * 3 + 0) * 48 + tag),
                    in1=o1v[:, :, 1:rowlen], op0=ALU.mult, op1=ALU.add), r=list(pc.all) + list(o1.all), w=o1.all)
                kb.op("dve", lambda e: e.scalar_tensor_tensor(
                    out=o1v[:, :, 0:rowlen - 1], in0=pcv[:, :, 1:rowlen], scalar=V("cw", (j * 3 + 2) * 48 + tag),
                    in1=o1v[:, :, 0:rowlen - 1], op0=ALU.mult, op1=ALU.add), r=list(pc.all) + list(o1.all), w=o1.all)
                kb.op("dve", lambda e: e.tensor_copy(out=ob[:, 0:n], in_=o1[:, 0:n]), r=o1.all, w=ob.all)
                fm_to_tm(es2, ob, n, VT[tag // KC], tok0, (tag % KC) * P, tsb)
            linear(hv, hT.all, KC, hy_w_in[j], std_blocks(0, 3 * D), ttiles_of(n), epi)

    def range_reduce_sin(es2, arg, m, n, out_ap, tmp):
        MAGIC = 12582912.0
        kb.op("dve", lambda e: e.tensor_scalar(out=tmp[0:m, 0:n], in0=arg[0:m, 0:n], scalar1=1.0 / (2 * math.pi), scalar2=MAGIC,
                                               op0=ALU.mult, op1=ALU.add), r=arg.all, w=tmp.all)
        kb.op("dve", lambda e: e.tensor_scalar(out=tmp[0:m, 0:n], in0=tmp[0:m, 0:n], scalar1=-MAGIC, scalar2=None,
                                               op0=ALU.add), r=tmp.all, w=tmp.all)
        kb.op("dve", lambda e: e.scalar_tensor_tensor(out=arg[0:m, 0:n], in0=tmp[0:m, 0:n], scalar=-2 * math.pi, in1=arg[0:m, 0:n],
                                                      op0=ALU.mult, op1=ALU.add), r=list(tmp.all) + list(arg.all), w=arg.all)
        kb.op("dve", lambda e: e.tensor_scalar(out=arg[0:m, 0:n], in0=arg[0:m, 0:n], scalar1=-3.14159, scalar2=3.14159,
                                               op0=ALU.max, op1=ALU.min), r=arg.all, w=arg.all)
        kb.op("act", lambda e: e.activation(out=out_ap, in_=arg[0:m, 0:n], func=AF.Sin), r=arg.all, w=[])

    def hyena_filters(j, L):
        c = CL[L]
        TCn, KF = c["TC"], c["KF"]
        with kb.scope() as es2:
            zT = kb.tile([P, L], F32, es=es2)
            h1 = kb.tile([P, L], F32, es=es2)
            h2 = kb.tile([P, L], BF16, es=es2)
            fw1 = kb.tile([P, 64], F32, es=es2)
            fw2 = kb.tile([P, 64], F32, es=es2)
            fwo = kb.tile([P, 4 * D], BF16, es=es2)
            tn = kb.tile([P, TCn], F32, es=es2)
            negd = kb.tile([P, 512], F32, es=es2)
            skp = kb.tile([P, 2 * D], F32, es=es2)
            fb = kb.tile([P, 2], F32, es=es2)
            arg = kb.tile([P, 512], F32, es=es2)
            tmp = kb.tile([P, 512], F32, es=es2)
            kb.dma("sp", zT[0:33, :], c["zT"], zT, w=zT.all)
            kb.dma("sp", fw1[0:33, :], hy_fw1[j], fw1, w=fw1.all)
            kb.dma("sp", fw2[0:64, :], hy_fw2[j], fw2, w=fw2.all)
            kb.dma("pool", fwo[0:64, :], hy_fwout[j], fwo, w=fwo.all)
            kb.dma("sp", tn[:], c["tn"], tn, w=tn.all)
            kb.dma("sp", skp[:], fskip[j], skp, w=skp.all)
            for q in range(2):
                kb.op("dve", lambda e, q=q: e.tensor_tensor(out=fb[0:64, q:q + 1], in0=V("ffreq", j * 2 + q)[0:64, :],
                                                             in1=V("fbias", j * 2 + q)[0:64, :], op=ALU.mult), r=vecs.all, w=fb.all)
            for t0 in range(0, L, 512):
                tn_ = min(512, L - t0)
                ps = nps()
                kb.op("pe", lambda e: e.matmul(ps[0:64, 0:tn_], lhsT=fw1[0:33, 0:64], rhs=zT[0:33, t0:t0 + tn_], start=True, stop=True),
                      r=list(fw1.all) + list(zT.all), w=ps.all)
                kb.op("act", lambda e: e.activation(out=arg[0:64, 0:tn_], in_=ps[0:64, 0:tn_], func=AF.Identity,
                                                    bias=fb[0:64, 0:1], scale=V("ffreq", j * 2 + 0)[0:64, :]),
                      r=list(ps.all) + list(fb.all), w=arg.all)
                range_reduce_sin(es2, arg, 64, tn_, h1[0:64, t0:t0 + tn_], tmp)
                kb._mark((kb.sem["act"], kb.cnt["act"], "act"), [], h1.all)
            for t0 in range(0, L, 512):
                tn_ = min(512, L - t0)
                ps = nps()
                kb.op("pe", lambda e: e.matmul(ps[0:64, 0:tn_], lhsT=fw2[0:64, 0:64], rhs=h1[0:64, t0:t0 + tn_], start=True, stop=True),
                      r=list(fw2.all) + list(h1.all), w=ps.all)
                kb.op("act", lambda e: e.activation(out=arg[0:64, 0:tn_], in_=ps[0:64, 0:tn_], func=AF.Identity,
                                                    bias=fb[0:64, 1:2], scale=V("ffreq", j * 2 + 1)[0:64, :]),
                      r=list(ps.all) + list(fb.all), w=arg.all)
                range_reduce_sin(es2, arg, 64, tn_, h2[0:64, t0:t0 + tn_], tmp)
                kb._mark((kb.sem["act"], kb.cnt["act"], "act"), [], h2.all)
            HF = kb.tile([P, 2 * TCn * 512], BF16, es=es2)
            hfv = HF[:].rearrange("p (k c) -> p k c", c=512)
            dec = [kb.tile([P, 512], F32, es=es2) for _ in range(2)]
            ab = [kb.tile([P, 512], BF16, es=es2) for _ in range(2)]
            rn = kb.tile([P, 512], F32, es=es2)
            ho = [kb.tile([P, 512], F32, es=es2) for _ in range(2)]
            sel = [0]
            for o in range(2):
                for ct in range(4):
                    kb.dma("sp", negd[:], C_negd[:, ct * 512:(ct + 1) * 512], negd, w=negd.all)
                    nps_ = nps()
                    first = True
                    for tc in range(TCn):
                        d = dec[tc % 2]
                        kb.op("act", lambda e, d=d, tc=tc: e.activation(out=d[:], in_=negd[:], func=AF.Exp,
                                                                        scale=tn[:, tc:tc + 1]), r=list(negd.all) + list(tn.all), w=d.all)
                        for dr in range(2):
                            ps = nps()
                            if ps is nps_:
                                ps = nps()
                            col0 = (dr * 2 + o) * D + ct * 512
                            kb.op("pe", lambda e, ps=ps, tc=tc, col0=col0: e.matmul(
                                ps[:, :], lhsT=h2[0:64, tc * P:(tc + 1) * P], rhs=fwo[0:64, col0:col0 + 512], start=True, stop=True),
                                r=list(h2.all) + list(fwo.all), w=ps.all)
                            kk = dr * TCn + tc
                            kb.op("dve", lambda e, ps=ps, d=d, kk=kk: e.tensor_tensor(out=hfv[:, kk, :], in0=ps[:, :], in1=d[:], op=ALU.mult),
                                  r=list(ps.all) + list(d.all), w=HF.all)
                            sel[0] ^= 1
                            a = ab[sel[0]]
                            kb.op("act", lambda e, a=a, kk=kk: e.activation(out=a[:], in_=hfv[:, kk, :], func=AF.Abs),
                                  r=HF.all, w=a.all)
                            last = (dr == 1 and tc == TCn - 1)
                            oc0 = P if last else 0
                            kb.op("pe", lambda e, a=a, oc0=oc0, first=first, last=last: e.matmul(
                                nps_[:, :], lhsT=ones[:, oc0:oc0 + P], rhs=a[:], start=first, stop=last),
                                r=list(a.all) + list(ones.all), w=nps_.all, inc=last)
                            first = False
                    kb.op("dve", lambda e: e.reciprocal(out=rn[:], in_=nps_[:, :]), r=nps_.all, w=rn.all)
                    for blk in range(2 * KF):
                        wt, view = load_w(None, 2 * TCn, [(0, P)], cast=False,
                                          pre=c["FF"][blk].rearrange("p (k n) -> p k n", n=P))
                        ps = nps()
                        for kk in range(2 * TCn):
                            kb.op("pe", lambda e, kk=kk: e.matmul(ps[:, :], lhsT=view[:, kk, :], rhs=hfv[:, kk, :],
                                                                  start=(kk == 0), stop=(kk == 2 * TCn - 1)),
                                  r=list(wt.all) + list(HF.all), w=ps.all, inc=(kk == 2 * TCn - 1))
                        sel[0] ^= 1
                        h = ho[sel[0]]
                        kb.op("dve", lambda e, h=h: e.tensor_tensor(out=h[:], in0=ps[:, :], in1=rn[:], op=ALU.mult),
                              r=list(ps.all) + list(rn.all), w=h.all)
                        if blk < KF:
                            kb.op("pool", lambda e, h=h: e.tensor_tensor(out=h[:], in0=h[:], in1=skp[:, o * D + ct * 512:o * D + (ct + 1) * 512],
                                                                         op=ALU.add), r=list(h.all) + list(skp.all), w=h.all)
                        kb.dma("sp", c["HS"][o, blk, :, ct * 512:(ct + 1) * 512], h[:], h, r=h.all, w=[SCR])

    def hyena_conv(L, tok0):
        c = CL[L]
        TCn, KF = c["TC"], c["KF"]
        with kb.scope() as es2:
            Vt = kb.tile([P, TCn * 512], BF16, es=es2)
            Z1 = kb.tile([P, TCn * 512], BF16, es=es2)
            Y = kb.tile([P, 2 * KF * 512], BF16, es=es2)
            Hr = [kb.tile([P, 512], F32, es=es2) for _ in range(2)]
            Hi = [kb.tile([P, 512], F32, es=es2) for _ in range(2)]
            t1 = kb.tile([P, 512], F32, es=es2)
            t2 = kb.tile([P, 512], F32, es=es2)
            xm = [kb.tile([P, 512], BF16, es=es2) for _ in range(2)]
            zb = kb.tile([P, 512], BF16, es=es2)
            tsb = kb.tile([P, 512], BF16, es=es2)
            vv = Vt[:].rearrange("p (k c) -> p k c", c=512)
            z1v = Z1[:].rearrange("p (k c) -> p k c", c=512)
            yv = Y[:].rearrange("p (k c) -> p k c", c=512)
            sel = [0]
            for ct in range(4):
                c0 = ct * 512
                kb.dma("sp", vv, VT[0, tok0:tok0 + L, c0:c0 + 512].rearrange("(k p) c -> p k c", p=P), Vt, r=[SCR], w=Vt.all)
                for o in range(2):
                    src, srcv = (Vt, vv) if o == 0 else (Z1, z1v)
                    for m in range(KF):
                        wt, view = load_w(None, TCn, [(0, 256)], cast=False, pre=c["Ff"][m].rearrange("p (k n) -> p k n", n=256))
                        pr, pi = nps(), nps()
                        for part, ps in ((0, pr), (1, pi)):
                            for kk in range(TCn):
                                kb.op("pe", lambda e, ps=ps, kk=kk, part=part: e.matmul(
                                    ps[:, :], lhsT=view[:, kk, part * P:(part + 1) * P], rhs=srcv[:, kk, :],
                                    start=(kk == 0), stop=(kk == TCn - 1)), r=list(wt.all) + list(src.all), w=ps.all, inc=(kk == TCn - 1))
                        sel[0] ^= 1
                        hr, hi = Hr[sel[0]], Hi[sel[0]]
                        kb.dma("sp", hr[:], c["HS"][o, m, :, c0:c0 + 512], hr, r=[SCR], w=hr.all)
                        kb.dma("sp", hi[:], c["HS"][o, KF + m, :, c0:c0 + 512], hi, r=[SCR], w=hi.all)
                        kb.op("dve", lambda e: e.tensor_tensor(out=t1[:], in0=pr[:, :], in1=hr[:], op=ALU.mult), r=list(pr.all) + list(hr.all), w=t1.all)
                        kb.op("dve", lambda e: e.tensor_tensor(out=t2[:], in0=pi[:, :], in1=hi[:], op=ALU.mult), r=list(pi.all) + list(hi.all), w=t2.all)
                        kb.op("pool", lambda e, m=m: e.tensor_tensor(out=yv[:, m, :], in0=t1[:], in1=t2[:], op=ALU.subtract),
                              r=list(t1.all) + list(t2.all), w=Y.all)
                        kb.op("dve", lambda e: e.tensor_tensor(out=t1[:], in0=pr[:, :], in1=hi[:], op=ALU.mult), r=list(pr.all) + list(hi.all), w=t1.all)
                        kb.op("dve", lambda e: e.tensor_tensor(out=t2[:], in0=pi[:, :], in1=hr[:], op=ALU.mult), r=list(pi.all) + list(hr.all), w=t2.all)
                        kb.op("pool", lambda e, m=m: e.tensor_tensor(out=yv[:, KF + m, :], in0=t1[:], in1=t2[:], op=ALU.add),
                              r=list(t1.all) + list(t2.all), w=Y.all)
                    for tc in range(TCn):
                        wt, view = load_w(None, 2 * KF, [(0, P)], cast=False, pre=c["G"][tc].rearrange("p (k n) -> p k n", n=P))
                        ps = nps()
                        for kk in range(2 * KF):
                            kb.op("pe", lambda e, kk=kk: e.matmul(ps[:, :], lhsT=view[:, kk, :], rhs=yv[:, kk, :],
                                                                  start=(kk == 0), stop=(kk == 2 * KF - 1)),
                                  r=list(wt.all) + list(Y.all), w=ps.all, inc=(kk == 2 * KF - 1))
                        sel[0] ^= 1
                        x = xm[sel[0]]
                        kb.dma("sp", x[:], VT[1 + o, tok0 + tc * P:tok0 + (tc + 1) * P, c0:c0 + 512], x, r=[SCR], w=x.all)
                        if o == 0:
                            kb.op("dve", lambda e, tc=tc: e.tensor_tensor(out=z1v[:, tc, :], in0=ps[:, :], in1=x[:], op=ALU.mult),
                                  r=list(ps.all) + list(x.all), w=Z1.all)
                        else:
                            kb.op("dve", lambda e: e.tensor_tensor(out=zb[:], in0=ps[:, :], in1=x[:], op=ALU.mult),
                                  r=list(ps.all) + list(x.all), w=zb.all)
                            psb = npsb()
                            pv = psb[:, 0:512].rearrange("p (a b) -> p a b", b=P)
                            for cc in range(4):
                                kb.op("pe", lambda e, cc=cc: e.transpose(out=pv[:, cc, :], in_=zb[:, cc * P:(cc + 1) * P], identity=ident[:]),
                                      r=list(zb.all) + list(ident.all), w=psb.all, inc=(cc == 3))
                            kb.op("act", lambda e: e.copy(out=tsb[:], in_=psb[:, 0:512]), r=psb.all, w=tsb.all)
                            kb.dma("sp", ZT[c0:c0 + 512, tok0 + tc * P:tok0 + (tc + 1) * P].rearrange("(a p) t -> p a t", p=P),
                                   tsb[:].rearrange("p (a b) -> p a b", b=P), tsb, r=tsb.all, w=[SCR])

    def gate_chain(es2, T, q_sb, k_of_dir, lf, dr, row0, tok0, n, U=P):
        nu = n // U
        EXd = EX if U == P else EX64
        smk = smask if U == P else smask64
        cs, d, e1, ob, ctot, nct, ex = T["cs"], T["d"], T["e1"], T["ob"], T["ctot"], T["nct"], T["ex"]
        kb.op("dve", lambda e: e.tensor_tensor_scan(out=cs[:, 0:n], data0=smk[:, 0:n], data1=lf[:, 0:n], initial=0.0,
                                                    op0=ALU.mult, op1=ALU.add), r=list(lf.all) + list(smk.all), w=cs.all)
        csv = cs[:, 0:n].rearrange("p (u t) -> p u t", t=U)
        kb.op("dve", lambda e: e.tensor_copy(out=ctot[:, 0:nu], in_=csv[:, :, U - 1]), r=cs.all, w=ctot.all)
        kb.op("dve", lambda e: e.tensor_scalar(out=nct[:, 0:nu], in0=ctot[:, 0:nu], scalar1=-1.0, scalar2=None, op0=ALU.mult),
              r=ctot.all, w=nct.all)
        kb.op("act", lambda e: e.activation(out=ex[:, 0:nu], in_=ctot[:, 0:nu], func=AF.Exp), r=ctot.all, w=ex.all)
        u0 = tok0 // U
        kb.dma("sp", EXd[dr, row0:row0 + P, u0:u0 + nu], ex[:, 0:nu], ex, r=ex.all, w=[SCR])

        def emit(dst, src_mul, make_exp):
            make_exp()
            kb.op("dve", lambda e: e.tensor_tensor(out=ob[:, 0:n], in0=src_mul[:, 0:n], in1=e1[:, 0:n], op=ALU.mult),
                  r=list(src_mul.all) + list(e1.all), w=ob.all)
            kb.dma("sp", dst[dr, row0:row0 + P, tok0:tok0 + n], ob[:, 0:n], ob, r=ob.all, w=[SCR])

        def full_exp(src, scale):
            return lambda: kb.op("act", lambda e: e.activation(out=e1[:, 0:n], in_=src[:, 0:n], func=AF.Exp, scale=scale),
                                 r=src.all, w=e1.all)

        def unit_exp(src, scale, bias_t):
            def f():
                for u in range(nu):
                    kb.op("act", lambda e, u=u: e.activation(out=e1[:, u * U:(u + 1) * U], in_=src[:, u * U:(u + 1) * U], func=AF.Exp,
                                                             scale=scale, bias=bias_t[:, u:u + 1]),
                          r=list(src.all) + list(bias_t.all), w=e1.all)
            return f
        if dr == 0:
            emit(QT, q_sb, full_exp(cs, 1.0))
            emit(KTs, k_of_dir, full_exp(cs, -1.0))
            emit(KHs, k_of_dir, unit_exp(cs, -1.0, ctot))
        else:
            kb.op("dve", lambda e: e.tensor_tensor(out=d[:, 0:n], in0=lf[:, 0:n], in1=cs[:, 0:n], op=ALU.subtract),
                  r=list(lf.all) + list(cs.all), w=d.all)
            emit(QT, q_sb, unit_exp(d, 1.0, ctot))
            emit(KTs, k_of_dir, unit_exp(d, -1.0, nct))
            emit(KHs, k_of_dir, full_exp(d, -1.0))

    def chain_tiles(es2):
        T = {}
        for nm in ("cs", "d", "e1"):
            T[nm] = kb.tile([P, 1024], F32, es=es2)
        T["ob"] = kb.tile([P, 1024], BF16, es=es2)
        for nm in ("ctot", "nct", "ex"):
            T[nm] = kb.tile([P, 16], F32, es=es2)
        return T

    def hgrn_pass_a(i, si, mv, mav, lbt):
        tok0, n, col = SUPER[si]
        with kb.scope() as es2:
            hT = kb.tile([P, KC * 1024], BF16, es=es2)
            with kb.scope() as es3:
                norm_mod(es3, si, mv, mav, 0, 0, hT)
            hv = hT[:, 0:KC * n].rearrange("p (k t) -> p k t", k=KC)
            T = chain_tiles(es2)
            q_sb = kb.tile([P, 1024], F32, es=es2)
            fv = kb.tile([P, 1024], F32, es=es2)
            lf = kb.tile([P, 1024], F32, es=es2)
            k_sb = kb.tile([P, 1024], F32, es=es2)
            ob2 = kb.tile([P, 1024], BF16, es=es2)
            tsb = kb.tile([P, 1024], BF16, es=es2)

            def epi(tag, pst):
                kind, h = tag
                for ti, (t0, tn) in enumerate(ttiles_of(n)):
                    ps = pst[ti]
                    if kind == "q":
                        kb.op("act", lambda e, ps=ps, t0=t0, tn=tn: e.activation(out=q_sb[:, t0:t0 + tn], in_=ps[:, 0:tn], func=AF.Silu),
                              r=ps.all, w=q_sb.all)
                    elif kind == "i":
                        kb.op("act", lambda e, ps=ps, t0=t0, tn=tn: e.copy(out=ob2[:, t0:t0 + tn], in_=ps[:, 0:tn]), r=ps.all, w=ob2.all)
                    elif kind == "g":
                        kb.op("act", lambda e, ps=ps, t0=t0, tn=tn: e.activation(out=ob2[:, t0:t0 + tn], in_=ps[:, 0:tn], func=AF.Silu),
                              r=ps.all, w=ob2.all)
                    else:
                        kb.op("act", lambda e, ps=ps, t0=t0, tn=tn: e.activation(out=fv[:, t0:t0 + tn], in_=ps[:, 0:tn], func=AF.Sigmoid),
                              r=ps.all, w=fv.all)
                if kind == "i":
                    fm_to_tm(es2, ob2, n, VT[0], tok0, h * P, tsb)
                elif kind == "g":
                    kb.dma("sp", GATE[h * P:(h + 1) * P, tok0:tok0 + n], ob2[:, 0:n], ob2, r=ob2.all, w=[SCR])
                elif kind in ("f0", "f1"):
                    dr = 0 if kind == "f0" else 1
                    kb.op("dve", lambda e: e.tensor_scalar(out=fv[:, 0:n], in0=fv[:, 0:n], scalar1=lbt[:, (2 + dr) * KC + h:(2 + dr) * KC + h + 1],
                                                           scalar2=lbt[:, dr * KC + h:dr * KC + h + 1], op0=ALU.mult, op1=ALU.add),
                          r=list(fv.all) + list(lbt.all), w=fv.all)
                    kb.op("act", lambda e: e.activation(out=lf[:, 0:n], in_=fv[:, 0:n], func=AF.Ln), r=fv.all, w=lf.all)
                    kb.op("dve", lambda e: e.tensor_scalar(out=k_sb[:, 0:n], in0=fv[:, 0:n], scalar1=-1.0, scalar2=1.0, op0=ALU.mult, op1=ALU.add),
                          r=fv.all, w=k_sb.all)
                    gate_chain(es2, T, q_sb, k_sb, lf, dr, h * P, tok0, n, U=64)
            blocks = []
            for h in range(KC):
                blocks.append(dict(ranges=[(h * P, P), (3 * D + h * P, P), (4 * D + h * P, P)],
                                   chunks=[(0, P, ("q", h)), (P, P, ("f0", h)), (2 * P, P, ("f1", h))]))
            for h in range(0, KC, 4):
                blocks.append(dict(ranges=[(D + h * P, 512)], chunks=[(o * P, P, ("i", h + o)) for o in range(4)]))
            for h in range(0, KC, 4):
                blocks.append(dict(ranges=[(2 * D + h * P, 512)], chunks=[(o * P, P, ("g", h + o)) for o in range(4)]))
            linear(hv, hT.all, KC, hg_w_in[0], blocks, ttiles_of(n), epi)

    def gla_pass_a(i, si, mv, mav, wup):
        tok0, n, col = SUPER[si]
        with kb.scope() as es2:
            hT = kb.tile([P, KC * 1024], BF16, es=es2)
            with kb.scope() as es3:
                norm_mod(es3, si, mv, mav, 0, 0, hT)
            hv = hT[:, 0:KC * n].rearrange("p (k t) -> p k t", k=KC)
            T = chain_tiles(es2)
            q_sb = kb.tile([P, 1024], F32, es=es2)
            k_sb = kb.tile([P, 1024], F32, es=es2)
            lf = kb.tile([P, 1024], F32, es=es2)
            aT = [kb.tile([P, 1024], BF16, es=es2) for _ in range(2)]
            ob2 = kb.tile([P, 1024], BF16, es=es2)
            tsb = kb.tile([P, 1024], BF16, es=es2)

            def epi(tag, pst):
                kind, h = tag
                for ti, (t0, tn) in enumerate(ttiles_of(n)):
                    ps = pst[ti]
                    if kind == "a":
                        kb.op("act", lambda e, ps=ps, t0=t0, tn=tn: e.copy(out=aT[h][0:16, t0:t0 + tn], in_=ps[0:16, 0:tn]), r=ps.all, w=aT[h].all)
                    elif kind == "q":
                        kb.op("act", lambda e, ps=ps, t0=t0, tn=tn: e.activation(out=q_sb[:, t0:t0 + tn], in_=ps[:, 0:tn], func=AF.Identity, scale=1.0 / 16.0),
                              r=ps.all, w=q_sb.all)
                    elif kind == "k":
                        kb.op("act", lambda e, ps=ps, t0=t0, tn=tn: e.copy(out=k_sb[:, t0:t0 + tn], in_=ps[:, 0:tn]), r=ps.all, w=k_sb.all)
                    elif kind == "v":
                        kb.op("act", lambda e, ps=ps, t0=t0, tn=tn: e.copy(out=ob2[:, t0:t0 + tn], in_=ps[:, 0:tn]), r=ps.all, w=ob2.all)
                    elif kind == "g":
                        kb.op("act", lambda e, ps=ps, t0=t0, tn=tn: e.activation(out=ob2[:, t0:t0 + tn], in_=ps[:, 0:tn], func=AF.Silu),
                              r=ps.all, w=ob2.all)
                if kind == "v":
                    fm_to_tm(es2, ob2, n, VT[0], tok0, h * P, tsb)
                elif kind == "g":
                    kb.dma("sp", GATE[h * P:(h + 1) * P, tok0:tok0 + n], ob2[:, 0:n], ob2, r=ob2.all, w=[SCR])
                elif kind == "k":
                    for dr in range(2):
                        for ti, (t0, tn) in enumerate(ttiles_of(n)):
                            ps = nps()
                            kb.op("pe", lambda e, ps=ps, t0=t0, tn=tn: e.matmul(
                                ps[:, 0:tn], lhsT=wup[0:16, dr * 1024 + h * P:dr * 1024 + (h + 1) * P], rhs=aT[dr][0:16, t0:t0 + tn],
                                start=True, stop=True), r=list(wup.all) + list(aT[dr].all), w=ps.all)
                            kb.op("act", lambda e, ps=ps, t0=t0, tn=tn: e.activation(
                                out=lf[:, t0:t0 + tn], in_=ps[:, 0:tn], func=AF.Exp, scale=-1.0, bias=V("nbup", dr * 8 + h)),
                                r=ps.all, w=lf.all)
                        kb.op("act", lambda e: e.activation(out=lf[:, 0:n], in_=lf[:, 0:n], func=AF.Ln, bias=1.0, scale=1.0), r=lf.all, w=lf.all)
                        kb.op("dve", lambda e: e.tensor_scalar(out=lf[:, 0:n], in0=lf[:, 0:n], scalar1=-1.0 / 16.0, scalar2=None, op0=ALU.mult),
                              r=lf.all, w=lf.all)
                        gate_chain(es2, T, q_sb, k_sb, lf, dr, h * P, tok0, n)
            blocks = [dict(ranges=[(6144, 16), (6160, 16)], chunks=[(0, 16, ("a", 0)), (16, 16, ("a", 1))])]
            for h in range(8):
                blocks.append(dict(ranges=[(h * P, P), (1024 + h * P, P)], chunks=[(0, P, ("q", h)), (P, P, ("k", h))]))
            for h in range(0, KC, 4):
                blocks.append(dict(ranges=[(2048 + h * P, 512)], chunks=[(o * P, P, ("v", h + o)) for o in range(4)]))
            for h in range(0, KC, 4):
                blocks.append(dict(ranges=[(4096 + h * P, 512)], chunks=[(o * P, P, ("g", h + o)) for o in range(4)]))
            linear(hv, hT.all, KC, gla_w_in[0], blocks, ttiles_of(n), epi)

    def scan_core(H, DKC, DV, gname, U=P):
        NUu = TT // U
        EXd = EX if U == P else EX64
        with kb.scope() as es2:
            qt = kb.tile([P, DKC * TT], BF16, es=es2)
            kt = kb.tile([P, DKC * TT], BF16, es=es2)
            khu = [kb.tile([P, DKC * P], BF16, es=es2) for _ in range(2)]
            vt = kb.tile([P, NUu * DV], BF16, es=es2)
            oacc = kb.tile([P, NUu * DV], F32, NUu, es=es2)
            ext = kb.tile([P, DKC * NUu], F32, es=es2)
            S = [kb.tile([P, DV], F32, es=es2) for _ in range(DKC)]
            Sb = [kb.tile([P, DV], BF16, es=es2) for _ in range(DKC)]
            asb = [kb.tile([P, P], BF16, es=es2) for _ in range(2)]
            khs = [kb.tile([P, DKC * P], BF16, es=es2) for _ in range(2)]
            sq = kb.tile([P, DV], F32, es=es2)
            ssq = kb.tile([P, 1], F32, es=es2)
            on = kb.tile([P, DV], BF16, es=es2)
            gt = [kb.tile([P, P], BF16, es=es2) for _ in range(2)]
            osb = [kb.tile([P, P], BF16, es=es2) for _ in range(2)]
            sel = [0]
            qv = qt[:].rearrange("p (k t) -> p k t", k=DKC)
            kv = kt[:].rearrange("p (k t) -> p k t", k=DKC)
            vv = vt[:].rearrange("p (u v) -> p u v", v=DV)
            ov = oacc[:].rearrange("p (u v) -> p u v", v=DV)
            exv = ext[:].rearrange("p (k u) -> p k u", k=DKC)
            for h in range(H):
                r0 = h * DKC * P
                kb.dma("sp", vv[0:U], VT[0, :, h * DV:(h + 1) * DV].rearrange("(u p) v -> p u v", p=U), vt, r=[SCR], w=vt.all)
                for dr in range(2):
                    for (t, vw, src) in ((qt, qv, QT), (kt, kv, KTs)):
                        kb.dma("sp", vw, src[dr, r0:r0 + DKC * P, :].rearrange("(k p) t -> p k t", p=P), t, r=[SCR], w=t.all)
                    kb.dma("sp", exv, EXd[dr, r0:r0 + DKC * P, :].rearrange("(k p) u -> p k u", p=P), ext, r=[SCR], w=ext.all)
                    for k in range(DKC):
                        kb.op("dve", lambda e, k=k: e.memset(S[k][:], 0.0), w=S[k].all)
                        kb.op("pool", lambda e, k=k: e.memset(Sb[k][:], 0.0), w=Sb[k].all)
                    nc_ = TCX // U
                    order = list(range(NUu)) if dr == 0 else list(range(nc_ - 1, -1, -1)) + list(range(NUu - 1, nc_ - 1, -1))
                    for u in order:
                        ts = slice(u * U, (u + 1) * U)
                        sel[0] ^= 1
                        a, khb = asb[sel[0]], khs[sel[0]]
                        pa = nps()
                        for k in range(DKC):
                            kb.op("pe", lambda e, k=k: e.matmul(pa[0:U, 0:U], lhsT=kv[:, k, ts], rhs=qv[:, k, ts], start=(k == 0), stop=(k == DKC - 1)),
                                  r=list(kt.all) + list(qt.all), w=pa.all, inc=(k == DKC - 1))
                        kb.op("dve", lambda e: e.tensor_tensor(out=a[0:U, 0:U], in0=pa[0:U, 0:U], in1=tri[0:U, dr * P:dr * P + U], op=ALU.mult),
                              r=list(pa.all) + list(tri.all), w=a.all)
                        po = nps()
                        kb.op("pe", lambda e: e.matmul(po[0:U, 0:DV], lhsT=a[0:U, 0:U], rhs=vv[0:U, u, :], start=True, stop=False),
                              r=list(a.all) + list(vt.all), w=po.all, inc=False)
                        for k in range(DKC):
                            kb.op("pe", lambda e, k=k: e.matmul(po[0:U, 0:DV], lhsT=qv[:, k, ts], rhs=Sb[k][:], start=False, stop=(k == DKC - 1)),
                                  r=list(qt.all) + list(Sb[k].all), w=po.all, inc=(k == DKC - 1))
                        if dr == 0:
                            kb.op("act", lambda e: e.copy(out=ov[0:U, u, :], in_=po[0:U, 0:DV]), r=po.all, w=[oacc.parts[u]])
                        else:
                            kb.op("dve", lambda e: e.tensor_tensor(out=ov[0:U, u, :], in0=po[0:U, 0:DV], in1=ov[0:U, u, :], op=ALU.add),
                                  r=list(po.all) + [oacc.parts[u]], w=[oacc.parts[u]])
                        kht = khu[sel[0]]
                        khv = kht[:, 0:DKC * U].rearrange("p (k t) -> p k t", k=DKC)
                        kb.dma("sp", khv, KHs[dr, r0:r0 + DKC * P, u * U:(u + 1) * U].rearrange("(k p) t -> p k t", p=P), kht, r=[SCR], w=kht.all)
                        psb = npsb()
                        pv = psb[:, 0:DKC * P].rearrange("p (k d) -> p k d", d=P)
                        for k in range(DKC):
                            kb.op("pe", lambda e, k=k: e.transpose(out=pv[0:U, k, :], in_=khv[:, k, :], identity=ident[:]),
                                  r=list(kht.all) + list(ident.all), w=psb.all, inc=(k == DKC - 1))
                        kb.op("act", lambda e: e.copy(out=khb[0:U, :], in_=psb[0:U, 0:DKC * P]), r=psb.all, w=khb.all)
                        for k in range(DKC):
                            pd = nps()
                            kb.op("pe", lambda e, k=k, pd=pd: e.matmul(pd[:, 0:DV], lhsT=khb[0:U, k * P:(k + 1) * P], rhs=vv[0:U, u, :], start=True, stop=True),
                                  r=list(khb.all) + list(vt.all), w=pd.all)
                            kb.op("dve", lambda e, k=k, pd=pd: e.scalar_tensor_tensor(
                                out=S[k][:], in0=S[k][:], scalar=exv[:, k, u:u + 1], in1=pd[:, 0:DV], op0=ALU.mult, op1=ALU.add),
                                r=list(S[k].all) + list(pd.all) + list(ext.all), w=S[k].all)
                            kb.op("act", lambda e, k=k: e.copy(out=Sb[k][:], in_=S[k][:]), r=S[k].all, w=Sb[k].all)
                for u in range(NUu):
                    kb.op("act", lambda e, u=u: e.activation(out=sq[0:U, :], in_=ov[0:U, u, :], func=AF.Square),
                          r=[oacc.parts[u]], w=list(sq.all))
                    kb.op("dve", lambda e: e.reduce_sum(out=ssq[0:U, :], in_=sq[0:U, :], axis=mybir.AxisListType.X), r=sq.all, w=ssq.all)
                    kb.op("act", lambda e: e.activation(out=ssq[0:U, :], in_=ssq[0:U, :], func=AF.Sqrt, bias=V("eps")[0:U, :], scale=1.0 / DV), r=ssq.all, w=ssq.all)
                    kb.op("dve", lambda e: e.reciprocal(out=ssq[0:U, :], in_=ssq[0:U, :]), r=ssq.all, w=ssq.all)
                    kb.op("dve", lambda e, u=u: e.tensor_scalar(out=on[0:U, :], in0=ov[0:U, u, :], scalar1=ssq[0:U, 0:1], scalar2=None, op0=ALU.mult),
                          r=[oacc.parts[u]] + list(ssq.all), w=on.all)
                    for cb in range(DV // P):
                        ch = (h * DV) // P + cb
                        sel[0] ^= 1
                        g, ob = gt[sel[0]], osb[sel[0]]
                        kb.dma("sp", g[:, 0:U], GATE[ch * P:(ch + 1) * P, u * U:(u + 1) * U], g, r=[SCR], w=g.all)
                        psb = npsb()
                        kb.op("pe", lambda e, cb=cb: e.transpose(out=psb[:, 0:U], in_=on[0:U, cb * P:(cb + 1) * P], identity=ident[0:U, 0:U]),
                              r=list(on.all) + list(ident.all), w=psb.all)
                        kb.op("dve", lambda e, ch=ch, g=g, ob=ob: e.scalar_tensor_tensor(
                            out=ob[:, 0:U], in0=psb[:, 0:U], scalar=V(gname, ch), in1=g[:, 0:U], op0=ALU.mult, op1=ALU.mult),
                            r=list(psb.all) + list(g.all), w=ob.all)
                        kb.dma("sp", ZT[ch * P:(ch + 1) * P, u * U:(u + 1) * U], ob[:, 0:U], ob, r=ob.all, w=[SCR])

    with kb.scope() as es2:
        cp = [kb.tile([P, 4352], F32, es=es2) for _ in range(2)]
        for r in range(KC):
            t = cp[r % 2]
            kb.dma("sp", t[:], XIN[r * P:(r + 1) * P, :], t, w=t.all)
            kb.dma("sp", XR[r * P:(r + 1) * P, :], t[:], t, r=t.all, w=XRb)
    kb.barrier()

    for i in range(nlayers):
        kind, j = i % 3, i // 3
        last = i == DEPTH - 1
        sis = list(range(len(SUPER)))
        if last and kind == 0:
            sis = sis[1:]
        mv, mav = pass_mod(i)
        if kind == 0:
            for si in sis:
                hyena_pass_a(i, j, si, mv, mav)
            kb.barrier()
            hyena_filters(j, TL)
            kb.barrier()
            hyena_conv(TL, TCX)
            kb.barrier()
            if 0 in sis:
                hyena_filters(j, TCX)
                kb.barrier()
                hyena_conv(TCX, 0)
                kb.barrier()
            Wout = hy_w_out[j]
        elif kind == 1:
            with kb.scope() as esl:
                lbt = kb.tile([P, 4 * KC], F32, es=esl)
                ee = kb.tile([P, 2 * 4 * KC], F32, es=esl)
                sm = kb.tile([P, 2 * KC], F32, es=esl)
                kb.op("act", lambda e: e.activation(out=ee[:], in_=V("lbl", 0, 2 * 4 * KC), func=AF.Exp), r=vecs.all, w=ee.all)
                ev = ee[:].rearrange("p (d l k) -> p d l k", d=2, l=4)
                smv = sm[:].rearrange("p (d k) -> p d k", d=2)
                kb.op("dve", lambda e: e.tensor_tensor(out=smv, in0=ev[:, :, 0, :], in1=ev[:, :, 1, :], op=ALU.add), r=ee.all, w=sm.all)
                kb.op("dve", lambda e: e.tensor_tensor(out=smv, in0=smv, in1=ev[:, :, 2, :], op=ALU.add), r=list(ee.all) + list(sm.all), w=sm.all)
                kb.op("dve", lambda e: e.tensor_tensor(out=smv, in0=smv, in1=ev[:, :, 3, :], op=ALU.add), r=list(ee.all) + list(sm.all), w=sm.all)
                kb.op("dve", lambda e: e.reciprocal(out=sm[:], in_=sm[:]), r=sm.all, w=sm.all)
                lv = lbt[:].rearrange("p (a d k) -> p a d k", a=2, d=2)
                kb.op("dve", lambda e: e.tensor_tensor(out=lv[:, 0], in0=ev[:, :, 1, :], in1=smv, op=ALU.mult), r=list(ee.all) + list(sm.all), w=lbt.all)
                kb.op("dve", lambda e: e.tensor_scalar(out=lv[:, 1], in0=lv[:, 0], scalar1=-1.0, scalar2=1.0, op0=ALU.mult, op1=ALU.add),
                      r=lbt.all, w=lbt.all)
                for si in sis:
                    hgrn_pass_a(i, si, mv, mav, lbt)
            kb.barrier()
            scan_core(16, 1, 128, "hgg", U=64)
            kb.barrier()
            Wout = hg_w_out[0]
        else:
            with kb.scope() as esl:
                wup = kb.tile([P, 2048], BF16, es=esl)
                kb.dma("pool", wup[0:16, :].rearrange("p (d n) -> p d n", d=2), gla_w_up[0].rearrange("d p n -> p d n"), wup, w=wup.all)
                for si in sis:
                    gla_pass_a(i, si, mv, mav, wup)
            kb.barrier()
            scan_core(4, 2, 512, "glg")
            kb.barrier()
            Wout = gla_w_out[0]
        for si in sis:
            pass_out_ffn(i, si, mv, mav, Wout)
        kb.barrier()

    with kb.scope() as es2:
        xs = kb.tile([P, KC * 256], F32, es=es2)
        sq = kb.tile([P, KC * 256], BF16, es=es2)
        rstd = kb.tile([P, 256], F32, es=es2)
        ot = kb.tile([P, KC * 256], F32, es=es2)
        xv = xs[:].rearrange("p (k t) -> p k t", t=256)
        sv = sq[:].rearrange("p (k t) -> p k t", t=256)
        otv = ot[:].rearrange("p (k t) -> p k t", t=256)
        for s0 in range(0, TL, 256):
            kb.dma("sp", xv, XR[:, TCX + s0:TCX + s0 + 256].rearrange("(k p) t -> p k t", p=P), xs, r=XRb, w=xs.all)
            kb.op("act", lambda e: e.activation(out=sq[:], in_=xs[:], func=AF.Square), r=xs.all, w=sq.all)
            ps = nps()
            for kc in range(KC):
                kb.op("pe", lambda e, kc=kc: e.matmul(ps[:, 0:256], lhsT=ones[:, 0:P], rhs=sv[:, kc, :], start=(kc == 0), stop=(kc == KC - 1)),
                      r=list(sq.all) + list(ones.all), w=ps.all, inc=(kc == KC - 1))
            kb.op("act", lambda e: e.activation(out=rstd[:], in_=ps[:, 0:256], func=AF.Sqrt, bias=V("eps"), scale=1.0 / D), r=ps.all, w=rstd.all)
            kb.op("dve", lambda e: e.reciprocal(out=rstd[:], in_=rstd[:]), r=rstd.all, w=rstd.all)
            for kc in range(KC):
                kb.op("dve", lambda e, kc=kc: e.scalar_tensor_tensor(out=otv[:, kc, :], in0=xv[:, kc, :], scalar=V("fing", kc), in1=rstd[:],
                                                                     op0=ALU.mult, op1=ALU.mult), r=list(xs.all) + list(rstd.all), w=ot.all)
            kb.dma("sp", OUT[:, s0:s0 + 256].rearrange("(k p) t -> p k t", p=P), otv, ot, r=ot.all, w=[SCR])
    kb.barrier()
    es.close()
    return nc


VOFF = {}
NV = 0


def _voff():
    global NV
    o = 0
    for name, n in (("eps", 1), ("bmod", DEPTH * 96), ("n1g", DEPTH * KC), ("n2g", DEPTH * KC), ("fing", KC),
                    ("cw", 2 * 3 * 48), ("ffreq", 4), ("fbias", 4), ("lbl", 2 * 4 * KC), ("hgg", KC), ("glg", KC), ("nbup", 16)):
        VOFF[name] = o
        o += n
    NV = o


_voff()
nlayers = DEPTH


def fm(v):
    v = np.asarray(v, np.float32).reshape(-1, P)
    return v.T


def pack_vecs(inp):
    vecs = np.zeros((P, NV), np.float32)

    def put(name, arr, off=0):
        arr = np.asarray(arr, np.float32)
        vecs[:arr.shape[0], VOFF[name] + off:VOFF[name] + off + arr.shape[1]] = arr
    put("eps", np.full((P, 1), EPS, np.float32))
    for i in range(DEPTH):
        put("bmod", fm(inp["b_mod"][i]), i * 96)
        put("n1g", fm(inp["norm1_g"][i]), i * KC)
        put("n2g", fm(inp["norm2_g"][i]), i * KC)
    put("fing", fm(inp["final_g"]))
    for j in range(2):
        for tap in range(3):
            put("cw", fm(inp["hy_conv_w"][j, tap]), (j * 3 + tap) * 48)
        for q in range(2):
            put("ffreq", inp["hy_ffreq"][j, q].reshape(64, 1), j * 2 + q)
        put("fbias", inp["hy_fb1"][j].reshape(64, 1), j * 2 + 0)
        put("fbias", inp["hy_fb2"][j].reshape(64, 1), j * 2 + 1)
    for d in range(2):
        for l in range(4):
            put("lbl", fm(inp["hg_lb_logits"][d, l]), (d * 4 + l) * KC)
    put("hgg", fm(inp["hg_onorm_g"][0]))
    put("glg", fm(inp["gla_onorm_g"][0]))
    for d in range(2):
        put("nbup", -fm(inp["gla_b_up"][0, d]), d * 8)
    return vecs


_CONST = {}


def consts():
    if _CONST:
        return _CONST
    bf = ml_dtypes.bfloat16
    c = {}
    c["c_ident"] = np.eye(P, dtype=np.float32).astype(bf)
    on = np.ones((P, 2 * P), np.float32)
    on[P - 1, P:] = 0.0
    c["c_ones"] = on.astype(bf)
    s = np.arange(P)[:, None]
    t = np.arange(P)[None, :]
    c["c_tri"] = np.concatenate([(s <= t), (s >= t)], axis=1).astype(np.float32)
    sm = np.ones((P, 1024), np.float32)
    sm[:, ::P] = 0.0
    c["c_smask"] = sm
    sm2 = np.ones((P, 1024), np.float32)
    sm2[:, ::64] = 0.0
    c["c_smask64"] = sm2
    delt = np.abs(np.linspace(DMIN, DMAX, D, dtype=np.float32))
    c["c_negd"] = np.broadcast_to(-delt[None, :], (P, D)).astype(np.float32).copy()
    for L in (TL, TCX):
        TCn, KF = L // P, L // P + 1
        N = 2 * L
        pos = np.arange(L, dtype=np.float32)
        tt = pos / max(L - 1, 1)
        bands = np.arange(1, 17, dtype=np.float32)
        ang = (2.0 * math.pi / L) * pos[:, None] * bands[None, :]
        z = np.concatenate([tt[:, None], np.cos(ang), -np.sin(ang)], axis=-1).astype(np.float32)
        c["c_zT%d" % L] = np.ascontiguousarray(z.T)
        c["c_tn%d" % L] = np.ascontiguousarray(tt.reshape(TCn, P).T)
        kpad = KF * P
        kk = np.arange(kpad, dtype=np.float64)
        valid = (kk <= L).astype(np.float64)
        tpos = np.arange(L, dtype=np.float64)
        ph = 2 * math.pi * ((tpos[:, None] * kk[None, :]) % N) / N
        Fre = np.cos(ph) * valid
        Fim = -np.sin(ph) * valid
        Ff = np.stack([Fre.reshape(TCn, P, KF, P), Fim.reshape(TCn, P, KF, P)], axis=3)
        c["c_Ff%d" % L] = np.ascontiguousarray(Ff.transpose(2, 1, 0, 3, 4).reshape(KF, P, TCn * 256)).astype(bf)
        phb = 2 * math.pi * (((tpos[:, None] + 1) * kk[None, :]) % N) / N
        rowv = (tpos < L - 1).astype(np.float64)[:, None]
        Bre = np.cos(phb) * valid * rowv
        Bim = np.sin(phb) * valid * rowv
        Kre = np.concatenate([Fre, Bre], axis=0)
        Kim = np.concatenate([Fim, Bim], axis=0)
        FFm = np.concatenate([Kre, Kim], axis=1)
        FFb = FFm.reshape(2 * TCn, P, 2 * KF, P).transpose(2, 1, 0, 3).reshape(2 * KF, P, 2 * TCn * P)
        c["c_FF%d" % L] = np.ascontiguousarray(FFb).astype(bf)
        wk = np.where((kk == 0) | (kk == L), 1.0, 2.0) * valid / N
        phi = 2 * math.pi * ((kk[:, None] * tpos[None, :]) % N) / N
        Gre = np.cos(phi) * wk[:, None]
        Gim = -np.sin(phi) * wk[:, None]
        Gm = np.concatenate([Gre, Gim], axis=0)
        Gb = Gm.reshape(2 * KF, P, TCn, P).transpose(2, 1, 0, 3).reshape(TCn, P, 2 * KF * P)
        c["c_G%d" % L] = np.ascontiguousarray(Gb).astype(bf)
    _CONST.update(c)
    return _CONST


def kernel(**inp):
    inp = {k: np.asarray(v) for k, v in inp.items()}
    nc = build()
    cst = consts()
    vecs = pack_vecs(inp)
    B = inp["x"].shape[0]
    shared = {k: np.ascontiguousarray(inp[k], dtype=np.float32) for k in
              ("w_mod", "w_ffn_in", "w_ffn_out", "hy_w_in", "hy_w_out", "hy_fw1", "hy_fw2", "hy_fwout",
               "hg_w_in", "hg_w_out", "gla_w_in", "gla_w_up", "gla_w_out")}
    shared["fskip"] = np.ascontiguousarray(np.broadcast_to(inp["hy_fskip"].reshape(2, 1, 2 * D), (2, P, 2 * D)), dtype=np.float32)
    shared["vecs"] = vecs
    shared.update(cst)
    in_maps = []
    for core in range(8):
        b = core % B
        m = dict(shared)
        m["xin"] = np.ascontiguousarray(np.concatenate([inp["ctx"][b].T, inp["x"][b].T], axis=1), dtype=np.float32)
        sc = np.stack([fm(inp["c"][b]), fm(inp["c_ctx"])], axis=2)
        m["scin"] = np.ascontiguousarray(sc.reshape(P, KC * 2), dtype=np.float32)
        in_maps.append(m)
    res = run_bass_kernel_spmd(nc, in_maps, core_ids=list(range(8)))
    out = np.stack([np.asarray(res.results[b]["out"]).T for b in range(B)], axis=0)
    return np.ascontiguousarray(out, dtype=np.float32)
```

```python
import math
from contextlib import ExitStack
import numpy as np
import ml_dtypes
import concourse.bass as bass
import concourse.mybir as mybir
from concourse.bass_utils import run_bass_kernel_spmd

F32 = mybir.dt.float32
BF16 = mybir.dt.bfloat16
ALU = mybir.AluOpType
AF = mybir.ActivationFunctionType
P = 128
D = 2048
KC = 16
TL = 4096
TCX = 256
TT = TL + TCX
CH = D // 2
FF = 5632
NU = TT // P
DEPTH = 4
EPS = 1e-6
DMIN = math.log(1e-2) / 1.5
DMAX = math.log(1e-2) / 0.3
WELE = 8704
SUPER = [(0, TCX, 1)] + [(TCX + i * 1024, 1024, 0) for i in range(4)]


class Buf:
    __slots__ = ("w", "r")

    def __init__(self):
        self.w = None
        self.r = {}


class Tile:
    def __init__(self, h, nparts=1):
        self.h = h
        self.parts = [Buf() for _ in range(nparts)]
        self.sem = None
        self.cnt = 0

    def __getitem__(self, k):
        return self.h[k]

    @property
    def b(self):
        return self.parts[0]

    @property
    def all(self):
        return self.parts


class KB:
    def __init__(self, nc, es):
        self.nc = nc
        self.es = es
        self.eng = {"pe": nc.tensor, "act": nc.scalar, "dve": nc.vector, "pool": nc.gpsimd, "sp": nc.sync}
        self.sem = {}
        self.cnt = {}
        for e in ("pe", "act", "dve", "pool"):
            self.sem[e] = es.enter_context(nc.semaphore("s_" + e))
            self.cnt[e] = 0
        self.waited = {e: {} for e in self.eng}
        self.tiles = []
        self.nm = 0
        self.sempool = []
        self.ccsem = es.enter_context(nc.semaphore("s_cc"))
        self.cccnt = 0
        self.bsem = es.enter_context(nc.semaphore("s_bar"))
        self.bcnt = 0

    def name(self, p):
        self.nm += 1
        return "%s%d" % (p, self.nm)

    def tile(self, shape, dt, nparts=1, es=None):
        h = (es or self.es).enter_context(self.nc.sbuf_tensor(self.name("t"), list(shape), dt))
        t = Tile(h, nparts)
        if self.sempool:
            t.sem, t.cnt, t.key = self.sempool.pop()
        else:
            t.sem = self.es.enter_context(self.nc.semaphore(self.name("d")))
            t.key = self.name("k")
        self.tiles.append(t)
        if es is not None and hasattr(es, "mine"):
            es.mine.append(t)
        return t

    def psum(self, shape, dt):
        h = self.es.enter_context(self.nc.psum_tensor(self.name("p"), list(shape), dt))
        return Tile(h, 1)

    def _wait(self, eng, ev):
        if ev is None:
            return
        sem, val, key = ev
        if eng == "pe" and key == "pe":
            return
        if self.waited[eng].get(key, 0) >= val:
            return
        self.waited[eng][key] = val
        self.eng[eng].wait_ge(sem, val)

    def _deps(self, eng, r, w):
        for b in r:
            self._wait(eng, b.w)
        for b in w:
            self._wait(eng, b.w)
            for ev in list(b.r.values()):
                self._wait(eng, ev)

    def _mark(self, ev, r, w):
        for b in r:
            b.r[ev[2]] = ev
        for b in w:
            b.w = ev
            b.r = {}

    def op(self, eng, fn, r=(), w=(), inc=True):
        self._deps(eng, r, w)
        ins = fn(self.eng[eng])
        if inc:
            self.cnt[eng] += 1
            ins.then_inc(self.sem[eng], 1)
            ev = (self.sem[eng], self.cnt[eng], eng)
        else:
            ev = (self.sem[eng], self.cnt[eng] + 1, eng)
        self._mark(ev, r, w)
        return ins

    def dma(self, q, out, in_, tile, r=(), w=()):
        self._deps(q, r, w)
        ins = self.eng[q].dma_start(out=out, in_=in_)
        tile.cnt += 16
        ins.then_inc(tile.sem, 16)
        ev = (tile.sem, tile.cnt, tile.key)
        self._mark(ev, r, w)

    def barrier(self):
        sp = self.eng["sp"]
        for e in ("pe", "act", "dve", "pool"):
            if self.cnt[e] > 0:
                self._wait("sp", (self.sem[e], self.cnt[e], e))
        for t in self.tiles:
            if t.cnt > 0:
                self._wait("sp", (t.sem, t.cnt, t.key))
        self.bcnt += 1
        ins = sp.nop()
        ins.then_inc(self.bsem, 1)
        for e in ("pe", "act", "dve", "pool"):
            self.eng[e].wait_ge(self.bsem, self.bcnt)
        for e in self.eng:
            for e2 in ("pe", "act", "dve", "pool"):
                self.waited[e][e2] = self.cnt[e2]
            for t in self.tiles:
                self.waited[e][t.key] = t.cnt

    def allgather(self, pairs, groups):
        self.barrier()
        for (src, dst) in pairs:
            self.cccnt += 1
            self.nc.gpsimd.collective_compute("AllGather", ALU.bypass, replica_groups=groups, ins=[src], outs=[dst]).then_inc(self.ccsem, 1)
        self.eng["sp"].wait_ge(self.ccsem, self.cccnt)
        self.barrier()

    def scope(self):
        kb = self

        class _S(ExitStack):
            def __exit__(s, *a):
                if a[0] is None:
                    kb.barrier()
                    kb.tiles = [t for t in kb.tiles if t not in s.mine]
                    for t in s.mine:
                        kb.sempool.append((t.sem, t.cnt, t.key))
                return ExitStack.__exit__(s, *a)
        st = _S()
        st.mine = []
        return st


def build(dbg=False, ncores=8):
    nc = bass.Bass("TRN2", target_bir_lowering=False)
    es = ExitStack()
    kb = KB(nc, es)

    def din(name, shape, dt=F32):
        return nc.dram_tensor(name, list(shape), dt, kind="ExternalInput").ap()

    def dint(name, shape, dt=F32):
        return nc.dram_tensor(name, list(shape), dt, kind="Internal").ap()

    GROUPS = [[2 * g, 2 * g + 1] for g in range(ncores // 2)]
    par = nc.sync.partition_id() % 2
    pars = nc.scalar.partition_id() % 2
    parg = nc.gpsimd.partition_id() % 2
    parCH, parsCH, pargCH = par * CH, pars * CH, parg * CH
    XIN = din("xin", [D, TT])
    SCIN = din("scin", [P, KC * 2])
    VECS = din("vecs", [P, NV])
    w_mod = din("w_mod", [DEPTH, D, 6 * D])
    w_ffn_in = din("w_ffn_in", [DEPTH, D, 2 * FF])
    w_ffn_out = din("w_ffn_out", [DEPTH, FF, D])
    hy_w_in = din("hy_w_in", [2, D, 3 * D])
    hy_w_out = din("hy_w_out", [2, D, D])
    hy_fw1 = din("hy_fw1", [2, 33, 64])
    hy_fw2 = din("hy_fw2", [2, 64, 64])
    hy_fwout = din("hy_fwout", [2, 64, 4 * D])
    fskip = din("fskip", [2, P, 2 * D])
    hg_w_in = din("hg_w_in", [1, D, 5 * D])
    hg_w_out = din("hg_w_out", [1, D, D])
    gla_w_in = din("gla_w_in", [1, D, 6176])
    gla_w_up = din("gla_w_up", [1, 2, 16, 1024])
    gla_w_out = din("gla_w_out", [1, D, D])
    C_ident = din("c_ident", [P, P], BF16)
    C_ones = din("c_ones", [P, 2 * P], BF16)
    C_tri = din("c_tri", [P, 2 * P])
    C_smask = din("c_smask", [P, 1024])
    C_negd = din("c_negd", [P, D])
    C_smask64 = din("c_smask64", [P, 1024])
    CL = {}
    for L in (TL, TCX):
        tc_, kf = L // P, (L // P) + 1
        CL[L] = dict(
            zT=din("c_zT%d" % L, [33, L]), tn=din("c_tn%d" % L, [P, tc_]),
            Ff=din("c_Ff%d" % L, [kf, P, tc_ * 256], BF16),
            FF=din("c_FF%d" % L, [2 * kf, P, 2 * tc_ * P], BF16),
            G=din("c_G%d" % L, [tc_, P, 2 * kf * P], BF16),
            HS=dint("hs%d" % L, [2, 2 * kf, P, CH]), TC=tc_, KF=kf)
    OUT = nc.dram_tensor("out", [D, TL], F32, kind="ExternalOutput").ap()
    XR = (nc.dram_tensor("xr", [D, TT], F32, kind="ExternalOutput").ap() if dbg else dint("xr", [D, TT]))
    ZT = None
    VT = (nc.dram_tensor("vt", [3, TT, D], BF16, kind="ExternalOutput").ap() if dbg else dint("vt", [3, TT, D], BF16))
    ZT = (nc.dram_tensor("zt", [D, TT], BF16, kind="ExternalOutput").ap() if dbg else dint("zt", [D, TT], BF16))
    QT = (nc.dram_tensor("qt", [2, D, TT], BF16, kind="ExternalOutput").ap() if dbg else dint("qt", [2, D, TT], BF16))
    KTs = (nc.dram_tensor("kts", [2, D, TT], BF16, kind="ExternalOutput").ap() if dbg else dint("kts", [2, D, TT], BF16))
    KHs = (nc.dram_tensor("khs", [2, D, TT], BF16, kind="ExternalOutput").ap() if dbg else dint("khs", [2, D, TT], BF16))
    EX = (nc.dram_tensor("ex", [2, D, NU], F32, kind="ExternalOutput").ap() if dbg else dint("ex", [2, D, NU]))
    EX64 = dint("ex64", [2, D, TT // 64])
    GATE = (nc.dram_tensor("gate", [D, TT], BF16, kind="ExternalOutput").ap() if dbg else dint("gate", [D, TT], BF16))
    VTm = dint("vtm", [3, TT, CH], BF16)
    ZTs = dint("zts", [CH, TT], BF16)
    ZTg = dint("ztg", [2 * CH, TT], BF16)
    XRb = [Buf() for _ in SUPER]
    SCR = Buf()

    dh = [kb.tile([P, 8], F32) for _ in range(3)]
    vecs = kb.tile([P, NV], F32)
    ident = kb.tile([P, P], BF16)
    ones = kb.tile([P, 2 * P], BF16)
    tri = kb.tile([P, 2 * P], F32)
    smask = kb.tile([P, 1024], F32)
    smask64 = kb.tile([P, 1024], F32)
    scT = kb.tile([P, KC * 2], BF16)
    MOD = kb.tile([P, 96 * 2], F32)
    MA = kb.tile([P, 2 * KC * 2], F32)
    wbuf = [kb.tile([P, WELE], BF16) for _ in range(2)]
    wsel = [0]
    PS = [kb.psum([P, 512], F32) for _ in range(6)]
    PSB = [kb.psum([P, 1024], BF16) for _ in range(2)]
    pssel = [0]
    psbsel = [0]

    def nps():
        pssel[0] = (pssel[0] + 1) % 6
        return PS[pssel[0]]

    def npsb():
        psbsel[0] = (psbsel[0] + 1) % 2
        return PSB[psbsel[0]]

    def V(name, i=0, n=1):
        o = VOFF[name] + i
        return vecs[:, o:o + n]

    for (t, src) in ((vecs, VECS), (ident, C_ident), (ones, C_ones), (tri, C_tri), (smask, C_smask), (smask64, C_smask64)):
        kb.dma("sp", t[:], src, t, w=t.all)
    sc32 = kb.tile([P, KC * 2], F32)
    kb.dma("sp", sc32[:], SCIN, sc32, w=sc32.all)
    kb.op("act", lambda e: e.activation(out=scT[:], in_=sc32[:], func=AF.Silu), r=sc32.all, w=scT.all)

    def load_w(W, KCn, ranges, cast=True, pre=None):
        wsel[0] ^= 1
        wt = wbuf[wsel[0]]
        ntot = sum(n for _, n in ranges)
        assert KCn * ntot <= WELE
        view = wt[:, 0:KCn * ntot].rearrange("p (k n) -> p k n", n=ntot)
        off = 0
        for (c0, n) in ranges:
            if pre is not None:
                src = pre
            else:
                src = W[:, c0:c0 + n].rearrange("(k p) n -> p k n", p=P)
            kb.dma("pool" if cast else "sp", view[:, :, off:off + n], src, wt, w=wt.all)
            off += n
        return wt, view

    def linear(xT, xbufs, KCn, W, blocks, ttiles, epi, mparts=None):
        for blk in blocks:
            wt, view = load_w(W, KCn, blk["ranges"], pre=blk.get("pre"), cast=blk.get("cast", True))
            for (off, m, tag) in blk["chunks"]:
                pst = []
                for (t0, tn) in ttiles:
                    ps = nps()
                    for kc in range(KCn):
                        kb.op("pe", lambda e, ps=ps, kc=kc, off=off, m=m, t0=t0, tn=tn: e.matmul(
                            ps[0:m, 0:tn], lhsT=view[:, kc, off:off + m], rhs=xT[:, kc, t0:t0 + tn],
                            start=(kc == 0), stop=(kc == KCn - 1)),
                            r=list(wt.all) + list(xbufs), w=ps.all, inc=(kc == KCn - 1))
                    pst.append(ps)
                epi(tag, pst)

    def std_blocks(c0, ncols, bw=512, tag0=0):
        blocks = []
        t = tag0
        for b0 in range(c0, c0 + ncols, bw):
            n = min(bw, c0 + ncols - b0)
            ch = []
            for o in range(0, n, P):
                ch.append((o, min(P, n - o), t))
                t += 1
            blocks.append(dict(ranges=[(b0, n)], chunks=ch))
        return blocks

    def pass_mod(i):
        def epi(tag, pst):
            ps = pst[0]
            kb.op("act", lambda e: e.activation(out=MOD[:, tag * 2:tag * 2 + 2], in_=ps[:, 0:2], func=AF.Identity,
                                                bias=V("bmod", i * 96 + tag), scale=1.0), r=ps.all, w=MOD.all)
        sv = scT[:].rearrange("p (k c) -> p k c", c=2)
        linear(sv, scT.all, KC, w_mod[i], std_blocks(0, 6 * D), [(0, 2)], epi)
        mv = MOD[:].rearrange("p (s k c) -> p s k c", s=6, c=2)
        mav = MA[:].rearrange("p (n k c) -> p n k c", n=2, c=2)
        for n, (sidx, gname) in enumerate(((1, "n1g"), (4, "n2g"))):
            for col in range(2):
                kb.op("dve", lambda e, n=n, sidx=sidx, col=col, gname=gname: e.scalar_tensor_tensor(
                    out=mav[:, n, :, col], in0=mv[:, sidx, :, col], scalar=1.0, in1=V(gname, i * KC, KC),
                    op0=ALU.add, op1=ALU.mult), r=MOD.all, w=MA.all)
        return mv, mav

    def norm_mod(es2, si, mv, mav, n_idx, shift_idx, hT):
        tok0, n, col = SUPER[si]
        xs = kb.tile([P, KC * 256], F32, es=es2)
        sq = kb.tile([P, KC * 256], BF16, es=es2)
        rstd = kb.tile([P, 256], F32, es=es2)
        tmp = kb.tile([P, 256], F32, es=es2)
        xv = xs[:].rearrange("p (k t) -> p k t", t=256)
        sv = sq[:].rearrange("p (k t) -> p k t", t=256)
        hv = hT[:, 0:KC * n].rearrange("p (k t) -> p k t", k=KC)
        for s0 in range(0, n, 256):
            kb.dma("sp", xv, XR[:, tok0 + s0:tok0 + s0 + 256].rearrange("(k p) t -> p k t", p=P), xs,
                   r=[XRb[si]], w=xs.all)
            kb.op("act", lambda e: e.activation(out=sq[:], in_=xs[:], func=AF.Square), r=xs.all, w=sq.all)
            ps = nps()
            for kc in range(KC):
                kb.op("pe", lambda e, kc=kc: e.matmul(ps[:, 0:256], lhsT=ones[:, 0:P], rhs=sv[:, kc, :],
                                                      start=(kc == 0), stop=(kc == KC - 1)),
                      r=list(sq.all) + list(ones.all), w=ps.all, inc=(kc == KC - 1))
            kb.op("act", lambda e: e.activation(out=rstd[:], in_=ps[:, 0:256], func=AF.Sqrt, bias=V("eps"), scale=1.0 / D),
                  r=ps.all, w=rstd.all)
            kb.op("dve", lambda e: e.reciprocal(out=rstd[:], in_=rstd[:]), r=rstd.all, w=rstd.all)
            for kc in range(KC):
                kb.op("dve", lambda e, kc=kc: e.tensor_tensor(out=tmp[:], in0=xv[:, kc, :], in1=rstd[:], op=ALU.mult),
                      r=list(xs.all) + list(rstd.all), w=tmp.all)
                kb.op("act", lambda e, kc=kc: e.activation(out=hv[:, kc, s0:s0 + 256], in_=tmp[:], func=AF.Identity,
                                                           bias=mv[:, shift_idx, kc, col:col + 1],
                                                           scale=mav[:, n_idx, kc, col:col + 1]),
                      r=list(tmp.all) + list(MOD.all) + list(MA.all), w=hT.all)

    def ttiles_of(n):
        return [(t, min(512, n - t)) for t in range(0, n, 512)]

    def resid_epi(es2, si, mv, gate_idx):
        tok0, n, col = SUPER[si]
        xo = [kb.tile([P, 1024], F32, es=es2) for _ in range(2)]
        sel = [0]

        def epi(tag, pst):
            sel[0] ^= 1
            x = xo[sel[0]]
            kb.dma("sp", x[:, 0:n], XR[tag * P:(tag + 1) * P, tok0:tok0 + n], x, r=[XRb[si]], w=x.all)
            for ti, (t0, tn) in enumerate(ttiles_of(n)):
                ps = pst[ti]
                kb.op("dve", lambda e, ps=ps, t0=t0, tn=tn: e.scalar_tensor_tensor(
                    out=x[:, t0:t0 + tn], in0=ps[:, 0:tn], scalar=mv[:, gate_idx, tag, col:col + 1], in1=x[:, t0:t0 + tn],
                    op0=ALU.mult, op1=ALU.add), r=list(ps.all) + list(x.all) + list(MOD.all), w=x.all)
            kb.dma("sp", XR[tag * P:(tag + 1) * P, tok0:tok0 + n], x[:, 0:n], x, r=x.all, w=[XRb[si]])
        return epi

    def pass_out_ffn(i, si, mv, mav, Wout, hy=False):
        tok0, n, col = SUPER[si]
        with kb.scope() as es2:
            zt = kb.tile([P, KC * 1024], BF16, es=es2)
            zv = zt[:, 0:KC * n].rearrange("p (k t) -> p k t", k=KC)
            if hy:
                for s_ in range(2):
                    kb.dma("sp", zv[:, s_ * 8:(s_ + 1) * 8, :], ZTg[:, tok0:tok0 + n].rearrange("(c s p) t -> s p c t", s=2, p=P)[s_], zt, r=[SCR], w=zt.all)
            else:
                kb.dma("sp", zv, ZT[:, tok0:tok0 + n].rearrange("(k p) t -> p k t", p=P), zt, r=[SCR], w=zt.all)
            linear(zv, zt.all, KC, Wout, std_blocks(0, D), ttiles_of(n), resid_epi(es2, si, mv, 2))
        with kb.scope() as es2:
            hT = kb.tile([P, KC * 1024], BF16, es=es2)
            hid = kb.tile([P, 44 * 1024], BF16, es=es2)
            sg = [kb.tile([P, 512], F32, es=es2) for _ in range(2)]
            sgs = [0]
            with kb.scope() as es3:
                norm_mod(es3, si, mv, mav, 1, 3, hT)
            hv = hT[:, 0:KC * n].rearrange("p (k t) -> p k t", k=KC)
            hidv = hid[:, 0:44 * n].rearrange("p (k t) -> p k t", k=44)
            pend = {}

            def epi_in(tag, pst):
                j, isup = tag
                if not isup:
                    pend[j] = pst
                    return
                gp = pend.pop(j)
                for ti, (t0, tn) in enumerate(ttiles_of(n)):
                    sgs[0] ^= 1
                    s = sg[sgs[0]]
                    kb.op("act", lambda e, s=s, g=gp[ti], tn=tn: e.activation(out=s[:, 0:tn], in_=g[:, 0:tn], func=AF.Silu),
                          r=gp[ti].all, w=s.all)
                    kb.op("dve", lambda e, s=s, u=pst[ti], t0=t0, tn=tn: e.tensor_tensor(
                        out=hidv[:, j, t0:t0 + tn], in0=s[:, 0:tn], in1=u[:, 0:tn], op=ALU.mult),
                        r=list(s.all) + list(pst[ti].all), w=hid.all)
            blocks = []
            for j in range(44):
                blocks.append(dict(ranges=[(j * P, P), (FF + j * P, P)], chunks=[(0, P, (j, 0)), (P, P, (j, 1))]))
            linear(hv, hT.all, KC, w_ffn_in[i], blocks, ttiles_of(n), epi_in)
            linear(hidv, hid.all, 44, w_ffn_out[i], std_blocks(0, D, bw=P), ttiles_of(n), resid_epi(es2, si, mv, 5))

    def fm_to_tm(es2, src_tile, n, dst, tok0, c0, tsb):
        nt = n // P
        psb = npsb()
        pv = psb[:, 0:nt * P].rearrange("p (a b) -> p a b", b=P)
        for tt in range(nt):
            kb.op("pe", lambda e, tt=tt: e.transpose(out=pv[:, tt, :], in_=src_tile[:, tt * P:(tt + 1) * P], identity=ident[:]),
                  r=list(src_tile.all) + list(ident.all), w=psb.all, inc=(tt == nt - 1))
        tv = tsb[:, 0:nt * P].rearrange("p (a b) -> p a b", b=P)
        kb.op("act", lambda e: e.copy(out=tsb[:, 0:nt * P], in_=psb[:, 0:nt * P]), r=psb.all, w=tsb.all)
        kb.dma("sp", dst[tok0:tok0 + n, c0:c0 + P].rearrange("(a p) c -> p a c", p=P), tv, tsb, r=tsb.all, w=[SCR])

    def hyena_pass_a(i, j, si, mv, mav):
        tok0, n, col = SUPER[si]
        rowlen = 256 if si == 0 else 64
        with kb.scope() as es2:
            hT = kb.tile([P, KC * 1024], BF16, es=es2)
            with kb.scope() as es3:
                norm_mod(es3, si, mv, mav, 0, 0, hT)
            hv = hT[:, 0:KC * n].rearrange("p (k t) -> p k t", k=KC)
            pc = kb.tile([P, 1024], F32, es=es2)
            o1 = kb.tile([P, 1024], F32, es=es2)
            ob = kb.tile([P, 1024], BF16, es=es2)
            tsb = kb.tile([P, 1024], BF16, es=es2)

            def epi(tag, pst):
                for ti, (t0, tn) in enumerate(ttiles_of(n)):
                    ps = pst[ti]
                    kb.op("act", lambda e, ps=ps, t0=t0, tn=tn: e.copy(out=pc[:, t0:t0 + tn], in_=ps[:, 0:tn]), r=ps.all, w=pc.all)
                    kb.op("act", lambda e, ps=ps, t0=t0, tn=tn: e.activation(
                        out=o1[:, t0:t0 + tn], in_=ps[:, 0:tn], func=AF.Identity, bias=0.0,
                        scale=V("cw", (j * 3 + 1) * 48 + tag)), r=ps.all, w=o1.all)
                o1v = o1[:, 0:n].rearrange("p (a b) -> p a b", b=rowlen)
                pcv = pc[:, 0:n].rearrange("p (a b) -> p a b", b=rowlen)
                kb.op("dve", lambda e: e.scalar_tensor_tensor(
                    out=o1v[:, :, 1:rowlen], in0=pcv[:, :, 0:rowlen - 1], scalar=V("cw", (j * 3 + 0) * 48 + tag),
                    in1=o1v[:, :, 1:rowlen], op0=ALU.mult, op1=ALU.add), r=list(pc.all) + list(o1.all), w=o1.all)
                kb.op("dve", lambda e: e.scalar_tensor_tensor(
                    out=o1v[:, :, 0:rowlen - 1], in0=pcv[:, :, 1:rowlen], scalar=V("cw", (j * 3 + 2) * 48 + tag),
                    in1=o1v[:, :, 0:rowlen - 1], op0=ALU.mult, op1=ALU.add), r=list(pc.all) + list(o1.all), w=o1.all)
                kb.op("dve", lambda e: e.tensor_copy(out=ob[:, 0:n], in_=o1[:, 0:n]), r=o1.all, w=ob.all)
                fm_to_tm(es2, ob, n, VT[tag // KC], tok0, (tag % KC) * P, tsb)
            linear(hv, hT.all, KC, hy_w_in[j], std_blocks(0, 3 * D), ttiles_of(n), epi)

    def range_reduce_sin(es2, arg, m, n, out_ap, tmp):
        MAGIC = 12582912.0
        kb.op("dve", lambda e: e.tensor_scalar(out=tmp[0:m, 0:n], in0=arg[0:m, 0:n], scalar1=1.0 / (2 * math.pi), scalar2=MAGIC,
                                               op0=ALU.mult, op1=ALU.add), r=arg.all, w=tmp.all)
        kb.op("dve", lambda e: e.tensor_scalar(out=tmp[0:m, 0:n], in0=tmp[0:m, 0:n], scalar1=-MAGIC, scalar2=None,
                                               op0=ALU.add), r=tmp.all, w=tmp.all)
        kb.op("dve", lambda e: e.scalar_tensor_tensor(out=arg[0:m, 0:n], in0=tmp[0:m, 0:n], scalar=-2 * math.pi, in1=arg[0:m, 0:n],
                                                      op0=ALU.mult, op1=ALU.add), r=list(tmp.all) + list(arg.all), w=arg.all)
        kb.op("dve", lambda e: e.tensor_scalar(out=arg[0:m, 0:n], in0=arg[0:m, 0:n], scalar1=-3.14159, scalar2=3.14159,
                                               op0=ALU.max, op1=ALU.min), r=arg.all, w=arg.all)
        kb.op("act", lambda e: e.activation(out=out_ap, in_=arg[0:m, 0:n], func=AF.Sin), r=arg.all, w=[])

    def hyena_filters(j, L):
        c = CL[L]
        TCn, KF = c["TC"], c["KF"]
        with kb.scope() as es2:
            zT = kb.tile([P, L], F32, es=es2)
            h1 = kb.tile([P, L], F32, es=es2)
            h2 = kb.tile([P, L], BF16, es=es2)
            fw1 = kb.tile([P, 64], F32, es=es2)
            fw2 = kb.tile([P, 64], F32, es=es2)
            fwo = kb.tile([P, 4 * CH], BF16, es=es2)
            tn = kb.tile([P, TCn], F32, es=es2)
            negd = kb.tile([P, CH], F32, es=es2)
            skp = kb.tile([P, 2 * CH], F32, es=es2)
            fb = kb.tile([P, 2], F32, es=es2)
            arg = kb.tile([P, 512], F32, es=es2)
            tmp = kb.tile([P, 512], F32, es=es2)
            kb.dma("sp", zT[0:33, :], c["zT"], zT, w=zT.all)
            kb.dma("sp", fw1[0:33, :], hy_fw1[j], fw1, w=fw1.all)
            kb.dma("sp", fw2[0:64, :], hy_fw2[j], fw2, w=fw2.all)
            kb.dma("pool", fwo[0:64, :].rearrange("p (b c) -> p b c", b=4),
                   hy_fwout[j].rearrange("p (b c) -> p b c", b=4)[:, :, bass.ds(pargCH, CH)], fwo, w=fwo.all)
            kb.dma("sp", tn[:], c["tn"], tn, w=tn.all)
            kb.dma("sp", skp[:].rearrange("p (b c) -> p b c", b=2),
                   fskip[j].rearrange("p (b c) -> p b c", b=2)[:, :, bass.ds(parCH, CH)], skp, w=skp.all)
            kb.dma("sp", negd[:], C_negd[:, bass.ds(parCH, CH)], negd, w=negd.all)
            for q in range(2):
                kb.op("dve", lambda e, q=q: e.tensor_tensor(out=fb[0:64, q:q + 1], in0=V("ffreq", j * 2 + q)[0:64, :],
                                                             in1=V("fbias", j * 2 + q)[0:64, :], op=ALU.mult), r=vecs.all, w=fb.all)
            for t0 in range(0, L, 512):
                tn_ = min(512, L - t0)
                ps = nps()
                kb.op("pe", lambda e: e.matmul(ps[0:64, 0:tn_], lhsT=fw1[0:33, 0:64], rhs=zT[0:33, t0:t0 + tn_], start=True, stop=True),
                      r=list(fw1.all) + list(zT.all), w=ps.all)
                kb.op("act", lambda e: e.activation(out=arg[0:64, 0:tn_], in_=ps[0:64, 0:tn_], func=AF.Identity,
                                                    bias=fb[0:64, 0:1], scale=V("ffreq", j * 2 + 0)[0:64, :]),
                      r=list(ps.all) + list(fb.all), w=arg.all)
                range_reduce_sin(es2, arg, 64, tn_, h1[0:64, t0:t0 + tn_], tmp)
                kb._mark((kb.sem["act"], kb.cnt["act"], "act"), [], h1.all)
            for t0 in range(0, L, 512):
                tn_ = min(512, L - t0)
                ps = nps()
                kb.op("pe", lambda e: e.matmul(ps[0:64, 0:tn_], lhsT=fw2[0:64, 0:64], rhs=h1[0:64, t0:t0 + tn_], start=True, stop=True),
                      r=list(fw2.all) + list(h1.all), w=ps.all)
                kb.op("act", lambda e: e.activation(out=arg[0:64, 0:tn_], in_=ps[0:64, 0:tn_], func=AF.Identity,
                                                    bias=fb[0:64, 1:2], scale=V("ffreq", j * 2 + 1)[0:64, :]),
                      r=list(ps.all) + list(fb.all), w=arg.all)
                range_reduce_sin(es2, arg, 64, tn_, h2[0:64, t0:t0 + tn_], tmp)
                kb._mark((kb.sem["act"], kb.cnt["act"], "act"), [], h2.all)
            HF = kb.tile([P, 2 * TCn * 512], BF16, es=es2)
            hfv = HF[:].rearrange("p (k c) -> p k c", c=512)
            dec = [kb.tile([P, 512], F32, es=es2) for _ in range(2)]
            ab = [kb.tile([P, 512], BF16, es=es2) for _ in range(2)]
            rn = kb.tile([P, 512], F32, es=es2)
            ho = [kb.tile([P, 512], F32, es=es2) for _ in range(2)]
            sel = [0]
            for o in range(2):
                for ct in range(2):
                    nps_ = nps()
                    first = True
                    for tc in range(TCn):
                        d = dec[tc % 2]
                        kb.op("act", lambda e, d=d, tc=tc: e.activation(out=d[:], in_=negd[:, ct * 512:(ct + 1) * 512], func=AF.Exp,
                                                                        scale=tn[:, tc:tc + 1]), r=list(negd.all) + list(tn.all), w=d.all)
                        for dr in range(2):
                            ps = nps()
                            if ps is nps_:
                                ps = nps()
                            col0 = (dr * 2 + o) * CH + ct * 512
                            kb.op("pe", lambda e, ps=ps, tc=tc, col0=col0: e.matmul(
                                ps[:, :], lhsT=h2[0:64, tc * P:(tc + 1) * P], rhs=fwo[0:64, col0:col0 + 512], start=True, stop=True),
                                r=list(h2.all) + list(fwo.all), w=ps.all)
                            kk = dr * TCn + tc
                            kb.op("dve", lambda e, ps=ps, d=d, kk=kk: e.tensor_tensor(out=hfv[:, kk, :], in0=ps[:, :], in1=d[:], op=ALU.mult),
                                  r=list(ps.all) + list(d.all), w=HF.all)
                            sel[0] ^= 1
                            a = ab[sel[0]]
                            kb.op("act", lambda e, a=a, kk=kk: e.activation(out=a[:], in_=hfv[:, kk, :], func=AF.Abs),
                                  r=HF.all, w=a.all)
                            last = (dr == 1 and tc == TCn - 1)
                            oc0 = P if last else 0
                            kb.op("pe", lambda e, a=a, oc0=oc0, first=first, last=last: e.matmul(
                                nps_[:, :], lhsT=ones[:, oc0:oc0 + P], rhs=a[:], start=first, stop=last),
                                r=list(a.all) + list(ones.all), w=nps_.all, inc=last)
                            first = False
                    kb.op("dve", lambda e: e.reciprocal(out=rn[:], in_=nps_[:, :]), r=nps_.all, w=rn.all)
                    for blk in range(2 * KF):
                        wt, view = load_w(None, 2 * TCn, [(0, P)], cast=False,
                                          pre=c["FF"][blk].rearrange("p (k n) -> p k n", n=P))
                        ps = nps()
                        for kk in range(2 * TCn):
                            kb.op("pe", lambda e, kk=kk: e.matmul(ps[:, :], lhsT=view[:, kk, :], rhs=hfv[:, kk, :],
                                                                  start=(kk == 0), stop=(kk == 2 * TCn - 1)),
                                  r=list(wt.all) + list(HF.all), w=ps.all, inc=(kk == 2 * TCn - 1))
                        sel[0] ^= 1
                        h = ho[sel[0]]
                        kb.op("dve", lambda e, h=h: e.tensor_tensor(out=h[:], in0=ps[:, :], in1=rn[:], op=ALU.mult),
                              r=list(ps.all) + list(rn.all), w=h.all)
                        if blk < KF:
                            kb.op("pool", lambda e, h=h: e.tensor_tensor(out=h[:], in0=h[:], in1=skp[:, o * CH + ct * 512:o * CH + (ct + 1) * 512],
                                                                         op=ALU.add), r=list(h.all) + list(skp.all), w=h.all)
                        kb.dma("sp", c["HS"][o, blk, :, ct * 512:(ct + 1) * 512], h[:], h, r=h.all, w=[SCR])

    def hyena_conv(L, tok0):
        c = CL[L]
        TCn, KF = c["TC"], c["KF"]
        with kb.scope() as es2:
            Vt = kb.tile([P, TCn * 512], BF16, es=es2)
            Z1 = kb.tile([P, TCn * 512], BF16, es=es2)
            Y = kb.tile([P, 2 * KF * 512], BF16, es=es2)
            Hr = [kb.tile([P, 512], F32, es=es2) for _ in range(2)]
            Hi = [kb.tile([P, 512], F32, es=es2) for _ in range(2)]
            t1 = kb.tile([P, 512], F32, es=es2)
            t2 = kb.tile([P, 512], F32, es=es2)
            xm = [kb.tile([P, 512], BF16, es=es2) for _ in range(2)]
            zb = kb.tile([P, 512], BF16, es=es2)
            tsb = kb.tile([P, 512], BF16, es=es2)
            vv = Vt[:].rearrange("p (k c) -> p k c", c=512)
            z1v = Z1[:].rearrange("p (k c) -> p k c", c=512)
            yv = Y[:].rearrange("p (k c) -> p k c", c=512)
            sel = [0]
            for ct in range(2):
                c0 = ct * 512
                kb.dma("sp", vv, VTm[0, tok0:tok0 + L, c0:c0 + 512].rearrange("(k p) c -> p k c", p=P), Vt, r=[SCR], w=Vt.all)
                for o in range(2):
                    src, srcv = (Vt, vv) if o == 0 else (Z1, z1v)
                    for m in range(KF):
                        wt, view = load_w(None, TCn, [(0, 256)], cast=False, pre=c["Ff"][m].rearrange("p (k n) -> p k n", n=256))
                        pr, pi = nps(), nps()
                        for part, ps in ((0, pr), (1, pi)):
                            for kk in range(TCn):
                                kb.op("pe", lambda e, ps=ps, kk=kk, part=part: e.matmul(
                                    ps[:, :], lhsT=view[:, kk, part * P:(part + 1) * P], rhs=srcv[:, kk, :],
                                    start=(kk == 0), stop=(kk == TCn - 1)), r=list(wt.all) + list(src.all), w=ps.all, inc=(kk == TCn - 1))
                        sel[0] ^= 1
                        hr, hi = Hr[sel[0]], Hi[sel[0]]
                        kb.dma("sp", hr[:], c["HS"][o, m, :, c0:c0 + 512], hr, r=[SCR], w=hr.all)
                        kb.dma("sp", hi[:], c["HS"][o, KF + m, :, c0:c0 + 512], hi, r=[SCR], w=hi.all)
                        kb.op("dve", lambda e: e.tensor_tensor(out=t1[:], in0=pr[:, :], in1=hr[:], op=ALU.mult), r=list(pr.all) + list(hr.all), w=t1.all)
                        kb.op("dve", lambda e: e.tensor_tensor(out=t2[:], in0=pi[:, :], in1=hi[:], op=ALU.mult), r=list(pi.all) + list(hi.all), w=t2.all)
                        kb.op("pool", lambda e, m=m: e.tensor_tensor(out=yv[:, m, :], in0=t1[:], in1=t2[:], op=ALU.subtract),
                              r=list(t1.all) + list(t2.all), w=Y.all)
                        kb.op("dve", lambda e: e.tensor_tensor(out=t1[:], in0=pr[:, :], in1=hi[:], op=ALU.mult), r=list(pr.all) + list(hi.all), w=t1.all)
                        kb.op("dve", lambda e: e.tensor_tensor(out=t2[:], in0=pi[:, :], in1=hr[:], op=ALU.mult), r=list(pi.all) + list(hr.all), w=t2.all)
                        kb.op("pool", lambda e, m=m: e.tensor_tensor(out=yv[:, KF + m, :], in0=t1[:], in1=t2[:], op=ALU.add),
                              r=list(t1.all) + list(t2.all), w=Y.all)
                    for tc in range(TCn):
                        wt, view = load_w(None, 2 * KF, [(0, P)], cast=False, pre=c["G"][tc].rearrange("p (k n) -> p k n", n=P))
                        ps = nps()
                        for kk in range(2 * KF):
                            kb.op("pe", lambda e, kk=kk: e.matmul(ps[:, :], lhsT=view[:, kk, :], rhs=yv[:, kk, :],
                                                                  start=(kk == 0), stop=(kk == 2 * KF - 1)),
                                  r=list(wt.all) + list(Y.all), w=ps.all, inc=(kk == 2 * KF - 1))
                        sel[0] ^= 1
                        x = xm[sel[0]]
                        kb.dma("sp", x[:], VTm[1 + o, tok0 + tc * P:tok0 + (tc + 1) * P, c0:c0 + 512], x, r=[SCR], w=x.all)
                        if o == 0:
                            kb.op("dve", lambda e, tc=tc: e.tensor_tensor(out=z1v[:, tc, :], in0=ps[:, :], in1=x[:], op=ALU.mult),
                                  r=list(ps.all) + list(x.all), w=Z1.all)
                        else:
                            kb.op("dve", lambda e: e.tensor_tensor(out=zb[:], in0=ps[:, :], in1=x[:], op=ALU.mult),
                                  r=list(ps.all) + list(x.all), w=zb.all)
                            psb = npsb()
                            pv = psb[:, 0:512].rearrange("p (a b) -> p a b", b=P)
                            for cc in range(4):
                                kb.op("pe", lambda e, cc=cc: e.transpose(out=pv[:, cc, :], in_=zb[:, cc * P:(cc + 1) * P], identity=ident[:]),
                                      r=list(zb.all) + list(ident.all), w=psb.all, inc=(cc == 3))
                            kb.op("act", lambda e: e.copy(out=tsb[:], in_=psb[:, 0:512]), r=psb.all, w=tsb.all)
                            kb.dma("sp", ZTs[c0:c0 + 512, tok0 + tc * P:tok0 + (tc + 1) * P].rearrange("(a p) t -> p a t", p=P),
                                   tsb[:].rearrange("p (a b) -> p a b", b=P), tsb, r=tsb.all, w=[SCR])

    def gate_chain(es2, T, q_sb, k_of_dir, lf, dr, row0, tok0, n, U=P):
        nu = n // U
        EXd = EX if U == P else EX64
        smk = smask if U == P else smask64
        cs, d, e1, ob, ctot, nct, ex = T["cs"], T["d"], T["e1"], T["ob"], T["ctot"], T["nct"], T["ex"]
        kb.op("dve", lambda e: e.tensor_tensor_scan(out=cs[:, 0:n], data0=smk[:, 0:n], data1=lf[:, 0:n], initial=0.0,
                                                    op0=ALU.mult, op1=ALU.add), r=list(lf.all) + list(smk.all), w=cs.all)
        csv = cs[:, 0:n].rearrange("p (u t) -> p u t", t=U)
        kb.op("dve", lambda e: e.tensor_copy(out=ctot[:, 0:nu], in_=csv[:, :, U - 1]), r=cs.all, w=ctot.all)
        kb.op("dve", lambda e: e.tensor_scalar(out=nct[:, 0:nu], in0=ctot[:, 0:nu], scalar1=-1.0, scalar2=None, op0=ALU.mult),
              r=ctot.all, w=nct.all)
        kb.op("act", lambda e: e.activation(out=ex[:, 0:nu], in_=ctot[:, 0:nu], func=AF.Exp), r=ctot.all, w=ex.all)
        u0 = tok0 // U
        kb.dma("sp", EXd[dr, row0:row0 + P, u0:u0 + nu], ex[:, 0:nu], ex, r=ex.all, w=[SCR])

        def emit(dst, src_mul, make_exp):
            make_exp()
            kb.op("dve", lambda e: e.tensor_tensor(out=ob[:, 0:n], in0=src_mul[:, 0:n], in1=e1[:, 0:n], op=ALU.mult),
                  r=list(src_mul.all) + list(e1.all), w=ob.all)
            kb.dma("sp", dst[dr, row0:row0 + P, tok0:tok0 + n], ob[:, 0:n], ob, r=ob.all, w=[SCR])

        def full_exp(src, scale):
            return lambda: kb.op("act", lambda e: e.activation(out=e1[:, 0:n], in_=src[:, 0:n], func=AF.Exp, scale=scale),
                                 r=src.all, w=e1.all)

        def unit_exp(src, scale, bias_t):
            def f():
                for u in range(nu):
                    kb.op("act", lambda e, u=u: e.activation(out=e1[:, u * U:(u + 1) * U], in_=src[:, u * U:(u + 1) * U], func=AF.Exp,
                                                             scale=scale, bias=bias_t[:, u:u + 1]),
                          r=list(src.all) + list(bias_t.all), w=e1.all)
            return f
        if dr == 0:
            emit(QT, q_sb, full_exp(cs, 1.0))
            emit(KTs, k_of_dir, full_exp(cs, -1.0))
            emit(KHs, k_of_dir, unit_exp(cs, -1.0, ctot))
        else:
            kb.op("dve", lambda e: e.tensor_tensor(out=d[:, 0:n], in0=lf[:, 0:n], in1=cs[:, 0:n], op=ALU.subtract),
                  r=list(lf.all) + list(cs.all), w=d.all)
            emit(QT, q_sb, unit_exp(d, 1.0, ctot))
            emit(KTs, k_of_dir, unit_exp(d, -1.0, nct))
            emit(KHs, k_of_dir, full_exp(d, -1.0))

    def chain_tiles(es2):
        T = {}
        for nm in ("cs", "d", "e1"):
            T[nm] = kb.tile([P, 1024], F32, es=es2)
        T["ob"] = kb.tile([P, 1024], BF16, es=es2)
        for nm in ("ctot", "nct", "ex"):
            T[nm] = kb.tile([P, 16], F32, es=es2)
        return T

    def hgrn_pass_a(i, si, mv, mav, lbt):
        tok0, n, col = SUPER[si]
        with kb.scope() as es2:
            hT = kb.tile([P, KC * 1024], BF16, es=es2)
            with kb.scope() as es3:
                norm_mod(es3, si, mv, mav, 0, 0, hT)
            hv = hT[:, 0:KC * n].rearrange("p (k t) -> p k t", k=KC)
            T = chain_tiles(es2)
            q_sb = kb.tile([P, 1024], F32, es=es2)
            fv = kb.tile([P, 1024], F32, es=es2)
            lf = kb.tile([P, 1024], F32, es=es2)
            k_sb = kb.tile([P, 1024], F32, es=es2)
            ob2 = kb.tile([P, 1024], BF16, es=es2)
            tsb = kb.tile([P, 1024], BF16, es=es2)

            def epi(tag, pst):
                kind, h = tag
                for ti, (t0, tn) in enumerate(ttiles_of(n)):
                    ps = pst[ti]
                    if kind == "q":
                        kb.op("act", lambda e, ps=ps, t0=t0, tn=tn: e.activation(out=q_sb[:, t0:t0 + tn], in_=ps[:, 0:tn], func=AF.Silu),
                              r=ps.all, w=q_sb.all)
                    elif kind == "i":
                        kb.op("act", lambda e, ps=ps, t0=t0, tn=tn: e.copy(out=ob2[:, t0:t0 + tn], in_=ps[:, 0:tn]), r=ps.all, w=ob2.all)
                    elif kind == "g":
                        kb.op("act", lambda e, ps=ps, t0=t0, tn=tn: e.activation(out=ob2[:, t0:t0 + tn], in_=ps[:, 0:tn], func=AF.Silu),
                              r=ps.all, w=ob2.all)
                    else:
                        kb.op("act", lambda e, ps=ps, t0=t0, tn=tn: e.activation(out=fv[:, t0:t0 + tn], in_=ps[:, 0:tn], func=AF.Sigmoid),
                              r=ps.all, w=fv.all)
                if kind == "i":
                    fm_to_tm(es2, ob2, n, VT[0], tok0, h * P, tsb)
                elif kind == "g":
                    kb.dma("sp", GATE[h * P:(h + 1) * P, tok0:tok0 + n], ob2[:, 0:n], ob2, r=ob2.all, w=[SCR])
                elif kind in ("f0", "f1"):
                    dr = 0 if kind == "f0" else 1
                    kb.op("dve", lambda e: e.tensor_scalar(out=fv[:, 0:n], in0=fv[:, 0:n], scalar1=lbt[:, (2 + dr) * KC + h:(2 + dr) * KC + h + 1],
                                                           scalar2=lbt[:, dr * KC + h:dr * KC + h + 1], op0=ALU.mult, op1=ALU.add),
                          r=list(fv.all) + list(lbt.all), w=fv.all)
                    kb.op("act", lambda e: e.activation(out=lf[:, 0:n], in_=fv[:, 0:n], func=AF.Ln), r=fv.all, w=lf.all)
                    kb.op("dve", lambda e: e.tensor_scalar(out=k_sb[:, 0:n], in0=fv[:, 0:n], scalar1=-1.0, scalar2=1.0, op0=ALU.mult, op1=ALU.add),
                          r=fv.all, w=k_sb.all)
                    gate_chain(es2, T, q_sb, k_sb, lf, dr, h * P, tok0, n, U=64)
            blocks = []
            for h in range(KC):
                blocks.append(dict(ranges=[(h * P, P), (3 * D + h * P, P), (4 * D + h * P, P)],
                                   chunks=[(0, P, ("q", h)), (P, P, ("f0", h)), (2 * P, P, ("f1", h))]))
            for h in range(0, KC, 4):
                blocks.append(dict(ranges=[(D + h * P, 512)], chunks=[(o * P, P, ("i", h + o)) for o in range(4)]))
            for h in range(0, KC, 4):
                blocks.append(dict(ranges=[(2 * D + h * P, 512)], chunks=[(o * P, P, ("g", h + o)) for o in range(4)]))
            linear(hv, hT.all, KC, hg_w_in[0], blocks, ttiles_of(n), epi)

    def gla_pass_a(i, si, mv, mav, wup):
        tok0, n, col = SUPER[si]
        with kb.scope() as es2:
            hT = kb.tile([P, KC * 1024], BF16, es=es2)
            with kb.scope() as es3:
                norm_mod(es3, si, mv, mav, 0, 0, hT)
            hv = hT[:, 0:KC * n].rearrange("p (k t) -> p k t", k=KC)
            T = chain_tiles(es2)
            q_sb = kb.tile([P, 1024], F32, es=es2)
            k_sb = kb.tile([P, 1024], F32, es=es2)
            lf = kb.tile([P, 1024], F32, es=es2)
            aT = [kb.tile([P, 1024], BF16, es=es2) for _ in range(2)]
            ob2 = kb.tile([P, 1024], BF16, es=es2)
            tsb = kb.tile([P, 1024], BF16, es=es2)

            def epi(tag, pst):
                kind, h = tag
                for ti, (t0, tn) in enumerate(ttiles_of(n)):
                    ps = pst[ti]
                    if kind == "a":
                        kb.op("act", lambda e, ps=ps, t0=t0, tn=tn: e.copy(out=aT[h][0:16, t0:t0 + tn], in_=ps[0:16, 0:tn]), r=ps.all, w=aT[h].all)
                    elif kind == "q":
                        kb.op("act", lambda e, ps=ps, t0=t0, tn=tn: e.activation(out=q_sb[:, t0:t0 + tn], in_=ps[:, 0:tn], func=AF.Identity, scale=1.0 / 16.0),
                              r=ps.all, w=q_sb.all)
                    elif kind == "k":
                        kb.op("act", lambda e, ps=ps, t0=t0, tn=tn: e.copy(out=k_sb[:, t0:t0 + tn], in_=ps[:, 0:tn]), r=ps.all, w=k_sb.all)
                    elif kind == "v":
                        kb.op("act", lambda e, ps=ps, t0=t0, tn=tn: e.copy(out=ob2[:, t0:t0 + tn], in_=ps[:, 0:tn]), r=ps.all, w=ob2.all)
                    elif kind == "g":
                        kb.op("act", lambda e, ps=ps, t0=t0, tn=tn: e.activation(out=ob2[:, t0:t0 + tn], in_=ps[:, 0:tn], func=AF.Silu),
                              r=ps.all, w=ob2.all)
                if kind == "v":
                    fm_to_tm(es2, ob2, n, VT[0], tok0, h * P, tsb)
                elif kind == "g":
                    kb.dma("sp", GATE[h * P:(h + 1) * P, tok0:tok0 + n], ob2[:, 0:n], ob2, r=ob2.all, w=[SCR])
                elif kind == "k":
                    for dr in range(2):
                        for ti, (t0, tn) in enumerate(ttiles_of(n)):
                            ps = nps()
                            kb.op("pe", lambda e, ps=ps, t0=t0, tn=tn: e.matmul(
                                ps[:, 0:tn], lhsT=wup[0:16, dr * 1024 + h * P:dr * 1024 + (h + 1) * P], rhs=aT[dr][0:16, t0:t0 + tn],
                                start=True, stop=True), r=list(wup.all) + list(aT[dr].all), w=ps.all)
                            kb.op("act", lambda e, ps=ps, t0=t0, tn=tn: e.activation(
                                out=lf[:, t0:t0 + tn], in_=ps[:, 0:tn], func=AF.Exp, scale=-1.0, bias=V("nbup", dr * 8 + h)),
                                r=ps.all, w=lf.all)
                        kb.op("act", lambda e: e.activation(out=lf[:, 0:n], in_=lf[:, 0:n], func=AF.Ln, bias=1.0, scale=1.0), r=lf.all, w=lf.all)
                        kb.op("dve", lambda e: e.tensor_scalar(out=lf[:, 0:n], in0=lf[:, 0:n], scalar1=-1.0 / 16.0, scalar2=None, op0=ALU.mult),
                              r=lf.all, w=lf.all)
                        gate_chain(es2, T, q_sb, k_sb, lf, dr, h * P, tok0, n)
            blocks = [dict(ranges=[(6144, 16), (6160, 16)], chunks=[(0, 16, ("a", 0)), (16, 16, ("a", 1))])]
            for h in range(8):
                blocks.append(dict(ranges=[(h * P, P), (1024 + h * P, P)], chunks=[(0, P, ("q", h)), (P, P, ("k", h))]))
            for h in range(0, KC, 4):
                blocks.append(dict(ranges=[(2048 + h * P, 512)], chunks=[(o * P, P, ("v", h + o)) for o in range(4)]))
            for h in range(0, KC, 4):
                blocks.append(dict(ranges=[(4096 + h * P, 512)], chunks=[(o * P, P, ("g", h + o)) for o in range(4)]))
            linear(hv, hT.all, KC, gla_w_in[0], blocks, ttiles_of(n), epi)

    def scan_core(H, DKC, DV, gname, U=P):
        NUu = TT // U
        EXd = EX if U == P else EX64
        with kb.scope() as es2:
            qt = kb.tile([P, DKC * TT], BF16, es=es2)
            kt = kb.tile([P, DKC * TT], BF16, es=es2)
            khu = [kb.tile([P, DKC * P], BF16, es=es2) for _ in range(2)]
            vt = kb.tile([P, NUu * DV], BF16, es=es2)
            oacc = kb.tile([P, NUu * DV], F32, NUu, es=es2)
            ext = kb.tile([P, DKC * NUu], F32, es=es2)
            S = [kb.tile([P, DV], F32, es=es2) for _ in range(DKC)]
            Sb = [kb.tile([P, DV], BF16, es=es2) for _ in range(DKC)]
            asb = [kb.tile([P, P], BF16, es=es2) for _ in range(2)]
            khs = [kb.tile([P, DKC * P], BF16, es=es2) for _ in range(2)]
            sq = kb.tile([P, DV], F32, es=es2)
            ssq = kb.tile([P, 1], F32, es=es2)
            on = kb.tile([P, DV], BF16, es=es2)
            gt = [kb.tile([P, P], BF16, es=es2) for _ in range(2)]
            osb = [kb.tile([P, P], BF16, es=es2) for _ in range(2)]
            sel = [0]
            qv = qt[:].rearrange("p (k t) -> p k t", k=DKC)
            kv = kt[:].rearrange("p (k t) -> p k t", k=DKC)
            vv = vt[:].rearrange("p (u v) -> p u v", v=DV)
            ov = oacc[:].rearrange("p (u v) -> p u v", v=DV)
            exv = ext[:].rearrange("p (k u) -> p k u", k=DKC)
            for h in range(H):
                r0 = h * DKC * P
                kb.dma("sp", vv[0:U], VT[0, :, h * DV:(h + 1) * DV].rearrange("(u p) v -> p u v", p=U), vt, r=[SCR], w=vt.all)
                for dr in range(2):
                    for (t, vw, src) in ((qt, qv, QT), (kt, kv, KTs)):
                        kb.dma("sp", vw, src[dr, r0:r0 + DKC * P, :].rearrange("(k p) t -> p k t", p=P), t, r=[SCR], w=t.all)
                    kb.dma("sp", exv, EXd[dr, r0:r0 + DKC * P, :].rearrange("(k p) u -> p k u", p=P), ext, r=[SCR], w=ext.all)
                    for k in range(DKC):
                        kb.op("dve", lambda e, k=k: e.memset(S[k][:], 0.0), w=S[k].all)
                        kb.op("pool", lambda e, k=k: e.memset(Sb[k][:], 0.0), w=Sb[k].all)
                    nc_ = TCX // U
                    order = list(range(NUu)) if dr == 0 else list(range(nc_ - 1, -1, -1)) + list(range(NUu - 1, nc_ - 1, -1))
                    for u in order:
                        ts = slice(u * U, (u + 1) * U)
                        sel[0] ^= 1
                        a, khb = asb[sel[0]], khs[sel[0]]
                        pa = nps()
                        for k in range(DKC):
                            kb.op("pe", lambda e, k=k: e.matmul(pa[0:U, 0:U], lhsT=kv[:, k, ts], rhs=qv[:, k, ts], start=(k == 0), stop=(k == DKC - 1)),
                                  r=list(kt.all) + list(qt.all), w=pa.all, inc=(k == DKC - 1))
                        kb.op("dve", lambda e: e.tensor_tensor(out=a[0:U, 0:U], in0=pa[0:U, 0:U], in1=tri[0:U, dr * P:dr * P + U], op=ALU.mult),
                              r=list(pa.all) + list(tri.all), w=a.all)
                        po = nps()
                        kb.op("pe", lambda e: e.matmul(po[0:U, 0:DV], lhsT=a[0:U, 0:U], rhs=vv[0:U, u, :], start=True, stop=False),
                              r=list(a.all) + list(vt.all), w=po.all, inc=False)
                        for k in range(DKC):
                            kb.op("pe", lambda e, k=k: e.matmul(po[0:U, 0:DV], lhsT=qv[:, k, ts], rhs=Sb[k][:], start=False, stop=(k == DKC - 1)),
                                  r=list(qt.all) + list(Sb[k].all), w=po.all, inc=(k == DKC - 1))
                        if dr == 0:
                            kb.op("act", lambda e: e.copy(out=ov[0:U, u, :], in_=po[0:U, 0:DV]), r=po.all, w=[oacc.parts[u]])
                        else:
                            kb.op("dve", lambda e: e.tensor_tensor(out=ov[0:U, u, :], in0=po[0:U, 0:DV], in1=ov[0:U, u, :], op=ALU.add),
                                  r=list(po.all) + [oacc.parts[u]], w=[oacc.parts[u]])
                        kht = khu[sel[0]]
                        khv = kht[:, 0:DKC * U].rearrange("p (k t) -> p k t", k=DKC)
                        kb.dma("sp", khv, KHs[dr, r0:r0 + DKC * P, u * U:(u + 1) * U].rearrange("(k p) t -> p k t", p=P), kht, r=[SCR], w=kht.all)
                        psb = npsb()
                        pv = psb[:, 0:DKC * P].rearrange("p (k d) -> p k d", d=P)
                        for k in range(DKC):
                            kb.op("pe", lambda e, k=k: e.transpose(out=pv[0:U, k, :], in_=khv[:, k, :], identity=ident[:]),
                                  r=list(kht.all) + list(ident.all), w=psb.all, inc=(k == DKC - 1))
                        kb.op("act", lambda e: e.copy(out=khb[0:U, :], in_=psb[0:U, 0:DKC * P]), r=psb.all, w=khb.all)
                        for k in range(DKC):
                            pd = nps()
                            kb.op("pe", lambda e, k=k, pd=pd: e.matmul(pd[:, 0:DV], lhsT=khb[0:U, k * P:(k + 1) * P], rhs=vv[0:U, u, :], start=True, stop=True),
                                  r=list(khb.all) + list(vt.all), w=pd.all)
                            kb.op("dve", lambda e, k=k, pd=pd: e.scalar_tensor_tensor(
                                out=S[k][:], in0=S[k][:], scalar=exv[:, k, u:u + 1], in1=pd[:, 0:DV], op0=ALU.mult, op1=ALU.add),
                                r=list(S[k].all) + list(pd.all) + list(ext.all), w=S[k].all)
                            kb.op("act", lambda e, k=k: e.copy(out=Sb[k][:], in_=S[k][:]), r=S[k].all, w=Sb[k].all)
                for u in range(NUu):
                    kb.op("act", lambda e, u=u: e.activation(out=sq[0:U, :], in_=ov[0:U, u, :], func=AF.Square),
                          r=[oacc.parts[u]], w=list(sq.all))
                    kb.op("dve", lambda e: e.reduce_sum(out=ssq[0:U, :], in_=sq[0:U, :], axis=mybir.AxisListType.X), r=sq.all, w=ssq.all)
                    kb.op("act", lambda e: e.activation(out=ssq[0:U, :], in_=ssq[0:U, :], func=AF.Sqrt, bias=V("eps")[0:U, :], scale=1.0 / DV), r=ssq.all, w=ssq.all)
                    kb.op("dve", lambda e: e.reciprocal(out=ssq[0:U, :], in_=ssq[0:U, :]), r=ssq.all, w=ssq.all)
                    kb.op("dve", lambda e, u=u: e.tensor_scalar(out=on[0:U, :], in0=ov[0:U, u, :], scalar1=ssq[0:U, 0:1], scalar2=None, op0=ALU.mult),
                          r=[oacc.parts[u]] + list(ssq.all), w=on.all)
                    for cb in range(DV // P):
                        ch = (h * DV) // P + cb
                        sel[0] ^= 1
                        g, ob = gt[sel[0]], osb[sel[0]]
                        kb.dma("sp", g[:, 0:U], GATE[ch * P:(ch + 1) * P, u * U:(u + 1) * U], g, r=[SCR], w=g.all)
                        psb = npsb()
                        kb.op("pe", lambda e, cb=cb: e.transpose(out=psb[:, 0:U], in_=on[0:U, cb * P:(cb + 1) * P], identity=ident[0:U, 0:U]),
                              r=list(on.all) + list(ident.all), w=psb.all)
                        kb.op("dve", lambda e, ch=ch, g=g, ob=ob: e.scalar_tensor_tensor(
                            out=ob[:, 0:U], in0=psb[:, 0:U], scalar=V(gname, ch), in1=g[:, 0:U], op0=ALU.mult, op1=ALU.mult),
                            r=list(psb.all) + list(g.all), w=ob.all)
                        kb.dma("sp", ZT[ch * P:(ch + 1) * P, u * U:(u + 1) * U], ob[:, 0:U], ob, r=ob.all, w=[SCR])

    with kb.scope() as es2:
        cp = [kb.tile([P, 4352], F32, es=es2) for _ in range(2)]
        for r in range(KC):
            t = cp[r % 2]
            kb.dma("sp", t[:], XIN[r * P:(r + 1) * P, :], t, w=t.all)
            kb.dma("sp", XR[r * P:(r + 1) * P, :], t[:], t, r=t.all, w=XRb)
    kb.barrier()

    for i in range(nlayers):
        kind, j = i % 3, i // 3
        last = i == DEPTH - 1
        sis = list(range(len(SUPER)))
        if last and kind == 0:
            sis = sis[1:]
        mv, mav = pass_mod(i)
        if kind == 0:
            for si in sis:
                hyena_pass_a(i, j, si, mv, mav)
            kb.barrier()
            for w_, (qq, ee) in enumerate((("sp", parCH), ("act", parsCH), ("pool", pargCH))):
                for r0_, r1_ in ((0, TT // 2), (TT // 2, TT)):
                    kb.dma(qq, VTm[w_, r0_:r1_, :], VT[w_, r0_:r1_, :][:, bass.ds(ee, CH)], dh[w_], r=[SCR], w=[SCR])
            kb.barrier()
            hyena_filters(j, TL)
            kb.barrier()
            hyena_conv(TL, TCX)
            kb.barrier()
            if 0 in sis:
                hyena_filters(j, TCX)
                kb.barrier()
                hyena_conv(TCX, 0)
            kb.allgather([(ZTs[c_ * P:(c_ + 1) * P, :], ZTg[c_ * 2 * P:(c_ + 1) * 2 * P, :]) for c_ in range(CH // P)], GROUPS)
            Wout = hy_w_out[j]
        elif kind == 1:
            with kb.scope() as esl:
                lbt = kb.tile([P, 4 * KC], F32, es=esl)
                ee = kb.tile([P, 2 * 4 * KC], F32, es=esl)
                sm = kb.tile([P, 2 * KC], F32, es=esl)
                kb.op("act", lambda e: e.activation(out=ee[:], in_=V("lbl", 0, 2 * 4 * KC), func=AF.Exp), r=vecs.all, w=ee.all)
                ev = ee[:].rearrange("p (d l k) -> p d l k", d=2, l=4)
                smv = sm[:].rearrange("p (d k) -> p d k", d=2)
                kb.op("dve", lambda e: e.tensor_tensor(out=smv, in0=ev[:, :, 0, :], in1=ev[:, :, 1, :], op=ALU.add), r=ee.all, w=sm.all)
                kb.op("dve", lambda e: e.tensor_tensor(out=smv, in0=smv, in1=ev[:, :, 2, :], op=ALU.add), r=list(ee.all) + list(sm.all), w=sm.all)
                kb.op("dve", lambda e: e.tensor_tensor(out=smv, in0=smv, in1=ev[:, :, 3, :], op=ALU.add), r=list(ee.all) + list(sm.all), w=sm.all)
                kb.op("dve", lambda e: e.reciprocal(out=sm[:], in_=sm[:]), r=sm.all, w=sm.all)
                lv = lbt[:].rearrange("p (a d k) -> p a d k", a=2, d=2)
                kb.op("dve", lambda e: e.tensor_tensor(out=lv[:, 0], in0=ev[:, :, 1, :], in1=smv, op=ALU.mult), r=list(ee.all) + list(sm.all), w=lbt.all)
                kb.op("dve", lambda e: e.tensor_scalar(out=lv[:, 1], in0=lv[:, 0], scalar1=-1.0, scalar2=1.0, op0=ALU.mult, op1=ALU.add),
                      r=lbt.all, w=lbt.all)
                for si in sis:
                    hgrn_pass_a(i, si, mv, mav, lbt)
            kb.barrier()
            scan_core(16, 1, 128, "hgg", U=64)
            kb.barrier()
            Wout = hg_w_out[0]
        else:
            with kb.scope() as esl:
                wup = kb.tile([P, 2048], BF16, es=esl)
                kb.dma("pool", wup[0:16, :].rearrange("p (d n) -> p d n", d=2), gla_w_up[0].rearrange("d p n -> p d n"), wup, w=wup.all)
                for si in sis:
                    gla_pass_a(i, si, mv, mav, wup)
            kb.barrier()
            scan_core(4, 2, 512, "glg")
            kb.barrier()
            Wout = gla_w_out[0]
        for si in sis:
            pass_out_ffn(i, si, mv, mav, Wout, hy=(kind == 0))
        kb.barrier()

    with kb.scope() as es2:
        xs = kb.tile([P, KC * 256], F32, es=es2)
        sq = kb.tile([P, KC * 256], BF16, es=es2)
        rstd = kb.tile([P, 256], F32, es=es2)
        ot = kb.tile([P, KC * 256], F32, es=es2)
        xv = xs[:].rearrange("p (k t) -> p k t", t=256)
        sv = sq[:].rearrange("p (k t) -> p k t", t=256)
        otv = ot[:].rearrange("p (k t) -> p k t", t=256)
        for s0 in range(0, TL, 256):
            kb.dma("sp", xv, XR[:, TCX + s0:TCX + s0 + 256].rearrange("(k p) t -> p k t", p=P), xs, r=XRb, w=xs.all)
            kb.op("act", lambda e: e.activation(out=sq[:], in_=xs[:], func=AF.Square), r=xs.all, w=sq.all)
            ps = nps()
            for kc in range(KC):
                kb.op("pe", lambda e, kc=kc: e.matmul(ps[:, 0:256], lhsT=ones[:, 0:P], rhs=sv[:, kc, :], start=(kc == 0), stop=(kc == KC - 1)),
                      r=list(sq.all) + list(ones.all), w=ps.all, inc=(kc == KC - 1))
            kb.op("act", lambda e: e.activation(out=rstd[:], in_=ps[:, 0:256], func=AF.Sqrt, bias=V("eps"), scale=1.0 / D), r=ps.all, w=rstd.all)
            kb.op("dve", lambda e: e.reciprocal(out=rstd[:], in_=rstd[:]), r=rstd.all, w=rstd.all)
            for kc in range(KC):
                kb.op("dve", lambda e, kc=kc: e.scalar_tensor_tensor(out=otv[:, kc, :], in0=xv[:, kc, :], scalar=V("fing", kc), in1=rstd[:],
                                                                     op0=ALU.mult, op1=ALU.mult), r=list(xs.all) + list(rstd.all), w=ot.all)
            kb.dma("sp", OUT[:, s0:s0 + 256].rearrange("(k p) t -> p k t", p=P), otv, ot, r=ot.all, w=[SCR])
    kb.barrier()
    es.close()
    return nc


VOFF = {}
NV = 0


def _voff():
    global NV
    o = 0
    for name, n in (("eps", 1), ("bmod", DEPTH * 96), ("n1g", DEPTH * KC), ("n2g", DEPTH * KC), ("fing", KC),
                    ("cw", 2 * 3 * 48), ("ffreq", 4), ("fbias", 4), ("lbl", 2 * 4 * KC), ("hgg", KC), ("glg", KC), ("nbup", 16)):
        VOFF[name] = o
        o += n
    NV = o


_voff()
nlayers = DEPTH


def fm(v):
    v = np.asarray(v, np.float32).reshape(-1, P)
    return v.T


def pack_vecs(inp):
    vecs = np.zeros((P, NV), np.float32)

    def put(name, arr, off=0):
        arr = np.asarray(arr, np.float32)
        vecs[:arr.shape[0], VOFF[name] + off:VOFF[name] + off + arr.shape[1]] = arr
    put("eps", np.full((P, 1), EPS, np.float32))
    for i in range(DEPTH):
        put("bmod", fm(inp["b_mod"][i]), i * 96)
        put("n1g", fm(inp["norm1_g"][i]), i * KC)
        put("n2g", fm(inp["norm2_g"][i]), i * KC)
    put("fing", fm(inp["final_g"]))
    for j in range(2):
        for tap in range(3):
            put("cw", fm(inp["hy_conv_w"][j, tap]), (j * 3 + tap) * 48)
        for q in range(2):
            put("ffreq", inp["hy_ffreq"][j, q].reshape(64, 1), j * 2 + q)
        put("fbias", inp["hy_fb1"][j].reshape(64, 1), j * 2 + 0)
        put("fbias", inp["hy_fb2"][j].reshape(64, 1), j * 2 + 1)
    for d in range(2):
        for l in range(4):
            put("lbl", fm(inp["hg_lb_logits"][d, l]), (d * 4 + l) * KC)
    put("hgg", fm(inp["hg_onorm_g"][0]))
    put("glg", fm(inp["gla_onorm_g"][0]))
    for d in range(2):
        put("nbup", -fm(inp["gla_b_up"][0, d]), d * 8)
    return vecs


_CONST = {}


def consts():
    if _CONST:
        return _CONST
    bf = ml_dtypes.bfloat16
    c = {}
    c["c_ident"] = np.eye(P, dtype=np.float32).astype(bf)
    on = np.ones((P, 2 * P), np.float32)
    on[P - 1, P:] = 0.0
    c["c_ones"] = on.astype(bf)
    s = np.arange(P)[:, None]
    t = np.arange(P)[None, :]
    c["c_tri"] = np.concatenate([(s <= t), (s >= t)], axis=1).astype(np.float32)
    sm = np.ones((P, 1024), np.float32)
    sm[:, ::P] = 0.0
    c["c_smask"] = sm
    sm2 = np.ones((P, 1024), np.float32)
    sm2[:, ::64] = 0.0
    c["c_smask64"] = sm2
    delt = np.abs(np.linspace(DMIN, DMAX, D, dtype=np.float32))
    c["c_negd"] = np.broadcast_to(-delt[None, :], (P, D)).astype(np.float32).copy()
    for L in (TL, TCX):
        TCn, KF = L // P, L // P + 1
        N = 2 * L
        pos = np.arange(L, dtype=np.float32)
        tt = pos / max(L - 1, 1)
        bands = np.arange(1, 17, dtype=np.float32)
        ang = (2.0 * math.pi / L) * pos[:, None] * bands[None, :]
        z = np.concatenate([tt[:, None], np.cos(ang), -np.sin(ang)], axis=-1).astype(np.float32)
        c["c_zT%d" % L] = np.ascontiguousarray(z.T)
        c["c_tn%d" % L] = np.ascontiguousarray(tt.reshape(TCn, P).T)
        kpad = KF * P
        kk = np.arange(kpad, dtype=np.float64)
        valid = (kk <= L).astype(np.float64)
        tpos = np.arange(L, dtype=np.float64)
        ph = 2 * math.pi * ((tpos[:, None] * kk[None, :]) % N) / N
        Fre = np.cos(ph) * valid
        Fim = -np.sin(ph) * valid
        Ff = np.stack([Fre.reshape(TCn, P, KF, P), Fim.reshape(TCn, P, KF, P)], axis=3)
        c["c_Ff%d" % L] = np.ascontiguousarray(Ff.transpose(2, 1, 0, 3, 4).reshape(KF, P, TCn * 256)).astype(bf)
        phb = 2 * math.pi * (((tpos[:, None] + 1) * kk[None, :]) % N) / N
        rowv = (tpos < L - 1).astype(np.float64)[:, None]
        Bre = np.cos(phb) * valid * rowv
        Bim = np.sin(phb) * valid * rowv
        Kre = np.concatenate([Fre, Bre], axis=0)
        Kim = np.concatenate([Fim, Bim], axis=0)
        FFm = np.concatenate([Kre, Kim], axis=1)
        FFb = FFm.reshape(2 * TCn, P, 2 * KF, P).transpose(2, 1, 0, 3).reshape(2 * KF, P, 2 * TCn * P)
        c["c_FF%d" % L] = np.ascontiguousarray(FFb).astype(bf)
        wk = np.where((kk == 0) | (kk == L), 1.0, 2.0) * valid / N
        phi = 2 * math.pi * ((kk[:, None] * tpos[None, :]) % N) / N
        Gre = np.cos(phi) * wk[:, None]
        Gim = -np.sin(phi) * wk[:, None]
        Gm = np.concatenate([Gre, Gim], axis=0)
        Gb = Gm.reshape(2 * KF, P, TCn, P).transpose(2, 1, 0, 3).reshape(TCn, P, 2 * KF * P)
        c["c_G%d" % L] = np.ascontiguousarray(Gb).astype(bf)
    _CONST.update(c)
    return _CONST


def make_in_maps(inp):
    cst = consts()
    vecs = pack_vecs(inp)
    B = inp["x"].shape[0]
    shared = {k: np.ascontiguousarray(inp[k], dtype=np.float32) for k in
              ("w_mod", "w_ffn_in", "w_ffn_out", "hy_w_in", "hy_w_out", "hy_fw1", "hy_fw2", "hy_fwout",
               "hg_w_in", "hg_w_out", "gla_w_in", "gla_w_up", "gla_w_out")}
    shared["fskip"] = np.ascontiguousarray(np.broadcast_to(inp["hy_fskip"].reshape(2, 1, 2 * D), (2, P, 2 * D)), dtype=np.float32)
    shared["vecs"] = vecs
    shared.update(cst)
    in_maps = []
    for core in range(8):
        b = (core // 2) % B
        m = dict(shared)
        m["xin"] = np.ascontiguousarray(np.concatenate([inp["ctx"][b].T, inp["x"][b].T], axis=1), dtype=np.float32)
        sc = np.stack([fm(inp["c"][b]), fm(inp["c_ctx"])], axis=2)
        m["scin"] = np.ascontiguousarray(sc.reshape(P, KC * 2), dtype=np.float32)
        in_maps.append(m)
    return in_maps


def kernel(**inp):
    inp = {k: np.asarray(v) for k, v in inp.items()}
    nc = build()
    in_maps = make_in_maps(inp)
    B = inp["x"].shape[0]
    res = run_bass_kernel_spmd(nc, in_maps, core_ids=list(range(8)))
    out = np.stack([np.asarray(res.results[2 * b]["out"]).T for b in range(B)], axis=0)
    return np.ascontiguousarray(out, dtype=np.float32)
```

```python
import math
from contextlib import ExitStack
import numpy as np
import ml_dtypes
import concourse.bass as bass
import concourse.mybir as mybir
from concourse.bass_utils import run_bass_kernel_spmd

F32 = mybir.dt.float32
BF16 = mybir.dt.bfloat16
ALU = mybir.AluOpType
AF = mybir.ActivationFunctionType
P = 128
D = 2048
KC = 16
TL = 4096
TCX = 256
TT = TL + TCX
CH = D // 2
FF = 5632
NU = TT // P
DEPTH = 4
EPS = 1e-6
DMIN = math.log(1e-2) / 1.5
DMAX = math.log(1e-2) / 0.3
WELE = 8704
SUPER = [(0, TCX, 1)] + [(TCX + i * 1024, 1024, 0) for i in range(4)]


class Buf:
    __slots__ = ("w", "r")

    def __init__(self):
        self.w = None
        self.r = {}


class Tile:
    def __init__(self, h, nparts=1):
        self.h = h
        self.parts = [Buf() for _ in range(nparts)]
        self.sem = None
        self.cnt = 0

    def __getitem__(self, k):
        return self.h[k]

    @property
    def b(self):
        return self.parts[0]

    @property
    def all(self):
        return self.parts


class KB:
    def __init__(self, nc, es):
        self.nc = nc
        self.es = es
        self.eng = {"pe": nc.tensor, "act": nc.scalar, "dve": nc.vector, "pool": nc.gpsimd, "sp": nc.sync}
        self.sem = {}
        self.cnt = {}
        for e in ("pe", "act", "dve", "pool"):
            self.sem[e] = es.enter_context(nc.semaphore("s_" + e))
            self.cnt[e] = 0
        self.waited = {e: {} for e in self.eng}
        self.tiles = []
        self.nm = 0
        self.sempool = []
        self.ccsem = es.enter_context(nc.semaphore("s_cc"))
        self.cccnt = 0
        self.bsem = es.enter_context(nc.semaphore("s_bar"))
        self.bcnt = 0

    def name(self, p):
        self.nm += 1
        return "%s%d" % (p, self.nm)

    def tile(self, shape, dt, nparts=1, es=None):
        h = (es or self.es).enter_context(self.nc.sbuf_tensor(self.name("t"), list(shape), dt))
        t = Tile(h, nparts)
        if self.sempool:
            t.sem, t.cnt, t.key = self.sempool.pop()
        else:
            t.sem = self.es.enter_context(self.nc.semaphore(self.name("d")))
            t.key = self.name("k")
        self.tiles.append(t)
        if es is not None and hasattr(es, "mine"):
            es.mine.append(t)
        return t

    def psum(self, shape, dt):
        h = self.es.enter_context(self.nc.psum_tensor(self.name("p"), list(shape), dt))
        return Tile(h, 1)

    def _wait(self, eng, ev):
        if ev is None:
            return
        sem, val, key = ev
        if eng == "pe" and key == "pe":
            return
        if self.waited[eng].get(key, 0) >= val:
            return
        self.waited[eng][key] = val
        self.eng[eng].wait_ge(sem, val)

    def _deps(self, eng, r, w):
        for b in r:
            self._wait(eng, b.w)
        for b in w:
            self._wait(eng, b.w)
            for ev in list(b.r.values()):
                self._wait(eng, ev)

    def _mark(self, ev, r, w):
        for b in r:
            b.r[ev[2]] = ev
        for b in w:
            b.w = ev
            b.r = {}

    def op(self, eng, fn, r=(), w=(), inc=True):
        self._deps(eng, r, w)
        ins = fn(self.eng[eng])
        if inc:
            self.cnt[eng] += 1
            ins.then_inc(self.sem[eng], 1)
            ev = (self.sem[eng], self.cnt[eng], eng)
        else:
            ev = (self.sem[eng], self.cnt[eng] + 1, eng)
        self._mark(ev, r, w)
        return ins

    def dma(self, q, out, in_, tile, r=(), w=()):
        self._deps(q, r, w)
        ins = self.eng[q].dma_start(out=out, in_=in_)
        tile.cnt += 16
        ins.then_inc(tile.sem, 16)
        ev = (tile.sem, tile.cnt, tile.key)
        self._mark(ev, r, w)

    def barrier(self):
        sp = self.eng["sp"]
        for e in ("pe", "act", "dve", "pool"):
            if self.cnt[e] > 0:
                self._wait("sp", (self.sem[e], self.cnt[e], e))
        for t in self.tiles:
            if t.cnt > 0:
                self._wait("sp", (t.sem, t.cnt, t.key))
        self.bcnt += 1
        ins = sp.nop()
        ins.then_inc(self.bsem, 1)
        for e in ("pe", "act", "dve", "pool"):
            self.eng[e].wait_ge(self.bsem, self.bcnt)
        for e in self.eng:
            for e2 in ("pe", "act", "dve", "pool"):
                self.waited[e][e2] = self.cnt[e2]
            for t in self.tiles:
                self.waited[e][t.key] = t.cnt

    def allgather(self, pairs, groups):
        self.barrier()
        for (src, dst) in pairs:
            self.cccnt += 1
            self.nc.gpsimd.collective_compute("AllGather", ALU.bypass, replica_groups=groups, ins=[src], outs=[dst]).then_inc(self.ccsem, 1)
        self.eng["sp"].wait_ge(self.ccsem, self.cccnt)
        self.barrier()

    def scope(self):
        kb = self

        class _S(ExitStack):
            def __exit__(s, *a):
                if a[0] is None:
                    kb.barrier()
                    kb.tiles = [t for t in kb.tiles if t not in s.mine]
                    for t in s.mine:
                        kb.sempool.append((t.sem, t.cnt, t.key))
                return ExitStack.__exit__(s, *a)
        st = _S()
        st.mine = []
        return st


def build(dbg=False, ncores=8):
    nc = bass.Bass("TRN2", target_bir_lowering=False)
    es = ExitStack()
    kb = KB(nc, es)

    def din(name, shape, dt=F32):
        return nc.dram_tensor(name, list(shape), dt, kind="ExternalInput").ap()

    def dint(name, shape, dt=F32):
        return nc.dram_tensor(name, list(shape), dt, kind="Internal").ap()

    GROUPS = [[2 * g, 2 * g + 1] for g in range(ncores // 2)]
    par = nc.sync.partition_id() % 2
    pars = nc.scalar.partition_id() % 2
    parg = nc.gpsimd.partition_id() % 2
    parCH, parsCH, pargCH = par * CH, pars * CH, parg * CH
    par512, pars512, parg512 = par * 512, pars * 512, parg * 512
    XIN = din("xin", [D, TT])
    SCIN = din("scin", [P, KC * 2])
    VECS = din("vecs", [P, NV])
    w_mod = din("w_mod", [DEPTH, D, 6 * D])
    w_ffn_in = din("w_ffn_in", [DEPTH, D, 2 * FF])
    w_ffn_out = din("w_ffn_out", [DEPTH, FF, D])
    hy_w_in = din("hy_w_in", [2, D, 3 * D])
    hy_w_out = din("hy_w_out", [2, D, D])
    hy_fw1 = din("hy_fw1", [2, 33, 64])
    hy_fw2 = din("hy_fw2", [2, 64, 64])
    hy_fwout = din("hy_fwout", [2, 64, 4 * D])
    fskip = din("fskip", [2, P, 2 * D])
    hg_w_in = din("hg_w_in", [1, D, 5 * D])
    hg_w_out = din("hg_w_out", [1, D, D])
    gla_w_in = din("gla_w_in", [1, D, 6176])
    gla_w_up = din("gla_w_up", [1, 2, 16, 1024])
    gla_w_out = din("gla_w_out", [1, D, D])
    C_ident = din("c_ident", [P, P], BF16)
    C_ones = din("c_ones", [P, 2 * P], BF16)
    C_tri = din("c_tri", [P, 2 * P])
    C_smask = din("c_smask", [P, 1024])
    C_negd = din("c_negd", [P, D])
    C_smask64 = din("c_smask64", [P, 1024])
    CL = {}
    for L in (TL, TCX):
        tc_, kf = L // P, (L // P) + 1
        CL[L] = dict(
            zT=din("c_zT%d" % L, [33, L]), tn=din("c_tn%d" % L, [P, tc_]),
            Ff=din("c_Ff%d" % L, [kf, P, tc_ * 256], BF16),
            FF=din("c_FF%d" % L, [2 * kf, P, 2 * tc_ * P], BF16),
            G=din("c_G%d" % L, [tc_, P, 2 * kf * P], BF16),
            HS=dint("hs%d" % L, [2, 2 * kf, P, CH]), TC=tc_, KF=kf)
    OUT = nc.dram_tensor("out", [D, TL], F32, kind="ExternalOutput").ap()
    XR = (nc.dram_tensor("xr", [D, TT], F32, kind="ExternalOutput").ap() if dbg else dint("xr", [D, TT]))
    ZT = None
    VT = (nc.dram_tensor("vt", [3, TT, D], BF16, kind="ExternalOutput").ap() if dbg else dint("vt", [3, TT, D], BF16))
    ZT = (nc.dram_tensor("zt", [D, TT], BF16, kind="ExternalOutput").ap() if dbg else dint("zt", [D, TT], BF16))
    QT = (nc.dram_tensor("qt", [2, D, TT], BF16, kind="ExternalOutput").ap() if dbg else dint("qt", [2, D, TT], BF16))
    KTs = (nc.dram_tensor("kts", [2, D, TT], BF16, kind="ExternalOutput").ap() if dbg else dint("kts", [2, D, TT], BF16))
    KHs = (nc.dram_tensor("khs", [2, D, TT], BF16, kind="ExternalOutput").ap() if dbg else dint("khs", [2, D, TT], BF16))
    EX = (nc.dram_tensor("ex", [2, D, NU], F32, kind="ExternalOutput").ap() if dbg else dint("ex", [2, D, NU]))
    EX64 = dint("ex64", [2, D, TT // 64])
    GATE = (nc.dram_tensor("gate", [D, TT], BF16, kind="ExternalOutput").ap() if dbg else dint("gate", [D, TT], BF16))
    VTm = dint("vtm", [3, TT, CH], BF16)
    ZTs = dint("zts", [CH, TT], BF16)
    FMm = {nm: dint(nm + "m", [2, CH, TT], BF16) for nm in ("q", "k", "h")}
    EXm = dint("exm", [2, CH, 68])
    GATEm = dint("gatem", [CH, TT], BF16)
    ZTg = dint("ztg", [2 * CH, TT], BF16)
    XRb = [Buf() for _ in SUPER]
    SCR = Buf()

    dh = [kb.tile([P, 8], F32) for _ in range(3)]
    vecs = kb.tile([P, NV], F32)
    ident = kb.tile([P, P], BF16)
    ones = kb.tile([P, 2 * P], BF16)
    tri = kb.tile([P, 2 * P], F32)
    smask = kb.tile([P, 1024], F32)
    smask64 = kb.tile([P, 1024], F32)
    scT = kb.tile([P, KC * 2], BF16)
    MOD = kb.tile([P, 96 * 2], F32)
    MA = kb.tile([P, 2 * KC * 2], F32)
    wbuf = [kb.tile([P, WELE], BF16) for _ in range(2)]
    wsel = [0]
    PS = [kb.psum([P, 512], F32) for _ in range(6)]
    PSB = [kb.psum([P, 1024], BF16) for _ in range(2)]
    pssel = [0]
    psbsel = [0]

    def nps():
        pssel[0] = (pssel[0] + 1) % 6
        return PS[pssel[0]]

    def npsb():
        psbsel[0] = (psbsel[0] + 1) % 2
        return PSB[psbsel[0]]

    def V(name, i=0, n=1):
        o = VOFF[name] + i
        return vecs[:, o:o + n]

    for (t, src) in ((vecs, VECS), (ident, C_ident), (ones, C_ones), (tri, C_tri), (smask, C_smask), (smask64, C_smask64)):
        kb.dma("sp", t[:], src, t, w=t.all)
    sc32 = kb.tile([P, KC * 2], F32)
    kb.dma("sp", sc32[:], SCIN, sc32, w=sc32.all)
    kb.op("act", lambda e: e.activation(out=scT[:], in_=sc32[:], func=AF.Silu), r=sc32.all, w=scT.all)

    def load_w(W, KCn, ranges, cast=True, pre=None):
        wsel[0] ^= 1
        wt = wbuf[wsel[0]]
        ntot = sum(n for _, n in ranges)
        assert KCn * ntot <= WELE
        view = wt[:, 0:KCn * ntot].rearrange("p (k n) -> p k n", n=ntot)
        off = 0
        for (c0, n) in ranges:
            if pre is not None:
                src = pre
            else:
                src = W[:, c0:c0 + n].rearrange("(k p) n -> p k n", p=P)
            kb.dma("pool" if cast else "sp", view[:, :, off:off + n], src, wt, w=wt.all)
            off += n
        return wt, view

    def linear(xT, xbufs, KCn, W, blocks, ttiles, epi, mparts=None):
        for blk in blocks:
            wt, view = load_w(W, KCn, blk["ranges"], pre=blk.get("pre"), cast=blk.get("cast", True))
            for (off, m, tag) in blk["chunks"]:
                pst = []
                for (t0, tn) in ttiles:
                    ps = nps()
                    for kc in range(KCn):
                        kb.op("pe", lambda e, ps=ps, kc=kc, off=off, m=m, t0=t0, tn=tn: e.matmul(
                            ps[0:m, 0:tn], lhsT=view[:, kc, off:off + m], rhs=xT[:, kc, t0:t0 + tn],
                            start=(kc == 0), stop=(kc == KCn - 1)),
                            r=list(wt.all) + list(xbufs), w=ps.all, inc=(kc == KCn - 1))
                    pst.append(ps)
                epi(tag, pst)

    def std_blocks(c0, ncols, bw=512, tag0=0):
        blocks = []
        t = tag0
        for b0 in range(c0, c0 + ncols, bw):
            n = min(bw, c0 + ncols - b0)
            ch = []
            for o in range(0, n, P):
                ch.append((o, min(P, n - o), t))
                t += 1
            blocks.append(dict(ranges=[(b0, n)], chunks=ch))
        return blocks

    def pass_mod(i):
        def epi(tag, pst):
            ps = pst[0]
            kb.op("act", lambda e: e.activation(out=MOD[:, tag * 2:tag * 2 + 2], in_=ps[:, 0:2], func=AF.Identity,
                                                bias=V("bmod", i * 96 + tag), scale=1.0), r=ps.all, w=MOD.all)
        sv = scT[:].rearrange("p (k c) -> p k c", c=2)
        linear(sv, scT.all, KC, w_mod[i], std_blocks(0, 6 * D), [(0, 2)], epi)
        mv = MOD[:].rearrange("p (s k c) -> p s k c", s=6, c=2)
        mav = MA[:].rearrange("p (n k c) -> p n k c", n=2, c=2)
        for n, (sidx, gname) in enumerate(((1, "n1g"), (4, "n2g"))):
            for col in range(2):
                kb.op("dve", lambda e, n=n, sidx=sidx, col=col, gname=gname: e.scalar_tensor_tensor(
                    out=mav[:, n, :, col], in0=mv[:, sidx, :, col], scalar=1.0, in1=V(gname, i * KC, KC),
                    op0=ALU.add, op1=ALU.mult), r=MOD.all, w=MA.all)
        return mv, mav

    def norm_mod(es2, si, mv, mav, n_idx, shift_idx, hT):
        tok0, n, col = SUPER[si]
        xs = kb.tile([P, KC * 256], F32, es=es2)
        sq = kb.tile([P, KC * 256], BF16, es=es2)
        rstd = kb.tile([P, 256], F32, es=es2)
        tmp = kb.tile([P, 256], F32, es=es2)
        xv = xs[:].rearrange("p (k t) -> p k t", t=256)
        sv = sq[:].rearrange("p (k t) -> p k t", t=256)
        hv = hT[:, 0:KC * n].rearrange("p (k t) -> p k t", k=KC)
        for s0 in range(0, n, 256):
            kb.dma("sp", xv, XR[:, tok0 + s0:tok0 + s0 + 256].rearrange("(k p) t -> p k t", p=P), xs,
                   r=[XRb[si]], w=xs.all)
            kb.op("act", lambda e: e.activation(out=sq[:], in_=xs[:], func=AF.Square), r=xs.all, w=sq.all)
            ps = nps()
            for kc in range(KC):
                kb.op("pe", lambda e, kc=kc: e.matmul(ps[:, 0:256], lhsT=ones[:, 0:P], rhs=sv[:, kc, :],
                                                      start=(kc == 0), stop=(kc == KC - 1)),
                      r=list(sq.all) + list(ones.all), w=ps.all, inc=(kc == KC - 1))
            kb.op("act", lambda e: e.activation(out=rstd[:], in_=ps[:, 0:256], func=AF.Sqrt, bias=V("eps"), scale=1.0 / D),
                  r=ps.all, w=rstd.all)
            kb.op("dve", lambda e: e.reciprocal(out=rstd[:], in_=rstd[:]), r=rstd.all, w=rstd.all)
            for kc in range(KC):
                kb.op("dve", lambda e, kc=kc: e.tensor_tensor(out=tmp[:], in0=xv[:, kc, :], in1=rstd[:], op=ALU.mult),
                      r=list(xs.all) + list(rstd.all), w=tmp.all)
                kb.op("act", lambda e, kc=kc: e.activation(out=hv[:, kc, s0:s0 + 256], in_=tmp[:], func=AF.Identity,
                                                           bias=mv[:, shift_idx, kc, col:col + 1],
                                                           scale=mav[:, n_idx, kc, col:col + 1]),
                      r=list(tmp.all) + list(MOD.all) + list(MA.all), w=hT.all)

    def ttiles_of(n):
        return [(t, min(512, n - t)) for t in range(0, n, 512)]

    def resid_epi(es2, si, mv, gate_idx):
        tok0, n, col = SUPER[si]
        xo = [kb.tile([P, 1024], F32, es=es2) for _ in range(2)]
        sel = [0]

        def epi(tag, pst):
            sel[0] ^= 1
            x = xo[sel[0]]
            kb.dma("sp", x[:, 0:n], XR[tag * P:(tag + 1) * P, tok0:tok0 + n], x, r=[XRb[si]], w=x.all)
            for ti, (t0, tn) in enumerate(ttiles_of(n)):
                ps = pst[ti]
                kb.op("dve", lambda e, ps=ps, t0=t0, tn=tn: e.scalar_tensor_tensor(
                    out=x[:, t0:t0 + tn], in0=ps[:, 0:tn], scalar=mv[:, gate_idx, tag, col:col + 1], in1=x[:, t0:t0 + tn],
                    op0=ALU.mult, op1=ALU.add), r=list(ps.all) + list(x.all) + list(MOD.all), w=x.all)
            kb.dma("sp", XR[tag * P:(tag + 1) * P, tok0:tok0 + n], x[:, 0:n], x, r=x.all, w=[XRb[si]])
        return epi

    def pass_out_ffn(i, si, mv, mav, Wout, hy=False):
        tok0, n, col = SUPER[si]
        with kb.scope() as es2:
            zt = kb.tile([P, KC * 1024], BF16, es=es2)
            zv = zt[:, 0:KC * n].rearrange("p (k t) -> p k t", k=KC)
            if hy:
                for s_ in range(2):
                    kb.dma("sp", zv[:, s_ * 8:(s_ + 1) * 8, :], ZTg[:, tok0:tok0 + n].rearrange("(c s p) t -> s p c t", s=2, p=P)[s_], zt, r=[SCR], w=zt.all)
            else:
                kb.dma("sp", zv, ZT[:, tok0:tok0 + n].rearrange("(k p) t -> p k t", p=P), zt, r=[SCR], w=zt.all)
            linear(zv, zt.all, KC, Wout, std_blocks(0, D), ttiles_of(n), resid_epi(es2, si, mv, 2))
        with kb.scope() as es2:
            hT = kb.tile([P, KC * 1024], BF16, es=es2)
            hid = kb.tile([P, 44 * 1024], BF16, es=es2)
            sg = [kb.tile([P, 512], F32, es=es2) for _ in range(2)]
            sgs = [0]
            with kb.scope() as es3:
                norm_mod(es3, si, mv, mav, 1, 3, hT)
            hv = hT[:, 0:KC * n].rearrange("p (k t) -> p k t", k=KC)
            hidv = hid[:, 0:44 * n].rearrange("p (k t) -> p k t", k=44)
            pend = {}

            def epi_in(tag, pst):
                j, isup = tag
                if not isup:
                    pend[j] = pst
                    return
                gp = pend.pop(j)
                for ti, (t0, tn) in enumerate(ttiles_of(n)):
                    sgs[0] ^= 1
                    s = sg[sgs[0]]
                    kb.op("act", lambda e, s=s, g=gp[ti], tn=tn: e.activation(out=s[:, 0:tn], in_=g[:, 0:tn], func=AF.Silu),
                          r=gp[ti].all, w=s.all)
                    kb.op("dve", lambda e, s=s, u=pst[ti], t0=t0, tn=tn: e.tensor_tensor(
                        out=hidv[:, j, t0:t0 + tn], in0=s[:, 0:tn], in1=u[:, 0:tn], op=ALU.mult),
                        r=list(s.all) + list(pst[ti].all), w=hid.all)
            blocks = []
            for j in range(44):
                blocks.append(dict(ranges=[(j * P, P), (FF + j * P, P)], chunks=[(0, P, (j, 0)), (P, P, (j, 1))]))
            linear(hv, hT.all, KC, w_ffn_in[i], blocks, ttiles_of(n), epi_in)
            linear(hidv, hid.all, 44, w_ffn_out[i], std_blocks(0, D, bw=P), ttiles_of(n), resid_epi(es2, si, mv, 5))

    def fm_to_tm(es2, src_tile, n, dst, tok0, c0, tsb):
        nt = n // P
        psb = npsb()
        pv = psb[:, 0:nt * P].rearrange("p (a b) -> p a b", b=P)
        for tt in range(nt):
            kb.op("pe", lambda e, tt=tt: e.transpose(out=pv[:, tt, :], in_=src_tile[:, tt * P:(tt + 1) * P], identity=ident[:]),
                  r=list(src_tile.all) + list(ident.all), w=psb.all, inc=(tt == nt - 1))
        tv = tsb[:, 0:nt * P].rearrange("p (a b) -> p a b", b=P)
        kb.op("act", lambda e: e.copy(out=tsb[:, 0:nt * P], in_=psb[:, 0:nt * P]), r=psb.all, w=tsb.all)
        kb.dma("sp", dst[tok0:tok0 + n, c0:c0 + P].rearrange("(a p) c -> p a c", p=P), tv, tsb, r=tsb.all, w=[SCR])

    def hyena_pass_a(i, j, si, mv, mav):
        tok0, n, col = SUPER[si]
        rowlen = 256 if si == 0 else 64
        with kb.scope() as es2:
            hT = kb.tile([P, KC * 1024], BF16, es=es2)
            with kb.scope() as es3:
                norm_mod(es3, si, mv, mav, 0, 0, hT)
            hv = hT[:, 0:KC * n].rearrange("p (k t) -> p k t", k=KC)
            pc = kb.tile([P, 1024], F32, es=es2)
            o1 = kb.tile([P, 1024], F32, es=es2)
            ob = kb.tile([P, 1024], BF16, es=es2)
            tsb = kb.tile([P, 1024], BF16, es=es2)

            def epi(tag, pst):
                for ti, (t0, tn) in enumerate(ttiles_of(n)):
                    ps = pst[ti]
                    kb.op("act", lambda e, ps=ps, t0=t0, tn=tn: e.copy(out=pc[:, t0:t0 + tn], in_=ps[:, 0:tn]), r=ps.all, w=pc.all)
                    kb.op("act", lambda e, ps=ps, t0=t0, tn=tn: e.activation(
                        out=o1[:, t0:t0 + tn], in_=ps[:, 0:tn], func=AF.Identity, bias=0.0,
                        scale=V("cw", (j * 3 + 1) * 48 + tag)), r=ps.all, w=o1.all)
                o1v = o1[:, 0:n].rearrange("p (a b) -> p a b", b=rowlen)
                pcv = pc[:, 0:n].rearrange("p (a b) -> p a b", b=rowlen)
                kb.op("dve", lambda e: e.scalar_tensor_tensor(
                    out=o1v[:, :, 1:rowlen], in0=pcv[:, :, 0:rowlen - 1], scalar=V("cw", (j * 3 + 0) * 48 + tag),
                    in1=o1v[:, :, 1:rowlen], op0=ALU.mult, op1=ALU.add), r=list(pc.all) + list(o1.all), w=o1.all)
                kb.op("dve", lambda e: e.scalar_tensor_tensor(
                    out=o1v[:, :, 0:rowlen - 1], in0=pcv[:, :, 1:rowlen], scalar=V("cw", (j * 3 + 2) * 48 + tag),
                    in1=o1v[:, :, 0:rowlen - 1], op0=ALU.mult, op1=ALU.add), r=list(pc.all) + list(o1.all), w=o1.all)
                kb.op("dve", lambda e: e.tensor_copy(out=ob[:, 0:n], in_=o1[:, 0:n]), r=o1.all, w=ob.all)
                fm_to_tm(es2, ob, n, VT[tag // KC], tok0, (tag % KC) * P, tsb)
            linear(hv, hT.all, KC, hy_w_in[j], std_blocks(0, 3 * D), ttiles_of(n), epi)

    def range_reduce_sin(es2, arg, m, n, out_ap, tmp):
        MAGIC = 12582912.0
        kb.op("dve", lambda e: e.tensor_scalar(out=tmp[0:m, 0:n], in0=arg[0:m, 0:n], scalar1=1.0 / (2 * math.pi), scalar2=MAGIC,
                                               op0=ALU.mult, op1=ALU.add), r=arg.all, w=tmp.all)
        kb.op("dve", lambda e: e.tensor_scalar(out=tmp[0:m, 0:n], in0=tmp[0:m, 0:n], scalar1=-MAGIC, scalar2=None,
                                               op0=ALU.add), r=tmp.all, w=tmp.all)
        kb.op("dve", lambda e: e.scalar_tensor_tensor(out=arg[0:m, 0:n], in0=tmp[0:m, 0:n], scalar=-2 * math.pi, in1=arg[0:m, 0:n],
                                                      op0=ALU.mult, op1=ALU.add), r=list(tmp.all) + list(arg.all), w=arg.all)
        kb.op("dve", lambda e: e.tensor_scalar(out=arg[0:m, 0:n], in0=arg[0:m, 0:n], scalar1=-3.14159, scalar2=3.14159,
                                               op0=ALU.max, op1=ALU.min), r=arg.all, w=arg.all)
        kb.op("act", lambda e: e.activation(out=out_ap, in_=arg[0:m, 0:n], func=AF.Sin), r=arg.all, w=[])

    def hyena_filters(j, L):
        c = CL[L]
        TCn, KF = c["TC"], c["KF"]
        with kb.scope() as es2:
            zT = kb.tile([P, L], F32, es=es2)
            h1 = kb.tile([P, L], F32, es=es2)
            h2 = kb.tile([P, L], BF16, es=es2)
            fw1 = kb.tile([P, 64], F32, es=es2)
            fw2 = kb.tile([P, 64], F32, es=es2)
            fwo = kb.tile([P, 4 * CH], BF16, es=es2)
            tn = kb.tile([P, TCn], F32, es=es2)
            negd = kb.tile([P, CH], F32, es=es2)
            skp = kb.tile([P, 2 * CH], F32, es=es2)
            fb = kb.tile([P, 2], F32, es=es2)
            arg = kb.tile([P, 512], F32, es=es2)
            tmp = kb.tile([P, 512], F32, es=es2)
            kb.dma("sp", zT[0:33, :], c["zT"], zT, w=zT.all)
            kb.dma("sp", fw1[0:33, :], hy_fw1[j], fw1, w=fw1.all)
            kb.dma("sp", fw2[0:64, :], hy_fw2[j], fw2, w=fw2.all)
            kb.dma("pool", fwo[0:64, :].rearrange("p (b c) -> p b c", b=4),
                   hy_fwout[j].rearrange("p (b c) -> p b c", b=4)[:, :, bass.ds(pargCH, CH)], fwo, w=fwo.all)
            kb.dma("sp", tn[:], c["tn"], tn, w=tn.all)
            kb.dma("sp", skp[:].rearrange("p (b c) -> p b c", b=2),
                   fskip[j].rearrange("p (b c) -> p b c", b=2)[:, :, bass.ds(parCH, CH)], skp, w=skp.all)
            kb.dma("sp", negd[:], C_negd[:, bass.ds(parCH, CH)], negd, w=negd.all)
            for q in range(2):
                kb.op("dve", lambda e, q=q: e.tensor_tensor(out=fb[0:64, q:q + 1], in0=V("ffreq", j * 2 + q)[0:64, :],
                                                             in1=V("fbias", j * 2 + q)[0:64, :], op=ALU.mult), r=vecs.all, w=fb.all)
            for t0 in range(0, L, 512):
                tn_ = min(512, L - t0)
                ps = nps()
                kb.op("pe", lambda e: e.matmul(ps[0:64, 0:tn_], lhsT=fw1[0:33, 0:64], rhs=zT[0:33, t0:t0 + tn_], start=True, stop=True),
                      r=list(fw1.all) + list(zT.all), w=ps.all)
                kb.op("act", lambda e: e.activation(out=arg[0:64, 0:tn_], in_=ps[0:64, 0:tn_], func=AF.Identity,
                                                    bias=fb[0:64, 0:1], scale=V("ffreq", j * 2 + 0)[0:64, :]),
                      r=list(ps.all) + list(fb.all), w=arg.all)
                range_reduce_sin(es2, arg, 64, tn_, h1[0:64, t0:t0 + tn_], tmp)
                kb._mark((kb.sem["act"], kb.cnt["act"], "act"), [], h1.all)
            for t0 in range(0, L, 512):
                tn_ = min(512, L - t0)
                ps = nps()
                kb.op("pe", lambda e: e.matmul(ps[0:64, 0:tn_], lhsT=fw2[0:64, 0:64], rhs=h1[0:64, t0:t0 + tn_], start=True, stop=True),
                      r=list(fw2.all) + list(h1.all), w=ps.all)
                kb.op("act", lambda e: e.activation(out=arg[0:64, 0:tn_], in_=ps[0:64, 0:tn_], func=AF.Identity,
                                                    bias=fb[0:64, 1:2], scale=V("ffreq", j * 2 + 1)[0:64, :]),
                      r=list(ps.all) + list(fb.all), w=arg.all)
                range_reduce_sin(es2, arg, 64, tn_, h2[0:64, t0:t0 + tn_], tmp)
                kb._mark((kb.sem["act"], kb.cnt["act"], "act"), [], h2.all)
            HF = kb.tile([P, 2 * TCn * 512], BF16, es=es2)
            hfv = HF[:].rearrange("p (k c) -> p k c", c=512)
            dec = [kb.tile([P, 512], F32, es=es2) for _ in range(2)]
            ab = [kb.tile([P, 512], BF16, es=es2) for _ in range(2)]
            rn = kb.tile([P, 512], F32, es=es2)
            ho = [kb.tile([P, 512], F32, es=es2) for _ in range(2)]
            sel = [0]
            for o in range(2):
                for ct in range(2):
                    nps_ = nps()
                    first = True
                    for tc in range(TCn):
                        d = dec[tc % 2]
                        kb.op("act", lambda e, d=d, tc=tc: e.activation(out=d[:], in_=negd[:, ct * 512:(ct + 1) * 512], func=AF.Exp,
                                                                        scale=tn[:, tc:tc + 1]), r=list(negd.all) + list(tn.all), w=d.all)
                        for dr in range(2):
                            ps = nps()
                            if ps is nps_:
                                ps = nps()
                            col0 = (dr * 2 + o) * CH + ct * 512
                            kb.op("pe", lambda e, ps=ps, tc=tc, col0=col0: e.matmul(
                                ps[:, :], lhsT=h2[0:64, tc * P:(tc + 1) * P], rhs=fwo[0:64, col0:col0 + 512], start=True, stop=True),
                                r=list(h2.all) + list(fwo.all), w=ps.all)
                            kk = dr * TCn + tc
                            kb.op("dve", lambda e, ps=ps, d=d, kk=kk: e.tensor_tensor(out=hfv[:, kk, :], in0=ps[:, :], in1=d[:], op=ALU.mult),
                                  r=list(ps.all) + list(d.all), w=HF.all)
                            sel[0] ^= 1
                            a = ab[sel[0]]
                            kb.op("act", lambda e, a=a, kk=kk: e.activation(out=a[:], in_=hfv[:, kk, :], func=AF.Abs),
                                  r=HF.all, w=a.all)
                            last = (dr == 1 and tc == TCn - 1)
                            oc0 = P if last else 0
                            kb.op("pe", lambda e, a=a, oc0=oc0, first=first, last=last: e.matmul(
                                nps_[:, :], lhsT=ones[:, oc0:oc0 + P], rhs=a[:], start=first, stop=last),
                                r=list(a.all) + list(ones.all), w=nps_.all, inc=last)
                            first = False
                    kb.op("dve", lambda e: e.reciprocal(out=rn[:], in_=nps_[:, :]), r=nps_.all, w=rn.all)
                    for blk in range(2 * KF):
                        wt, view = load_w(None, 2 * TCn, [(0, P)], cast=False,
                                          pre=c["FF"][blk].rearrange("p (k n) -> p k n", n=P))
                        ps = nps()
                        for kk in range(2 * TCn):
                            kb.op("pe", lambda e, kk=kk: e.matmul(ps[:, :], lhsT=view[:, kk, :], rhs=hfv[:, kk, :],
                                                                  start=(kk == 0), stop=(kk == 2 * TCn - 1)),
                                  r=list(wt.all) + list(HF.all), w=ps.all, inc=(kk == 2 * TCn - 1))
                        sel[0] ^= 1
                        h = ho[sel[0]]
                        kb.op("dve", lambda e, h=h: e.tensor_tensor(out=h[:], in0=ps[:, :], in1=rn[:], op=ALU.mult),
                              r=list(ps.all) + list(rn.all), w=h.all)
                        if blk < KF:
                            kb.op("pool", lambda e, h=h: e.tensor_tensor(out=h[:], in0=h[:], in1=skp[:, o * CH + ct * 512:o * CH + (ct + 1) * 512],
                                                                         op=ALU.add), r=list(h.all) + list(skp.all), w=h.all)
                        kb.dma("sp", c["HS"][o, blk, :, ct * 512:(ct + 1) * 512], h[:], h, r=h.all, w=[SCR])

    def hyena_conv(L, tok0):
        c = CL[L]
        TCn, KF = c["TC"], c["KF"]
        with kb.scope() as es2:
            Vt = kb.tile([P, TCn * 512], BF16, es=es2)
            Z1 = kb.tile([P, TCn * 512], BF16, es=es2)
            Y = kb.tile([P, 2 * KF * 512], BF16, es=es2)
            Hr = [kb.tile([P, 512], F32, es=es2) for _ in range(2)]
            Hi = [kb.tile([P, 512], F32, es=es2) for _ in range(2)]
            t1 = kb.tile([P, 512], F32, es=es2)
            t2 = kb.tile([P, 512], F32, es=es2)
            xm = [kb.tile([P, 512], BF16, es=es2) for _ in range(2)]
            zb = kb.tile([P, 512], BF16, es=es2)
            tsb = kb.tile([P, 512], BF16, es=es2)
            vv = Vt[:].rearrange("p (k c) -> p k c", c=512)
            z1v = Z1[:].rearrange("p (k c) -> p k c", c=512)
            yv = Y[:].rearrange("p (k c) -> p k c", c=512)
            sel = [0]
            for ct in range(2):
                c0 = ct * 512
                kb.dma("sp", vv, VTm[0, tok0:tok0 + L, c0:c0 + 512].rearrange("(k p) c -> p k c", p=P), Vt, r=[SCR], w=Vt.all)
                for o in range(2):
                    src, srcv = (Vt, vv) if o == 0 else (Z1, z1v)
                    for m in range(KF):
                        wt, view = load_w(None, TCn, [(0, 256)], cast=False, pre=c["Ff"][m].rearrange("p (k n) -> p k n", n=256))
                        pr, pi = nps(), nps()
                        for part, ps in ((0, pr), (1, pi)):
                            for kk in range(TCn):
                                kb.op("pe", lambda e, ps=ps, kk=kk, part=part: e.matmul(
                                    ps[:, :], lhsT=view[:, kk, part * P:(part + 1) * P], rhs=srcv[:, kk, :],
                                    start=(kk == 0), stop=(kk == TCn - 1)), r=list(wt.all) + list(src.all), w=ps.all, inc=(kk == TCn - 1))
                        sel[0] ^= 1
                        hr, hi = Hr[sel[0]], Hi[sel[0]]
                        kb.dma("sp", hr[:], c["HS"][o, m, :, c0:c0 + 512], hr, r=[SCR], w=hr.all)
                        kb.dma("sp", hi[:], c["HS"][o, KF + m, :, c0:c0 + 512], hi, r=[SCR], w=hi.all)
                        kb.op("dve", lambda e: e.tensor_tensor(out=t1[:], in0=pr[:, :], in1=hr[:], op=ALU.mult), r=list(pr.all) + list(hr.all), w=t1.all)
                        kb.op("dve", lambda e: e.tensor_tensor(out=t2[:], in0=pi[:, :], in1=hi[:], op=ALU.mult), r=list(pi.all) + list(hi.all), w=t2.all)
                        kb.op("pool", lambda e, m=m: e.tensor_tensor(out=yv[:, m, :], in0=t1[:], in1=t2[:], op=ALU.subtract),
                              r=list(t1.all) + list(t2.all), w=Y.all)
                        kb.op("dve", lambda e: e.tensor_tensor(out=t1[:], in0=pr[:, :], in1=hi[:], op=ALU.mult), r=list(pr.all) + list(hi.all), w=t1.all)
                        kb.op("dve", lambda e: e.tensor_tensor(out=t2[:], in0=pi[:, :], in1=hr[:], op=ALU.mult), r=list(pi.all) + list(hr.all), w=t2.all)
                        kb.op("pool", lambda e, m=m: e.tensor_tensor(out=yv[:, KF + m, :], in0=t1[:], in1=t2[:], op=ALU.add),
                              r=list(t1.all) + list(t2.all), w=Y.all)
                    for tc in range(TCn):
                        wt, view = load_w(None, 2 * KF, [(0, P)], cast=False, pre=c["G"][tc].rearrange("p (k n) -> p k n", n=P))
                        ps = nps()
                        for kk in range(2 * KF):
                            kb.op("pe", lambda e, kk=kk: e.matmul(ps[:, :], lhsT=view[:, kk, :], rhs=yv[:, kk, :],
                                                                  start=(kk == 0), stop=(kk == 2 * KF - 1)),
                                  r=list(wt.all) + list(Y.all), w=ps.all, inc=(kk == 2 * KF - 1))
                        sel[0] ^= 1
                        x = xm[sel[0]]
                        kb.dma("sp", x[:], VTm[1 + o, tok0 + tc * P:tok0 + (tc + 1) * P, c0:c0 + 512], x, r=[SCR], w=x.all)
                        if o == 0:
                            kb.op("dve", lambda e, tc=tc: e.tensor_tensor(out=z1v[:, tc, :], in0=ps[:, :], in1=x[:], op=ALU.mult),
                                  r=list(ps.all) + list(x.all), w=Z1.all)
                        else:
                            kb.op("dve", lambda e: e.tensor_tensor(out=zb[:], in0=ps[:, :], in1=x[:], op=ALU.mult),
                                  r=list(ps.all) + list(x.all), w=zb.all)
                            psb = npsb()
                            pv = psb[:, 0:512].rearrange("p (a b) -> p a b", b=P)
                            for cc in range(4):
                                kb.op("pe", lambda e, cc=cc: e.transpose(out=pv[:, cc, :], in_=zb[:, cc * P:(cc + 1) * P], identity=ident[:]),
                                      r=list(zb.all) + list(ident.all), w=psb.all, inc=(cc == 3))
                            kb.op("act", lambda e: e.copy(out=tsb[:], in_=psb[:, 0:512]), r=psb.all, w=tsb.all)
                            kb.dma("sp", ZTs[c0:c0 + 512, tok0 + tc * P:tok0 + (tc + 1) * P].rearrange("(a p) t -> p a t", p=P),
                                   tsb[:].rearrange("p (a b) -> p a b", b=P), tsb, r=tsb.all, w=[SCR])

    def gate_chain(es2, T, q_sb, k_of_dir, lf, dr, row0, tok0, n, U=P):
        nu = n // U
        EXd = EX if U == P else EX64
        smk = smask if U == P else smask64
        cs, d, e1, ob, ctot, nct, ex = T["cs"], T["d"], T["e1"], T["ob"], T["ctot"], T["nct"], T["ex"]
        kb.op("dve", lambda e: e.tensor_tensor_scan(out=cs[:, 0:n], data0=smk[:, 0:n], data1=lf[:, 0:n], initial=0.0,
                                                    op0=ALU.mult, op1=ALU.add), r=list(lf.all) + list(smk.all), w=cs.all)
        csv = cs[:, 0:n].rearrange("p (u t) -> p u t", t=U)
        kb.op("dve", lambda e: e.tensor_copy(out=ctot[:, 0:nu], in_=csv[:, :, U - 1]), r=cs.all, w=ctot.all)
        kb.op("dve", lambda e: e.tensor_scalar(out=nct[:, 0:nu], in0=ctot[:, 0:nu], scalar1=-1.0, scalar2=None, op0=ALU.mult),
              r=ctot.all, w=nct.all)
        kb.op("act", lambda e: e.activation(out=ex[:, 0:nu], in_=ctot[:, 0:nu], func=AF.Exp), r=ctot.all, w=ex.all)
        u0 = tok0 // U
        kb.dma("sp", EXd[dr, row0:row0 + P, u0:u0 + nu], ex[:, 0:nu], ex, r=ex.all, w=[SCR])

        def emit(dst, src_mul, make_exp):
            make_exp()
            kb.op("dve", lambda e: e.tensor_tensor(out=ob[:, 0:n], in0=src_mul[:, 0:n], in1=e1[:, 0:n], op=ALU.mult),
                  r=list(src_mul.all) + list(e1.all), w=ob.all)
            kb.dma("sp", dst[dr, row0:row0 + P, tok0:tok0 + n], ob[:, 0:n], ob, r=ob.all, w=[SCR])

        def full_exp(src, scale):
            return lambda: kb.op("act", lambda e: e.activation(out=e1[:, 0:n], in_=src[:, 0:n], func=AF.Exp, scale=scale),
                                 r=src.all, w=e1.all)

        def unit_exp(src, scale, bias_t):
            def f():
                for u in range(nu):
                    kb.op("act", lambda e, u=u: e.activation(out=e1[:, u * U:(u + 1) * U], in_=src[:, u * U:(u + 1) * U], func=AF.Exp,
                                                             scale=scale, bias=bias_t[:, u:u + 1]),
                          r=list(src.all) + list(bias_t.all), w=e1.all)
            return f
        if dr == 0:
            emit(QT, q_sb, full_exp(cs, 1.0))
            emit(KTs, k_of_dir, full_exp(cs, -1.0))
            emit(KHs, k_of_dir, unit_exp(cs, -1.0, ctot))
        else:
            kb.op("dve", lambda e: e.tensor_tensor(out=d[:, 0:n], in0=lf[:, 0:n], in1=cs[:, 0:n], op=ALU.subtract),
                  r=list(lf.all) + list(cs.all), w=d.all)
            emit(QT, q_sb, unit_exp(d, 1.0, ctot))
            emit(KTs, k_of_dir, unit_exp(d, -1.0, nct))
            emit(KHs, k_of_dir, full_exp(d, -1.0))

    def chain_tiles(es2):
        T = {}
        for nm in ("cs", "d", "e1"):
            T[nm] = kb.tile([P, 1024], F32, es=es2)
        T["ob"] = kb.tile([P, 1024], BF16, es=es2)
        for nm in ("ctot", "nct", "ex"):
            T[nm] = kb.tile([P, 16], F32, es=es2)
        return T

    def hgrn_pass_a(i, si, mv, mav, lbt):
        tok0, n, col = SUPER[si]
        with kb.scope() as es2:
            hT = kb.tile([P, KC * 1024], BF16, es=es2)
            with kb.scope() as es3:
                norm_mod(es3, si, mv, mav, 0, 0, hT)
            hv = hT[:, 0:KC * n].rearrange("p (k t) -> p k t", k=KC)
            T = chain_tiles(es2)
            q_sb = kb.tile([P, 1024], F32, es=es2)
            fv = kb.tile([P, 1024], F32, es=es2)
            lf = kb.tile([P, 1024], F32, es=es2)
            k_sb = kb.tile([P, 1024], F32, es=es2)
            ob2 = kb.tile([P, 1024], BF16, es=es2)
            tsb = kb.tile([P, 1024], BF16, es=es2)

            def epi(tag, pst):
                kind, h = tag
                for ti, (t0, tn) in enumerate(ttiles_of(n)):
                    ps = pst[ti]
                    if kind == "q":
                        kb.op("act", lambda e, ps=ps, t0=t0, tn=tn: e.activation(out=q_sb[:, t0:t0 + tn], in_=ps[:, 0:tn], func=AF.Silu),
                              r=ps.all, w=q_sb.all)
                    elif kind == "i":
                        kb.op("act", lambda e, ps=ps, t0=t0, tn=tn: e.copy(out=ob2[:, t0:t0 + tn], in_=ps[:, 0:tn]), r=ps.all, w=ob2.all)
                    elif kind == "g":
                        kb.op("act", lambda e, ps=ps, t0=t0, tn=tn: e.activation(out=ob2[:, t0:t0 + tn], in_=ps[:, 0:tn], func=AF.Silu),
                              r=ps.all, w=ob2.all)
                    else:
                        kb.op("act", lambda e, ps=ps, t0=t0, tn=tn: e.activation(out=fv[:, t0:t0 + tn], in_=ps[:, 0:tn], func=AF.Sigmoid),
                              r=ps.all, w=fv.all)
                if kind == "i":
                    fm_to_tm(es2, ob2, n, VT[0], tok0, h * P, tsb)
                elif kind == "g":
                    kb.dma("sp", GATE[h * P:(h + 1) * P, tok0:tok0 + n], ob2[:, 0:n], ob2, r=ob2.all, w=[SCR])
                elif kind in ("f0", "f1"):
                    dr = 0 if kind == "f0" else 1
                    kb.op("dve", lambda e: e.tensor_scalar(out=fv[:, 0:n], in0=fv[:, 0:n], scalar1=lbt[:, (2 + dr) * KC + h:(2 + dr) * KC + h + 1],
                                                           scalar2=lbt[:, dr * KC + h:dr * KC + h + 1], op0=ALU.mult, op1=ALU.add),
                          r=list(fv.all) + list(lbt.all), w=fv.all)
                    kb.op("act", lambda e: e.activation(out=lf[:, 0:n], in_=fv[:, 0:n], func=AF.Ln), r=fv.all, w=lf.all)
                    kb.op("dve", lambda e: e.tensor_scalar(out=k_sb[:, 0:n], in0=fv[:, 0:n], scalar1=-1.0, scalar2=1.0, op0=ALU.mult, op1=ALU.add),
                          r=fv.all, w=k_sb.all)
                    gate_chain(es2, T, q_sb, k_sb, lf, dr, h * P, tok0, n, U=64)
            blocks = []
            for h in range(KC):
                blocks.append(dict(ranges=[(h * P, P), (3 * D + h * P, P), (4 * D + h * P, P)],
                                   chunks=[(0, P, ("q", h)), (P, P, ("f0", h)), (2 * P, P, ("f1", h))]))
            for h in range(0, KC, 4):
                blocks.append(dict(ranges=[(D + h * P, 512)], chunks=[(o * P, P, ("i", h + o)) for o in range(4)]))
            for h in range(0, KC, 4):
                blocks.append(dict(ranges=[(2 * D + h * P, 512)], chunks=[(o * P, P, ("g", h + o)) for o in range(4)]))
            linear(hv, hT.all, KC, hg_w_in[0], blocks, ttiles_of(n), epi)

    def gla_pass_a(i, si, mv, mav, wup):
        tok0, n, col = SUPER[si]
        with kb.scope() as es2:
            hT = kb.tile([P, KC * 1024], BF16, es=es2)
            with kb.scope() as es3:
                norm_mod(es3, si, mv, mav, 0, 0, hT)
            hv = hT[:, 0:KC * n].rearrange("p (k t) -> p k t", k=KC)
            T = chain_tiles(es2)
            q_sb = kb.tile([P, 1024], F32, es=es2)
            k_sb = kb.tile([P, 1024], F32, es=es2)
            lf = kb.tile([P, 1024], F32, es=es2)
            aT = [kb.tile([P, 1024], BF16, es=es2) for _ in range(2)]
            ob2 = kb.tile([P, 1024], BF16, es=es2)
            tsb = kb.tile([P, 1024], BF16, es=es2)

            def epi(tag, pst):
                kind, h = tag
                for ti, (t0, tn) in enumerate(ttiles_of(n)):
                    ps = pst[ti]
                    if kind == "a":
                        kb.op("act", lambda e, ps=ps, t0=t0, tn=tn: e.copy(out=aT[h][0:16, t0:t0 + tn], in_=ps[0:16, 0:tn]), r=ps.all, w=aT[h].all)
                    elif kind == "q":
                        kb.op("act", lambda e, ps=ps, t0=t0, tn=tn: e.activation(out=q_sb[:, t0:t0 + tn], in_=ps[:, 0:tn], func=AF.Identity, scale=1.0 / 16.0),
                              r=ps.all, w=q_sb.all)
                    elif kind == "k":
                        kb.op("act", lambda e, ps=ps, t0=t0, tn=tn: e.copy(out=k_sb[:, t0:t0 + tn], in_=ps[:, 0:tn]), r=ps.all, w=k_sb.all)
                    elif kind == "v":
                        kb.op("act", lambda e, ps=ps, t0=t0, tn=tn: e.copy(out=ob2[:, t0:t0 + tn], in_=ps[:, 0:tn]), r=ps.all, w=ob2.all)
                    elif kind == "g":
                        kb.op("act", lambda e, ps=ps, t0=t0, tn=tn: e.activation(out=ob2[:, t0:t0 + tn], in_=ps[:, 0:tn], func=AF.Silu),
                              r=ps.all, w=ob2.all)
                if kind == "v":
                    fm_to_tm(es2, ob2, n, VT[0], tok0, h * P, tsb)
                elif kind == "g":
                    kb.dma("sp", GATE[h * P:(h + 1) * P, tok0:tok0 + n], ob2[:, 0:n], ob2, r=ob2.all, w=[SCR])
                elif kind == "k":
                    for dr in range(2):
                        for ti, (t0, tn) in enumerate(ttiles_of(n)):
                            ps = nps()
                            kb.op("pe", lambda e, ps=ps, t0=t0, tn=tn: e.matmul(
                                ps[:, 0:tn], lhsT=wup[0:16, dr * 1024 + h * P:dr * 1024 + (h + 1) * P], rhs=aT[dr][0:16, t0:t0 + tn],
                                start=True, stop=True), r=list(wup.all) + list(aT[dr].all), w=ps.all)
                            kb.op("act", lambda e, ps=ps, t0=t0, tn=tn: e.activation(
                                out=lf[:, t0:t0 + tn], in_=ps[:, 0:tn], func=AF.Exp, scale=-1.0, bias=V("nbup", dr * 8 + h)),
                                r=ps.all, w=lf.all)
                        kb.op("act", lambda e: e.activation(out=lf[:, 0:n], in_=lf[:, 0:n], func=AF.Ln, bias=1.0, scale=1.0), r=lf.all, w=lf.all)
                        kb.op("dve", lambda e: e.tensor_scalar(out=lf[:, 0:n], in0=lf[:, 0:n], scalar1=-1.0 / 16.0, scalar2=None, op0=ALU.mult),
                              r=lf.all, w=lf.all)
                        gate_chain(es2, T, q_sb, k_sb, lf, dr, h * P, tok0, n)
            blocks = [dict(ranges=[(6144, 16), (6160, 16)], chunks=[(0, 16, ("a", 0)), (16, 16, ("a", 1))])]
            for h in range(8):
                blocks.append(dict(ranges=[(h * P, P), (1024 + h * P, P)], chunks=[(0, P, ("q", h)), (P, P, ("k", h))]))
            for h in range(0, KC, 4):
                blocks.append(dict(ranges=[(2048 + h * P, 512)], chunks=[(o * P, P, ("v", h + o)) for o in range(4)]))
            for h in range(0, KC, 4):
                blocks.append(dict(ranges=[(4096 + h * P, 512)], chunks=[(o * P, P, ("g", h + o)) for o in range(4)]))
            linear(hv, hT.all, KC, gla_w_in[0], blocks, ttiles_of(n), epi)

    def scan_core(H, DKC, DV, gname, U=P):
        NUu = TT // U
        EXd = EX if U == P else EX64
        with kb.scope() as es2:
            qt = kb.tile([P, DKC * TT], BF16, es=es2)
            kt = kb.tile([P, DKC * TT], BF16, es=es2)
            khu = [kb.tile([P, DKC * P], BF16, es=es2) for _ in range(2)]
            vt = kb.tile([P, NUu * DV], BF16, es=es2)
            oacc = kb.tile([P, NUu * DV], F32, NUu, es=es2)
            ext = kb.tile([P, DKC * NUu], F32, es=es2)
            S = [kb.tile([P, DV], F32, es=es2) for _ in range(DKC)]
            Sb = [kb.tile([P, DV], BF16, es=es2) for _ in range(DKC)]
            asb = [kb.tile([P, P], BF16, es=es2) for _ in range(2)]
            khs = [kb.tile([P, DKC * P], BF16, es=es2) for _ in range(2)]
            sq = kb.tile([P, DV], F32, es=es2)
            ssq = kb.tile([P, 1], F32, es=es2)
            on = kb.tile([P, DV], BF16, es=es2)
            gt = [kb.tile([P, P], BF16, es=es2) for _ in range(2)]
            osb = [kb.tile([P, P], BF16, es=es2) for _ in range(2)]
            sel = [0]
            qv = qt[:].rearrange("p (k t) -> p k t", k=DKC)
            kv = kt[:].rearrange("p (k t) -> p k t", k=DKC)
            vv = vt[:].rearrange("p (u v) -> p u v", v=DV)
            ov = oacc[:].rearrange("p (u v) -> p u v", v=DV)
            exv = ext[:].rearrange("p (k u) -> p k u", k=DKC)
            for h in range(H // 2):
                r0 = h * DKC * P
                kb.dma("sp", vv[0:U], VTm[0, :, h * DV:(h + 1) * DV].rearrange("(u p) v -> p u v", p=U), vt, r=[SCR], w=vt.all)
                for dr in range(2):
                    for (t, vw, src) in ((qt, qv, FMm["q"]), (kt, kv, FMm["k"])):
                        kb.dma("sp", vw, src[dr, r0:r0 + DKC * P, :].rearrange("(k p) t -> p k t", p=P), t, r=[SCR], w=t.all)
                    kb.dma("sp", exv, EXm[dr, r0:r0 + DKC * P, 0:NUu].rearrange("(k p) u -> p k u", p=P), ext, r=[SCR], w=ext.all)
                    for k in range(DKC):
                        kb.op("dve", lambda e, k=k: e.memset(S[k][:], 0.0), w=S[k].all)
                        kb.op("pool", lambda e, k=k: e.memset(Sb[k][:], 0.0), w=Sb[k].all)
                    nc_ = TCX // U
                    order = list(range(NUu)) if dr == 0 else list(range(nc_ - 1, -1, -1)) + list(range(NUu - 1, nc_ - 1, -1))
                    for u in order:
                        ts = slice(u * U, (u + 1) * U)
                        sel[0] ^= 1
                        a, khb = asb[sel[0]], khs[sel[0]]
                        pa = nps()
                        for k in range(DKC):
                            kb.op("pe", lambda e, k=k: e.matmul(pa[0:U, 0:U], lhsT=kv[:, k, ts], rhs=qv[:, k, ts], start=(k == 0), stop=(k == DKC - 1)),
                                  r=list(kt.all) + list(qt.all), w=pa.all, inc=(k == DKC - 1))
                        kb.op("dve", lambda e: e.tensor_tensor(out=a[0:U, 0:U], in0=pa[0:U, 0:U], in1=tri[0:U, dr * P:dr * P + U], op=ALU.mult),
                              r=list(pa.all) + list(tri.all), w=a.all)
                        po = nps()
                        kb.op("pe", lambda e: e.matmul(po[0:U, 0:DV], lhsT=a[0:U, 0:U], rhs=vv[0:U, u, :], start=True, stop=False),
                              r=list(a.all) + list(vt.all), w=po.all, inc=False)
                        for k in range(DKC):
                            kb.op("pe", lambda e, k=k: e.matmul(po[0:U, 0:DV], lhsT=qv[:, k, ts], rhs=Sb[k][:], start=False, stop=(k == DKC - 1)),
                                  r=list(qt.all) + list(Sb[k].all), w=po.all, inc=(k == DKC - 1))
                        if dr == 0:
                            kb.op("act", lambda e: e.copy(out=ov[0:U, u, :], in_=po[0:U, 0:DV]), r=po.all, w=[oacc.parts[u]])
                        else:
                            kb.op("dve", lambda e: e.tensor_tensor(out=ov[0:U, u, :], in0=po[0:U, 0:DV], in1=ov[0:U, u, :], op=ALU.add),
                                  r=list(po.all) + [oacc.parts[u]], w=[oacc.parts[u]])
                        kht = khu[sel[0]]
                        khv = kht[:, 0:DKC * U].rearrange("p (k t) -> p k t", k=DKC)
                        kb.dma("sp", khv, FMm["h"][dr, r0:r0 + DKC * P, u * U:(u + 1) * U].rearrange("(k p) t -> p k t", p=P), kht, r=[SCR], w=kht.all)
                        psb = npsb()
                        pv = psb[:, 0:DKC * P].rearrange("p (k d) -> p k d", d=P)
                        for k in range(DKC):
                            kb.op("pe", lambda e, k=k: e.transpose(out=pv[0:U, k, :], in_=khv[:, k, :], identity=ident[:]),
                                  r=list(kht.all) + list(ident.all), w=psb.all, inc=(k == DKC - 1))
                        kb.op("act", lambda e: e.copy(out=khb[0:U, :], in_=psb[0:U, 0:DKC * P]), r=psb.all, w=khb.all)
                        for k in range(DKC):
                            pd = nps()
                            kb.op("pe", lambda e, k=k, pd=pd: e.matmul(pd[:, 0:DV], lhsT=khb[0:U, k * P:(k + 1) * P], rhs=vv[0:U, u, :], start=True, stop=True),
                                  r=list(khb.all) + list(vt.all), w=pd.all)
                            kb.op("dve", lambda e, k=k, pd=pd: e.scalar_tensor_tensor(
                                out=S[k][:], in0=S[k][:], scalar=exv[:, k, u:u + 1], in1=pd[:, 0:DV], op0=ALU.mult, op1=ALU.add),
                                r=list(S[k].all) + list(pd.all) + list(ext.all), w=S[k].all)
                            kb.op("act", lambda e, k=k: e.copy(out=Sb[k][:], in_=S[k][:]), r=S[k].all, w=Sb[k].all)
                for u in range(NUu):
                    kb.op("act", lambda e, u=u: e.activation(out=sq[0:U, :], in_=ov[0:U, u, :], func=AF.Square),
                          r=[oacc.parts[u]], w=list(sq.all))
                    kb.op("dve", lambda e: e.reduce_sum(out=ssq[0:U, :], in_=sq[0:U, :], axis=mybir.AxisListType.X), r=sq.all, w=ssq.all)
                    kb.op("act", lambda e: e.activation(out=ssq[0:U, :], in_=ssq[0:U, :], func=AF.Sqrt, bias=V("eps")[0:U, :], scale=1.0 / DV), r=ssq.all, w=ssq.all)
                    kb.op("dve", lambda e: e.reciprocal(out=ssq[0:U, :], in_=ssq[0:U, :]), r=ssq.all, w=ssq.all)
                    kb.op("dve", lambda e, u=u: e.tensor_scalar(out=on[0:U, :], in0=ov[0:U, u, :], scalar1=ssq[0:U, 0:1], scalar2=None, op0=ALU.mult),
                          r=[oacc.parts[u]] + list(ssq.all), w=on.all)
                    for cb in range(DV // P):
                        ch = (h * DV) // P + cb
                        sel[0] ^= 1
                        g, ob = gt[sel[0]], osb[sel[0]]
                        kb.dma("sp", g[:, 0:U], GATEm[ch * P:(ch + 1) * P, u * U:(u + 1) * U], g, r=[SCR], w=g.all)
                        psb = npsb()
                        kb.op("pe", lambda e, cb=cb: e.transpose(out=psb[:, 0:U], in_=on[0:U, cb * P:(cb + 1) * P], identity=ident[0:U, 0:U]),
                              r=list(on.all) + list(ident.all), w=psb.all)
                        kb.op("dve", lambda e, ch=ch, g=g, ob=ob: e.scalar_tensor_tensor(
                            out=ob[:, 0:U], in0=psb[:, 0:U], scalar=V(gname, ch), in1=g[:, 0:U], op0=ALU.mult, op1=ALU.mult),
                            r=list(psb.all) + list(g.all), w=ob.all)
                        kb.dma("sp", ZTs[ch * P:(ch + 1) * P, u * U:(u + 1) * U], ob[:, 0:U], ob, r=ob.all, w=[SCR])

    with kb.scope() as es2:
        cp = [kb.tile([P, 4352], F32, es=es2) for _ in range(2)]
        for r in range(KC):
            t = cp[r % 2]
            kb.dma("sp", t[:], XIN[r * P:(r + 1) * P, :], t, w=t.all)
            kb.dma("sp", XR[r * P:(r + 1) * P, :], t[:], t, r=t.all, w=XRb)
    kb.barrier()

    def loc_scan(R, U):
        NUu = TT // U
        EXd = EX if U == P else EX64
        e_sp, e_act, e_pool = (parCH, parsCH, pargCH) if R == CH else (par512, pars512, parg512)
        kb.dma("sp", FMm["q"][:, 0:R, :], QT[:, bass.ds(e_sp, R), :], dh[0], r=[SCR], w=[SCR])
        kb.dma("act", FMm["k"][:, 0:R, :], KTs[:, bass.ds(e_act, R), :], dh[1], r=[SCR], w=[SCR])
        kb.dma("pool", FMm["h"][:, 0:R, :], KHs[:, bass.ds(e_pool, R), :], dh[2], r=[SCR], w=[SCR])
        kb.dma("sp", EXm[:, 0:R, 0:NUu], EXd[:, bass.ds(e_sp, R), :], dh[0], r=[SCR], w=[SCR])
        kb.dma("sp", VTm[0, 0:TT // 2, :], VT[0, 0:TT // 2, :][:, bass.ds(parCH, CH)], dh[0], r=[SCR], w=[SCR])
        kb.dma("act", VTm[0, TT // 2:TT, :], VT[0, TT // 2:TT, :][:, bass.ds(parsCH, CH)], dh[1], r=[SCR], w=[SCR])
        kb.dma("pool", GATEm, GATE[bass.ds(pargCH, CH), :], dh[2], r=[SCR], w=[SCR])
        kb.barrier()

    def gather_zt():
        kb.allgather([(ZTs[c_ * P:(c_ + 1) * P, :], ZTg[c_ * 2 * P:(c_ + 1) * 2 * P, :]) for c_ in range(CH // P)], GROUPS)

    for i in range(nlayers):
        kind, j = i % 3, i // 3
        last = i == DEPTH - 1
        sis = list(range(len(SUPER)))
        if last and kind == 0:
            sis = sis[1:]
        mv, mav = pass_mod(i)
        if kind == 0:
            for si in sis:
                hyena_pass_a(i, j, si, mv, mav)
            kb.barrier()
            for w_, (qq, ee) in enumerate((("sp", parCH), ("act", parsCH), ("pool", pargCH))):
                for r0_, r1_ in ((0, TT // 2), (TT // 2, TT)):
                    kb.dma(qq, VTm[w_, r0_:r1_, :], VT[w_, r0_:r1_, :][:, bass.ds(ee, CH)], dh[w_], r=[SCR], w=[SCR])
            kb.barrier()
            hyena_filters(j, TL)
            kb.barrier()
            hyena_conv(TL, TCX)
            kb.barrier()
            if 0 in sis:
                hyena_filters(j, TCX)
                kb.barrier()
                hyena_conv(TCX, 0)
            kb.allgather([(ZTs[c_ * P:(c_ + 1) * P, :], ZTg[c_ * 2 * P:(c_ + 1) * 2 * P, :]) for c_ in range(CH // P)], GROUPS)
            Wout = hy_w_out[j]
        elif kind == 1:
            with kb.scope() as esl:
                lbt = kb.tile([P, 4 * KC], F32, es=esl)
                ee = kb.tile([P, 2 * 4 * KC], F32, es=esl)
                sm = kb.tile([P, 2 * KC], F32, es=esl)
                kb.op("act", lambda e: e.activation(out=ee[:], in_=V("lbl", 0, 2 * 4 * KC), func=AF.Exp), r=vecs.all, w=ee.all)
                ev = ee[:].rearrange("p (d l k) -> p d l k", d=2, l=4)
                smv = sm[:].rearrange("p (d k) -> p d k", d=2)
                kb.op("dve", lambda e: e.tensor_tensor(out=smv, in0=ev[:, :, 0, :], in1=ev[:, :, 1, :], op=ALU.add), r=ee.all, w=sm.all)
                kb.op("dve", lambda e: e.tensor_tensor(out=smv, in0=smv, in1=ev[:, :, 2, :], op=ALU.add), r=list(ee.all) + list(sm.all), w=sm.all)
                kb.op("dve", lambda e: e.tensor_tensor(out=smv, in0=smv, in1=ev[:, :, 3, :], op=ALU.add), r=list(ee.all) + list(sm.all), w=sm.all)
                kb.op("dve", lambda e: e.reciprocal(out=sm[:], in_=sm[:]), r=sm.all, w=sm.all)
                lv = lbt[:].rearrange("p (a d k) -> p a d k", a=2, d=2)
                kb.op("dve", lambda e: e.tensor_tensor(out=lv[:, 0], in0=ev[:, :, 1, :], in1=smv, op=ALU.mult), r=list(ee.all) + list(sm.all), w=lbt.all)
                kb.op("dve", lambda e: e.tensor_scalar(out=lv[:, 1], in0=lv[:, 0], scalar1=-1.0, scalar2=1.0, op0=ALU.mult, op1=ALU.add),
                      r=lbt.all, w=lbt.all)
                for si in sis:
                    hgrn_pass_a(i, si, mv, mav, lbt)
            kb.barrier()
            loc_scan(CH, 64)
            scan_core(16, 1, 128, "hgg", U=64)
            gather_zt()
            Wout = hg_w_out[0]
        else:
            with kb.scope() as esl:
                wup = kb.tile([P, 2048], BF16, es=esl)
                kb.dma("pool", wup[0:16, :].rearrange("p (d n) -> p d n", d=2), gla_w_up[0].rearrange("d p n -> p d n"), wup, w=wup.all)
                for si in sis:
                    gla_pass_a(i, si, mv, mav, wup)
            kb.barrier()
            loc_scan(512, P)
            scan_core(4, 2, 512, "glg")
            gather_zt()
            Wout = gla_w_out[0]
        for si in sis:
            pass_out_ffn(i, si, mv, mav, Wout, hy=True)
        kb.barrier()

    with kb.scope() as es2:
        xs = kb.tile([P, KC * 256], F32, es=es2)
        sq = kb.tile([P, KC * 256], BF16, es=es2)
        rstd = kb.tile([P, 256], F32, es=es2)
        ot = kb.tile([P, KC * 256], F32, es=es2)
        xv = xs[:].rearrange("p (k t) -> p k t", t=256)
        sv = sq[:].rearrange("p (k t) -> p k t", t=256)
        otv = ot[:].rearrange("p (k t) -> p k t", t=256)
        for s0 in range(0, TL, 256):
            kb.dma("sp", xv, XR[:, TCX + s0:TCX + s0 + 256].rearrange("(k p) t -> p k t", p=P), xs, r=XRb, w=xs.all)
            kb.op("act", lambda e: e.activation(out=sq[:], in_=xs[:], func=AF.Square), r=xs.all, w=sq.all)
            ps = nps()
            for kc in range(KC):
                kb.op("pe", lambda e, kc=kc: e.matmul(ps[:, 0:256], lhsT=ones[:, 0:P], rhs=sv[:, kc, :], start=(kc == 0), stop=(kc == KC - 1)),
                      r=list(sq.all) + list(ones.all), w=ps.all, inc=(kc == KC - 1))
            kb.op("act", lambda e: e.activation(out=rstd[:], in_=ps[:, 0:256], func=AF.Sqrt, bias=V("eps"), scale=1.0 / D), r=ps.all, w=rstd.all)
            kb.op("dve", lambda e: e.reciprocal(out=rstd[:], in_=rstd[:]), r=rstd.all, w=rstd.all)
            for kc in range(KC):
                kb.op("dve", lambda e, kc=kc: e.scalar_tensor_tensor(out=otv[:, kc, :], in0=xv[:, kc, :], scalar=V("fing", kc), in1=rstd[:],
                                                                     op0=ALU.mult, op1=ALU.mult), r=list(xs.all) + list(rstd.all), w=ot.all)
            kb.dma("sp", OUT[:, s0:s0 + 256].rearrange("(k p) t -> p k t", p=P), otv, ot, r=ot.all, w=[SCR])
    kb.barrier()
    es.close()
    return nc


VOFF = {}
NV = 0


def _voff():
    global NV
    o = 0
    for name, n in (("eps", 1), ("bmod", DEPTH * 96), ("n1g", DEPTH * KC), ("n2g", DEPTH * KC), ("fing", KC),
                    ("cw", 2 * 3 * 48), ("ffreq", 4), ("fbias", 4), ("lbl", 2 * 4 * KC), ("hgg", KC), ("glg", KC), ("nbup", 16)):
        VOFF[name] = o
        o += n
    NV = o


_voff()
nlayers = DEPTH


def fm(v):
    v = np.asarray(v, np.float32).reshape(-1, P)
    return v.T


def pack_vecs(inp, par=0):
    vecs = np.zeros((P, NV), np.float32)

    def put(name, arr, off=0):
        arr = np.asarray(arr, np.float32)
        vecs[:arr.shape[0], VOFF[name] + off:VOFF[name] + off + arr.shape[1]] = arr
    put("eps", np.full((P, 1), EPS, np.float32))
    for i in range(DEPTH):
        put("bmod", fm(inp["b_mod"][i]), i * 96)
        put("n1g", fm(inp["norm1_g"][i]), i * KC)
        put("n2g", fm(inp["norm2_g"][i]), i * KC)
    put("fing", fm(inp["final_g"]))
    for j in range(2):
        for tap in range(3):
            put("cw", fm(inp["hy_conv_w"][j, tap]), (j * 3 + tap) * 48)
        for q in range(2):
            put("ffreq", inp["hy_ffreq"][j, q].reshape(64, 1), j * 2 + q)
        put("fbias", inp["hy_fb1"][j].reshape(64, 1), j * 2 + 0)
        put("fbias", inp["hy_fb2"][j].reshape(64, 1), j * 2 + 1)
    for d in range(2):
        for l in range(4):
            put("lbl", fm(inp["hg_lb_logits"][d, l]), (d * 4 + l) * KC)
    put("hgg", np.roll(fm(inp["hg_onorm_g"][0]), -8 * par, axis=1))
    put("glg", np.roll(fm(inp["gla_onorm_g"][0]), -8 * par, axis=1))
    for d in range(2):
        put("nbup", -fm(inp["gla_b_up"][0, d]), d * 8)
    return vecs


_CONST = {}


def consts():
    if _CONST:
        return _CONST
    bf = ml_dtypes.bfloat16
    c = {}
    c["c_ident"] = np.eye(P, dtype=np.float32).astype(bf)
    on = np.ones((P, 2 * P), np.float32)
    on[P - 1, P:] = 0.0
    c["c_ones"] = on.astype(bf)
    s = np.arange(P)[:, None]
    t = np.arange(P)[None, :]
    c["c_tri"] = np.concatenate([(s <= t), (s >= t)], axis=1).astype(np.float32)
    sm = np.ones((P, 1024), np.float32)
    sm[:, ::P] = 0.0
    c["c_smask"] = sm
    sm2 = np.ones((P, 1024), np.float32)
    sm2[:, ::64] = 0.0
    c["c_smask64"] = sm2
    delt = np.abs(np.linspace(DMIN, DMAX, D, dtype=np.float32))
    c["c_negd"] = np.broadcast_to(-delt[None, :], (P, D)).astype(np.float32).copy()
    for L in (TL, TCX):
        TCn, KF = L // P, L // P + 1
        N = 2 * L
        pos = np.arange(L, dtype=np.float32)
        tt = pos / max(L - 1, 1)
        bands = np.arange(1, 17, dtype=np.float32)
        ang = (2.0 * math.pi / L) * pos[:, None] * bands[None, :]
        z = np.concatenate([tt[:, None], np.cos(ang), -np.sin(ang)], axis=-1).astype(np.float32)
        c["c_zT%d" % L] = np.ascontiguousarray(z.T)
        c["c_tn%d" % L] = np.ascontiguousarray(tt.reshape(TCn, P).T)
        kpad = KF * P
        kk = np.arange(kpad, dtype=np.float64)
        valid = (kk <= L).astype(np.float64)
        tpos = np.arange(L, dtype=np.float64)
        ph = 2 * math.pi * ((tpos[:, None] * kk[None, :]) % N) / N
        Fre = np.cos(ph) * valid
        Fim = -np.sin(ph) * valid
        Ff = np.stack([Fre.reshape(TCn, P, KF, P), Fim.reshape(TCn, P, KF, P)], axis=3)
        c["c_Ff%d" % L] = np.ascontiguousarray(Ff.transpose(2, 1, 0, 3, 4).reshape(KF, P, TCn * 256)).astype(bf)
        phb = 2 * math.pi * (((tpos[:, None] + 1) * kk[None, :]) % N) / N
        rowv = (tpos < L - 1).astype(np.float64)[:, None]
        Bre = np.cos(phb) * valid * rowv
        Bim = np.sin(phb) * valid * rowv
        Kre = np.concatenate([Fre, Bre], axis=0)
        Kim = np.concatenate([Fim, Bim], axis=0)
        FFm = np.concatenate([Kre, Kim], axis=1)
        FFb = FFm.reshape(2 * TCn, P, 2 * KF, P).transpose(2, 1, 0, 3).reshape(2 * KF, P, 2 * TCn * P)
        c["c_FF%d" % L] = np.ascontiguousarray(FFb).astype(bf)
        wk = np.where((kk == 0) | (kk == L), 1.0, 2.0) * valid / N
        phi = 2 * math.pi * ((kk[:, None] * tpos[None, :]) % N) / N
        Gre = np.cos(phi) * wk[:, None]
        Gim = -np.sin(phi) * wk[:, None]
        Gm = np.concatenate([Gre, Gim], axis=0)
        Gb = Gm.reshape(2 * KF, P, TCn, P).transpose(2, 1, 0, 3).reshape(TCn, P, 2 * KF * P)
        c["c_G%d" % L] = np.ascontiguousarray(Gb).astype(bf)
    _CONST.update(c)
    return _CONST


def make_in_maps(inp):
    cst = consts()
    vecs = [pack_vecs(inp, 0), pack_vecs(inp, 1)]
    B = inp["x"].shape[0]
    shared = {k: np.ascontiguousarray(inp[k], dtype=np.float32) for k in
              ("w_mod", "w_ffn_in", "w_ffn_out", "hy_w_in", "hy_w_out", "hy_fw1", "hy_fw2", "hy_fwout",
               "hg_w_in", "hg_w_out", "gla_w_in", "gla_w_up", "gla_w_out")}
    shared["fskip"] = np.ascontiguousarray(np.broadcast_to(inp["hy_fskip"].reshape(2, 1, 2 * D), (2, P, 2 * D)), dtype=np.float32)
    shared.update(cst)
    in_maps = []
    for core in range(8):
        b = (core // 2) % B
        m = dict(shared)
        m["vecs"] = vecs[core % 2]
        m["xin"] = np.ascontiguousarray(np.concatenate([inp["ctx"][b].T, inp["x"][b].T], axis=1), dtype=np.float32)
        sc = np.stack([fm(inp["c"][b]), fm(inp["c_ctx"])], axis=2)
        m["scin"] = np.ascontiguousarray(sc.reshape(P, KC * 2), dtype=np.float32)
        in_maps.append(m)
    return in_maps


def kernel(**inp):
    inp = {k: np.asarray(v) for k, v in inp.items()}
    nc = build()
    in_maps = make_in_maps(inp)
    B = inp["x"].shape[0]
    res = run_bass_kernel_spmd(nc, in_maps, core_ids=list(range(8)))
    out = np.stack([np.asarray(res.results[2 * b]["out"]).T for b in range(B)], axis=0)
    return np.ascontiguousarray(out, dtype=np.float32)
```

```python
import math
from contextlib import ExitStack
import numpy as np
import ml_dtypes
import concourse.bass as bass
import concourse.mybir as mybir
from concourse.bass_utils import run_bass_kernel_spmd

F32 = mybir.dt.float32
BF16 = mybir.dt.bfloat16
ALU = mybir.AluOpType
AF = mybir.ActivationFunctionType
P = 128
D = 2048
KC = 16
TL = 4096
TCX = 256
TT = TL + TCX
CH = D // 2
FF = 5632
NU = TT // P
DEPTH = 4
EPS = 1e-6
DMIN = math.log(1e-2) / 1.5
DMAX = math.log(1e-2) / 0.3
WELE = 8704
SUPER = [(0, TCX, 1)] + [(TCX + i * 1024, 1024, 0) for i in range(4)]


class Buf:
    __slots__ = ("w", "r")

    def __init__(self):
        self.w = None
        self.r = {}


class Tile:
    def __init__(self, h, nparts=1):
        self.h = h
        self.parts = [Buf() for _ in range(nparts)]
        self.sem = None
        self.cnt = 0

    def __getitem__(self, k):
        return self.h[k]

    @property
    def b(self):
        return self.parts[0]

    @property
    def all(self):
        return self.parts


class KB:
    def __init__(self, nc, es):
        self.nc = nc
        self.es = es
        self.eng = {"pe": nc.tensor, "act": nc.scalar, "dve": nc.vector, "pool": nc.gpsimd, "sp": nc.sync}
        self.sem = {}
        self.cnt = {}
        for e in ("pe", "act", "dve", "pool"):
            self.sem[e] = es.enter_context(nc.semaphore("s_" + e))
            self.cnt[e] = 0
        self.waited = {e: {} for e in self.eng}
        self.tiles = []
        self.nm = 0
        self.sempool = []
        self.ccsem = es.enter_context(nc.semaphore("s_cc"))
        self.cccnt = 0
        self.bsem = es.enter_context(nc.semaphore("s_bar"))
        self.bcnt = 0

    def name(self, p):
        self.nm += 1
        return "%s%d" % (p, self.nm)

    def tile(self, shape, dt, nparts=1, es=None):
        h = (es or self.es).enter_context(self.nc.sbuf_tensor(self.name("t"), list(shape), dt))
        t = Tile(h, nparts)
        if self.sempool:
            t.sem, t.cnt, t.key = self.sempool.pop()
        else:
            t.sem = self.es.enter_context(self.nc.semaphore(self.name("d")))
            t.key = self.name("k")
        self.tiles.append(t)
        if es is not None and hasattr(es, "mine"):
            es.mine.append(t)
        return t

    def psum(self, shape, dt):
        h = self.es.enter_context(self.nc.psum_tensor(self.name("p"), list(shape), dt))
        return Tile(h, 1)

    def _wait(self, eng, ev):
        if ev is None:
            return
        sem, val, key = ev
        if eng == "pe" and key == "pe":
            return
        if self.waited[eng].get(key, 0) >= val:
            return
        self.waited[eng][key] = val
        self.eng[eng].wait_ge(sem, val)

    def _deps(self, eng, r, w):
        for b in r:
            self._wait(eng, b.w)
        for b in w:
            self._wait(eng, b.w)
            for ev in list(b.r.values()):
                self._wait(eng, ev)

    def _mark(self, ev, r, w):
        for b in r:
            b.r[ev[2]] = ev
        for b in w:
            b.w = ev
            b.r = {}

    def op(self, eng, fn, r=(), w=(), inc=True):
        self._deps(eng, r, w)
        ins = fn(self.eng[eng])
        if inc:
            self.cnt[eng] += 1
            ins.then_inc(self.sem[eng], 1)
            ev = (self.sem[eng], self.cnt[eng], eng)
        else:
            ev = (self.sem[eng], self.cnt[eng] + 1, eng)
        self._mark(ev, r, w)
        return ins

    def dma(self, q, out, in_, tile, r=(), w=()):
        self._deps(q, r, w)
        ins = self.eng[q].dma_start(out=out, in_=in_)
        tile.cnt += 16
        ins.then_inc(tile.sem, 16)
        ev = (tile.sem, tile.cnt, tile.key)
        self._mark(ev, r, w)

    def barrier(self):
        sp = self.eng["sp"]
        for e in ("pe", "act", "dve", "pool"):
            if self.cnt[e] > 0:
                self._wait("sp", (self.sem[e], self.cnt[e], e))
        for t in self.tiles:
            if t.cnt > 0:
                self._wait("sp", (t.sem, t.cnt, t.key))
        self.bcnt += 1
        ins = sp.nop()
        ins.then_inc(self.bsem, 1)
        for e in ("pe", "act", "dve", "pool"):
            self.eng[e].wait_ge(self.bsem, self.bcnt)
        for e in self.eng:
            for e2 in ("pe", "act", "dve", "pool"):
                self.waited[e][e2] = self.cnt[e2]
            for t in self.tiles:
                self.waited[e][t.key] = t.cnt

    def allgather(self, pairs, groups):
        self.barrier()
        for (src, dst) in pairs:
            self.cccnt += 1
            self.nc.gpsimd.collective_compute("AllGather", ALU.bypass, replica_groups=groups, ins=[src], outs=[dst]).then_inc(self.ccsem, 1)
        self.eng["sp"].wait_ge(self.ccsem, self.cccnt)
        self.barrier()

    def scope(self):
        kb = self

        class _S(ExitStack):
            def __exit__(s, *a):
                if a[0] is None:
                    kb.barrier()
                    kb.tiles = [t for t in kb.tiles if t not in s.mine]
                    for t in s.mine:
                        kb.sempool.append((t.sem, t.cnt, t.key))
                return ExitStack.__exit__(s, *a)
        st = _S()
        st.mine = []
        return st


def build(dbg=False, ncores=8):
    nc = bass.Bass("TRN2", target_bir_lowering=False)
    es = ExitStack()
    kb = KB(nc, es)

    def din(name, shape, dt=F32):
        return nc.dram_tensor(name, list(shape), dt, kind="ExternalInput").ap()

    def dint(name, shape, dt=F32):
        return nc.dram_tensor(name, list(shape), dt, kind="Internal").ap()

    GROUPS = [[2 * g, 2 * g + 1] for g in range(ncores // 2)]
    par = nc.sync.partition_id() % 2
    pars = nc.scalar.partition_id() % 2
    parg = nc.gpsimd.partition_id() % 2
    parCH, parsCH, pargCH = par * CH, pars * CH, parg * CH
    par512, pars512, parg512 = par * 512, pars * 512, parg * 512
    XIN = din("xin", [D, TT])
    SCIN = din("scin", [P, KC * 2])
    VECS = din("vecs", [P, NV])
    w_mod = din("w_mod", [DEPTH, D, 6 * D])
    w_ffn_in = din("w_ffn_in", [DEPTH, D, 2 * FF])
    w_ffn_out = din("w_ffn_out", [DEPTH, FF, D])
    hy_w_in = din("hy_w_in", [2, D, 3 * D])
    hy_w_out = din("hy_w_out", [2, D, D])
    hy_fw1 = din("hy_fw1", [2, 33, 64])
    hy_fw2 = din("hy_fw2", [2, 64, 64])
    hy_fwout = din("hy_fwout", [2, 64, 4 * D])
    fskip = din("fskip", [2, P, 2 * D])
    hg_w_in = din("hg_w_in", [1, D, 5 * D])
    hg_w_out = din("hg_w_out", [1, D, D])
    gla_w_in = din("gla_w_in", [1, D, 6176])
    gla_w_up = din("gla_w_up", [1, 2, 16, 1024])
    gla_w_out = din("gla_w_out", [1, D, D])
    C_ident = din("c_ident", [P, P], BF16)
    C_ones = din("c_ones", [P, 2 * P], BF16)
    C_tri = din("c_tri", [P, 2 * P])
    C_smask = din("c_smask", [P, 1024])
    C_negd = din("c_negd", [P, D])
    C_smask64 = din("c_smask64", [P, 1024])
    CL = {}
    for L in (TL, TCX):
        tc_, kf = L // P, (L // P) + 1
        CL[L] = dict(
            zT=din("c_zT%d" % L, [33, L]), tn=din("c_tn%d" % L, [P, tc_]),
            Ff=din("c_Ff%d" % L, [kf, P, tc_ * 256], BF16),
            FF=din("c_FF%d" % L, [2 * kf, P, 2 * tc_ * P], BF16),
            G=din("c_G%d" % L, [tc_, P, 2 * kf * P], BF16),
            HS=dint("hs%d" % L, [2, 2 * kf, P, CH]), TC=tc_, KF=kf)
    OUT = nc.dram_tensor("out", [D, TL], F32, kind="ExternalOutput").ap()
    XR = (nc.dram_tensor("xr", [D, TT], F32, kind="ExternalOutput").ap() if dbg else dint("xr", [D, TT]))
    ZT = None
    VT = (nc.dram_tensor("vt", [3, TT, D], BF16, kind="ExternalOutput").ap() if dbg else dint("vt", [3, TT, D], BF16))
    ZT = (nc.dram_tensor("zt", [D, TT], BF16, kind="ExternalOutput").ap() if dbg else dint("zt", [D, TT], BF16))
    QT = (nc.dram_tensor("qt", [2, D, TT], BF16, kind="ExternalOutput").ap() if dbg else dint("qt", [2, D, TT], BF16))
    KTs = (nc.dram_tensor("kts", [2, D, TT], BF16, kind="ExternalOutput").ap() if dbg else dint("kts", [2, D, TT], BF16))
    KHs = (nc.dram_tensor("khs", [2, D, TT], BF16, kind="ExternalOutput").ap() if dbg else dint("khs", [2, D, TT], BF16))
    EX = (nc.dram_tensor("ex", [2, D, NU], F32, kind="ExternalOutput").ap() if dbg else dint("ex", [2, D, NU]))
    EX64 = dint("ex64", [2, D, TT // 64])
    GATE = (nc.dram_tensor("gate", [D, TT], BF16, kind="ExternalOutput").ap() if dbg else dint("gate", [D, TT], BF16))
    VTm = dint("vtm", [3, TT, CH], BF16)
    ZTs = dint("zts", [CH, TT], BF16)
    FMm = {nm: dint(nm + "m", [2, CH, TT], BF16) for nm in ("q", "k", "h")}
    EXm = dint("exm", [2, CH, 68])
    GATEm = dint("gatem", [CH, TT], BF16)
    ZTg = dint("ztg", [2 * CH, TT], BF16)
    TLH = TL // 2
    TLOC = TCX + TLH
    SUPERL = [(0, TCX, 1), (TCX, 1024, 0), (TCX + 1024, 1024, 0)]
    ZTl = dint("ztl", [2 * CH, TLOC], BF16)
    XRl = dint("xrl", [D, TLOC])
    XRg = dint("xrg", [2 * D, TLH])
    XRs = dint("xrs", [D, TLH])
    parTLH, parsTLH, pargTLH = par * TLH, pars * TLH, parg * TLH
    XRb = [Buf() for _ in SUPER]
    XRlb = [Buf() for _ in SUPERL]
    SPC = {"SUPER": SUPER, "XR": XR, "XRb": XRb, "ZT": ZTg}
    SCR = Buf()

    dh = [kb.tile([P, 8], F32) for _ in range(3)]
    vecs = kb.tile([P, NV], F32)
    ident = kb.tile([P, P], BF16)
    ones = kb.tile([P, 2 * P], BF16)
    tri = kb.tile([P, 2 * P], F32)
    smask = kb.tile([P, 1024], F32)
    smask64 = kb.tile([P, 1024], F32)
    scT = kb.tile([P, KC * 2], BF16)
    MOD = kb.tile([P, 96 * 2], F32)
    MA = kb.tile([P, 2 * KC * 2], F32)
    wbuf = [kb.tile([P, WELE], BF16) for _ in range(2)]
    wsel = [0]
    PS = [kb.psum([P, 512], F32) for _ in range(6)]
    PSB = [kb.psum([P, 1024], BF16) for _ in range(2)]
    pssel = [0]
    psbsel = [0]

    def nps():
        pssel[0] = (pssel[0] + 1) % 6
        return PS[pssel[0]]

    def npsb():
        psbsel[0] = (psbsel[0] + 1) % 2
        return PSB[psbsel[0]]

    def V(name, i=0, n=1):
        o = VOFF[name] + i
        return vecs[:, o:o + n]

    for (t, src) in ((vecs, VECS), (ident, C_ident), (ones, C_ones), (tri, C_tri), (smask, C_smask), (smask64, C_smask64)):
        kb.dma("sp", t[:], src, t, w=t.all)
    sc32 = kb.tile([P, KC * 2], F32)
    kb.dma("sp", sc32[:], SCIN, sc32, w=sc32.all)
    kb.op("act", lambda e: e.activation(out=scT[:], in_=sc32[:], func=AF.Silu), r=sc32.all, w=scT.all)

    def load_w(W, KCn, ranges, cast=True, pre=None):
        wsel[0] ^= 1
        wt = wbuf[wsel[0]]
        ntot = sum(n for _, n in ranges)
        assert KCn * ntot <= WELE
        view = wt[:, 0:KCn * ntot].rearrange("p (k n) -> p k n", n=ntot)
        off = 0
        for (c0, n) in ranges:
            if pre is not None:
                src = pre
            else:
                src = W[:, c0:c0 + n].rearrange("(k p) n -> p k n", p=P)
            kb.dma("pool" if cast else "sp", view[:, :, off:off + n], src, wt, w=wt.all)
            off += n
        return wt, view

    def linear(xT, xbufs, KCn, W, blocks, ttiles, epi, mparts=None):
        for blk in blocks:
            wt, view = load_w(W, KCn, blk["ranges"], pre=blk.get("pre"), cast=blk.get("cast", True))
            for (off, m, tag) in blk["chunks"]:
                pst = []
                for (t0, tn) in ttiles:
                    ps = nps()
                    for kc in range(KCn):
                        kb.op("pe", lambda e, ps=ps, kc=kc, off=off, m=m, t0=t0, tn=tn: e.matmul(
                            ps[0:m, 0:tn], lhsT=view[:, kc, off:off + m], rhs=xT[:, kc, t0:t0 + tn],
                            start=(kc == 0), stop=(kc == KCn - 1)),
                            r=list(wt.all) + list(xbufs), w=ps.all, inc=(kc == KCn - 1))
                    pst.append(ps)
                epi(tag, pst)

    def std_blocks(c0, ncols, bw=512, tag0=0):
        blocks = []
        t = tag0
        for b0 in range(c0, c0 + ncols, bw):
            n = min(bw, c0 + ncols - b0)
            ch = []
            for o in range(0, n, P):
                ch.append((o, min(P, n - o), t))
                t += 1
            blocks.append(dict(ranges=[(b0, n)], chunks=ch))
        return blocks

    def pass_mod(i):
        def epi(tag, pst):
            ps = pst[0]
            kb.op("act", lambda e: e.activation(out=MOD[:, tag * 2:tag * 2 + 2], in_=ps[:, 0:2], func=AF.Identity,
                                                bias=V("bmod", i * 96 + tag), scale=1.0), r=ps.all, w=MOD.all)
        sv = scT[:].rearrange("p (k c) -> p k c", c=2)
        linear(sv, scT.all, KC, w_mod[i], std_blocks(0, 6 * D), [(0, 2)], epi)
        mv = MOD[:].rearrange("p (s k c) -> p s k c", s=6, c=2)
        mav = MA[:].rearrange("p (n k c) -> p n k c", n=2, c=2)
        for n, (sidx, gname) in enumerate(((1, "n1g"), (4, "n2g"))):
            for col in range(2):
                kb.op("dve", lambda e, n=n, sidx=sidx, col=col, gname=gname: e.scalar_tensor_tensor(
                    out=mav[:, n, :, col], in0=mv[:, sidx, :, col], scalar=1.0, in1=V(gname, i * KC, KC),
                    op0=ALU.add, op1=ALU.mult), r=MOD.all, w=MA.all)
        return mv, mav

    def norm_mod(es2, si, mv, mav, n_idx, shift_idx, hT):
        tok0, n, col = SPC["SUPER"][si]
        xs = kb.tile([P, KC * 256], F32, es=es2)
        sq = kb.tile([P, KC * 256], BF16, es=es2)
        rstd = kb.tile([P, 256], F32, es=es2)
        tmp = kb.tile([P, 256], F32, es=es2)
        xv = xs[:].rearrange("p (k t) -> p k t", t=256)
        sv = sq[:].rearrange("p (k t) -> p k t", t=256)
        hv = hT[:, 0:KC * n].rearrange("p (k t) -> p k t", k=KC)
        for s0 in range(0, n, 256):
            kb.dma("sp", xv, SPC["XR"][:, tok0 + s0:tok0 + s0 + 256].rearrange("(k p) t -> p k t", p=P), xs,
                   r=[SPC["XRb"][si]], w=xs.all)
            kb.op("act", lambda e: e.activation(out=sq[:], in_=xs[:], func=AF.Square), r=xs.all, w=sq.all)
            ps = nps()
            for kc in range(KC):
                kb.op("pe", lambda e, kc=kc: e.matmul(ps[:, 0:256], lhsT=ones[:, 0:P], rhs=sv[:, kc, :],
                                                      start=(kc == 0), stop=(kc == KC - 1)),
                      r=list(sq.all) + list(ones.all), w=ps.all, inc=(kc == KC - 1))
            kb.op("act", lambda e: e.activation(out=rstd[:], in_=ps[:, 0:256], func=AF.Sqrt, bias=V("eps"), scale=1.0 / D),
                  r=ps.all, w=rstd.all)
            kb.op("dve", lambda e: e.reciprocal(out=rstd[:], in_=rstd[:]), r=rstd.all, w=rstd.all)
            for kc in range(KC):
                kb.op("dve", lambda e, kc=kc: e.tensor_tensor(out=tmp[:], in0=xv[:, kc, :], in1=rstd[:], op=ALU.mult),
                      r=list(xs.all) + list(rstd.all), w=tmp.all)
                kb.op("act", lambda e, kc=kc: e.activation(out=hv[:, kc, s0:s0 + 256], in_=tmp[:], func=AF.Identity,
                                                           bias=mv[:, shift_idx, kc, col:col + 1],
                                                           scale=mav[:, n_idx, kc, col:col + 1]),
                      r=list(tmp.all) + list(MOD.all) + list(MA.all), w=hT.all)

    def ttiles_of(n):
        return [(t, min(512, n - t)) for t in range(0, n, 512)]

    def resid_epi(es2, si, mv, gate_idx):
        tok0, n, col = SPC["SUPER"][si]
        xo = [kb.tile([P, 1024], F32, es=es2) for _ in range(2)]
        sel = [0]

        def epi(tag, pst):
            sel[0] ^= 1
            x = xo[sel[0]]
            kb.dma("sp", x[:, 0:n], SPC["XR"][tag * P:(tag + 1) * P, tok0:tok0 + n], x, r=[SPC["XRb"][si]], w=x.all)
            for ti, (t0, tn) in enumerate(ttiles_of(n)):
                ps = pst[ti]
                kb.op("dve", lambda e, ps=ps, t0=t0, tn=tn: e.scalar_tensor_tensor(
                    out=x[:, t0:t0 + tn], in0=ps[:, 0:tn], scalar=mv[:, gate_idx, tag, col:col + 1], in1=x[:, t0:t0 + tn],
                    op0=ALU.mult, op1=ALU.add), r=list(ps.all) + list(x.all) + list(MOD.all), w=x.all)
            kb.dma("sp", SPC["XR"][tag * P:(tag + 1) * P, tok0:tok0 + n], x[:, 0:n], x, r=x.all, w=[SPC["XRb"][si]])
        return epi

    def pass_out_ffn(i, si, mv, mav, Wout, hy=False):
        tok0, n, col = SPC["SUPER"][si]
        with kb.scope() as es2:
            zt = kb.tile([P, KC * 1024], BF16, es=es2)
            zv = zt[:, 0:KC * n].rearrange("p (k t) -> p k t", k=KC)
            if hy:
                for s_ in range(2):
                    kb.dma("sp", zv[:, s_ * 8:(s_ + 1) * 8, :], SPC["ZT"][:, tok0:tok0 + n].rearrange("(c s p) t -> s p c t", s=2, p=P)[s_], zt, r=[SCR], w=zt.all)
            else:
                kb.dma("sp", zv, ZT[:, tok0:tok0 + n].rearrange("(k p) t -> p k t", p=P), zt, r=[SCR], w=zt.all)
            linear(zv, zt.all, KC, Wout, std_blocks(0, D), ttiles_of(n), resid_epi(es2, si, mv, 2))
        with kb.scope() as es2:
            hT = kb.tile([P, KC * 1024], BF16, es=es2)
            hid = kb.tile([P, 44 * 1024], BF16, es=es2)
            sg = [kb.tile([P, 512], F32, es=es2) for _ in range(2)]
            sgs = [0]
            with kb.scope() as es3:
                norm_mod(es3, si, mv, mav, 1, 3, hT)
            hv = hT[:, 0:KC * n].rearrange("p (k t) -> p k t", k=KC)
            hidv = hid[:, 0:44 * n].rearrange("p (k t) -> p k t", k=44)
            pend = {}

            def epi_in(tag, pst):
                j, isup = tag
                if not isup:
                    pend[j] = pst
                    return
                gp = pend.pop(j)
                for ti, (t0, tn) in enumerate(ttiles_of(n)):
                    sgs[0] ^= 1
                    s = sg[sgs[0]]
                    kb.op("act", lambda e, s=s, g=gp[ti], tn=tn: e.activation(out=s[:, 0:tn], in_=g[:, 0:tn], func=AF.Silu),
                          r=gp[ti].all, w=s.all)
                    kb.op("dve", lambda e, s=s, u=pst[ti], t0=t0, tn=tn: e.tensor_tensor(
                        out=hidv[:, j, t0:t0 + tn], in0=s[:, 0:tn], in1=u[:, 0:tn], op=ALU.mult),
                        r=list(s.all) + list(pst[ti].all), w=hid.all)
            blocks = []
            for j in range(44):
                blocks.append(dict(ranges=[(j * P, P), (FF + j * P, P)], chunks=[(0, P, (j, 0)), (P, P, (j, 1))]))
            linear(hv, hT.all, KC, w_ffn_in[i], blocks, ttiles_of(n), epi_in)
            linear(hidv, hid.all, 44, w_ffn_out[i], std_blocks(0, D, bw=P), ttiles_of(n), resid_epi(es2, si, mv, 5))

    def fm_to_tm(es2, src_tile, n, dst, tok0, c0, tsb):
        nt = n // P
        psb = npsb()
        pv = psb[:, 0:nt * P].rearrange("p (a b) -> p a b", b=P)
        for tt in range(nt):
            kb.op("pe", lambda e, tt=tt: e.transpose(out=pv[:, tt, :], in_=src_tile[:, tt * P:(tt + 1) * P], identity=ident[:]),
                  r=list(src_tile.all) + list(ident.all), w=psb.all, inc=(tt == nt - 1))
        tv = tsb[:, 0:nt * P].rearrange("p (a b) -> p a b", b=P)
        kb.op("act", lambda e: e.copy(out=tsb[:, 0:nt * P], in_=psb[:, 0:nt * P]), r=psb.all, w=tsb.all)
        kb.dma("sp", dst[tok0:tok0 + n, c0:c0 + P].rearrange("(a p) c -> p a c", p=P), tv, tsb, r=tsb.all, w=[SCR])

    def hyena_pass_a(i, j, si, mv, mav):
        tok0, n, col = SUPER[si]
        rowlen = 256 if si == 0 else 64
        with kb.scope() as es2:
            hT = kb.tile([P, KC * 1024], BF16, es=es2)
            with kb.scope() as es3:
                norm_mod(es3, si, mv, mav, 0, 0, hT)
            hv = hT[:, 0:KC * n].rearrange("p (k t) -> p k t", k=KC)
            pc = kb.tile([P, 1024], F32, es=es2)
            o1 = kb.tile([P, 1024], F32, es=es2)
            ob = kb.tile([P, 1024], BF16, es=es2)
            tsb = kb.tile([P, 1024], BF16, es=es2)

            def epi(tag, pst):
                for ti, (t0, tn) in enumerate(ttiles_of(n)):
                    ps = pst[ti]
                    kb.op("act", lambda e, ps=ps, t0=t0, tn=tn: e.copy(out=pc[:, t0:t0 + tn], in_=ps[:, 0:tn]), r=ps.all, w=pc.all)
                    kb.op("act", lambda e, ps=ps, t0=t0, tn=tn: e.activation(
                        out=o1[:, t0:t0 + tn], in_=ps[:, 0:tn], func=AF.Identity, bias=0.0,
                        scale=V("cw", (j * 3 + 1) * 48 + tag)), r=ps.all, w=o1.all)
                o1v = o1[:, 0:n].rearrange("p (a b) -> p a b", b=rowlen)
                pcv = pc[:, 0:n].rearrange("p (a b) -> p a b", b=rowlen)
                kb.op("dve", lambda e: e.scalar_tensor_tensor(
                    out=o1v[:, :, 1:rowlen], in0=pcv[:, :, 0:rowlen - 1], scalar=V("cw", (j * 3 + 0) * 48 + tag),
                    in1=o1v[:, :, 1:rowlen], op0=ALU.mult, op1=ALU.add), r=list(pc.all) + list(o1.all), w=o1.all)
                kb.op("dve", lambda e: e.scalar_tensor_tensor(
                    out=o1v[:, :, 0:rowlen - 1], in0=pcv[:, :, 1:rowlen], scalar=V("cw", (j * 3 + 2) * 48 + tag),
                    in1=o1v[:, :, 0:rowlen - 1], op0=ALU.mult, op1=ALU.add), r=list(pc.all) + list(o1.all), w=o1.all)
                kb.op("dve", lambda e: e.tensor_copy(out=ob[:, 0:n], in_=o1[:, 0:n]), r=o1.all, w=ob.all)
                fm_to_tm(es2, ob, n, VT[tag // KC], tok0, (tag % KC) * P, tsb)
            linear(hv, hT.all, KC, hy_w_in[j], std_blocks(0, 3 * D), ttiles_of(n), epi)

    def range_reduce_sin(es2, arg, m, n, out_ap, tmp):
        MAGIC = 12582912.0
        kb.op("dve", lambda e: e.tensor_scalar(out=tmp[0:m, 0:n], in0=arg[0:m, 0:n], scalar1=1.0 / (2 * math.pi), scalar2=MAGIC,
                                               op0=ALU.mult, op1=ALU.add), r=arg.all, w=tmp.all)
        kb.op("dve", lambda e: e.tensor_scalar(out=tmp[0:m, 0:n], in0=tmp[0:m, 0:n], scalar1=-MAGIC, scalar2=None,
                                               op0=ALU.add), r=tmp.all, w=tmp.all)
        kb.op("dve", lambda e: e.scalar_tensor_tensor(out=arg[0:m, 0:n], in0=tmp[0:m, 0:n], scalar=-2 * math.pi, in1=arg[0:m, 0:n],
                                                      op0=ALU.mult, op1=ALU.add), r=list(tmp.all) + list(arg.all), w=arg.all)
        kb.op("dve", lambda e: e.tensor_scalar(out=arg[0:m, 0:n], in0=arg[0:m, 0:n], scalar1=-3.14159, scalar2=3.14159,
                                               op0=ALU.max, op1=ALU.min), r=arg.all, w=arg.all)
        kb.op("act", lambda e: e.activation(out=out_ap, in_=arg[0:m, 0:n], func=AF.Sin), r=arg.all, w=[])

    def hyena_filters(j, L):
        c = CL[L]
        TCn, KF = c["TC"], c["KF"]
        with kb.scope() as es2:
            zT = kb.tile([P, L], F32, es=es2)
            h1 = kb.tile([P, L], F32, es=es2)
            h2 = kb.tile([P, L], BF16, es=es2)
            fw1 = kb.tile([P, 64], F32, es=es2)
            fw2 = kb.tile([P, 64], F32, es=es2)
            fwo = kb.tile([P, 4 * CH], BF16, es=es2)
            tn = kb.tile([P, TCn], F32, es=es2)
            negd = kb.tile([P, CH], F32, es=es2)
            skp = kb.tile([P, 2 * CH], F32, es=es2)
            fb = kb.tile([P, 2], F32, es=es2)
            arg = kb.tile([P, 512], F32, es=es2)
            tmp = kb.tile([P, 512], F32, es=es2)
            kb.dma("sp", zT[0:33, :], c["zT"], zT, w=zT.all)
            kb.dma("sp", fw1[0:33, :], hy_fw1[j], fw1, w=fw1.all)
            kb.dma("sp", fw2[0:64, :], hy_fw2[j], fw2, w=fw2.all)
            kb.dma("pool", fwo[0:64, :].rearrange("p (b c) -> p b c", b=4),
                   hy_fwout[j].rearrange("p (b c) -> p b c", b=4)[:, :, bass.ds(pargCH, CH)], fwo, w=fwo.all)
            kb.dma("sp", tn[:], c["tn"], tn, w=tn.all)
            kb.dma("sp", skp[:].rearrange("p (b c) -> p b c", b=2),
                   fskip[j].rearrange("p (b c) -> p b c", b=2)[:, :, bass.ds(parCH, CH)], skp, w=skp.all)
            kb.dma("sp", negd[:], C_negd[:, bass.ds(parCH, CH)], negd, w=negd.all)
            for q in range(2):
                kb.op("dve", lambda e, q=q: e.tensor_tensor(out=fb[0:64, q:q + 1], in0=V("ffreq", j * 2 + q)[0:64, :],
                                                             in1=V("fbias", j * 2 + q)[0:64, :], op=ALU.mult), r=vecs.all, w=fb.all)
            for t0 in range(0, L, 512):
                tn_ = min(512, L - t0)
                ps = nps()
                kb.op("pe", lambda e: e.matmul(ps[0:64, 0:tn_], lhsT=fw1[0:33, 0:64], rhs=zT[0:33, t0:t0 + tn_], start=True, stop=True),
                      r=list(fw1.all) + list(zT.all), w=ps.all)
                kb.op("act", lambda e: e.activation(out=arg[0:64, 0:tn_], in_=ps[0:64, 0:tn_], func=AF.Identity,
                                                    bias=fb[0:64, 0:1], scale=V("ffreq", j * 2 + 0)[0:64, :]),
                      r=list(ps.all) + list(fb.all), w=arg.all)
                range_reduce_sin(es2, arg, 64, tn_, h1[0:64, t0:t0 + tn_], tmp)
                kb._mark((kb.sem["act"], kb.cnt["act"], "act"), [], h1.all)
            for t0 in range(0, L, 512):
                tn_ = min(512, L - t0)
                ps = nps()
                kb.op("pe", lambda e: e.matmul(ps[0:64, 0:tn_], lhsT=fw2[0:64, 0:64], rhs=h1[0:64, t0:t0 + tn_], start=True, stop=True),
                      r=list(fw2.all) + list(h1.all), w=ps.all)
                kb.op("act", lambda e: e.activation(out=arg[0:64, 0:tn_], in_=ps[0:64, 0:tn_], func=AF.Identity,
                                                    bias=fb[0:64, 1:2], scale=V("ffreq", j * 2 + 1)[0:64, :]),
                      r=list(ps.all) + list(fb.all), w=arg.all)
                range_reduce_sin(es2, arg, 64, tn_, h2[0:64, t0:t0 + tn_], tmp)
                kb._mark((kb.sem["act"], kb.cnt["act"], "act"), [], h2.all)
            HF = kb.tile([P, 2 * TCn * 512], BF16, es=es2)
            hfv = HF[:].rearrange("p (k c) -> p k c", c=512)
            dec = [kb.tile([P, 512], F32, es=es2) for _ in range(2)]
            ab = [kb.tile([P, 512], BF16, es=es2) for _ in range(2)]
            rn = kb.tile([P, 512], F32, es=es2)
            ho = [kb.tile([P, 512], F32, es=es2) for _ in range(2)]
            sel = [0]
            for o in range(2):
                for ct in range(2):
                    nps_ = nps()
                    first = True
                    for tc in range(TCn):
                        d = dec[tc % 2]
                        kb.op("act", lambda e, d=d, tc=tc: e.activation(out=d[:], in_=negd[:, ct * 512:(ct + 1) * 512], func=AF.Exp,
                                                                        scale=tn[:, tc:tc + 1]), r=list(negd.all) + list(tn.all), w=d.all)
                        for dr in range(2):
                            ps = nps()
                            if ps is nps_:
                                ps = nps()
                            col0 = (dr * 2 + o) * CH + ct * 512
                            kb.op("pe", lambda e, ps=ps, tc=tc, col0=col0: e.matmul(
                                ps[:, :], lhsT=h2[0:64, tc * P:(tc + 1) * P], rhs=fwo[0:64, col0:col0 + 512], start=True, stop=True),
                                r=list(h2.all) + list(fwo.all), w=ps.all)
                            kk = dr * TCn + tc
                            kb.op("dve", lambda e, ps=ps, d=d, kk=kk: e.tensor_tensor(out=hfv[:, kk, :], in0=ps[:, :], in1=d[:], op=ALU.mult),
                                  r=list(ps.all) + list(d.all), w=HF.all)
                            sel[0] ^= 1
                            a = ab[sel[0]]
                            kb.op("act", lambda e, a=a, kk=kk: e.activation(out=a[:], in_=hfv[:, kk, :], func=AF.Abs),
                                  r=HF.all, w=a.all)
                            last = (dr == 1 and tc == TCn - 1)
                            oc0 = P if last else 0
                            kb.op("pe", lambda e, a=a, oc0=oc0, first=first, last=last: e.matmul(
                                nps_[:, :], lhsT=ones[:, oc0:oc0 + P], rhs=a[:], start=first, stop=last),
                                r=list(a.all) + list(ones.all), w=nps_.all, inc=last)
                            first = False
                    kb.op("dve", lambda e: e.reciprocal(out=rn[:], in_=nps_[:, :]), r=nps_.all, w=rn.all)
                    for blk in range(2 * KF):
                        wt, view = load_w(None, 2 * TCn, [(0, P)], cast=False,
                                          pre=c["FF"][blk].rearrange("p (k n) -> p k n", n=P))
                        ps = nps()
                        for kk in range(2 * TCn):
                            kb.op("pe", lambda e, kk=kk: e.matmul(ps[:, :], lhsT=view[:, kk, :], rhs=hfv[:, kk, :],
                                                                  start=(kk == 0), stop=(kk == 2 * TCn - 1)),
                                  r=list(wt.all) + list(HF.all), w=ps.all, inc=(kk == 2 * TCn - 1))
                        sel[0] ^= 1
                        h = ho[sel[0]]
                        kb.op("dve", lambda e, h=h: e.tensor_tensor(out=h[:], in0=ps[:, :], in1=rn[:], op=ALU.mult),
                              r=list(ps.all) + list(rn.all), w=h.all)
                        if blk < KF:
                            kb.op("pool", lambda e, h=h: e.tensor_tensor(out=h[:], in0=h[:], in1=skp[:, o * CH + ct * 512:o * CH + (ct + 1) * 512],
                                                                         op=ALU.add), r=list(h.all) + list(skp.all), w=h.all)
                        kb.dma("sp", c["HS"][o, blk, :, ct * 512:(ct + 1) * 512], h[:], h, r=h.all, w=[SCR])

    def hyena_conv(L, tok0):
        c = CL[L]
        TCn, KF = c["TC"], c["KF"]
        with kb.scope() as es2:
            Vt = kb.tile([P, TCn * 512], BF16, es=es2)
            Z1 = kb.tile([P, TCn * 512], BF16, es=es2)
            Y = kb.tile([P, 2 * KF * 512], BF16, es=es2)
            Hr = [kb.tile([P, 512], F32, es=es2) for _ in range(2)]
            Hi = [kb.tile([P, 512], F32, es=es2) for _ in range(2)]
            t1 = kb.tile([P, 512], F32, es=es2)
            t2 = kb.tile([P, 512], F32, es=es2)
            xm = [kb.tile([P, 512], BF16, es=es2) for _ in range(2)]
            zb = kb.tile([P, 512], BF16, es=es2)
            tsb = kb.tile([P, 512], BF16, es=es2)
            vv = Vt[:].rearrange("p (k c) -> p k c", c=512)
            z1v = Z1[:].rearrange("p (k c) -> p k c", c=512)
            yv = Y[:].rearrange("p (k c) -> p k c", c=512)
            sel = [0]
            for ct in range(2):
                c0 = ct * 512
                kb.dma("sp", vv, VTm[0, tok0:tok0 + L, c0:c0 + 512].rearrange("(k p) c -> p k c", p=P), Vt, r=[SCR], w=Vt.all)
                for o in range(2):
                    src, srcv = (Vt, vv) if o == 0 else (Z1, z1v)
                    for m in range(KF):
                        wt, view = load_w(None, TCn, [(0, 256)], cast=False, pre=c["Ff"][m].rearrange("p (k n) -> p k n", n=256))
                        pr, pi = nps(), nps()
                        for part, ps in ((0, pr), (1, pi)):
                            for kk in range(TCn):
                                kb.op("pe", lambda e, ps=ps, kk=kk, part=part: e.matmul(
                                    ps[:, :], lhsT=view[:, kk, part * P:(part + 1) * P], rhs=srcv[:, kk, :],
                                    start=(kk == 0), stop=(kk == TCn - 1)), r=list(wt.all) + list(src.all), w=ps.all, inc=(kk == TCn - 1))
                        sel[0] ^= 1
                        hr, hi = Hr[sel[0]], Hi[sel[0]]
                        kb.dma("sp", hr[:], c["HS"][o, m, :, c0:c0 + 512], hr, r=[SCR], w=hr.all)
                        kb.dma("sp", hi[:], c["HS"][o, KF + m, :, c0:c0 + 512], hi, r=[SCR], w=hi.all)
                        kb.op("dve", lambda e: e.tensor_tensor(out=t1[:], in0=pr[:, :], in1=hr[:], op=ALU.mult), r=list(pr.all) + list(hr.all), w=t1.all)
                        kb.op("dve", lambda e: e.tensor_tensor(out=t2[:], in0=pi[:, :], in1=hi[:], op=ALU.mult), r=list(pi.all) + list(hi.all), w=t2.all)
                        kb.op("pool", lambda e, m=m: e.tensor_tensor(out=yv[:, m, :], in0=t1[:], in1=t2[:], op=ALU.subtract),
                              r=list(t1.all) + list(t2.all), w=Y.all)
                        kb.op("dve", lambda e: e.tensor_tensor(out=t1[:], in0=pr[:, :], in1=hi[:], op=ALU.mult), r=list(pr.all) + list(hi.all), w=t1.all)
                        kb.op("dve", lambda e: e.tensor_tensor(out=t2[:], in0=pi[:, :], in1=hr[:], op=ALU.mult), r=list(pi.all) + list(hr.all), w=t2.all)
                        kb.op("pool", lambda e, m=m: e.tensor_tensor(out=yv[:, KF + m, :], in0=t1[:], in1=t2[:], op=ALU.add),
                              r=list(t1.all) + list(t2.all), w=Y.all)
                    for tc in range(TCn):
                        wt, view = load_w(None, 2 * KF, [(0, P)], cast=False, pre=c["G"][tc].rearrange("p (k n) -> p k n", n=P))
                        ps = nps()
                        for kk in range(2 * KF):
                            kb.op("pe", lambda e, kk=kk: e.matmul(ps[:, :], lhsT=view[:, kk, :], rhs=yv[:, kk, :],
                                                                  start=(kk == 0), stop=(kk == 2 * KF - 1)),
                                  r=list(wt.all) + list(Y.all), w=ps.all, inc=(kk == 2 * KF - 1))
                        sel[0] ^= 1
                        x = xm[sel[0]]
                        kb.dma("sp", x[:], VTm[1 + o, tok0 + tc * P:tok0 + (tc + 1) * P, c0:c0 + 512], x, r=[SCR], w=x.all)
                        if o == 0:
                            kb.op("dve", lambda e, tc=tc: e.tensor_tensor(out=z1v[:, tc, :], in0=ps[:, :], in1=x[:], op=ALU.mult),
                                  r=list(ps.all) + list(x.all), w=Z1.all)
                        else:
                            kb.op("dve", lambda e: e.tensor_tensor(out=zb[:], in0=ps[:, :], in1=x[:], op=ALU.mult),
                                  r=list(ps.all) + list(x.all), w=zb.all)
                            psb = npsb()
                            pv = psb[:, 0:512].rearrange("p (a b) -> p a b", b=P)
                            for cc in range(4):
                                kb.op("pe", lambda e, cc=cc: e.transpose(out=pv[:, cc, :], in_=zb[:, cc * P:(cc + 1) * P], identity=ident[:]),
                                      r=list(zb.all) + list(ident.all), w=psb.all, inc=(cc == 3))
                            kb.op("act", lambda e: e.copy(out=tsb[:], in_=psb[:, 0:512]), r=psb.all, w=tsb.all)
                            kb.dma("sp", ZTs[c0:c0 + 512, tok0 + tc * P:tok0 + (tc + 1) * P].rearrange("(a p) t -> p a t", p=P),
                                   tsb[:].rearrange("p (a b) -> p a b", b=P), tsb, r=tsb.all, w=[SCR])

    def gate_chain(es2, T, q_sb, k_of_dir, lf, dr, row0, tok0, n, U=P):
        nu = n // U
        EXd = EX if U == P else EX64
        smk = smask if U == P else smask64
        cs, d, e1, ob, ctot, nct, ex = T["cs"], T["d"], T["e1"], T["ob"], T["ctot"], T["nct"], T["ex"]
        kb.op("dve", lambda e: e.tensor_tensor_scan(out=cs[:, 0:n], data0=smk[:, 0:n], data1=lf[:, 0:n], initial=0.0,
                                                    op0=ALU.mult, op1=ALU.add), r=list(lf.all) + list(smk.all), w=cs.all)
        csv = cs[:, 0:n].rearrange("p (u t) -> p u t", t=U)
        kb.op("dve", lambda e: e.tensor_copy(out=ctot[:, 0:nu], in_=csv[:, :, U - 1]), r=cs.all, w=ctot.all)
        kb.op("dve", lambda e: e.tensor_scalar(out=nct[:, 0:nu], in0=ctot[:, 0:nu], scalar1=-1.0, scalar2=None, op0=ALU.mult),
              r=ctot.all, w=nct.all)
        kb.op("act", lambda e: e.activation(out=ex[:, 0:nu], in_=ctot[:, 0:nu], func=AF.Exp), r=ctot.all, w=ex.all)
        u0 = tok0 // U
        kb.dma("sp", EXd[dr, row0:row0 + P, u0:u0 + nu], ex[:, 0:nu], ex, r=ex.all, w=[SCR])

        def emit(dst, src_mul, make_exp):
            make_exp()
            kb.op("dve", lambda e: e.tensor_tensor(out=ob[:, 0:n], in0=src_mul[:, 0:n], in1=e1[:, 0:n], op=ALU.mult),
                  r=list(src_mul.all) + list(e1.all), w=ob.all)
            kb.dma("sp", dst[dr, row0:row0 + P, tok0:tok0 + n], ob[:, 0:n], ob, r=ob.all, w=[SCR])

        def full_exp(src, scale):
            return lambda: kb.op("act", lambda e: e.activation(out=e1[:, 0:n], in_=src[:, 0:n], func=AF.Exp, scale=scale),
                                 r=src.all, w=e1.all)

        def unit_exp(src, scale, bias_t):
            def f():
                for u in range(nu):
                    kb.op("act", lambda e, u=u: e.activation(out=e1[:, u * U:(u + 1) * U], in_=src[:, u * U:(u + 1) * U], func=AF.Exp,
                                                             scale=scale, bias=bias_t[:, u:u + 1]),
                          r=list(src.all) + list(bias_t.all), w=e1.all)
            return f
        if dr == 0:
            emit(QT, q_sb, full_exp(cs, 1.0))
            emit(KTs, k_of_dir, full_exp(cs, -1.0))
            emit(KHs, k_of_dir, unit_exp(cs, -1.0, ctot))
        else:
            kb.op("dve", lambda e: e.tensor_tensor(out=d[:, 0:n], in0=lf[:, 0:n], in1=cs[:, 0:n], op=ALU.subtract),
                  r=list(lf.all) + list(cs.all), w=d.all)
            emit(QT, q_sb, unit_exp(d, 1.0, ctot))
            emit(KTs, k_of_dir, unit_exp(d, -1.0, nct))
            emit(KHs, k_of_dir, full_exp(d, -1.0))

    def chain_tiles(es2):
        T = {}
        for nm in ("cs", "d", "e1"):
            T[nm] = kb.tile([P, 1024], F32, es=es2)
        T["ob"] = kb.tile([P, 1024], BF16, es=es2)
        for nm in ("ctot", "nct", "ex"):
            T[nm] = kb.tile([P, 16], F32, es=es2)
        return T

    def hgrn_pass_a(i, si, mv, mav, lbt):
        tok0, n, col = SUPER[si]
        with kb.scope() as es2:
            hT = kb.tile([P, KC * 1024], BF16, es=es2)
            with kb.scope() as es3:
                norm_mod(es3, si, mv, mav, 0, 0, hT)
            hv = hT[:, 0:KC * n].rearrange("p (k t) -> p k t", k=KC)
            T = chain_tiles(es2)
            q_sb = kb.tile([P, 1024], F32, es=es2)
            fv = kb.tile([P, 1024], F32, es=es2)
            lf = kb.tile([P, 1024], F32, es=es2)
            k_sb = kb.tile([P, 1024], F32, es=es2)
            ob2 = kb.tile([P, 1024], BF16, es=es2)
            tsb = kb.tile([P, 1024], BF16, es=es2)

            def epi(tag, pst):
                kind, h = tag
                for ti, (t0, tn) in enumerate(ttiles_of(n)):
                    ps = pst[ti]
                    if kind == "q":
                        kb.op("act", lambda e, ps=ps, t0=t0, tn=tn: e.activation(out=q_sb[:, t0:t0 + tn], in_=ps[:, 0:tn], func=AF.Silu),
                              r=ps.all, w=q_sb.all)
                    elif kind == "i":
                        kb.op("act", lambda e, ps=ps, t0=t0, tn=tn: e.copy(out=ob2[:, t0:t0 + tn], in_=ps[:, 0:tn]), r=ps.all, w=ob2.all)
                    elif kind == "g":
                        kb.op("act", lambda e, ps=ps, t0=t0, tn=tn: e.activation(out=ob2[:, t0:t0 + tn], in_=ps[:, 0:tn], func=AF.Silu),
                              r=ps.all, w=ob2.all)
                    else:
                        kb.op("act", lambda e, ps=ps, t0=t0, tn=tn: e.activation(out=fv[:, t0:t0 + tn], in_=ps[:, 0:tn], func=AF.Sigmoid),
                              r=ps.all, w=fv.all)
                if kind == "i":
                    fm_to_tm(es2, ob2, n, VT[0], tok0, h * P, tsb)
                elif kind == "g":
                    kb.dma("sp", GATE[h * P:(h + 1) * P, tok0:tok0 + n], ob2[:, 0:n], ob2, r=ob2.all, w=[SCR])
                elif kind in ("f0", "f1"):
                    dr = 0 if kind == "f0" else 1
                    kb.op("dve", lambda e: e.tensor_scalar(out=fv[:, 0:n], in0=fv[:, 0:n], scalar1=lbt[:, (2 + dr) * KC + h:(2 + dr) * KC + h + 1],
                                                           scalar2=lbt[:, dr * KC + h:dr * KC + h + 1], op0=ALU.mult, op1=ALU.add),
                          r=list(fv.all) + list(lbt.all), w=fv.all)
                    kb.op("act", lambda e: e.activation(out=lf[:, 0:n], in_=fv[:, 0:n], func=AF.Ln), r=fv.all, w=lf.all)
                    kb.op("dve", lambda e: e.tensor_scalar(out=k_sb[:, 0:n], in0=fv[:, 0:n], scalar1=-1.0, scalar2=1.0, op0=ALU.mult, op1=ALU.add),
                          r=fv.all, w=k_sb.all)
                    gate_chain(es2, T, q_sb, k_sb, lf, dr, h * P, tok0, n, U=64)
            blocks = []
            for h in range(KC):
                blocks.append(dict(ranges=[(h * P, P), (3 * D + h * P, P), (4 * D + h * P, P)],
                                   chunks=[(0, P, ("q", h)), (P, P, ("f0", h)), (2 * P, P, ("f1", h))]))
            for h in range(0, KC, 4):
                blocks.append(dict(ranges=[(D + h * P, 512)], chunks=[(o * P, P, ("i", h + o)) for o in range(4)]))
            for h in range(0, KC, 4):
                blocks.append(dict(ranges=[(2 * D + h * P, 512)], chunks=[(o * P, P, ("g", h + o)) for o in range(4)]))
            linear(hv, hT.all, KC, hg_w_in[0], blocks, ttiles_of(n), epi)

    def gla_pass_a(i, si, mv, mav, wup):
        tok0, n, col = SUPER[si]
        with kb.scope() as es2:
            hT = kb.tile([P, KC * 1024], BF16, es=es2)
            with kb.scope() as es3:
                norm_mod(es3, si, mv, mav, 0, 0, hT)
            hv = hT[:, 0:KC * n].rearrange("p (k t) -> p k t", k=KC)
            T = chain_tiles(es2)
            q_sb = kb.tile([P, 1024], F32, es=es2)
            k_sb = kb.tile([P, 1024], F32, es=es2)
            lf = kb.tile([P, 1024], F32, es=es2)
            aT = [kb.tile([P, 1024], BF16, es=es2) for _ in range(2)]
            ob2 = kb.tile([P, 1024], BF16, es=es2)
            tsb = kb.tile([P, 1024], BF16, es=es2)

            def epi(tag, pst):
                kind, h = tag
                for ti, (t0, tn) in enumerate(ttiles_of(n)):
                    ps = pst[ti]
                    if kind == "a":
                        kb.op("act", lambda e, ps=ps, t0=t0, tn=tn: e.copy(out=aT[h][0:16, t0:t0 + tn], in_=ps[0:16, 0:tn]), r=ps.all, w=aT[h].all)
                    elif kind == "q":
                        kb.op("act", lambda e, ps=ps, t0=t0, tn=tn: e.activation(out=q_sb[:, t0:t0 + tn], in_=ps[:, 0:tn], func=AF.Identity, scale=1.0 / 16.0),
                              r=ps.all, w=q_sb.all)
                    elif kind == "k":
                        kb.op("act", lambda e, ps=ps, t0=t0, tn=tn: e.copy(out=k_sb[:, t0:t0 + tn], in_=ps[:, 0:tn]), r=ps.all, w=k_sb.all)
                    elif kind == "v":
                        kb.op("act", lambda e, ps=ps, t0=t0, tn=tn: e.copy(out=ob2[:, t0:t0 + tn], in_=ps[:, 0:tn]), r=ps.all, w=ob2.all)
                    elif kind == "g":
                        kb.op("act", lambda e, ps=ps, t0=t0, tn=tn: e.activation(out=ob2[:, t0:t0 + tn], in_=ps[:, 0:tn], func=AF.Silu),
                              r=ps.all, w=ob2.all)
                if kind == "v":
                    fm_to_tm(es2, ob2, n, VT[0], tok0, h * P, tsb)
                elif kind == "g":
                    kb.dma("sp", GATE[h * P:(h + 1) * P, tok0:tok0 + n], ob2[:, 0:n], ob2, r=ob2.all, w=[SCR])
                elif kind == "k":
                    for dr in range(2):
                        for ti, (t0, tn) in enumerate(ttiles_of(n)):
                            ps = nps()
                            kb.op("pe", lambda e, ps=ps, t0=t0, tn=tn: e.matmul(
                                ps[:, 0:tn], lhsT=wup[0:16, dr * 1024 + h * P:dr * 1024 + (h + 1) * P], rhs=aT[dr][0:16, t0:t0 + tn],
                                start=True, stop=True), r=list(wup.all) + list(aT[dr].all), w=ps.all)
                            kb.op("act", lambda e, ps=ps, t0=t0, tn=tn: e.activation(
                                out=lf[:, t0:t0 + tn], in_=ps[:, 0:tn], func=AF.Exp, scale=-1.0, bias=V("nbup", dr * 8 + h)),
                                r=ps.all, w=lf.all)
                        kb.op("act", lambda e: e.activation(out=lf[:, 0:n], in_=lf[:, 0:n], func=AF.Ln, bias=1.0, scale=1.0), r=lf.all, w=lf.all)
                        kb.op("dve", lambda e: e.tensor_scalar(out=lf[:, 0:n], in0=lf[:, 0:n], scalar1=-1.0 / 16.0, scalar2=None, op0=ALU.mult),
                              r=lf.all, w=lf.all)
                        gate_chain(es2, T, q_sb, k_sb, lf, dr, h * P, tok0, n)
            blocks = [dict(ranges=[(6144, 16), (6160, 16)], chunks=[(0, 16, ("a", 0)), (16, 16, ("a", 1))])]
            for h in range(8):
                blocks.append(dict(ranges=[(h * P, P), (1024 + h * P, P)], chunks=[(0, P, ("q", h)), (P, P, ("k", h))]))
            for h in range(0, KC, 4):
                blocks.append(dict(ranges=[(2048 + h * P, 512)], chunks=[(o * P, P, ("v", h + o)) for o in range(4)]))
            for h in range(0, KC, 4):
                blocks.append(dict(ranges=[(4096 + h * P, 512)], chunks=[(o * P, P, ("g", h + o)) for o in range(4)]))
            linear(hv, hT.all, KC, gla_w_in[0], blocks, ttiles_of(n), epi)

    def scan_core(H, DKC, DV, gname, U=P):
        NUu = TT // U
        EXd = EX if U == P else EX64
        with kb.scope() as es2:
            qt = kb.tile([P, DKC * TT], BF16, es=es2)
            kt = kb.tile([P, DKC * TT], BF16, es=es2)
            khu = [kb.tile([P, DKC * P], BF16, es=es2) for _ in range(2)]
            vt = kb.tile([P, NUu * DV], BF16, es=es2)
            oacc = kb.tile([P, NUu * DV], F32, NUu, es=es2)
            ext = kb.tile([P, DKC * NUu], F32, es=es2)
            S = [kb.tile([P, DV], F32, es=es2) for _ in range(DKC)]
            Sb = [kb.tile([P, DV], BF16, es=es2) for _ in range(DKC)]
            asb = [kb.tile([P, P], BF16, es=es2) for _ in range(2)]
            khs = [kb.tile([P, DKC * P], BF16, es=es2) for _ in range(2)]
            sq = kb.tile([P, DV], F32, es=es2)
            ssq = kb.tile([P, 1], F32, es=es2)
            on = kb.tile([P, DV], BF16, es=es2)
            gt = [kb.tile([P, P], BF16, es=es2) for _ in range(2)]
            osb = [kb.tile([P, P], BF16, es=es2) for _ in range(2)]
            sel = [0]
            qv = qt[:].rearrange("p (k t) -> p k t", k=DKC)
            kv = kt[:].rearrange("p (k t) -> p k t", k=DKC)
            vv = vt[:].rearrange("p (u v) -> p u v", v=DV)
            ov = oacc[:].rearrange("p (u v) -> p u v", v=DV)
            exv = ext[:].rearrange("p (k u) -> p k u", k=DKC)
            for h in range(H // 2):
                r0 = h * DKC * P
                kb.dma("sp", vv[0:U], VTm[0, :, h * DV:(h + 1) * DV].rearrange("(u p) v -> p u v", p=U), vt, r=[SCR], w=vt.all)
                for dr in range(2):
                    for (t, vw, src) in ((qt, qv, FMm["q"]), (kt, kv, FMm["k"])):
                        kb.dma("sp", vw, src[dr, r0:r0 + DKC * P, :].rearrange("(k p) t -> p k t", p=P), t, r=[SCR], w=t.all)
                    kb.dma("sp", exv, EXm[dr, r0:r0 + DKC * P, 0:NUu].rearrange("(k p) u -> p k u", p=P), ext, r=[SCR], w=ext.all)
                    for k in range(DKC):
                        kb.op("dve", lambda e, k=k: e.memset(S[k][:], 0.0), w=S[k].all)
                        kb.op("pool", lambda e, k=k: e.memset(Sb[k][:], 0.0), w=Sb[k].all)
                    nc_ = TCX // U
                    order = list(range(NUu)) if dr == 0 else list(range(nc_ - 1, -1, -1)) + list(range(NUu - 1, nc_ - 1, -1))
                    for u in order:
                        ts = slice(u * U, (u + 1) * U)
                        sel[0] ^= 1
                        a, khb = asb[sel[0]], khs[sel[0]]
                        pa = nps()
                        for k in range(DKC):
                            kb.op("pe", lambda e, k=k: e.matmul(pa[0:U, 0:U], lhsT=kv[:, k, ts], rhs=qv[:, k, ts], start=(k == 0), stop=(k == DKC - 1)),
                                  r=list(kt.all) + list(qt.all), w=pa.all, inc=(k == DKC - 1))
                        kb.op("dve", lambda e: e.tensor_tensor(out=a[0:U, 0:U], in0=pa[0:U, 0:U], in1=tri[0:U, dr * P:dr * P + U], op=ALU.mult),
                              r=list(pa.all) + list(tri.all), w=a.all)
                        po = nps()
                        kb.op("pe", lambda e: e.matmul(po[0:U, 0:DV], lhsT=a[0:U, 0:U], rhs=vv[0:U, u, :], start=True, stop=False),
                              r=list(a.all) + list(vt.all), w=po.all, inc=False)
                        for k in range(DKC):
                            kb.op("pe", lambda e, k=k: e.matmul(po[0:U, 0:DV], lhsT=qv[:, k, ts], rhs=Sb[k][:], start=False, stop=(k == DKC - 1)),
                                  r=list(qt.all) + list(Sb[k].all), w=po.all, inc=(k == DKC - 1))
                        if dr == 0:
                            kb.op("act", lambda e: e.copy(out=ov[0:U, u, :], in_=po[0:U, 0:DV]), r=po.all, w=[oacc.parts[u]])
                        else:
                            kb.op("dve", lambda e: e.tensor_tensor(out=ov[0:U, u, :], in0=po[0:U, 0:DV], in1=ov[0:U, u, :], op=ALU.add),
                                  r=list(po.all) + [oacc.parts[u]], w=[oacc.parts[u]])
                        kht = khu[sel[0]]
                        khv = kht[:, 0:DKC * U].rearrange("p (k t) -> p k t", k=DKC)
                        kb.dma("sp", khv, FMm["h"][dr, r0:r0 + DKC * P, u * U:(u + 1) * U].rearrange("(k p) t -> p k t", p=P), kht, r=[SCR], w=kht.all)
                        psb = npsb()
                        pv = psb[:, 0:DKC * P].rearrange("p (k d) -> p k d", d=P)
                        for k in range(DKC):
                            kb.op("pe", lambda e, k=k: e.transpose(out=pv[0:U, k, :], in_=khv[:, k, :], identity=ident[:]),
                                  r=list(kht.all) + list(ident.all), w=psb.all, inc=(k == DKC - 1))
                        kb.op("act", lambda e: e.copy(out=khb[0:U, :], in_=psb[0:U, 0:DKC * P]), r=psb.all, w=khb.all)
                        for k in range(DKC):
                            pd = nps()
                            kb.op("pe", lambda e, k=k, pd=pd: e.matmul(pd[:, 0:DV], lhsT=khb[0:U, k * P:(k + 1) * P], rhs=vv[0:U, u, :], start=True, stop=True),
                                  r=list(khb.all) + list(vt.all), w=pd.all)
                            kb.op("dve", lambda e, k=k, pd=pd: e.scalar_tensor_tensor(
                                out=S[k][:], in0=S[k][:], scalar=exv[:, k, u:u + 1], in1=pd[:, 0:DV], op0=ALU.mult, op1=ALU.add),
                                r=list(S[k].all) + list(pd.all) + list(ext.all), w=S[k].all)
                            kb.op("act", lambda e, k=k: e.copy(out=Sb[k][:], in_=S[k][:]), r=S[k].all, w=Sb[k].all)
                for u in range(NUu):
                    kb.op("act", lambda e, u=u: e.activation(out=sq[0:U, :], in_=ov[0:U, u, :], func=AF.Square),
                          r=[oacc.parts[u]], w=list(sq.all))
                    kb.op("dve", lambda e: e.reduce_sum(out=ssq[0:U, :], in_=sq[0:U, :], axis=mybir.AxisListType.X), r=sq.all, w=ssq.all)
                    kb.op("act", lambda e: e.activation(out=ssq[0:U, :], in_=ssq[0:U, :], func=AF.Sqrt, bias=V("eps")[0:U, :], scale=1.0 / DV), r=ssq.all, w=ssq.all)
                    kb.op("dve", lambda e: e.reciprocal(out=ssq[0:U, :], in_=ssq[0:U, :]), r=ssq.all, w=ssq.all)
                    kb.op("dve", lambda e, u=u: e.tensor_scalar(out=on[0:U, :], in0=ov[0:U, u, :], scalar1=ssq[0:U, 0:1], scalar2=None, op0=ALU.mult),
                          r=[oacc.parts[u]] + list(ssq.all), w=on.all)
                    for cb in range(DV // P):
                        ch = (h * DV) // P + cb
                        sel[0] ^= 1
                        g, ob = gt[sel[0]], osb[sel[0]]
                        kb.dma("sp", g[:, 0:U], GATEm[ch * P:(ch + 1) * P, u * U:(u + 1) * U], g, r=[SCR], w=g.all)
                        psb = npsb()
                        kb.op("pe", lambda e, cb=cb: e.transpose(out=psb[:, 0:U], in_=on[0:U, cb * P:(cb + 1) * P], identity=ident[0:U, 0:U]),
                              r=list(on.all) + list(ident.all), w=psb.all)
                        kb.op("dve", lambda e, ch=ch, g=g, ob=ob: e.scalar_tensor_tensor(
                            out=ob[:, 0:U], in0=psb[:, 0:U], scalar=V(gname, ch), in1=g[:, 0:U], op0=ALU.mult, op1=ALU.mult),
                            r=list(psb.all) + list(g.all), w=ob.all)
                        kb.dma("sp", ZTs[ch * P:(ch + 1) * P, u * U:(u + 1) * U], ob[:, 0:U], ob, r=ob.all, w=[SCR])

    with kb.scope() as es2:
        cp = [kb.tile([P, 4352], F32, es=es2) for _ in range(2)]
        for r in range(KC):
            t = cp[r % 2]
            kb.dma("sp", t[:], XIN[r * P:(r + 1) * P, :], t, w=t.all)
            kb.dma("sp", XR[r * P:(r + 1) * P, :], t[:], t, r=t.all, w=XRb)
    kb.barrier()

    def loc_scan(R, U):
        NUu = TT // U
        EXd = EX if U == P else EX64
        e_sp, e_act, e_pool = (parCH, parsCH, pargCH) if R == CH else (par512, pars512, parg512)
        kb.dma("sp", FMm["q"][:, 0:R, :], QT[:, bass.ds(e_sp, R), :], dh[0], r=[SCR], w=[SCR])
        kb.dma("act", FMm["k"][:, 0:R, :], KTs[:, bass.ds(e_act, R), :], dh[1], r=[SCR], w=[SCR])
        kb.dma("pool", FMm["h"][:, 0:R, :], KHs[:, bass.ds(e_pool, R), :], dh[2], r=[SCR], w=[SCR])
        kb.dma("sp", EXm[:, 0:R, 0:NUu], EXd[:, bass.ds(e_sp, R), :], dh[0], r=[SCR], w=[SCR])
        kb.dma("sp", VTm[0, 0:TT // 2, :], VT[0, 0:TT // 2, :][:, bass.ds(parCH, CH)], dh[0], r=[SCR], w=[SCR])
        kb.dma("act", VTm[0, TT // 2:TT, :], VT[0, TT // 2:TT, :][:, bass.ds(parsCH, CH)], dh[1], r=[SCR], w=[SCR])
        kb.dma("pool", GATEm, GATE[bass.ds(pargCH, CH), :], dh[2], r=[SCR], w=[SCR])
        kb.barrier()

    def gather_zt():
        kb.allgather([(ZTs[c_ * P:(c_ + 1) * P, :], ZTg[c_ * 2 * P:(c_ + 1) * 2 * P, :]) for c_ in range(CH // P)], GROUPS)

    for i in range(nlayers):
        kind, j = i % 3, i // 3
        last = i == DEPTH - 1
        sis = list(range(len(SUPER)))
        if last and kind == 0:
            sis = sis[1:]
        mv, mav = pass_mod(i)
        if kind == 0:
            for si in sis:
                hyena_pass_a(i, j, si, mv, mav)
            kb.barrier()
            for w_, (qq, ee) in enumerate((("sp", parCH), ("act", parsCH), ("pool", pargCH))):
                for r0_, r1_ in ((0, TT // 2), (TT // 2, TT)):
                    kb.dma(qq, VTm[w_, r0_:r1_, :], VT[w_, r0_:r1_, :][:, bass.ds(ee, CH)], dh[w_], r=[SCR], w=[SCR])
            kb.barrier()
            hyena_filters(j, TL)
            kb.barrier()
            hyena_conv(TL, TCX)
            kb.barrier()
            if 0 in sis:
                hyena_filters(j, TCX)
                kb.barrier()
                hyena_conv(TCX, 0)
            kb.allgather([(ZTs[c_ * P:(c_ + 1) * P, :], ZTg[c_ * 2 * P:(c_ + 1) * 2 * P, :]) for c_ in range(CH // P)], GROUPS)
            Wout = hy_w_out[j]
        elif kind == 1:
            with kb.scope() as esl:
                lbt = kb.tile([P, 4 * KC], F32, es=esl)
                ee = kb.tile([P, 2 * 4 * KC], F32, es=esl)
                sm = kb.tile([P, 2 * KC], F32, es=esl)
                kb.op("act", lambda e: e.activation(out=ee[:], in_=V("lbl", 0, 2 * 4 * KC), func=AF.Exp), r=vecs.all, w=ee.all)
                ev = ee[:].rearrange("p (d l k) -> p d l k", d=2, l=4)
                smv = sm[:].rearrange("p (d k) -> p d k", d=2)
                kb.op("dve", lambda e: e.tensor_tensor(out=smv, in0=ev[:, :, 0, :], in1=ev[:, :, 1, :], op=ALU.add), r=ee.all, w=sm.all)
                kb.op("dve", lambda e: e.tensor_tensor(out=smv, in0=smv, in1=ev[:, :, 2, :], op=ALU.add), r=list(ee.all) + list(sm.all), w=sm.all)
                kb.op("dve", lambda e: e.tensor_tensor(out=smv, in0=smv, in1=ev[:, :, 3, :], op=ALU.add), r=list(ee.all) + list(sm.all), w=sm.all)
                kb.op("dve", lambda e: e.reciprocal(out=sm[:], in_=sm[:]), r=sm.all, w=sm.all)
                lv = lbt[:].rearrange("p (a d k) -> p a d k", a=2, d=2)
                kb.op("dve", lambda e: e.tensor_tensor(out=lv[:, 0], in0=ev[:, :, 1, :], in1=smv, op=ALU.mult), r=list(ee.all) + list(sm.all), w=lbt.all)
                kb.op("dve", lambda e: e.tensor_scalar(out=lv[:, 1], in0=lv[:, 0], scalar1=-1.0, scalar2=1.0, op0=ALU.mult, op1=ALU.add),
                      r=lbt.all, w=lbt.all)
                for si in sis:
                    hgrn_pass_a(i, si, mv, mav, lbt)
            kb.barrier()
            loc_scan(CH, 64)
            scan_core(16, 1, 128, "hgg", U=64)
            gather_zt()
            Wout = hg_w_out[0]
        else:
            with kb.scope() as esl:
                wup = kb.tile([P, 2048], BF16, es=esl)
                kb.dma("pool", wup[0:16, :].rearrange("p (d n) -> p d n", d=2), gla_w_up[0].rearrange("d p n -> p d n"), wup, w=wup.all)
                for si in sis:
                    gla_pass_a(i, si, mv, mav, wup)
            kb.barrier()
            loc_scan(512, P)
            scan_core(4, 2, 512, "glg")
            gather_zt()
            Wout = gla_w_out[0]
        has_ctx = 0 in sis
        kb.dma("act", ZTl[:, TCX:TLOC], ZTg[:, TCX:TT][:, bass.ds(parsTLH, TLH)], dh[1], r=[SCR], w=[SCR])
        kb.dma("pool", XRl[:, TCX:TLOC], XR[:, TCX:TT][:, bass.ds(pargTLH, TLH)], dh[2], r=XRb + [SCR], w=XRlb + [SCR])
        if has_ctx:
            kb.dma("sp", ZTl[:, 0:TCX], ZTg[:, 0:TCX], dh[0], r=[SCR], w=[SCR])
            kb.dma("sp", XRl[:, 0:TCX], XR[:, 0:TCX], dh[0], r=XRb + [SCR], w=XRlb + [SCR])
        kb.barrier()
        SPC.update(SUPER=SUPERL, XR=XRl, XRb=XRlb, ZT=ZTl)
        for si in ([0, 1, 2] if has_ctx else [1, 2]):
            pass_out_ffn(i, si, mv, mav, Wout, hy=True)
        SPC.update(SUPER=SUPER, XR=XR, XRb=XRb, ZT=ZTg)
        kb.dma("sp", XRs, XRl[:, TCX:TLOC], dh[0], r=XRlb + [SCR], w=[SCR])
        kb.allgather([(XRs[c_ * P:(c_ + 1) * P, :], XRg[c_ * 2 * P:(c_ + 1) * 2 * P, :]) for c_ in range(KC)], GROUPS)
        for s_ in range(2):
            kb.dma("sp", XR[:, TCX + s_ * TLH:TCX + (s_ + 1) * TLH].rearrange("(c p) t -> c p t", p=P),
                   XRg.rearrange("(c s p) t -> s c p t", s=2, p=P)[s_], dh[0], r=[SCR], w=XRb + [SCR])
        if has_ctx:
            kb.dma("sp", XR[:, 0:TCX], XRl[:, 0:TCX], dh[0], r=XRlb + [SCR], w=XRb + [SCR])
        kb.barrier()

    with kb.scope() as es2:
        xs = kb.tile([P, KC * 256], F32, es=es2)
        sq = kb.tile([P, KC * 256], BF16, es=es2)
        rstd = kb.tile([P, 256], F32, es=es2)
        ot = kb.tile([P, KC * 256], F32, es=es2)
        xv = xs[:].rearrange("p (k t) -> p k t", t=256)
        sv = sq[:].rearrange("p (k t) -> p k t", t=256)
        otv = ot[:].rearrange("p (k t) -> p k t", t=256)
        for s0 in range(0, TL, 256):
            kb.dma("sp", xv, XR[:, TCX + s0:TCX + s0 + 256].rearrange("(k p) t -> p k t", p=P), xs, r=XRb, w=xs.all)
            kb.op("act", lambda e: e.activation(out=sq[:], in_=xs[:], func=AF.Square), r=xs.all, w=sq.all)
            ps = nps()
            for kc in range(KC):
                kb.op("pe", lambda e, kc=kc: e.matmul(ps[:, 0:256], lhsT=ones[:, 0:P], rhs=sv[:, kc, :], start=(kc == 0), stop=(kc == KC - 1)),
                      r=list(sq.all) + list(ones.all), w=ps.all, inc=(kc == KC - 1))
            kb.op("act", lambda e: e.activation(out=rstd[:], in_=ps[:, 0:256], func=AF.Sqrt, bias=V("eps"), scale=1.0 / D), r=ps.all, w=rstd.all)
            kb.op("dve", lambda e: e.reciprocal(out=rstd[:], in_=rstd[:]), r=rstd.all, w=rstd.all)
            for kc in range(KC):
                kb.op("dve", lambda e, kc=kc: e.scalar_tensor_tensor(out=otv[:, kc, :], in0=xv[:, kc, :], scalar=V("fing", kc), in1=rstd[:],
                                                                     op0=ALU.mult, op1=ALU.mult), r=list(xs.all) + list(rstd.all), w=ot.all)
            kb.dma("sp", OUT[:, s0:s0 + 256].rearrange("(k p) t -> p k t", p=P), otv, ot, r=ot.all, w=[SCR])
    kb.barrier()
    es.close()
    return nc


VOFF = {}
NV = 0


def _voff():
    global NV
    o = 0
    for name, n in (("eps", 1), ("bmod", DEPTH * 96), ("n1g", DEPTH * KC), ("n2g", DEPTH * KC), ("fing", KC),
                    ("cw", 2 * 3 * 48), ("ffreq", 4), ("fbias", 4), ("lbl", 2 * 4 * KC), ("hgg", KC), ("glg", KC), ("nbup", 16)):
        VOFF[name] = o
        o += n
    NV = o


_voff()
nlayers = DEPTH


def fm(v):
    v = np.asarray(v, np.float32).reshape(-1, P)
    return v.T


def pack_vecs(inp, par=0):
    vecs = np.zeros((P, NV), np.float32)

    def put(name, arr, off=0):
        arr = np.asarray(arr, np.float32)
        vecs[:arr.shape[0], VOFF[name] + off:VOFF[name] + off + arr.shape[1]] = arr
    put("eps", np.full((P, 1), EPS, np.float32))
    for i in range(DEPTH):
        put("bmod", fm(inp["b_mod"][i]), i * 96)
        put("n1g", fm(inp["norm1_g"][i]), i * KC)
        put("n2g", fm(inp["norm2_g"][i]), i * KC)
    put("fing", fm(inp["final_g"]))
    for j in range(2):
        for tap in range(3):
            put("cw", fm(inp["hy_conv_w"][j, tap]), (j * 3 + tap) * 48)
        for q in range(2):
            put("ffreq", inp["hy_ffreq"][j, q].reshape(64, 1), j * 2 + q)
        put("fbias", inp["hy_fb1"][j].reshape(64, 1), j * 2 + 0)
        put("fbias", inp["hy_fb2"][j].reshape(64, 1), j * 2 + 1)
    for d in range(2):
        for l in range(4):
            put("lbl", fm(inp["hg_lb_logits"][d, l]), (d * 4 + l) * KC)
    put("hgg", np.roll(fm(inp["hg_onorm_g"][0]), -8 * par, axis=1))
    put("glg", np.roll(fm(inp["gla_onorm_g"][0]), -8 * par, axis=1))
    for d in range(2):
        put("nbup", -fm(inp["gla_b_up"][0, d]), d * 8)
    return vecs


_CONST = {}


def consts():
    if _CONST:
        return _CONST
    bf = ml_dtypes.bfloat16
    c = {}
    c["c_ident"] = np.eye(P, dtype=np.float32).astype(bf)
    on = np.ones((P, 2 * P), np.float32)
    on[P - 1, P:] = 0.0
    c["c_ones"] = on.astype(bf)
    s = np.arange(P)[:, None]
    t = np.arange(P)[None, :]
    c["c_tri"] = np.concatenate([(s <= t), (s >= t)], axis=1).astype(np.float32)
    sm = np.ones((P, 1024), np.float32)
    sm[:, ::P] = 0.0
    c["c_smask"] = sm
    sm2 = np.ones((P, 1024), np.float32)
    sm2[:, ::64] = 0.0
    c["c_smask64"] = sm2
    delt = np.abs(np.linspace(DMIN, DMAX, D, dtype=np.float32))
    c["c_negd"] = np.broadcast_to(-delt[None, :], (P, D)).astype(np.float32).copy()
    for L in (TL, TCX):
        TCn, KF = L // P, L // P + 1
        N = 2 * L
        pos = np.arange(L, dtype=np.float32)
        tt = pos / max(L - 1, 1)
        bands = np.arange(1, 17, dtype=np.float32)
        ang = (2.0 * math.pi / L) * pos[:, None] * bands[None, :]
        z = np.concatenate([tt[:, None], np.cos(ang), -np.sin(ang)], axis=-1).astype(np.float32)
        c["c_zT%d" % L] = np.ascontiguousarray(z.T)
        c["c_tn%d" % L] = np.ascontiguousarray(tt.reshape(TCn, P).T)
        kpad = KF * P
        kk = np.arange(kpad, dtype=np.float64)
        valid = (kk <= L).astype(np.float64)
        tpos = np.arange(L, dtype=np.float64)
        ph = 2 * math.pi * ((tpos[:, None] * kk[None, :]) % N) / N
        Fre = np.cos(ph) * valid
        Fim = -np.sin(ph) * valid
        Ff = np.stack([Fre.reshape(TCn, P, KF, P), Fim.reshape(TCn, P, KF, P)], axis=3)
        c["c_Ff%d" % L] = np.ascontiguousarray(Ff.transpose(2, 1, 0, 3, 4).reshape(KF, P, TCn * 256)).astype(bf)
        phb = 2 * math.pi * (((tpos[:, None] + 1) * kk[None, :]) % N) / N
        rowv = (tpos < L - 1).astype(np.float64)[:, None]
        Bre = np.cos(phb) * valid * rowv
        Bim = np.sin(phb) * valid * rowv
        Kre = np.concatenate([Fre, Bre], axis=0)
        Kim = np.concatenate([Fim, Bim], axis=0)
        FFm = np.concatenate([Kre, Kim], axis=1)
        FFb = FFm.reshape(2 * TCn, P, 2 * KF, P).transpose(2, 1, 0, 3).reshape(2 * KF, P, 2 * TCn * P)
        c["c_FF%d" % L] = np.ascontiguousarray(FFb).astype(bf)
        wk = np.where((kk == 0) | (kk == L), 1.0, 2.0) * valid / N
        phi = 2 * math.pi * ((kk[:, None] * tpos[None, :]) % N) / N
        Gre = np.cos(phi) * wk[:, None]
        Gim = -np.sin(phi) * wk[:, None]
        Gm = np.concatenate([Gre, Gim], axis=0)
        Gb = Gm.reshape(2 * KF, P, TCn, P).transpose(2, 1, 0, 3).reshape(TCn, P, 2 * KF * P)
        c["c_G%d" % L] = np.ascontiguousarray(Gb).astype(bf)
    _CONST.update(c)
    return _CONST


def make_in_maps(inp):
    cst = consts()
    vecs = [pack_vecs(inp, 0), pack_vecs(inp, 1)]
    B = inp["x"].shape[0]
    shared = {k: np.ascontiguousarray(inp[k], dtype=np.float32) for k in
              ("w_mod", "w_ffn_in", "w_ffn_out", "hy_w_in", "hy_w_out", "hy_fw1", "hy_fw2", "hy_fwout",
               "hg_w_in", "hg_w_out", "gla_w_in", "gla_w_up", "gla_w_out")}
    shared["fskip"] = np.ascontiguousarray(np.broadcast_to(inp["hy_fskip"].reshape(2, 1, 2 * D), (2, P, 2 * D)), dtype=np.float32)
    shared.update(cst)
    in_maps = []
    for core in range(8):
        b = (core // 2) % B
        m = dict(shared)
        m["vecs"] = vecs[core % 2]
        m["xin"] = np.ascontiguousarray(np.concatenate([inp["ctx"][b].T, inp["x"][b].T], axis=1), dtype=np.float32)
        sc = np.stack([fm(inp["c"][b]), fm(inp["c_ctx"])], axis=2)
        m["scin"] = np.ascontiguousarray(sc.reshape(P, KC * 2), dtype=np.float32)
        in_maps.append(m)
    return in_maps


def kernel(**inp):
    inp = {k: np.asarray(v) for k, v in inp.items()}
    nc = build()
    in_maps = make_in_maps(inp)
    B = inp["x"].shape[0]
    res = run_bass_kernel_spmd(nc, in_maps, core_ids=list(range(8)))
    out = np.stack([np.asarray(res.results[2 * b]["out"]).T for b in range(B)], axis=0)
    return np.ascontiguousarray(out, dtype=np.float32)
```

```python
import math
from contextlib import ExitStack
import numpy as np
import ml_dtypes
import concourse.bass as bass
import concourse.mybir as mybir
from concourse.bass_utils import run_bass_kernel_spmd

F32 = mybir.dt.float32
BF16 = mybir.dt.bfloat16
ALU = mybir.AluOpType
AF = mybir.ActivationFunctionType
P = 128
D = 2048
KC = 16
TL = 4096
TCX = 256
TT = TL + TCX
CH = D // 2
FF = 5632
NU = TT // P
DEPTH = 4
EPS = 1e-6
DMIN = math.log(1e-2) / 1.5
DMAX = math.log(1e-2) / 0.3
WELE = 8704
SUPER = [(0, TCX, 1)] + [(TCX + i * 1024, 1024, 0) for i in range(4)]


class Buf:
    __slots__ = ("w", "r")

    def __init__(self):
        self.w = None
        self.r = {}


class Tile:
    def __init__(self, h, nparts=1):
        self.h = h
        self.parts = [Buf() for _ in range(nparts)]
        self.sem = None
        self.cnt = 0

    def __getitem__(self, k):
        return self.h[k]

    @property
    def b(self):
        return self.parts[0]

    @property
    def all(self):
        return self.parts


class KB:
    def __init__(self, nc, es):
        self.nc = nc
        self.es = es
        self.eng = {"pe": nc.tensor, "act": nc.scalar, "dve": nc.vector, "pool": nc.gpsimd, "sp": nc.sync}
        self.sem = {}
        self.cnt = {}
        for e in ("pe", "act", "dve", "pool"):
            self.sem[e] = es.enter_context(nc.semaphore("s_" + e))
            self.cnt[e] = 0
        self.waited = {e: {} for e in self.eng}
        self.tiles = []
        self.nm = 0
        self.sempool = []
        self.ccsem = es.enter_context(nc.semaphore("s_cc"))
        self.cccnt = 0
        self.bsem = es.enter_context(nc.semaphore("s_bar"))
        self.bcnt = 0

    def name(self, p):
        self.nm += 1
        return "%s%d" % (p, self.nm)

    def tile(self, shape, dt, nparts=1, es=None):
        h = (es or self.es).enter_context(self.nc.sbuf_tensor(self.name("t"), list(shape), dt))
        t = Tile(h, nparts)
        if self.sempool:
            t.sem, t.cnt, t.key = self.sempool.pop()
        else:
            t.sem = self.es.enter_context(self.nc.semaphore(self.name("d")))
            t.key = self.name("k")
        self.tiles.append(t)
        if es is not None and hasattr(es, "mine"):
            es.mine.append(t)
        return t

    def psum(self, shape, dt):
        h = self.es.enter_context(self.nc.psum_tensor(self.name("p"), list(shape), dt))
        return Tile(h, 1)

    def _wait(self, eng, ev):
        if ev is None:
            return
        sem, val, key = ev
        if eng == "pe" and key == "pe":
            return
        if self.waited[eng].get(key, 0) >= val:
            return
        self.waited[eng][key] = val
        self.eng[eng].wait_ge(sem, val)

    def _deps(self, eng, r, w):
        for b in r:
            self._wait(eng, b.w)
        for b in w:
            self._wait(eng, b.w)
            for ev in list(b.r.values()):
                self._wait(eng, ev)

    def _mark(self, ev, r, w):
        for b in r:
            b.r[ev[2]] = ev
        for b in w:
            b.w = ev
            b.r = {}

    def op(self, eng, fn, r=(), w=(), inc=True):
        self._deps(eng, r, w)
        ins = fn(self.eng[eng])
        if inc:
            self.cnt[eng] += 1
            ins.then_inc(self.sem[eng], 1)
            ev = (self.sem[eng], self.cnt[eng], eng)
        else:
            ev = (self.sem[eng], self.cnt[eng] + 1, eng)
        self._mark(ev, r, w)
        return ins

    def dma(self, q, out, in_, tile, r=(), w=()):
        self._deps(q, r, w)
        ins = self.eng[q].dma_start(out=out, in_=in_)
        tile.cnt += 16
        ins.then_inc(tile.sem, 16)
        ev = (tile.sem, tile.cnt, tile.key)
        self._mark(ev, r, w)

    def barrier(self):
        sp = self.eng["sp"]
        for e in ("pe", "act", "dve", "pool"):
            if self.cnt[e] > 0:
                self._wait("sp", (self.sem[e], self.cnt[e], e))
        for t in self.tiles:
            if t.cnt > 0:
                self._wait("sp", (t.sem, t.cnt, t.key))
        self.bcnt += 1
        ins = sp.nop()
        ins.then_inc(self.bsem, 1)
        for e in ("pe", "act", "dve", "pool"):
            self.eng[e].wait_ge(self.bsem, self.bcnt)
        for e in self.eng:
            for e2 in ("pe", "act", "dve", "pool"):
                self.waited[e][e2] = self.cnt[e2]
            for t in self.tiles:
                self.waited[e][t.key] = t.cnt

    def allgather(self, pairs, groups):
        self.barrier()
        for (src, dst) in pairs:
            self.cccnt += 1
            self.nc.gpsimd.collective_compute("AllGather", ALU.bypass, replica_groups=groups, ins=[src], outs=[dst]).then_inc(self.ccsem, 1)
        self.eng["sp"].wait_ge(self.ccsem, self.cccnt)
        self.barrier()

    def scope(self):
        kb = self

        class _S(ExitStack):
            def __exit__(s, *a):
                if a[0] is None:
                    kb.barrier()
                    kb.tiles = [t for t in kb.tiles if t not in s.mine]
                    for t in s.mine:
                        kb.sempool.append((t.sem, t.cnt, t.key))
                return ExitStack.__exit__(s, *a)
        st = _S()
        st.mine = []
        return st


def build(dbg=False, ncores=8):
    nc = bass.Bass("TRN2", target_bir_lowering=False)
    es = ExitStack()
    kb = KB(nc, es)

    def din(name, shape, dt=F32):
        return nc.dram_tensor(name, list(shape), dt, kind="ExternalInput").ap()

    def dint(name, shape, dt=F32):
        return nc.dram_tensor(name, list(shape), dt, kind="Internal").ap()

    GROUPS = [[2 * g, 2 * g + 1] for g in range(ncores // 2)]
    par = nc.sync.partition_id() % 2
    pars = nc.scalar.partition_id() % 2
    parg = nc.gpsimd.partition_id() % 2
    parCH, parsCH, pargCH = par * CH, pars * CH, parg * CH
    par512, pars512, parg512 = par * 512, pars * 512, parg * 512
    XIN = din("xin", [D, TT])
    SCIN = din("scin", [P, KC * 2])
    VECS = din("vecs", [P, NV])
    w_mod = din("w_mod", [DEPTH, D, 6 * D])
    w_ffn_in = din("w_ffn_in", [DEPTH, D, 2 * FF])
    w_ffn_out = din("w_ffn_out", [DEPTH, FF, D])
    hy_w_in = din("hy_w_in", [2, D, 3 * D])
    hy_w_out = din("hy_w_out", [2, D, D])
    hy_fw1 = din("hy_fw1", [2, 33, 64])
    hy_fw2 = din("hy_fw2", [2, 64, 64])
    hy_fwout = din("hy_fwout", [2, 64, 4 * D])
    fskip = din("fskip", [2, P, 2 * D])
    hg_w_in = din("hg_w_in", [1, D, 5 * D])
    hg_w_out = din("hg_w_out", [1, D, D])
    gla_w_in = din("gla_w_in", [1, D, 6176])
    gla_w_up = din("gla_w_up", [1, 2, 16, 1024])
    gla_w_out = din("gla_w_out", [1, D, D])
    C_ident = din("c_ident", [P, P], BF16)
    C_ones = din("c_ones", [P, 2 * P], BF16)
    C_tri = din("c_tri", [P, 2 * P])
    C_smask = din("c_smask", [P, 1024])
    C_negd = din("c_negd", [P, D])
    C_smask64 = din("c_smask64", [P, 1024])
    CL = {}
    for L in (TL, TCX):
        tc_, kf = L // P, (L // P) + 1
        CL[L] = dict(
            zT=din("c_zT%d" % L, [33, L]), tn=din("c_tn%d" % L, [P, tc_]),
            Ff=din("c_Ff%d" % L, [kf, P, tc_ * 256], BF16),
            FF=din("c_FF%d" % L, [2 * kf, P, 2 * tc_ * P], BF16),
            G=din("c_G%d" % L, [tc_, P, 2 * kf * P], BF16),
            HS=dint("hs%d" % L, [2, 2 * kf, P, CH]), TC=tc_, KF=kf)
    OUT = nc.dram_tensor("out", [D, TL], F32, kind="ExternalOutput").ap()
    XR = (nc.dram_tensor("xr", [D, TT], F32, kind="ExternalOutput").ap() if dbg else dint("xr", [D, TT]))
    ZT = None
    VT = (nc.dram_tensor("vt", [3, TT, D], BF16, kind="ExternalOutput").ap() if dbg else dint("vt", [3, TT, D], BF16))
    ZT = (nc.dram_tensor("zt", [D, TT], BF16, kind="ExternalOutput").ap() if dbg else dint("zt", [D, TT], BF16))
    QT = (nc.dram_tensor("qt", [2, D, TT], BF16, kind="ExternalOutput").ap() if dbg else dint("qt", [2, D, TT], BF16))
    KTs = (nc.dram_tensor("kts", [2, D, TT], BF16, kind="ExternalOutput").ap() if dbg else dint("kts", [2, D, TT], BF16))
    KHs = (nc.dram_tensor("khs", [2, D, TT], BF16, kind="ExternalOutput").ap() if dbg else dint("khs", [2, D, TT], BF16))
    EX = (nc.dram_tensor("ex", [2, D, NU], F32, kind="ExternalOutput").ap() if dbg else dint("ex", [2, D, NU]))
    EX64 = dint("ex64", [2, D, TT // 64])
    GATE = (nc.dram_tensor("gate", [D, TT], BF16, kind="ExternalOutput").ap() if dbg else dint("gate", [D, TT], BF16))
    VTm = dint("vtm", [3, TT, CH], BF16)
    ZTs = dint("zts", [CH, TT], BF16)
    FMm = {nm: dint(nm + "m", [2, CH, TT], BF16) for nm in ("q", "k", "h")}
    EXm = dint("exm", [2, CH, 68])
    GATEm = dint("gatem", [CH, TT], BF16)
    ZTg = dint("ztg", [2 * CH, TT], BF16)
    TLH = TL // 2
    TLOC = TCX + TLH
    SUPERL = [(0, TCX, 1), (TCX, 1024, 0), (TCX + 1024, 1024, 0)]
    ZTl = dint("ztl", [2 * CH, TLOC], BF16)
    XRl = dint("xrl", [D, TLOC])
    XRg = dint("xrg", [2 * D, TLH])
    XRs = dint("xrs", [D, TLH])
    parTLH, parsTLH, pargTLH = par * TLH, pars * TLH, parg * TLH
    XRb = [Buf() for _ in SUPER]
    XRlb = [Buf() for _ in SUPERL]
    SPC = {"SUPER": SUPER, "XR": XR, "XRb": XRb, "ZT": ZTg}
    SCR = Buf()

    dh = [kb.tile([P, 8], F32) for _ in range(3)]
    vecs = kb.tile([P, NV], F32)
    ident = kb.tile([P, P], BF16)
    ones = kb.tile([P, 2 * P], BF16)
    tri = kb.tile([P, 2 * P], F32)
    smask = kb.tile([P, 1024], F32)
    smask64 = kb.tile([P, 1024], F32)
    scT = kb.tile([P, KC * 2], BF16)
    MOD = kb.tile([P, 96 * 2], F32)
    MA = kb.tile([P, 2 * KC * 2], F32)
    wbuf = [kb.tile([P, WELE], BF16) for _ in range(2)]
    wsel = [0]
    PS = [kb.psum([P, 512], F32) for _ in range(6)]
    PSB = [kb.psum([P, 1024], BF16) for _ in range(2)]
    pssel = [0]
    psbsel = [0]

    def nps():
        pssel[0] = (pssel[0] + 1) % 6
        return PS[pssel[0]]

    def npsb():
        psbsel[0] = (psbsel[0] + 1) % 2
        return PSB[psbsel[0]]

    def V(name, i=0, n=1):
        o = VOFF[name] + i
        return vecs[:, o:o + n]

    for (t, src) in ((vecs, VECS), (ident, C_ident), (ones, C_ones), (tri, C_tri), (smask, C_smask), (smask64, C_smask64)):
        kb.dma("sp", t[:], src, t, w=t.all)
    sc32 = kb.tile([P, KC * 2], F32)
    kb.dma("sp", sc32[:], SCIN, sc32, w=sc32.all)
    kb.op("act", lambda e: e.activation(out=scT[:], in_=sc32[:], func=AF.Silu), r=sc32.all, w=scT.all)

    def load_w(W, KCn, ranges, cast=True, pre=None):
        wsel[0] ^= 1
        wt = wbuf[wsel[0]]
        ntot = sum(n for _, n in ranges)
        assert KCn * ntot <= WELE
        view = wt[:, 0:KCn * ntot].rearrange("p (k n) -> p k n", n=ntot)
        off = 0
        for (c0, n) in ranges:
            if pre is not None:
                src = pre
            else:
                src = W[:, c0:c0 + n].rearrange("(k p) n -> p k n", p=P)
            kb.dma("pool" if cast else "sp", view[:, :, off:off + n], src, wt, w=wt.all)
            off += n
        return wt, view

    def linear(xT, xbufs, KCn, W, blocks, ttiles, epi, mparts=None):
        for blk in blocks:
            wt, view = load_w(W, KCn, blk["ranges"], pre=blk.get("pre"), cast=blk.get("cast", True))
            for (off, m, tag) in blk["chunks"]:
                pst = []
                for (t0, tn) in ttiles:
                    ps = nps()
                    for kc in range(KCn):
                        kb.op("pe", lambda e, ps=ps, kc=kc, off=off, m=m, t0=t0, tn=tn: e.matmul(
                            ps[0:m, 0:tn], lhsT=view[:, kc, off:off + m], rhs=xT[:, kc, t0:t0 + tn],
                            start=(kc == 0), stop=(kc == KCn - 1)),
                            r=list(wt.all) + list(xbufs), w=ps.all, inc=(kc == KCn - 1))
                    pst.append(ps)
                epi(tag, pst)

    def std_blocks(c0, ncols, bw=512, tag0=0):
        blocks = []
        t = tag0
        for b0 in range(c0, c0 + ncols, bw):
            n = min(bw, c0 + ncols - b0)
            ch = []
            for o in range(0, n, P):
                ch.append((o, min(P, n - o), t))
                t += 1
            blocks.append(dict(ranges=[(b0, n)], chunks=ch))
        return blocks

    def pass_mod(i):
        def epi(tag, pst):
            ps = pst[0]
            kb.op("act", lambda e: e.activation(out=MOD[:, tag * 2:tag * 2 + 2], in_=ps[:, 0:2], func=AF.Identity,
                                                bias=V("bmod", i * 96 + tag), scale=1.0), r=ps.all, w=MOD.all)
        sv = scT[:].rearrange("p (k c) -> p k c", c=2)
        linear(sv, scT.all, KC, w_mod[i], std_blocks(0, 6 * D), [(0, 2)], epi)
        mv = MOD[:].rearrange("p (s k c) -> p s k c", s=6, c=2)
        mav = MA[:].rearrange("p (n k c) -> p n k c", n=2, c=2)
        for n, (sidx, gname) in enumerate(((1, "n1g"), (4, "n2g"))):
            for col in range(2):
                kb.op("dve", lambda e, n=n, sidx=sidx, col=col, gname=gname: e.scalar_tensor_tensor(
                    out=mav[:, n, :, col], in0=mv[:, sidx, :, col], scalar=1.0, in1=V(gname, i * KC, KC),
                    op0=ALU.add, op1=ALU.mult), r=MOD.all, w=MA.all)
        return mv, mav

    def norm_mod(es2, si, mv, mav, n_idx, shift_idx, hT):
        tok0, n, col = SPC["SUPER"][si]
        xs = kb.tile([P, KC * 256], F32, es=es2)
        sq = kb.tile([P, KC * 256], BF16, es=es2)
        rstd = kb.tile([P, 256], F32, es=es2)
        tmp = kb.tile([P, 256], F32, es=es2)
        xv = xs[:].rearrange("p (k t) -> p k t", t=256)
        sv = sq[:].rearrange("p (k t) -> p k t", t=256)
        hv = hT[:, 0:KC * n].rearrange("p (k t) -> p k t", k=KC)
        for s0 in range(0, n, 256):
            kb.dma("sp", xv, SPC["XR"][:, tok0 + s0:tok0 + s0 + 256].rearrange("(k p) t -> p k t", p=P), xs,
                   r=[SPC["XRb"][si]], w=xs.all)
            kb.op("act", lambda e: e.activation(out=sq[:], in_=xs[:], func=AF.Square), r=xs.all, w=sq.all)
            ps = nps()
            for kc in range(KC):
                kb.op("pe", lambda e, kc=kc: e.matmul(ps[:, 0:256], lhsT=ones[:, 0:P], rhs=sv[:, kc, :],
                                                      start=(kc == 0), stop=(kc == KC - 1)),
                      r=list(sq.all) + list(ones.all), w=ps.all, inc=(kc == KC - 1))
            kb.op("act", lambda e: e.activation(out=rstd[:], in_=ps[:, 0:256], func=AF.Sqrt, bias=V("eps"), scale=1.0 / D),
                  r=ps.all, w=rstd.all)
            kb.op("dve", lambda e: e.reciprocal(out=rstd[:], in_=rstd[:]), r=rstd.all, w=rstd.all)
            for kc in range(KC):
                kb.op("dve", lambda e, kc=kc: e.tensor_tensor(out=tmp[:], in0=xv[:, kc, :], in1=rstd[:], op=ALU.mult),
                      r=list(xs.all) + list(rstd.all), w=tmp.all)
                kb.op("act", lambda e, kc=kc: e.activation(out=hv[:, kc, s0:s0 + 256], in_=tmp[:], func=AF.Identity,
                                                           bias=mv[:, shift_idx, kc, col:col + 1],
                                                           scale=mav[:, n_idx, kc, col:col + 1]),
                      r=list(tmp.all) + list(MOD.all) + list(MA.all), w=hT.all)

    def ttiles_of(n):
        return [(t, min(512, n - t)) for t in range(0, n, 512)]

    def resid_epi(es2, si, mv, gate_idx):
        tok0, n, col = SPC["SUPER"][si]
        xo = [kb.tile([P, 1024], F32, es=es2) for _ in range(2)]
        sel = [0]

        def epi(tag, pst):
            sel[0] ^= 1
            x = xo[sel[0]]
            kb.dma("sp", x[:, 0:n], SPC["XR"][tag * P:(tag + 1) * P, tok0:tok0 + n], x, r=[SPC["XRb"][si]], w=x.all)
            for ti, (t0, tn) in enumerate(ttiles_of(n)):
                ps = pst[ti]
                kb.op("dve", lambda e, ps=ps, t0=t0, tn=tn: e.scalar_tensor_tensor(
                    out=x[:, t0:t0 + tn], in0=ps[:, 0:tn], scalar=mv[:, gate_idx, tag, col:col + 1], in1=x[:, t0:t0 + tn],
                    op0=ALU.mult, op1=ALU.add), r=list(ps.all) + list(x.all) + list(MOD.all), w=x.all)
            kb.dma("sp", SPC["XR"][tag * P:(tag + 1) * P, tok0:tok0 + n], x[:, 0:n], x, r=x.all, w=[SPC["XRb"][si]])
        return epi

    def pass_out_ffn(i, si, mv, mav, Wout, hy=False):
        tok0, n, col = SPC["SUPER"][si]
        with kb.scope() as es2:
            zt = kb.tile([P, KC * 1024], BF16, es=es2)
            zv = zt[:, 0:KC * n].rearrange("p (k t) -> p k t", k=KC)
            if hy:
                for s_ in range(2):
                    kb.dma("sp", zv[:, s_ * 8:(s_ + 1) * 8, :], SPC["ZT"][:, tok0:tok0 + n].rearrange("(c s p) t -> s p c t", s=2, p=P)[s_], zt, r=[SCR], w=zt.all)
            else:
                kb.dma("sp", zv, ZT[:, tok0:tok0 + n].rearrange("(k p) t -> p k t", p=P), zt, r=[SCR], w=zt.all)
            linear(zv, zt.all, KC, Wout, std_blocks(0, D), ttiles_of(n), resid_epi(es2, si, mv, 2))
        with kb.scope() as es2:
            hT = kb.tile([P, KC * 1024], BF16, es=es2)
            hid = kb.tile([P, 44 * 1024], BF16, es=es2)
            sg = [kb.tile([P, 512], F32, es=es2) for _ in range(2)]
            sgs = [0]
            with kb.scope() as es3:
                norm_mod(es3, si, mv, mav, 1, 3, hT)
            hv = hT[:, 0:KC * n].rearrange("p (k t) -> p k t", k=KC)
            hidv = hid[:, 0:44 * n].rearrange("p (k t) -> p k t", k=44)
            pend = {}

            def epi_in(tag, pst):
                j, isup = tag
                if not isup:
                    pend[j] = pst
                    return
                gp = pend.pop(j)
                for ti, (t0, tn) in enumerate(ttiles_of(n)):
                    sgs[0] ^= 1
                    s = sg[sgs[0]]
                    kb.op("act", lambda e, s=s, g=gp[ti], tn=tn: e.activation(out=s[:, 0:tn], in_=g[:, 0:tn], func=AF.Silu),
                          r=gp[ti].all, w=s.all)
                    kb.op("dve", lambda e, s=s, u=pst[ti], t0=t0, tn=tn: e.tensor_tensor(
                        out=hidv[:, j, t0:t0 + tn], in0=s[:, 0:tn], in1=u[:, 0:tn], op=ALU.mult),
                        r=list(s.all) + list(pst[ti].all), w=hid.all)
            blocks = []
            for j in range(44):
                blocks.append(dict(ranges=[(j * P, P), (FF + j * P, P)], chunks=[(0, P, (j, 0)), (P, P, (j, 1))]))
            linear(hv, hT.all, KC, w_ffn_in[i], blocks, ttiles_of(n), epi_in)
            linear(hidv, hid.all, 44, w_ffn_out[i], std_blocks(0, D, bw=P), ttiles_of(n), resid_epi(es2, si, mv, 5))

    def fm_to_tm(es2, src_tile, n, dst, tok0, c0, tsb):
        nt = n // P
        psb = npsb()
        pv = psb[:, 0:nt * P].rearrange("p (a b) -> p a b", b=P)
        for tt in range(nt):
            kb.op("pe", lambda e, tt=tt: e.transpose(out=pv[:, tt, :], in_=src_tile[:, tt * P:(tt + 1) * P], identity=ident[:]),
                  r=list(src_tile.all) + list(ident.all), w=psb.all, inc=(tt == nt - 1))
        tv = tsb[:, 0:nt * P].rearrange("p (a b) -> p a b", b=P)
        kb.op("act", lambda e: e.copy(out=tsb[:, 0:nt * P], in_=psb[:, 0:nt * P]), r=psb.all, w=tsb.all)
        kb.dma("sp", dst[tok0:tok0 + n, c0:c0 + P].rearrange("(a p) c -> p a c", p=P), tv, tsb, r=tsb.all, w=[SCR])

    def hyena_pass_a(i, j, si, mv, mav):
        tok0, n, col = SUPER[si]
        rowlen = 256 if si == 0 else 64
        with kb.scope() as es2:
            hT = kb.tile([P, KC * 1024], BF16, es=es2)
            with kb.scope() as es3:
                norm_mod(es3, si, mv, mav, 0, 0, hT)
            hv = hT[:, 0:KC * n].rearrange("p (k t) -> p k t", k=KC)
            pc = kb.tile([P, 1024], F32, es=es2)
            o1 = kb.tile([P, 1024], F32, es=es2)
            ob = kb.tile([P, 1024], BF16, es=es2)
            tsb = kb.tile([P, 1024], BF16, es=es2)

            def epi(tag, pst):
                for ti, (t0, tn) in enumerate(ttiles_of(n)):
                    ps = pst[ti]
                    kb.op("act", lambda e, ps=ps, t0=t0, tn=tn: e.copy(out=pc[:, t0:t0 + tn], in_=ps[:, 0:tn]), r=ps.all, w=pc.all)
                    kb.op("act", lambda e, ps=ps, t0=t0, tn=tn: e.activation(
                        out=o1[:, t0:t0 + tn], in_=ps[:, 0:tn], func=AF.Identity, bias=0.0,
                        scale=V("cw", (j * 3 + 1) * 48 + tag)), r=ps.all, w=o1.all)
                o1v = o1[:, 0:n].rearrange("p (a b) -> p a b", b=rowlen)
                pcv = pc[:, 0:n].rearrange("p (a b) -> p a b", b=rowlen)
                kb.op("dve", lambda e: e.scalar_tensor_tensor(
                    out=o1v[:, :, 1:rowlen], in0=pcv[:, :, 0:rowlen - 1], scalar=V("cw", (j * 3 + 0) * 48 + tag),
                    in1=o1v[:, :, 1:rowlen], op0=ALU.mult, op1=ALU.add), r=list(pc.all) + list(o1.all), w=o1.all)
                kb.op("dve", lambda e: e.scalar_tensor_tensor(
                    out=o1v[:, :, 0:rowlen - 1], in0=pcv[:, :, 1:rowlen], scalar=V("cw", (j * 3 + 2) * 48 + tag),
                    in1=o1v[:, :, 0:rowlen - 1], op0=ALU.mult, op1=ALU.add), r=list(pc.all) + list(o1.all), w=o1.all)
                kb.op("dve", lambda e: e.tensor_copy(out=ob[:, 0:n], in_=o1[:, 0:n]), r=o1.all, w=ob.all)
                fm_to_tm(es2, ob, n, VT[tag // KC], tok0, (tag % KC) * P, tsb)
            linear(hv, hT.all, KC, hy_w_in[j], std_blocks(0, 3 * D), ttiles_of(n), epi)

    def range_reduce_sin(es2, arg, m, n, out_ap, tmp):
        MAGIC = 12582912.0
        kb.op("dve", lambda e: e.tensor_scalar(out=tmp[0:m, 0:n], in0=arg[0:m, 0:n], scalar1=1.0 / (2 * math.pi), scalar2=MAGIC,
                                               op0=ALU.mult, op1=ALU.add), r=arg.all, w=tmp.all)
        kb.op("dve", lambda e: e.tensor_scalar(out=tmp[0:m, 0:n], in0=tmp[0:m, 0:n], scalar1=-MAGIC, scalar2=None,
                                               op0=ALU.add), r=tmp.all, w=tmp.all)
        kb.op("dve", lambda e: e.scalar_tensor_tensor(out=arg[0:m, 0:n], in0=tmp[0:m, 0:n], scalar=-2 * math.pi, in1=arg[0:m, 0:n],
                                                      op0=ALU.mult, op1=ALU.add), r=list(tmp.all) + list(arg.all), w=arg.all)
        kb.op("dve", lambda e: e.tensor_scalar(out=arg[0:m, 0:n], in0=arg[0:m, 0:n], scalar1=-3.14159, scalar2=3.14159,
                                               op0=ALU.max, op1=ALU.min), r=arg.all, w=arg.all)
        kb.op("act", lambda e: e.activation(out=out_ap, in_=arg[0:m, 0:n], func=AF.Sin), r=arg.all, w=[])

    def hyena_filters(j, L):
        c = CL[L]
        TCn, KF = c["TC"], c["KF"]
        with kb.scope() as es2:
            zT = kb.tile([P, L], F32, es=es2)
            h1 = kb.tile([P, L], F32, es=es2)
            h2 = kb.tile([P, L], BF16, es=es2)
            fw1 = kb.tile([P, 64], F32, es=es2)
            fw2 = kb.tile([P, 64], F32, es=es2)
            fwo = kb.tile([P, 4 * CH], BF16, es=es2)
            tn = kb.tile([P, TCn], F32, es=es2)
            negd = kb.tile([P, CH], F32, es=es2)
            skp = kb.tile([P, 2 * CH], F32, es=es2)
            fb = kb.tile([P, 2], F32, es=es2)
            arg = kb.tile([P, 512], F32, es=es2)
            tmp = kb.tile([P, 512], F32, es=es2)
            kb.dma("sp", zT[0:33, :], c["zT"], zT, w=zT.all)
            kb.dma("sp", fw1[0:33, :], hy_fw1[j], fw1, w=fw1.all)
            kb.dma("sp", fw2[0:64, :], hy_fw2[j], fw2, w=fw2.all)
            kb.dma("pool", fwo[0:64, :].rearrange("p (b c) -> p b c", b=4),
                   hy_fwout[j].rearrange("p (b c) -> p b c", b=4)[:, :, bass.ds(pargCH, CH)], fwo, w=fwo.all)
            kb.dma("sp", tn[:], c["tn"], tn, w=tn.all)
            kb.dma("sp", skp[:].rearrange("p (b c) -> p b c", b=2),
                   fskip[j].rearrange("p (b c) -> p b c", b=2)[:, :, bass.ds(parCH, CH)], skp, w=skp.all)
            kb.dma("sp", negd[:], C_negd[:, bass.ds(parCH, CH)], negd, w=negd.all)
            for q in range(2):
                kb.op("dve", lambda e, q=q: e.tensor_tensor(out=fb[0:64, q:q + 1], in0=V("ffreq", j * 2 + q)[0:64, :],
                                                             in1=V("fbias", j * 2 + q)[0:64, :], op=ALU.mult), r=vecs.all, w=fb.all)
            for t0 in range(0, L, 512):
                tn_ = min(512, L - t0)
                ps = nps()
                kb.op("pe", lambda e: e.matmul(ps[0:64, 0:tn_], lhsT=fw1[0:33, 0:64], rhs=zT[0:33, t0:t0 + tn_], start=True, stop=True),
                      r=list(fw1.all) + list(zT.all), w=ps.all)
                kb.op("act", lambda e: e.activation(out=arg[0:64, 0:tn_], in_=ps[0:64, 0:tn_], func=AF.Identity,
                                                    bias=fb[0:64, 0:1], scale=V("ffreq", j * 2 + 0)[0:64, :]),
                      r=list(ps.all) + list(fb.all), w=arg.all)
                range_reduce_sin(es2, arg, 64, tn_, h1[0:64, t0:t0 + tn_], tmp)
                kb._mark((kb.sem["act"], kb.cnt["act"], "act"), [], h1.all)
            for t0 in range(0, L, 512):
                tn_ = min(512, L - t0)
                ps = nps()
                kb.op("pe", lambda e: e.matmul(ps[0:64, 0:tn_], lhsT=fw2[0:64, 0:64], rhs=h1[0:64, t0:t0 + tn_], start=True, stop=True),
                      r=list(fw2.all) + list(h1.all), w=ps.all)
                kb.op("act", lambda e: e.activation(out=arg[0:64, 0:tn_], in_=ps[0:64, 0:tn_], func=AF.Identity,
                                                    bias=fb[0:64, 1:2], scale=V("ffreq", j * 2 + 1)[0:64, :]),
                      r=list(ps.all) + list(fb.all), w=arg.all)
                range_reduce_sin(es2, arg, 64, tn_, h2[0:64, t0:t0 + tn_], tmp)
                kb._mark((kb.sem["act"], kb.cnt["act"], "act"), [], h2.all)
            HF = kb.tile([P, 2 * TCn * 512], BF16, es=es2)
            hfv = HF[:].rearrange("p (k c) -> p k c", c=512)
            dec = [kb.tile([P, 512], F32, es=es2) for _ in range(2)]
            ab = [kb.tile([P, 512], BF16, es=es2) for _ in range(2)]
            rn = kb.tile([P, 512], F32, es=es2)
            ho = [kb.tile([P, 512], F32, es=es2) for _ in range(2)]
            sel = [0]
            for o in range(2):
                for ct in range(2):
                    nps_ = nps()
                    first = True
                    for tc in range(TCn):
                        d = dec[tc % 2]
                        kb.op("act", lambda e, d=d, tc=tc: e.activation(out=d[:], in_=negd[:, ct * 512:(ct + 1) * 512], func=AF.Exp,
                                                                        scale=tn[:, tc:tc + 1]), r=list(negd.all) + list(tn.all), w=d.all)
                        for dr in range(2):
                            ps = nps()
                            if ps is nps_:
                                ps = nps()
                            col0 = (dr * 2 + o) * CH + ct * 512
                            kb.op("pe", lambda e, ps=ps, tc=tc, col0=col0: e.matmul(
                                ps[:, :], lhsT=h2[0:64, tc * P:(tc + 1) * P], rhs=fwo[0:64, col0:col0 + 512], start=True, stop=True),
                                r=list(h2.all) + list(fwo.all), w=ps.all)
                            kk = dr * TCn + tc
                            kb.op("dve", lambda e, ps=ps, d=d, kk=kk: e.tensor_tensor(out=hfv[:, kk, :], in0=ps[:, :], in1=d[:], op=ALU.mult),
                                  r=list(ps.all) + list(d.all), w=HF.all)
                            sel[0] ^= 1
                            a = ab[sel[0]]
                            kb.op("act", lambda e, a=a, kk=kk: e.activation(out=a[:], in_=hfv[:, kk, :], func=AF.Abs),
                                  r=HF.all, w=a.all)
                            last = (dr == 1 and tc == TCn - 1)
                            oc0 = P if last else 0
                            kb.op("pe", lambda e, a=a, oc0=oc0, first=first, last=last: e.matmul(
                                nps_[:, :], lhsT=ones[:, oc0:oc0 + P], rhs=a[:], start=first, stop=last),
                                r=list(a.all) + list(ones.all), w=nps_.all, inc=last)
                            first = False
                    kb.op("dve", lambda e: e.reciprocal(out=rn[:], in_=nps_[:, :]), r=nps_.all, w=rn.all)
                    for blk in range(2 * KF):
                        wt, view = load_w(None, 2 * TCn, [(0, P)], cast=False,
                                          pre=c["FF"][blk].rearrange("p (k n) -> p k n", n=P))
                        ps = nps()
                        for kk in range(2 * TCn):
                            kb.op("pe", lambda e, kk=kk: e.matmul(ps[:, :], lhsT=view[:, kk, :], rhs=hfv[:, kk, :],
                                                                  start=(kk == 0), stop=(kk == 2 * TCn - 1)),
                                  r=list(wt.all) + list(HF.all), w=ps.all, inc=(kk == 2 * TCn - 1))
                        sel[0] ^= 1
                        h = ho[sel[0]]
                        kb.op("dve", lambda e, h=h: e.tensor_tensor(out=h[:], in0=ps[:, :], in1=rn[:], op=ALU.mult),
                              r=list(ps.all) + list(rn.all), w=h.all)
                        if blk < KF:
                            kb.op("pool", lambda e, h=h: e.tensor_tensor(out=h[:], in0=h[:], in1=skp[:, o * CH + ct * 512:o * CH + (ct + 1) * 512],
                                                                         op=ALU.add), r=list(h.all) + list(skp.all), w=h.all)
                        kb.dma("sp", c["HS"][o, blk, :, ct * 512:(ct + 1) * 512], h[:], h, r=h.all, w=[SCR])

    def hyena_conv(L, tok0):
        c = CL[L]
        TCn, KF = c["TC"], c["KF"]
        with kb.scope() as es2:
            Vt = kb.tile([P, TCn * 512], BF16, es=es2)
            Z1 = kb.tile([P, TCn * 512], BF16, es=es2)
            Y = kb.tile([P, 2 * KF * 512], BF16, es=es2)
            Hr = [kb.tile([P, 512], F32, es=es2) for _ in range(2)]
            Hi = [kb.tile([P, 512], F32, es=es2) for _ in range(2)]
            t1 = kb.tile([P, 512], F32, es=es2)
            t2 = kb.tile([P, 512], F32, es=es2)
            xm = [kb.tile([P, 512], BF16, es=es2) for _ in range(2)]
            zb = kb.tile([P, 512], BF16, es=es2)
            tsb = kb.tile([P, 512], BF16, es=es2)
            vv = Vt[:].rearrange("p (k c) -> p k c", c=512)
            z1v = Z1[:].rearrange("p (k c) -> p k c", c=512)
            yv = Y[:].rearrange("p (k c) -> p k c", c=512)
            sel = [0]
            for ct in range(2):
                c0 = ct * 512
                kb.dma("sp", vv, VTm[0, tok0:tok0 + L, c0:c0 + 512].rearrange("(k p) c -> p k c", p=P), Vt, r=[SCR], w=Vt.all)
                for o in range(2):
                    src, srcv = (Vt, vv) if o == 0 else (Z1, z1v)
                    for m in range(KF):
                        wt, view = load_w(None, TCn, [(0, 256)], cast=False, pre=c["Ff"][m].rearrange("p (k n) -> p k n", n=256))
                        pr, pi = nps(), nps()
                        for part, ps in ((0, pr), (1, pi)):
                            for kk in range(TCn):
                                kb.op("pe", lambda e, ps=ps, kk=kk, part=part: e.matmul(
                                    ps[:, :], lhsT=view[:, kk, part * P:(part + 1) * P], rhs=srcv[:, kk, :],
                                    start=(kk == 0), stop=(kk == TCn - 1)), r=list(wt.all) + list(src.all), w=ps.all, inc=(kk == TCn - 1))
                        sel[0] ^= 1
                        hr, hi = Hr[sel[0]], Hi[sel[0]]
                        kb.dma("sp", hr[:], c["HS"][o, m, :, c0:c0 + 512], hr, r=[SCR], w=hr.all)
                        kb.dma("sp", hi[:], c["HS"][o, KF + m, :, c0:c0 + 512], hi, r=[SCR], w=hi.all)
                        kb.op("dve", lambda e: e.tensor_tensor(out=t1[:], in0=pr[:, :], in1=hr[:], op=ALU.mult), r=list(pr.all) + list(hr.all), w=t1.all)
                        kb.op("dve", lambda e: e.tensor_tensor(out=t2[:], in0=pi[:, :], in1=hi[:], op=ALU.mult), r=list(pi.all) + list(hi.all), w=t2.all)
                        kb.op("pool", lambda e, m=m: e.tensor_tensor(out=yv[:, m, :], in0=t1[:], in1=t2[:], op=ALU.subtract),
                              r=list(t1.all) + list(t2.all), w=Y.all)
                        kb.op("dve", lambda e: e.tensor_tensor(out=t1[:], in0=pr[:, :], in1=hi[:], op=ALU.mult), r=list(pr.all) + list(hi.all), w=t1.all)
                        kb.op("dve", lambda e: e.tensor_tensor(out=t2[:], in0=pi[:, :], in1=hr[:], op=ALU.mult), r=list(pi.all) + list(hr.all), w=t2.all)
                        kb.op("pool", lambda e, m=m: e.tensor_tensor(out=yv[:, KF + m, :], in0=t1[:], in1=t2[:], op=ALU.add),
                              r=list(t1.all) + list(t2.all), w=Y.all)
                    for tc in range(TCn):
                        wt, view = load_w(None, 2 * KF, [(0, P)], cast=False, pre=c["G"][tc].rearrange("p (k n) -> p k n", n=P))
                        ps = nps()
                        for kk in range(2 * KF):
                            kb.op("pe", lambda e, kk=kk: e.matmul(ps[:, :], lhsT=view[:, kk, :], rhs=yv[:, kk, :],
                                                                  start=(kk == 0), stop=(kk == 2 * KF - 1)),
                                  r=list(wt.all) + list(Y.all), w=ps.all, inc=(kk == 2 * KF - 1))
                        sel[0] ^= 1
                        x = xm[sel[0]]
                        kb.dma("sp", x[:], VTm[1 + o, tok0 + tc * P:tok0 + (tc + 1) * P, c0:c0 + 512], x, r=[SCR], w=x.all)
                        if o == 0:
                            kb.op("dve", lambda e, tc=tc: e.tensor_tensor(out=z1v[:, tc, :], in0=ps[:, :], in1=x[:], op=ALU.mult),
                                  r=list(ps.all) + list(x.all), w=Z1.all)
                        else:
                            kb.op("dve", lambda e: e.tensor_tensor(out=zb[:], in0=ps[:, :], in1=x[:], op=ALU.mult),
                                  r=list(ps.all) + list(x.all), w=zb.all)
                            psb = npsb()
                            pv = psb[:, 0:512].rearrange("p (a b) -> p a b", b=P)
                            for cc in range(4):
                                kb.op("pe", lambda e, cc=cc: e.transpose(out=pv[:, cc, :], in_=zb[:, cc * P:(cc + 1) * P], identity=ident[:]),
                                      r=list(zb.all) + list(ident.all), w=psb.all, inc=(cc == 3))
                            kb.op("act", lambda e: e.copy(out=tsb[:], in_=psb[:, 0:512]), r=psb.all, w=tsb.all)
                            kb.dma("sp", ZTs[c0:c0 + 512, tok0 + tc * P:tok0 + (tc + 1) * P].rearrange("(a p) t -> p a t", p=P),
                                   tsb[:].rearrange("p (a b) -> p a b", b=P), tsb, r=tsb.all, w=[SCR])

    def gate_chain(es2, T, q_sb, k_of_dir, lf, dr, row0, tok0, n, U=P):
        nu = n // U
        EXd = EX if U == P else EX64
        smk = smask if U == P else smask64
        cs, d, e1, ob, ctot, nct, ex = T["cs"], T["d"], T["e1"], T["ob"], T["ctot"], T["nct"], T["ex"]
        kb.op("dve", lambda e: e.tensor_tensor_scan(out=cs[:, 0:n], data0=smk[:, 0:n], data1=lf[:, 0:n], initial=0.0,
                                                    op0=ALU.mult, op1=ALU.add), r=list(lf.all) + list(smk.all), w=cs.all)
        csv = cs[:, 0:n].rearrange("p (u t) -> p u t", t=U)
        kb.op("dve", lambda e: e.tensor_copy(out=ctot[:, 0:nu], in_=csv[:, :, U - 1]), r=cs.all, w=ctot.all)
        kb.op("dve", lambda e: e.tensor_scalar(out=nct[:, 0:nu], in0=ctot[:, 0:nu], scalar1=-1.0, scalar2=None, op0=ALU.mult),
              r=ctot.all, w=nct.all)
        kb.op("act", lambda e: e.activation(out=ex[:, 0:nu], in_=ctot[:, 0:nu], func=AF.Exp), r=ctot.all, w=ex.all)
        u0 = tok0 // U
        kb.dma("sp", EXd[dr, row0:row0 + P, u0:u0 + nu], ex[:, 0:nu], ex, r=ex.all, w=[SCR])

        def emit(dst, src_mul, make_exp):
            make_exp()
            kb.op("dve", lambda e: e.tensor_tensor(out=ob[:, 0:n], in0=src_mul[:, 0:n], in1=e1[:, 0:n], op=ALU.mult),
                  r=list(src_mul.all) + list(e1.all), w=ob.all)
            kb.dma("sp", dst[dr, row0:row0 + P, tok0:tok0 + n], ob[:, 0:n], ob, r=ob.all, w=[SCR])

        def full_exp(src, scale):
            return lambda: kb.op("act", lambda e: e.activation(out=e1[:, 0:n], in_=src[:, 0:n], func=AF.Exp, scale=scale),
                                 r=src.all, w=e1.all)

        def unit_exp(src, scale, bias_t):
            def f():
                for u in range(nu):
                    kb.op("act", lambda e, u=u: e.activation(out=e1[:, u * U:(u + 1) * U], in_=src[:, u * U:(u + 1) * U], func=AF.Exp,
                                                             scale=scale, bias=bias_t[:, u:u + 1]),
                          r=list(src.all) + list(bias_t.all), w=e1.all)
            return f
        if dr == 0:
            emit(QT, q_sb, full_exp(cs, 1.0))
            emit(KTs, k_of_dir, full_exp(cs, -1.0))
            emit(KHs, k_of_dir, unit_exp(cs, -1.0, ctot))
        else:
            kb.op("dve", lambda e: e.tensor_tensor(out=d[:, 0:n], in0=lf[:, 0:n], in1=cs[:, 0:n], op=ALU.subtract),
                  r=list(lf.all) + list(cs.all), w=d.all)
            emit(QT, q_sb, unit_exp(d, 1.0, ctot))
            emit(KTs, k_of_dir, unit_exp(d, -1.0, nct))
            emit(KHs, k_of_dir, full_exp(d, -1.0))

    def chain_tiles(es2):
        T = {}
        for nm in ("cs", "d", "e1"):
            T[nm] = kb.tile([P, 1024], F32, es=es2)
        T["ob"] = kb.tile([P, 1024], BF16, es=es2)
        for nm in ("ctot", "nct", "ex"):
            T[nm] = kb.tile([P, 16], F32, es=es2)
        return T

    def hgrn_pass_a(i, si, mv, mav, lbt):
        tok0, n, col = SUPER[si]
        with kb.scope() as es2:
            hT = kb.tile([P, KC * 1024], BF16, es=es2)
            with kb.scope() as es3:
                norm_mod(es3, si, mv, mav, 0, 0, hT)
            hv = hT[:, 0:KC * n].rearrange("p (k t) -> p k t", k=KC)
            T = chain_tiles(es2)
            q_sb = kb.tile([P, 1024], F32, es=es2)
            fv = kb.tile([P, 1024], F32, es=es2)
            lf = kb.tile([P, 1024], F32, es=es2)
            k_sb = kb.tile([P, 1024], F32, es=es2)
            ob2 = kb.tile([P, 1024], BF16, es=es2)
            tsb = kb.tile([P, 1024], BF16, es=es2)

            def epi(tag, pst):
                kind, h = tag
                for ti, (t0, tn) in enumerate(ttiles_of(n)):
                    ps = pst[ti]
                    if kind == "q":
                        kb.op("act", lambda e, ps=ps, t0=t0, tn=tn: e.activation(out=q_sb[:, t0:t0 + tn], in_=ps[:, 0:tn], func=AF.Silu),
                              r=ps.all, w=q_sb.all)
                    elif kind == "i":
                        kb.op("act", lambda e, ps=ps, t0=t0, tn=tn: e.copy(out=ob2[:, t0:t0 + tn], in_=ps[:, 0:tn]), r=ps.all, w=ob2.all)
                    elif kind == "g":
                        kb.op("act", lambda e, ps=ps, t0=t0, tn=tn: e.activation(out=ob2[:, t0:t0 + tn], in_=ps[:, 0:tn], func=AF.Silu),
                              r=ps.all, w=ob2.all)
                    else:
                        kb.op("act", lambda e, ps=ps, t0=t0, tn=tn: e.activation(out=fv[:, t0:t0 + tn], in_=ps[:, 0:tn], func=AF.Sigmoid),
                              r=ps.all, w=fv.all)
                if kind == "i":
                    fm_to_tm(es2, ob2, n, VT[0], tok0, h * P, tsb)
                elif kind == "g":
                    kb.dma("sp", GATE[h * P:(h + 1) * P, tok0:tok0 + n], ob2[:, 0:n], ob2, r=ob2.all, w=[SCR])
                elif kind in ("f0", "f1"):
                    dr = 0 if kind == "f0" else 1
                    kb.op("dve", lambda e: e.tensor_scalar(out=fv[:, 0:n], in0=fv[:, 0:n], scalar1=lbt[:, (2 + dr) * KC + h:(2 + dr) * KC + h + 1],
                                                           scalar2=lbt[:, dr * KC + h:dr * KC + h + 1], op0=ALU.mult, op1=ALU.add),
                          r=list(fv.all) + list(lbt.all), w=fv.all)
                    kb.op("act", lambda e: e.activation(out=lf[:, 0:n], in_=fv[:, 0:n], func=AF.Ln), r=fv.all, w=lf.all)
                    kb.op("dve", lambda e: e.tensor_scalar(out=k_sb[:, 0:n], in0=fv[:, 0:n], scalar1=-1.0, scalar2=1.0, op0=ALU.mult, op1=ALU.add),
                          r=fv.all, w=k_sb.all)
                    gate_chain(es2, T, q_sb, k_sb, lf, dr, h * P, tok0, n, U=64)
            blocks = []
            for h in range(KC):
                blocks.append(dict(ranges=[(h * P, P), (3 * D + h * P, P), (4 * D + h * P, P)],
                                   chunks=[(0, P, ("q", h)), (P, P, ("f0", h)), (2 * P, P, ("f1", h))]))
            for h in range(0, KC, 4):
                blocks.append(dict(ranges=[(D + h * P, 512)], chunks=[(o * P, P, ("i", h + o)) for o in range(4)]))
            for h in range(0, KC, 4):
                blocks.append(dict(ranges=[(2 * D + h * P, 512)], chunks=[(o * P, P, ("g", h + o)) for o in range(4)]))
            linear(hv, hT.all, KC, hg_w_in[0], blocks, ttiles_of(n), epi)

    def gla_pass_a(i, si, mv, mav, wup):
        tok0, n, col = SUPER[si]
        with kb.scope() as es2:
            hT = kb.tile([P, KC * 1024], BF16, es=es2)
            with kb.scope() as es3:
                norm_mod(es3, si, mv, mav, 0, 0, hT)
            hv = hT[:, 0:KC * n].rearrange("p (k t) -> p k t", k=KC)
            T = chain_tiles(es2)
            q_sb = kb.tile([P, 1024], F32, es=es2)
            k_sb = kb.tile([P, 1024], F32, es=es2)
            lf = kb.tile([P, 1024], F32, es=es2)
            aT = [kb.tile([P, 1024], BF16, es=es2) for _ in range(2)]
            ob2 = kb.tile([P, 1024], BF16, es=es2)
            tsb = kb.tile([P, 1024], BF16, es=es2)

            def epi(tag, pst):
                kind, h = tag
                for ti, (t0, tn) in enumerate(ttiles_of(n)):
                    ps = pst[ti]
                    if kind == "a":
                        kb.op("act", lambda e, ps=ps, t0=t0, tn=tn: e.copy(out=aT[h][0:16, t0:t0 + tn], in_=ps[0:16, 0:tn]), r=ps.all, w=aT[h].all)
                    elif kind == "q":
                        kb.op("act", lambda e, ps=ps, t0=t0, tn=tn: e.activation(out=q_sb[:, t0:t0 + tn], in_=ps[:, 0:tn], func=AF.Identity, scale=1.0 / 16.0),
                              r=ps.all, w=q_sb.all)
                    elif kind == "k":
                        kb.op("act", lambda e, ps=ps, t0=t0, tn=tn: e.copy(out=k_sb[:, t0:t0 + tn], in_=ps[:, 0:tn]), r=ps.all, w=k_sb.all)
                    elif kind == "v":
                        kb.op("act", lambda e, ps=ps, t0=t0, tn=tn: e.copy(out=ob2[:, t0:t0 + tn], in_=ps[:, 0:tn]), r=ps.all, w=ob2.all)
                    elif kind == "g":
                        kb.op("act", lambda e, ps=ps, t0=t0, tn=tn: e.activation(out=ob2[:, t0:t0 + tn], in_=ps[:, 0:tn], func=AF.Silu),
                              r=ps.all, w=ob2.all)
                if kind == "v":
                    fm_to_tm(es2, ob2, n, VT[0], tok0, h * P, tsb)
                elif kind == "g":
                    kb.dma("sp", GATE[h * P:(h + 1) * P, tok0:tok0 + n], ob2[:, 0:n], ob2, r=ob2.all, w=[SCR])
                elif kind == "k":
                    for dr in range(2):
                        for ti, (t0, tn) in enumerate(ttiles_of(n)):
                            ps = nps()
                            kb.op("pe", lambda e, ps=ps, t0=t0, tn=tn: e.matmul(
                                ps[:, 0:tn], lhsT=wup[0:16, dr * 1024 + h * P:dr * 1024 + (h + 1) * P], rhs=aT[dr][0:16, t0:t0 + tn],
                                start=True, stop=True), r=list(wup.all) + list(aT[dr].all), w=ps.all)
                            kb.op("act", lambda e, ps=ps, t0=t0, tn=tn: e.activation(
                                out=lf[:, t0:t0 + tn], in_=ps[:, 0:tn], func=AF.Exp, scale=-1.0, bias=V("nbup", dr * 8 + h)),
                                r=ps.all, w=lf.all)
                        kb.op("act", lambda e: e.activation(out=lf[:, 0:n], in_=lf[:, 0:n], func=AF.Ln, bias=1.0, scale=1.0), r=lf.all, w=lf.all)
                        kb.op("dve", lambda e: e.tensor_scalar(out=lf[:, 0:n], in0=lf[:, 0:n], scalar1=-1.0 / 16.0, scalar2=None, op0=ALU.mult),
                              r=lf.all, w=lf.all)
                        gate_chain(es2, T, q_sb, k_sb, lf, dr, h * P, tok0, n)
            blocks = [dict(ranges=[(6144, 16), (6160, 16)], chunks=[(0, 16, ("a", 0)), (16, 16, ("a", 1))])]
            for h in range(8):
                blocks.append(dict(ranges=[(h * P, P), (1024 + h * P, P)], chunks=[(0, P, ("q", h)), (P, P, ("k", h))]))
            for h in range(0, KC, 4):
                blocks.append(dict(ranges=[(2048 + h * P, 512)], chunks=[(o * P, P, ("v", h + o)) for o in range(4)]))
            for h in range(0, KC, 4):
                blocks.append(dict(ranges=[(4096 + h * P, 512)], chunks=[(o * P, P, ("g", h + o)) for o in range(4)]))
            linear(hv, hT.all, KC, gla_w_in[0], blocks, ttiles_of(n), epi)

    def scan_core(H, DKC, DV, gname, U=P):
        NUu = TT // U
        EXd = EX if U == P else EX64
        with kb.scope() as es2:
            qt = kb.tile([P, DKC * TT], BF16, es=es2)
            kt = kb.tile([P, DKC * TT], BF16, es=es2)
            khu = [kb.tile([P, DKC * P], BF16, es=es2) for _ in range(2)]
            vt = kb.tile([P, NUu * DV], BF16, es=es2)
            oacc = kb.tile([P, NUu * DV], F32, NUu, es=es2)
            ext = kb.tile([P, DKC * NUu], F32, es=es2)
            S = [kb.tile([P, DV], F32, es=es2) for _ in range(DKC)]
            Sb = [kb.tile([P, DV], BF16, es=es2) for _ in range(DKC)]
            asb = [kb.tile([P, P], BF16, es=es2) for _ in range(2)]
            khs = [kb.tile([P, DKC * P], BF16, es=es2) for _ in range(2)]
            sq = kb.tile([P, DV], F32, es=es2)
            ssq = kb.tile([P, 1], F32, es=es2)
            on = kb.tile([P, DV], BF16, es=es2)
            gt = [kb.tile([P, P], BF16, es=es2) for _ in range(2)]
            osb = [kb.tile([P, P], BF16, es=es2) for _ in range(2)]
            sel = [0]
            qv = qt[:].rearrange("p (k t) -> p k t", k=DKC)
            kv = kt[:].rearrange("p (k t) -> p k t", k=DKC)
            vv = vt[:].rearrange("p (u v) -> p u v", v=DV)
            ov = oacc[:].rearrange("p (u v) -> p u v", v=DV)
            exv = ext[:].rearrange("p (k u) -> p k u", k=DKC)
            for h in range(H // 2):
                r0 = h * DKC * P
                kb.dma("sp", vv[0:U], VTm[0, :, h * DV:(h + 1) * DV].rearrange("(u p) v -> p u v", p=U), vt, r=[SCR], w=vt.all)
                for dr in range(2):
                    for (t, vw, src) in ((qt, qv, FMm["q"]), (kt, kv, FMm["k"])):
                        kb.dma("sp", vw, src[dr, r0:r0 + DKC * P, :].rearrange("(k p) t -> p k t", p=P), t, r=[SCR], w=t.all)
                    kb.dma("sp", exv, EXm[dr, r0:r0 + DKC * P, 0:NUu].rearrange("(k p) u -> p k u", p=P), ext, r=[SCR], w=ext.all)
                    for k in range(DKC):
                        kb.op("dve", lambda e, k=k: e.memset(S[k][:], 0.0), w=S[k].all)
                        kb.op("pool", lambda e, k=k: e.memset(Sb[k][:], 0.0), w=Sb[k].all)
                    nc_ = TCX // U
                    order = list(range(NUu)) if dr == 0 else list(range(nc_ - 1, -1, -1)) + list(range(NUu - 1, nc_ - 1, -1))
                    for u in order:
                        ts = slice(u * U, (u + 1) * U)
                        sel[0] ^= 1
                        a, khb = asb[sel[0]], khs[sel[0]]
                        pa = nps()
                        for k in range(DKC):
                            kb.op("pe", lambda e, k=k: e.matmul(pa[0:U, 0:U], lhsT=kv[:, k, ts], rhs=qv[:, k, ts], start=(k == 0), stop=(k == DKC - 1)),
                                  r=list(kt.all) + list(qt.all), w=pa.all, inc=(k == DKC - 1))
                        kb.op("dve", lambda e: e.tensor_tensor(out=a[0:U, 0:U], in0=pa[0:U, 0:U], in1=tri[0:U, dr * P:dr * P + U], op=ALU.mult),
                              r=list(pa.all) + list(tri.all), w=a.all)
                        po = nps()
                        kb.op("pe", lambda e: e.matmul(po[0:U, 0:DV], lhsT=a[0:U, 0:U], rhs=vv[0:U, u, :], start=True, stop=False),
                              r=list(a.all) + list(vt.all), w=po.all, inc=False)
                        for k in range(DKC):
                            kb.op("pe", lambda e, k=k: e.matmul(po[0:U, 0:DV], lhsT=qv[:, k, ts], rhs=Sb[k][:], start=False, stop=(k == DKC - 1)),
                                  r=list(qt.all) + list(Sb[k].all), w=po.all, inc=(k == DKC - 1))
                        if dr == 0:
                            kb.op("act", lambda e: e.copy(out=ov[0:U, u, :], in_=po[0:U, 0:DV]), r=po.all, w=[oacc.parts[u]])
                        else:
                            kb.op("dve", lambda e: e.tensor_tensor(out=ov[0:U, u, :], in0=po[0:U, 0:DV], in1=ov[0:U, u, :], op=ALU.add),
                                  r=list(po.all) + [oacc.parts[u]], w=[oacc.parts[u]])
                        kht = khu[sel[0]]
                        khv = kht[:, 0:DKC * U].rearrange("p (k t) -> p k t", k=DKC)
                        kb.dma("sp", khv, FMm["h"][dr, r0:r0 + DKC * P, u * U:(u + 1) * U].rearrange("(k p) t -> p k t", p=P), kht, r=[SCR], w=kht.all)
                        psb = npsb()
                        pv = psb[:, 0:DKC * P].rearrange("p (k d) -> p k d", d=P)
                        for k in range(DKC):
                            kb.op("pe", lambda e, k=k: e.transpose(out=pv[0:U, k, :], in_=khv[:, k, :], identity=ident[:]),
                                  r=list(kht.all) + list(ident.all), w=psb.all, inc=(k == DKC - 1))
                        kb.op("act", lambda e: e.copy(out=khb[0:U, :], in_=psb[0:U, 0:DKC * P]), r=psb.all, w=khb.all)
                        for k in range(DKC):
                            pd = nps()
                            kb.op("pe", lambda e, k=k, pd=pd: e.matmul(pd[:, 0:DV], lhsT=khb[0:U, k * P:(k + 1) * P], rhs=vv[0:U, u, :], start=True, stop=True),
                                  r=list(khb.all) + list(vt.all), w=pd.all)
                            kb.op("dve", lambda e, k=k, pd=pd: e.scalar_tensor_tensor(
                                out=S[k][:], in0=S[k][:], scalar=exv[:, k, u:u + 1], in1=pd[:, 0:DV], op0=ALU.mult, op1=ALU.add),
                                r=list(S[k].all) + list(pd.all) + list(ext.all), w=S[k].all)
                            kb.op("act", lambda e, k=k: e.copy(out=Sb[k][:], in_=S[k][:]), r=S[k].all, w=Sb[k].all)
                for u in range(NUu):
                    kb.op("act", lambda e, u=u: e.activation(out=sq[0:U, :], in_=ov[0:U, u, :], func=AF.Square),
                          r=[oacc.parts[u]], w=list(sq.all))
                    kb.op("dve", lambda e: e.reduce_sum(out=ssq[0:U, :], in_=sq[0:U, :], axis=mybir.AxisListType.X), r=sq.all, w=ssq.all)
                    kb.op("act", lambda e: e.activation(out=ssq[0:U, :], in_=ssq[0:U, :], func=AF.Sqrt, bias=V("eps")[0:U, :], scale=1.0 / DV), r=ssq.all, w=ssq.all)
                    kb.op("dve", lambda e: e.reciprocal(out=ssq[0:U, :], in_=ssq[0:U, :]), r=ssq.all, w=ssq.all)
                    kb.op("dve", lambda e, u=u: e.tensor_scalar(out=on[0:U, :], in0=ov[0:U, u, :], scalar1=ssq[0:U, 0:1], scalar2=None, op0=ALU.mult),
                          r=[oacc.parts[u]] + list(ssq.all), w=on.all)
                    for cb in range(DV // P):
                        ch = (h * DV) // P + cb
                        sel[0] ^= 1
                        g, ob = gt[sel[0]], osb[sel[0]]
                        kb.dma("sp", g[:, 0:U], GATEm[ch * P:(ch + 1) * P, u * U:(u + 1) * U], g, r=[SCR], w=g.all)
                        psb = npsb()
                        kb.op("pe", lambda e, cb=cb: e.transpose(out=psb[:, 0:U], in_=on[0:U, cb * P:(cb + 1) * P], identity=ident[0:U, 0:U]),
                              r=list(on.all) + list(ident.all), w=psb.all)
                        kb.op("dve", lambda e, ch=ch, g=g, ob=ob: e.scalar_tensor_tensor(
                            out=ob[:, 0:U], in0=psb[:, 0:U], scalar=V(gname, ch), in1=g[:, 0:U], op0=ALU.mult, op1=ALU.mult),
                            r=list(psb.all) + list(g.all), w=ob.all)
                        kb.dma("sp", ZTs[ch * P:(ch + 1) * P, u * U:(u + 1) * U], ob[:, 0:U], ob, r=ob.all, w=[SCR])

    with kb.scope() as es2:
        cp = [kb.tile([P, 4352], F32, es=es2) for _ in range(2)]
        for r in range(KC):
            t = cp[r % 2]
            kb.dma("sp", t[:], XIN[r * P:(r + 1) * P, :], t, w=t.all)
            kb.dma("sp", XR[r * P:(r + 1) * P, :], t[:], t, r=t.all, w=XRb)
    kb.barrier()

    def loc_scan(R, U):
        NUu = TT // U
        EXd = EX if U == P else EX64
        e_sp, e_act, e_pool = (parCH, parsCH, pargCH) if R == CH else (par512, pars512, parg512)
        kb.dma("sp", FMm["q"][:, 0:R, :], QT[:, bass.ds(e_sp, R), :], dh[0], r=[SCR], w=[SCR])
        kb.dma("act", FMm["k"][:, 0:R, :], KTs[:, bass.ds(e_act, R), :], dh[1], r=[SCR], w=[SCR])
        kb.dma("pool", FMm["h"][:, 0:R, :], KHs[:, bass.ds(e_pool, R), :], dh[2], r=[SCR], w=[SCR])
        kb.dma("sp", EXm[:, 0:R, 0:NUu], EXd[:, bass.ds(e_sp, R), :], dh[0], r=[SCR], w=[SCR])
        kb.dma("sp", VTm[0, 0:TT // 2, :], VT[0, 0:TT // 2, :][:, bass.ds(parCH, CH)], dh[0], r=[SCR], w=[SCR])
        kb.dma("act", VTm[0, TT // 2:TT, :], VT[0, TT // 2:TT, :][:, bass.ds(parsCH, CH)], dh[1], r=[SCR], w=[SCR])
        kb.dma("pool", GATEm, GATE[bass.ds(pargCH, CH), :], dh[2], r=[SCR], w=[SCR])
        kb.barrier()

    def gather_zt():
        kb.allgather([(ZTs[c_ * P:(c_ + 1) * P, :], ZTg[c_ * 2 * P:(c_ + 1) * 2 * P, :]) for c_ in range(CH // P)], GROUPS)

    for i in range(nlayers):
        kind, j = i % 3, i // 3
        last = i == DEPTH - 1
        sis = list(range(len(SUPER)))
        if last and kind == 0:
            sis = sis[1:]
        mv, mav = pass_mod(i)
        if kind == 0:
            for si in sis:
                hyena_pass_a(i, j, si, mv, mav)
            kb.barrier()
            for w_, (qq, ee) in enumerate((("sp", parCH), ("act", parsCH), ("pool", pargCH))):
                for r0_, r1_ in ((0, TT // 2), (TT // 2, TT)):
                    kb.dma(qq, VTm[w_, r0_:r1_, :], VT[w_, r0_:r1_, :][:, bass.ds(ee, CH)], dh[w_], r=[SCR], w=[SCR])
            kb.barrier()
            hyena_filters(j, TL)
            kb.barrier()
            hyena_conv(TL, TCX)
            kb.barrier()
            if 0 in sis:
                hyena_filters(j, TCX)
                kb.barrier()
                hyena_conv(TCX, 0)
            kb.allgather([(ZTs[c_ * P:(c_ + 1) * P, :], ZTg[c_ * 2 * P:(c_ + 1) * 2 * P, :]) for c_ in range(CH // P)], GROUPS)
            Wout = hy_w_out[j]
        elif kind == 1:
            with kb.scope() as esl:
                lbt = kb.tile([P, 4 * KC], F32, es=esl)
                ee = kb.tile([P, 2 * 4 * KC], F32, es=esl)
                sm = kb.tile([P, 2 * KC], F32, es=esl)
                kb.op("act", lambda e: e.activation(out=ee[:], in_=V("lbl", 0, 2 * 4 * KC), func=AF.Exp), r=vecs.all, w=ee.all)
                ev = ee[:].rearrange("p (d l k) -> p d l k", d=2, l=4)
                smv = sm[:].rearrange("p (d k) -> p d k", d=2)
                kb.op("dve", lambda e: e.tensor_tensor(out=smv, in0=ev[:, :, 0, :], in1=ev[:, :, 1, :], op=ALU.add), r=ee.all, w=sm.all)
                kb.op("dve", lambda e: e.tensor_tensor(out=smv, in0=smv, in1=ev[:, :, 2, :], op=ALU.add), r=list(ee.all) + list(sm.all), w=sm.all)
                kb.op("dve", lambda e: e.tensor_tensor(out=smv, in0=smv, in1=ev[:, :, 3, :], op=ALU.add), r=list(ee.all) + list(sm.all), w=sm.all)
                kb.op("dve", lambda e: e.reciprocal(out=sm[:], in_=sm[:]), r=sm.all, w=sm.all)
                lv = lbt[:].rearrange("p (a d k) -> p a d k", a=2, d=2)
                kb.op("dve", lambda e: e.tensor_tensor(out=lv[:, 0], in0=ev[:, :, 1, :], in1=smv, op=ALU.mult), r=list(ee.all) + list(sm.all), w=lbt.all)
                kb.op("dve", lambda e: e.tensor_scalar(out=lv[:, 1], in0=lv[:, 0], scalar1=-1.0, scalar2=1.0, op0=ALU.mult, op1=ALU.add),
                      r=lbt.all, w=lbt.all)
                for si in sis:
                    hgrn_pass_a(i, si, mv, mav, lbt)
            kb.barrier()
            loc_scan(CH, 64)
            scan_core(16, 1, 128, "hgg", U=64)
            gather_zt()
            Wout = hg_w_out[0]
        else:
            with kb.scope() as esl:
                wup = kb.tile([P, 2048], BF16, es=esl)
                kb.dma("pool", wup[0:16, :].rearrange("p (d n) -> p d n", d=2), gla_w_up[0].rearrange("d p n -> p d n"), wup, w=wup.all)
                for si in sis:
                    gla_pass_a(i, si, mv, mav, wup)
            kb.barrier()
            loc_scan(512, P)
            scan_core(4, 2, 512, "glg")
            gather_zt()
            Wout = gla_w_out[0]
        ctx_used_later = any((ii != DEPTH - 1) or (ii % 3 != 0) for ii in range(i + 1, DEPTH))
        has_ctx = (0 in sis) and ctx_used_later
        kb.dma("act", ZTl[:, TCX:TLOC], ZTg[:, TCX:TT][:, bass.ds(parsTLH, TLH)], dh[1], r=[SCR], w=[SCR])
        kb.dma("pool", XRl[:, TCX:TLOC], XR[:, TCX:TT][:, bass.ds(pargTLH, TLH)], dh[2], r=XRb + [SCR], w=XRlb + [SCR])
        if has_ctx:
            kb.dma("sp", ZTl[:, 0:TCX], ZTg[:, 0:TCX], dh[0], r=[SCR], w=[SCR])
            kb.dma("sp", XRl[:, 0:TCX], XR[:, 0:TCX], dh[0], r=XRb + [SCR], w=XRlb + [SCR])
        kb.barrier()
        SPC.update(SUPER=SUPERL, XR=XRl, XRb=XRlb, ZT=ZTl)
        for si in ([0, 1, 2] if has_ctx else [1, 2]):
            pass_out_ffn(i, si, mv, mav, Wout, hy=True)
        SPC.update(SUPER=SUPER, XR=XR, XRb=XRb, ZT=ZTg)
        kb.dma("sp", XRs, XRl[:, TCX:TLOC], dh[0], r=XRlb + [SCR], w=[SCR])
        kb.allgather([(XRs[c_ * P:(c_ + 1) * P, :], XRg[c_ * 2 * P:(c_ + 1) * 2 * P, :]) for c_ in range(KC)], GROUPS)
        for s_ in range(2):
            kb.dma("sp", XR[:, TCX + s_ * TLH:TCX + (s_ + 1) * TLH].rearrange("(c p) t -> c p t", p=P),
                   XRg.rearrange("(c s p) t -> s c p t", s=2, p=P)[s_], dh[0], r=[SCR], w=XRb + [SCR])
        if has_ctx:
            kb.dma("sp", XR[:, 0:TCX], XRl[:, 0:TCX], dh[0], r=XRlb + [SCR], w=XRb + [SCR])
        kb.barrier()

    with kb.scope() as es2:
        xs = kb.tile([P, KC * 256], F32, es=es2)
        sq = kb.tile([P, KC * 256], BF16, es=es2)
        rstd = kb.tile([P, 256], F32, es=es2)
        ot = kb.tile([P, KC * 256], F32, es=es2)
        xv = xs[:].rearrange("p (k t) -> p k t", t=256)
        sv = sq[:].rearrange("p (k t) -> p k t", t=256)
        otv = ot[:].rearrange("p (k t) -> p k t", t=256)
        for s0 in range(0, TL, 256):
            kb.dma("sp", xv, XR[:, TCX + s0:TCX + s0 + 256].rearrange("(k p) t -> p k t", p=P), xs, r=XRb, w=xs.all)
            kb.op("act", lambda e: e.activation(out=sq[:], in_=xs[:], func=AF.Square), r=xs.all, w=sq.all)
            ps = nps()
            for kc in range(KC):
                kb.op("pe", lambda e, kc=kc: e.matmul(ps[:, 0:256], lhsT=ones[:, 0:P], rhs=sv[:, kc, :], start=(kc == 0), stop=(kc == KC - 1)),
                      r=list(sq.all) + list(ones.all), w=ps.all, inc=(kc == KC - 1))
            kb.op("act", lambda e: e.activation(out=rstd[:], in_=ps[:, 0:256], func=AF.Sqrt, bias=V("eps"), scale=1.0 / D), r=ps.all, w=rstd.all)
            kb.op("dve", lambda e: e.reciprocal(out=rstd[:], in_=rstd[:]), r=rstd.all, w=rstd.all)
            for kc in range(KC):
                kb.op("dve", lambda e, kc=kc: e.scalar_tensor_tensor(out=otv[:, kc, :], in0=xv[:, kc, :], scalar=V("fing", kc), in1=rstd[:],
                                                                     op0=ALU.mult, op1=ALU.mult), r=list(xs.all) + list(rstd.all), w=ot.all)
            kb.dma("sp", OUT[:, s0:s0 + 256].rearrange("(k p) t -> p k t", p=P), otv, ot, r=ot.all, w=[SCR])
    kb.barrier()
    es.close()
    return nc


VOFF = {}
NV = 0


def _voff():
    global NV
    o = 0
    for name, n in (("eps", 1), ("bmod", DEPTH * 96), ("n1g", DEPTH * KC), ("n2g", DEPTH * KC), ("fing", KC),
                    ("cw", 2 * 3 * 48), ("ffreq", 4), ("fbias", 4), ("lbl", 2 * 4 * KC), ("hgg", KC), ("glg", KC), ("nbup", 16)):
        VOFF[name] = o
        o += n
    NV = o


_voff()
nlayers = DEPTH


def fm(v):
    v = np.asarray(v, np.float32).reshape(-1, P)
    return v.T


def pack_vecs(inp, par=0):
    vecs = np.zeros((P, NV), np.float32)

    def put(name, arr, off=0):
        arr = np.asarray(arr, np.float32)
        vecs[:arr.shape[0], VOFF[name] + off:VOFF[name] + off + arr.shape[1]] = arr
    put("eps", np.full((P, 1), EPS, np.float32))
    for i in range(DEPTH):
        put("bmod", fm(inp["b_mod"][i]), i * 96)
        put("n1g", fm(inp["norm1_g"][i]), i * KC)
        put("n2g", fm(inp["norm2_g"][i]), i * KC)
    put("fing", fm(inp["final_g"]))
    for j in range(2):
        for tap in range(3):
            put("cw", fm(inp["hy_conv_w"][j, tap]), (j * 3 + tap) * 48)
        for q in range(2):
            put("ffreq", inp["hy_ffreq"][j, q].reshape(64, 1), j * 2 + q)
        put("fbias", inp["hy_fb1"][j].reshape(64, 1), j * 2 + 0)
        put("fbias", inp["hy_fb2"][j].reshape(64, 1), j * 2 + 1)
    for d in range(2):
        for l in range(4):
            put("lbl", fm(inp["hg_lb_logits"][d, l]), (d * 4 + l) * KC)
    put("hgg", np.roll(fm(inp["hg_onorm_g"][0]), -8 * par, axis=1))
    put("glg", np.roll(fm(inp["gla_onorm_g"][0]), -8 * par, axis=1))
    for d in range(2):
        put("nbup", -fm(inp["gla_b_up"][0, d]), d * 8)
    return vecs


_CONST = {}


def consts():
    if _CONST:
        return _CONST
    bf = ml_dtypes.bfloat16
    c = {}
    c["c_ident"] = np.eye(P, dtype=np.float32).astype(bf)
    on = np.ones((P, 2 * P), np.float32)
    on[P - 1, P:] = 0.0
    c["c_ones"] = on.astype(bf)
    s = np.arange(P)[:, None]
    t = np.arange(P)[None, :]
    c["c_tri"] = np.concatenate([(s <= t), (s >= t)], axis=1).astype(np.float32)
    sm = np.ones((P, 1024), np.float32)
    sm[:, ::P] = 0.0
    c["c_smask"] = sm
    sm2 = np.ones((P, 1024), np.float32)
    sm2[:, ::64] = 0.0
    c["c_smask64"] = sm2
    delt = np.abs(np.linspace(DMIN, DMAX, D, dtype=np.float32))
    c["c_negd"] = np.broadcast_to(-delt[None, :], (P, D)).astype(np.float32).copy()
    for L in (TL, TCX):
        TCn, KF = L // P, L // P + 1
        N = 2 * L
        pos = np.arange(L, dtype=np.float32)
        tt = pos / max(L - 1, 1)
        bands = np.arange(1, 17, dtype=np.float32)
        ang = (2.0 * math.pi / L) * pos[:, None] * bands[None, :]
        z = np.concatenate([tt[:, None], np.cos(ang), -np.sin(ang)], axis=-1).astype(np.float32)
        c["c_zT%d" % L] = np.ascontiguousarray(z.T)
        c["c_tn%d" % L] = np.ascontiguousarray(tt.reshape(TCn, P).T)
        kpad = KF * P
        kk = np.arange(kpad, dtype=np.float64)
        valid = (kk <= L).astype(np.float64)
        tpos = np.arange(L, dtype=np.float64)
        ph = 2 * math.pi * ((tpos[:, None] * kk[None, :]) % N) / N
        Fre = np.cos(ph) * valid
        Fim = -np.sin(ph) * valid
        Ff = np.stack([Fre.reshape(TCn, P, KF, P), Fim.reshape(TCn, P, KF, P)], axis=3)
        c["c_Ff%d" % L] = np.ascontiguousarray(Ff.transpose(2, 1, 0, 3, 4).reshape(KF, P, TCn * 256)).astype(bf)
        phb = 2 * math.pi * (((tpos[:, None] + 1) * kk[None, :]) % N) / N
        rowv = (tpos < L - 1).astype(np.float64)[:, None]
        Bre = np.cos(phb) * valid * rowv
        Bim = np.sin(phb) * valid * rowv
        Kre = np.concatenate([Fre, Bre], axis=0)
        Kim = np.concatenate([Fim, Bim], axis=0)
        FFm = np.concatenate([Kre, Kim], axis=1)
        FFb = FFm.reshape(2 * TCn, P, 2 * KF, P).transpose(2, 1, 0, 3).reshape(2 * KF, P, 2 * TCn * P)
        c["c_FF%d" % L] = np.ascontiguousarray(FFb).astype(bf)
        wk = np.where((kk == 0) | (kk == L), 1.0, 2.0) * valid / N
        phi = 2 * math.pi * ((kk[:, None] * tpos[None, :]) % N) / N
        Gre = np.cos(phi) * wk[:, None]
        Gim = -np.sin(phi) * wk[:, None]
        Gm = np.concatenate([Gre, Gim], axis=0)
        Gb = Gm.reshape(2 * KF, P, TCn, P).transpose(2, 1, 0, 3).reshape(TCn, P, 2 * KF * P)
        c["c_G%d" % L] = np.ascontiguousarray(Gb).astype(bf)
    _CONST.update(c)
    return _CONST


def make_in_maps(inp):
    cst = consts()
    vecs = [pack_vecs(inp, 0), pack_vecs(inp, 1)]
    B = inp["x"].shape[0]
    shared = {k: np.ascontiguousarray(inp[k], dtype=np.float32) for k in
              ("w_mod", "w_ffn_in", "w_ffn_out", "hy_w_in", "hy_w_out", "hy_fw1", "hy_fw2", "hy_fwout",
               "hg_w_in", "hg_w_out", "gla_w_in", "gla_w_up", "gla_w_out")}
    shared["fskip"] = np.ascontiguousarray(np.broadcast_to(inp["hy_fskip"].reshape(2, 1, 2 * D), (2, P, 2 * D)), dtype=np.float32)
    shared.update(cst)
    in_maps = []
    for core in range(8):
        b = (core // 2) % B
        m = dict(shared)
        m["vecs"] = vecs[core % 2]
        m["xin"] = np.ascontiguousarray(np.concatenate([inp["ctx"][b].T, inp["x"][b].T], axis=1), dtype=np.float32)
        sc = np.stack([fm(inp["c"][b]), fm(inp["c_ctx"])], axis=2)
        m["scin"] = np.ascontiguousarray(sc.reshape(P, KC * 2), dtype=np.float32)
        in_maps.append(m)
    return in_maps


def kernel(**inp):
    inp = {k: np.asarray(v) for k, v in inp.items()}
    nc = build()
    in_maps = make_in_maps(inp)
    B = inp["x"].shape[0]
    res = run_bass_kernel_spmd(nc, in_maps, core_ids=list(range(8)))
    out = np.stack([np.asarray(res.results[2 * b]["out"]).T for b in range(B)], axis=0)
    return np.ascontiguousarray(out, dtype=np.float32)
```

```python
import math
from contextlib import ExitStack
import numpy as np
import ml_dtypes
import concourse.bass as bass
import concourse.mybir as mybir
from concourse.bass_utils import run_bass_kernel_spmd

F32 = mybir.dt.float32
BF16 = mybir.dt.bfloat16
ALU = mybir.AluOpType
AF = mybir.ActivationFunctionType
P = 128
D = 2048
KC = 16
TL = 4096
TCX = 256
TT = TL + TCX
CH = D // 2
FF = 5632
NU = TT // P
DEPTH = 4
EPS = 1e-6
DMIN = math.log(1e-2) / 1.5
DMAX = math.log(1e-2) / 0.3
WELE = 8704
SUPER = [(0, TCX, 1)] + [(TCX + i * 1024, 1024, 0) for i in range(4)]


class Buf:
    __slots__ = ("w", "r")

    def __init__(self):
        self.w = None
        self.r = {}


class Tile:
    def __init__(self, h, nparts=1):
        self.h = h
        self.parts = [Buf() for _ in range(nparts)]
        self.sem = None
        self.cnt = 0

    def __getitem__(self, k):
        return self.h[k]

    @property
    def b(self):
        return self.parts[0]

    @property
    def all(self):
        return self.parts


class KB:
    def __init__(self, nc, es):
        self.nc = nc
        self.es = es
        self.eng = {"pe": nc.tensor, "act": nc.scalar, "dve": nc.vector, "pool": nc.gpsimd, "sp": nc.sync}
        self.sem = {}
        self.cnt = {}
        for e in ("pe", "act", "dve", "pool"):
            self.sem[e] = es.enter_context(nc.semaphore("s_" + e))
            self.cnt[e] = 0
        self.waited = {e: {} for e in self.eng}
        self.tiles = []
        self.nm = 0
        self.sempool = []
        self.ccsem = es.enter_context(nc.semaphore("s_cc"))
        self.cccnt = 0
        self.bsem = es.enter_context(nc.semaphore("s_bar"))
        self.bcnt = 0

    def name(self, p):
        self.nm += 1
        return "%s%d" % (p, self.nm)

    def tile(self, shape, dt, nparts=1, es=None):
        h = (es or self.es).enter_context(self.nc.sbuf_tensor(self.name("t"), list(shape), dt))
        t = Tile(h, nparts)
        if self.sempool:
            t.sem, t.cnt, t.key = self.sempool.pop()
        else:
            t.sem = self.es.enter_context(self.nc.semaphore(self.name("d")))
            t.key = self.name("k")
        self.tiles.append(t)
        if es is not None and hasattr(es, "mine"):
            es.mine.append(t)
        return t

    def psum(self, shape, dt):
        h = self.es.enter_context(self.nc.psum_tensor(self.name("p"), list(shape), dt))
        return Tile(h, 1)

    def _wait(self, eng, ev):
        if ev is None:
            return
        sem, val, key = ev
        if eng == "pe" and key == "pe":
            return
        if self.waited[eng].get(key, 0) >= val:
            return
        self.waited[eng][key] = val
        self.eng[eng].wait_ge(sem, val)

    def _deps(self, eng, r, w):
        for b in r:
            self._wait(eng, b.w)
        for b in w:
            self._wait(eng, b.w)
            for ev in list(b.r.values()):
                self._wait(eng, ev)

    def _mark(self, ev, r, w):
        for b in r:
            b.r[ev[2]] = ev
        for b in w:
            b.w = ev
            b.r = {}

    def op(self, eng, fn, r=(), w=(), inc=True):
        self._deps(eng, r, w)
        ins = fn(self.eng[eng])
        if inc:
            self.cnt[eng] += 1
            ins.then_inc(self.sem[eng], 1)
            ev = (self.sem[eng], self.cnt[eng], eng)
        else:
            ev = (self.sem[eng], self.cnt[eng] + 1, eng)
        self._mark(ev, r, w)
        return ins

    def dma(self, q, out, in_, tile, r=(), w=()):
        self._deps(q, r, w)
        ins = self.eng[q].dma_start(out=out, in_=in_)
        tile.cnt += 16
        ins.then_inc(tile.sem, 16)
        ev = (tile.sem, tile.cnt, tile.key)
        self._mark(ev, r, w)

    def barrier(self):
        sp = self.eng["sp"]
        for e in ("pe", "act", "dve", "pool"):
            if self.cnt[e] > 0:
                self._wait("sp", (self.sem[e], self.cnt[e], e))
        for t in self.tiles:
            if t.cnt > 0:
                self._wait("sp", (t.sem, t.cnt, t.key))
        self.bcnt += 1
        ins = sp.nop()
        ins.then_inc(self.bsem, 1)
        for e in ("pe", "act", "dve", "pool"):
            self.eng[e].wait_ge(self.bsem, self.bcnt)
        for e in self.eng:
            for e2 in ("pe", "act", "dve", "pool"):
                self.waited[e][e2] = self.cnt[e2]
            for t in self.tiles:
                self.waited[e][t.key] = t.cnt

    def allgather(self, pairs, groups):
        self.barrier()
        for (src, dst) in pairs:
            self.cccnt += 1
            self.nc.gpsimd.collective_compute("AllGather", ALU.bypass, replica_groups=groups, ins=[src], outs=[dst]).then_inc(self.ccsem, 1)
        self.eng["sp"].wait_ge(self.ccsem, self.cccnt)
        self.barrier()

    def scope(self):
        kb = self

        class _S(ExitStack):
            def __exit__(s, *a):
                if a[0] is None:
                    kb.barrier()
                    kb.tiles = [t for t in kb.tiles if t not in s.mine]
                    for t in s.mine:
                        kb.sempool.append((t.sem, t.cnt, t.key))
                return ExitStack.__exit__(s, *a)
        st = _S()
        st.mine = []
        return st


def build(dbg=False, ncores=8):
    nc = bass.Bass("TRN2", target_bir_lowering=False)
    es = ExitStack()
    kb = KB(nc, es)

    def din(name, shape, dt=F32):
        return nc.dram_tensor(name, list(shape), dt, kind="ExternalInput").ap()

    def dint(name, shape, dt=F32):
        return nc.dram_tensor(name, list(shape), dt, kind="Internal").ap()

    GROUPS = [[2 * g, 2 * g + 1] for g in range(ncores // 2)]
    par = nc.sync.partition_id() % 2
    pars = nc.scalar.partition_id() % 2
    parg = nc.gpsimd.partition_id() % 2
    parCH, parsCH, pargCH = par * CH, pars * CH, parg * CH
    par512, pars512, parg512 = par * 512, pars * 512, parg * 512
    XIN = din("xin", [D, TT])
    SCIN = din("scin", [P, KC * 2])
    VECS = din("vecs", [P, NV])
    w_mod = din("w_mod", [DEPTH, D, 6 * D])
    w_ffn_in = din("w_ffn_in", [DEPTH, D, 2 * FF])
    w_ffn_out = din("w_ffn_out", [DEPTH, FF, D])
    hy_w_in = din("hy_w_in", [2, D, 3 * D])
    hy_w_out = din("hy_w_out", [2, D, D])
    hy_fw1 = din("hy_fw1", [2, 33, 64])
    hy_fw2 = din("hy_fw2", [2, 64, 64])
    hy_fwout = din("hy_fwout", [2, 64, 4 * D])
    fskip = din("fskip", [2, P, 2 * D])
    hg_w_in = din("hg_w_in", [1, D, 5 * D])
    hg_w_out = din("hg_w_out", [1, D, D])
    gla_w_in = din("gla_w_in", [1, D, 6176])
    gla_w_up = din("gla_w_up", [1, 2, 16, 1024])
    gla_w_out = din("gla_w_out", [1, D, D])
    C_ident = din("c_ident", [P, P], BF16)
    C_ones = din("c_ones", [P, 2 * P], BF16)
    C_tri = din("c_tri", [P, 2 * P])
    C_smask = din("c_smask", [P, 1024])
    C_negd = din("c_negd", [P, D])
    C_smask64 = din("c_smask64", [P, 1024])
    CL = {}
    for L in (TL, TCX):
        tc_, kf = L // P, (L // P) + 1
        CL[L] = dict(
            zT=din("c_zT%d" % L, [33, L]), tn=din("c_tn%d" % L, [P, tc_]),
            Ff=din("c_Ff%d" % L, [kf, P, tc_ * 256], BF16),
            FF=din("c_FF%d" % L, [2 * kf, P, 2 * tc_ * P], BF16),
            G=din("c_G%d" % L, [tc_, P, 2 * kf * P], BF16),
            HS=dint("hs%d" % L, [2, 2 * kf, P, CH]), TC=tc_, KF=kf)
    OUT = nc.dram_tensor("out", [D, TL], F32, kind="ExternalOutput").ap()
    XR = (nc.dram_tensor("xr", [D, TT], F32, kind="ExternalOutput").ap() if dbg else dint("xr", [D, TT]))
    ZT = None
    VT = (nc.dram_tensor("vt", [3, TT, D], BF16, kind="ExternalOutput").ap() if dbg else dint("vt", [3, TT, D], BF16))
    ZT = (nc.dram_tensor("zt", [D, TT], BF16, kind="ExternalOutput").ap() if dbg else dint("zt", [D, TT], BF16))
    QT = (nc.dram_tensor("qt", [2, D, TT], BF16, kind="ExternalOutput").ap() if dbg else dint("qt", [2, D, TT], BF16))
    KTs = (nc.dram_tensor("kts", [2, D, TT], BF16, kind="ExternalOutput").ap() if dbg else dint("kts", [2, D, TT], BF16))
    KHs = (nc.dram_tensor("khs", [2, D, TT], BF16, kind="ExternalOutput").ap() if dbg else dint("khs", [2, D, TT], BF16))
    EX = (nc.dram_tensor("ex", [2, D, NU], F32, kind="ExternalOutput").ap() if dbg else dint("ex", [2, D, NU]))
    EX64 = dint("ex64", [2, D, TT // 64])
    GATE = (nc.dram_tensor("gate", [D, TT], BF16, kind="ExternalOutput").ap() if dbg else dint("gate", [D, TT], BF16))
    VTm = dint("vtm", [3, TT, CH], BF16)
    ZTs = dint("zts", [CH, TT], BF16)
    FMm = {nm: dint(nm + "m", [2, CH, TT], BF16) for nm in ("q", "k", "h")}
    EXm = dint("exm", [2, CH, 68])
    GATEm = dint("gatem", [CH, TT], BF16)
    ZTg = dint("ztg", [2 * CH, TT], BF16)
    TLH = TL // 2
    TLOC = TCX + TLH
    SUPERL = [(0, TCX, 1), (TCX, 1024, 0), (TCX + 1024, 1024, 0)]
    ZTl = dint("ztl", [2 * CH, TLOC], BF16)
    XRl = dint("xrl", [D, TLOC])
    XRg = dint("xrg", [2 * D, TLH])
    XRs = dint("xrs", [D, TLH])
    parTLH, parsTLH, pargTLH = par * TLH, pars * TLH, parg * TLH
    XRb = [Buf() for _ in SUPER]
    XRlb = [Buf() for _ in SUPERL]
    SPC = {"SUPER": SUPER, "XR": XR, "XRb": XRb, "ZT": ZTg}
    SCR = Buf()

    dh = [kb.tile([P, 8], F32) for _ in range(3)]
    vecs = kb.tile([P, NV], F32)
    ident = kb.tile([P, P], BF16)
    ones = kb.tile([P, 2 * P], BF16)
    tri = kb.tile([P, 2 * P], F32)
    smask = kb.tile([P, 1024], F32)
    smask64 = kb.tile([P, 1024], F32)
    scT = kb.tile([P, KC * 2], BF16)
    MOD = kb.tile([P, 96 * 2], F32)
    MA = kb.tile([P, 2 * KC * 2], F32)
    wbuf = [kb.tile([P, WELE], BF16) for _ in range(2)]
    wsel = [0]
    PS = [kb.psum([P, 512], F32) for _ in range(6)]
    PSB = [kb.psum([P, 1024], BF16) for _ in range(2)]
    pssel = [0]
    psbsel = [0]

    def nps():
        pssel[0] = (pssel[0] + 1) % 6
        return PS[pssel[0]]

    def npsb():
        psbsel[0] = (psbsel[0] + 1) % 2
        return PSB[psbsel[0]]

    def V(name, i=0, n=1):
        o = VOFF[name] + i
        return vecs[:, o:o + n]

    for (t, src) in ((vecs, VECS), (ident, C_ident), (ones, C_ones), (tri, C_tri), (smask, C_smask), (smask64, C_smask64)):
        kb.dma("sp", t[:], src, t, w=t.all)
    sc32 = kb.tile([P, KC * 2], F32)
    kb.dma("sp", sc32[:], SCIN, sc32, w=sc32.all)
    kb.op("act", lambda e: e.activation(out=scT[:], in_=sc32[:], func=AF.Silu), r=sc32.all, w=scT.all)

    def load_w(W, KCn, ranges, cast=True, pre=None):
        wsel[0] ^= 1
        wt = wbuf[wsel[0]]
        ntot = sum(n for _, n in ranges)
        assert KCn * ntot <= WELE
        view = wt[:, 0:KCn * ntot].rearrange("p (k n) -> p k n", n=ntot)
        off = 0
        for (c0, n) in ranges:
            if pre is not None:
                src = pre
            else:
                src = W[:, c0:c0 + n].rearrange("(k p) n -> p k n", p=P)
            kb.dma("pool" if cast else "sp", view[:, :, off:off + n], src, wt, w=wt.all)
            off += n
        return wt, view

    def linear(xT, xbufs, KCn, W, blocks, ttiles, epi, mparts=None):
        for blk in blocks:
            wt, view = load_w(W, KCn, blk["ranges"], pre=blk.get("pre"), cast=blk.get("cast", True))
            for (off, m, tag) in blk["chunks"]:
                pst = []
                for (t0, tn) in ttiles:
                    ps = nps()
                    for kc in range(KCn):
                        kb.op("pe", lambda e, ps=ps, kc=kc, off=off, m=m, t0=t0, tn=tn: e.matmul(
                            ps[0:m, 0:tn], lhsT=view[:, kc, off:off + m], rhs=xT[:, kc, t0:t0 + tn],
                            start=(kc == 0), stop=(kc == KCn - 1)),
                            r=list(wt.all) + list(xbufs), w=ps.all, inc=(kc == KCn - 1))
                    pst.append(ps)
                epi(tag, pst)

    def std_blocks(c0, ncols, bw=512, tag0=0):
        blocks = []
        t = tag0
        for b0 in range(c0, c0 + ncols, bw):
            n = min(bw, c0 + ncols - b0)
            ch = []
            for o in range(0, n, P):
                ch.append((o, min(P, n - o), t))
                t += 1
            blocks.append(dict(ranges=[(b0, n)], chunks=ch))
        return blocks

    def pass_mod(i):
        def epi(tag, pst):
            ps = pst[0]
            kb.op("act", lambda e: e.activation(out=MOD[:, tag * 2:tag * 2 + 2], in_=ps[:, 0:2], func=AF.Identity,
                                                bias=V("bmod", i * 96 + tag), scale=1.0), r=ps.all, w=MOD.all)
        sv = scT[:].rearrange("p (k c) -> p k c", c=2)
        linear(sv, scT.all, KC, w_mod[i], std_blocks(0, 6 * D), [(0, 2)], epi)
        mv = MOD[:].rearrange("p (s k c) -> p s k c", s=6, c=2)
        mav = MA[:].rearrange("p (n k c) -> p n k c", n=2, c=2)
        for n, (sidx, gname) in enumerate(((1, "n1g"), (4, "n2g"))):
            for col in range(2):
                kb.op("dve", lambda e, n=n, sidx=sidx, col=col, gname=gname: e.scalar_tensor_tensor(
                    out=mav[:, n, :, col], in0=mv[:, sidx, :, col], scalar=1.0, in1=V(gname, i * KC, KC),
                    op0=ALU.add, op1=ALU.mult), r=MOD.all, w=MA.all)
        return mv, mav

    def norm_mod(es2, si, mv, mav, n_idx, shift_idx, hT):
        tok0, n, col = SPC["SUPER"][si]
        xs = kb.tile([P, KC * 256], F32, es=es2)
        sq = kb.tile([P, KC * 256], BF16, es=es2)
        rstd = kb.tile([P, 256], F32, es=es2)
        tmp = kb.tile([P, 256], F32, es=es2)
        xv = xs[:].rearrange("p (k t) -> p k t", t=256)
        sv = sq[:].rearrange("p (k t) -> p k t", t=256)
        hv = hT[:, 0:KC * n].rearrange("p (k t) -> p k t", k=KC)
        for s0 in range(0, n, 256):
            kb.dma("sp", xv, SPC["XR"][:, tok0 + s0:tok0 + s0 + 256].rearrange("(k p) t -> p k t", p=P), xs,
                   r=[SPC["XRb"][si]], w=xs.all)
            kb.op("act", lambda e: e.activation(out=sq[:], in_=xs[:], func=AF.Square), r=xs.all, w=sq.all)
            ps = nps()
            for kc in range(KC):
                kb.op("pe", lambda e, kc=kc: e.matmul(ps[:, 0:256], lhsT=ones[:, 0:P], rhs=sv[:, kc, :],
                                                      start=(kc == 0), stop=(kc == KC - 1)),
                      r=list(sq.all) + list(ones.all), w=ps.all, inc=(kc == KC - 1))
            kb.op("act", lambda e: e.activation(out=rstd[:], in_=ps[:, 0:256], func=AF.Sqrt, bias=V("eps"), scale=1.0 / D),
                  r=ps.all, w=rstd.all)
            kb.op("dve", lambda e: e.reciprocal(out=rstd[:], in_=rstd[:]), r=rstd.all, w=rstd.all)
            for kc in range(KC):
                kb.op("dve", lambda e, kc=kc: e.tensor_tensor(out=tmp[:], in0=xv[:, kc, :], in1=rstd[:], op=ALU.mult),
                      r=list(xs.all) + list(rstd.all), w=tmp.all)
                kb.op("act", lambda e, kc=kc: e.activation(out=hv[:, kc, s0:s0 + 256], in_=tmp[:], func=AF.Identity,
                                                           bias=mv[:, shift_idx, kc, col:col + 1],
                                                           scale=mav[:, n_idx, kc, col:col + 1]),
                      r=list(tmp.all) + list(MOD.all) + list(MA.all), w=hT.all)

    def ttiles_of(n):
        return [(t, min(512, n - t)) for t in range(0, n, 512)]

    def resid_epi(es2, si, mv, gate_idx):
        tok0, n, col = SPC["SUPER"][si]
        xo = [kb.tile([P, 1024], F32, es=es2) for _ in range(2)]
        sel = [0]

        def epi(tag, pst):
            sel[0] ^= 1
            x = xo[sel[0]]
            kb.dma("sp", x[:, 0:n], SPC["XR"][tag * P:(tag + 1) * P, tok0:tok0 + n], x, r=[SPC["XRb"][si]], w=x.all)
            for ti, (t0, tn) in enumerate(ttiles_of(n)):
                ps = pst[ti]
                kb.op("dve", lambda e, ps=ps, t0=t0, tn=tn: e.scalar_tensor_tensor(
                    out=x[:, t0:t0 + tn], in0=ps[:, 0:tn], scalar=mv[:, gate_idx, tag, col:col + 1], in1=x[:, t0:t0 + tn],
                    op0=ALU.mult, op1=ALU.add), r=list(ps.all) + list(x.all) + list(MOD.all), w=x.all)
            kb.dma("sp", SPC["XR"][tag * P:(tag + 1) * P, tok0:tok0 + n], x[:, 0:n], x, r=x.all, w=[SPC["XRb"][si]])
        return epi

    def pass_out_ffn(i, si, mv, mav, Wout, hy=False):
        tok0, n, col = SPC["SUPER"][si]
        with kb.scope() as es2:
            zt = kb.tile([P, KC * 1024], BF16, es=es2)
            zv = zt[:, 0:KC * n].rearrange("p (k t) -> p k t", k=KC)
            if hy:
                for s_ in range(2):
                    kb.dma("sp", zv[:, s_ * 8:(s_ + 1) * 8, :], SPC["ZT"][:, tok0:tok0 + n].rearrange("(c s p) t -> s p c t", s=2, p=P)[s_], zt, r=[SCR], w=zt.all)
            else:
                kb.dma("sp", zv, ZT[:, tok0:tok0 + n].rearrange("(k p) t -> p k t", p=P), zt, r=[SCR], w=zt.all)
            linear(zv, zt.all, KC, Wout, std_blocks(0, D), ttiles_of(n), resid_epi(es2, si, mv, 2))
        with kb.scope() as es2:
            hT = kb.tile([P, KC * 1024], BF16, es=es2)
            hid = kb.tile([P, 44 * 1024], BF16, es=es2)
            sg = [kb.tile([P, 512], F32, es=es2) for _ in range(2)]
            sgs = [0]
            with kb.scope() as es3:
                norm_mod(es3, si, mv, mav, 1, 3, hT)
            hv = hT[:, 0:KC * n].rearrange("p (k t) -> p k t", k=KC)
            hidv = hid[:, 0:44 * n].rearrange("p (k t) -> p k t", k=44)
            pend = {}

            def epi_in(tag, pst):
                j, isup = tag
                if not isup:
                    pend[j] = pst
                    return
                gp = pend.pop(j)
                for ti, (t0, tn) in enumerate(ttiles_of(n)):
                    sgs[0] ^= 1
                    s = sg[sgs[0]]
                    kb.op("act", lambda e, s=s, g=gp[ti], tn=tn: e.activation(out=s[:, 0:tn], in_=g[:, 0:tn], func=AF.Silu),
                          r=gp[ti].all, w=s.all)
                    kb.op("dve", lambda e, s=s, u=pst[ti], t0=t0, tn=tn: e.tensor_tensor(
                        out=hidv[:, j, t0:t0 + tn], in0=s[:, 0:tn], in1=u[:, 0:tn], op=ALU.mult),
                        r=list(s.all) + list(pst[ti].all), w=hid.all)
            blocks = []
            for j in range(0, 44, 2):
                blocks.append(dict(ranges=[(j * P, 2 * P), (FF + j * P, 2 * P)],
                                   chunks=[(0, P, (j, 0)), (2 * P, P, (j, 1)), (P, P, (j + 1, 0)), (3 * P, P, (j + 1, 1))]))
            linear(hv, hT.all, KC, w_ffn_in[i], blocks, ttiles_of(n), epi_in)
            linear(hidv, hid.all, 44, w_ffn_out[i], std_blocks(0, D, bw=P), ttiles_of(n), resid_epi(es2, si, mv, 5))

    def fm_to_tm(es2, src_tile, n, dst, tok0, c0, tsb):
        nt = n // P
        psb = npsb()
        pv = psb[:, 0:nt * P].rearrange("p (a b) -> p a b", b=P)
        for tt in range(nt):
            kb.op("pe", lambda e, tt=tt: e.transpose(out=pv[:, tt, :], in_=src_tile[:, tt * P:(tt + 1) * P], identity=ident[:]),
                  r=list(src_tile.all) + list(ident.all), w=psb.all, inc=(tt == nt - 1))
        tv = tsb[:, 0:nt * P].rearrange("p (a b) -> p a b", b=P)
        kb.op("act", lambda e: e.copy(out=tsb[:, 0:nt * P], in_=psb[:, 0:nt * P]), r=psb.all, w=tsb.all)
        kb.dma("sp", dst[tok0:tok0 + n, c0:c0 + P].rearrange("(a p) c -> p a c", p=P), tv, tsb, r=tsb.all, w=[SCR])

    def hyena_pass_a(i, j, si, mv, mav):
        tok0, n, col = SUPER[si]
        rowlen = 256 if si == 0 else 64
        with kb.scope() as es2:
            hT = kb.tile([P, KC * 1024], BF16, es=es2)
            with kb.scope() as es3:
                norm_mod(es3, si, mv, mav, 0, 0, hT)
            hv = hT[:, 0:KC * n].rearrange("p (k t) -> p k t", k=KC)
            pc = kb.tile([P, 1024], F32, es=es2)
            o1 = kb.tile([P, 1024], F32, es=es2)
            ob = kb.tile([P, 1024], BF16, es=es2)
            tsb = kb.tile([P, 1024], BF16, es=es2)

            def epi(tag, pst):
                for ti, (t0, tn) in enumerate(ttiles_of(n)):
                    ps = pst[ti]
                    kb.op("act", lambda e, ps=ps, t0=t0, tn=tn: e.copy(out=pc[:, t0:t0 + tn], in_=ps[:, 0:tn]), r=ps.all, w=pc.all)
                    kb.op("act", lambda e, ps=ps, t0=t0, tn=tn: e.activation(
                        out=o1[:, t0:t0 + tn], in_=ps[:, 0:tn], func=AF.Identity, bias=0.0,
                        scale=V("cw", (j * 3 + 1) * 48 + tag)), r=ps.all, w=o1.all)
                o1v = o1[:, 0:n].rearrange("p (a b) -> p a b", b=rowlen)
                pcv = pc[:, 0:n].rearrange("p (a b) -> p a b", b=rowlen)
                kb.op("dve", lambda e: e.scalar_tensor_tensor(
                    out=o1v[:, :, 1:rowlen], in0=pcv[:, :, 0:rowlen - 1], scalar=V("cw", (j * 3 + 0) * 48 + tag),
                    in1=o1v[:, :, 1:rowlen], op0=ALU.mult, op1=ALU.add), r=list(pc.all) + list(o1.all), w=o1.all)
                kb.op("dve", lambda e: e.scalar_tensor_tensor(
                    out=o1v[:, :, 0:rowlen - 1], in0=pcv[:, :, 1:rowlen], scalar=V("cw", (j * 3 + 2) * 48 + tag),
                    in1=o1v[:, :, 0:rowlen - 1], op0=ALU.mult, op1=ALU.add), r=list(pc.all) + list(o1.all), w=o1.all)
                kb.op("dve", lambda e: e.tensor_copy(out=ob[:, 0:n], in_=o1[:, 0:n]), r=o1.all, w=ob.all)
                fm_to_tm(es2, ob, n, VT[tag // KC], tok0, (tag % KC) * P, tsb)
            linear(hv, hT.all, KC, hy_w_in[j], std_blocks(0, 3 * D), ttiles_of(n), epi)

    def range_reduce_sin(es2, arg, m, n, out_ap, tmp):
        MAGIC = 12582912.0
        kb.op("dve", lambda e: e.tensor_scalar(out=tmp[0:m, 0:n], in0=arg[0:m, 0:n], scalar1=1.0 / (2 * math.pi), scalar2=MAGIC,
                                               op0=ALU.mult, op1=ALU.add), r=arg.all, w=tmp.all)
        kb.op("dve", lambda e: e.tensor_scalar(out=tmp[0:m, 0:n], in0=tmp[0:m, 0:n], scalar1=-MAGIC, scalar2=None,
                                               op0=ALU.add), r=tmp.all, w=tmp.all)
        kb.op("dve", lambda e: e.scalar_tensor_tensor(out=arg[0:m, 0:n], in0=tmp[0:m, 0:n], scalar=-2 * math.pi, in1=arg[0:m, 0:n],
                                                      op0=ALU.mult, op1=ALU.add), r=list(tmp.all) + list(arg.all), w=arg.all)
        kb.op("dve", lambda e: e.tensor_scalar(out=arg[0:m, 0:n], in0=arg[0:m, 0:n], scalar1=-3.14159, scalar2=3.14159,
                                               op0=ALU.max, op1=ALU.min), r=arg.all, w=arg.all)
        kb.op("act", lambda e: e.activation(out=out_ap, in_=arg[0:m, 0:n], func=AF.Sin), r=arg.all, w=[])

    def hyena_filters(j, L):
        c = CL[L]
        TCn, KF = c["TC"], c["KF"]
        with kb.scope() as es2:
            zT = kb.tile([P, L], F32, es=es2)
            h1 = kb.tile([P, L], F32, es=es2)
            h2 = kb.tile([P, L], BF16, es=es2)
            fw1 = kb.tile([P, 64], F32, es=es2)
            fw2 = kb.tile([P, 64], F32, es=es2)
            fwo = kb.tile([P, 4 * CH], BF16, es=es2)
            tn = kb.tile([P, TCn], F32, es=es2)
            negd = kb.tile([P, CH], F32, es=es2)
            skp = kb.tile([P, 2 * CH], F32, es=es2)
            fb = kb.tile([P, 2], F32, es=es2)
            arg = kb.tile([P, 512], F32, es=es2)
            tmp = kb.tile([P, 512], F32, es=es2)
            kb.dma("sp", zT[0:33, :], c["zT"], zT, w=zT.all)
            kb.dma("sp", fw1[0:33, :], hy_fw1[j], fw1, w=fw1.all)
            kb.dma("sp", fw2[0:64, :], hy_fw2[j], fw2, w=fw2.all)
            kb.dma("pool", fwo[0:64, :].rearrange("p (b c) -> p b c", b=4),
                   hy_fwout[j].rearrange("p (b c) -> p b c", b=4)[:, :, bass.ds(pargCH, CH)], fwo, w=fwo.all)
            kb.dma("sp", tn[:], c["tn"], tn, w=tn.all)
            kb.dma("sp", skp[:].rearrange("p (b c) -> p b c", b=2),
                   fskip[j].rearrange("p (b c) -> p b c", b=2)[:, :, bass.ds(parCH, CH)], skp, w=skp.all)
            kb.dma("sp", negd[:], C_negd[:, bass.ds(parCH, CH)], negd, w=negd.all)
            for q in range(2):
                kb.op("dve", lambda e, q=q: e.tensor_tensor(out=fb[0:64, q:q + 1], in0=V("ffreq", j * 2 + q)[0:64, :],
                                                             in1=V("fbias", j * 2 + q)[0:64, :], op=ALU.mult), r=vecs.all, w=fb.all)
            for t0 in range(0, L, 512):
                tn_ = min(512, L - t0)
                ps = nps()
                kb.op("pe", lambda e: e.matmul(ps[0:64, 0:tn_], lhsT=fw1[0:33, 0:64], rhs=zT[0:33, t0:t0 + tn_], start=True, stop=True),
                      r=list(fw1.all) + list(zT.all), w=ps.all)
                kb.op("act", lambda e: e.activation(out=arg[0:64, 0:tn_], in_=ps[0:64, 0:tn_], func=AF.Identity,
                                                    bias=fb[0:64, 0:1], scale=V("ffreq", j * 2 + 0)[0:64, :]),
                      r=list(ps.all) + list(fb.all), w=arg.all)
                range_reduce_sin(es2, arg, 64, tn_, h1[0:64, t0:t0 + tn_], tmp)
                kb._mark((kb.sem["act"], kb.cnt["act"], "act"), [], h1.all)
            for t0 in range(0, L, 512):
                tn_ = min(512, L - t0)
                ps = nps()
                kb.op("pe", lambda e: e.matmul(ps[0:64, 0:tn_], lhsT=fw2[0:64, 0:64], rhs=h1[0:64, t0:t0 + tn_], start=True, stop=True),
                      r=list(fw2.all) + list(h1.all), w=ps.all)
                kb.op("act", lambda e: e.activation(out=arg[0:64, 0:tn_], in_=ps[0:64, 0:tn_], func=AF.Identity,
                                                    bias=fb[0:64, 1:2], scale=V("ffreq", j * 2 + 1)[0:64, :]),
                      r=list(ps.all) + list(fb.all), w=arg.all)
                range_reduce_sin(es2, arg, 64, tn_, h2[0:64, t0:t0 + tn_], tmp)
                kb._mark((kb.sem["act"], kb.cnt["act"], "act"), [], h2.all)
            HF = kb.tile([P, 2 * TCn * 512], BF16, es=es2)
            hfv = HF[:].rearrange("p (k c) -> p k c", c=512)
            dec = [kb.tile([P, 512], F32, es=es2) for _ in range(2)]
            ab = [kb.tile([P, 512], BF16, es=es2) for _ in range(2)]
            rn = kb.tile([P, 512], F32, es=es2)
            ho = [kb.tile([P, 512], F32, es=es2) for _ in range(2)]
            sel = [0]
            for o in range(2):
                for ct in range(2):
                    nps_ = nps()
                    first = True
                    for tc in range(TCn):
                        d = dec[tc % 2]
                        kb.op("act", lambda e, d=d, tc=tc: e.activation(out=d[:], in_=negd[:, ct * 512:(ct + 1) * 512], func=AF.Exp,
                                                                        scale=tn[:, tc:tc + 1]), r=list(negd.all) + list(tn.all), w=d.all)
                        for dr in range(2):
                            ps = nps()
                            if ps is nps_:
                                ps = nps()
                            col0 = (dr * 2 + o) * CH + ct * 512
                            kb.op("pe", lambda e, ps=ps, tc=tc, col0=col0: e.matmul(
                                ps[:, :], lhsT=h2[0:64, tc * P:(tc + 1) * P], rhs=fwo[0:64, col0:col0 + 512], start=True, stop=True),
                                r=list(h2.all) + list(fwo.all), w=ps.all)
                            kk = dr * TCn + tc
                            kb.op("dve", lambda e, ps=ps, d=d, kk=kk: e.tensor_tensor(out=hfv[:, kk, :], in0=ps[:, :], in1=d[:], op=ALU.mult),
                                  r=list(ps.all) + list(d.all), w=HF.all)
                            sel[0] ^= 1
                            a = ab[sel[0]]
                            kb.op("act", lambda e, a=a, kk=kk: e.activation(out=a[:], in_=hfv[:, kk, :], func=AF.Abs),
                                  r=HF.all, w=a.all)
                            last = (dr == 1 and tc == TCn - 1)
                            oc0 = P if last else 0
                            kb.op("pe", lambda e, a=a, oc0=oc0, first=first, last=last: e.matmul(
                                nps_[:, :], lhsT=ones[:, oc0:oc0 + P], rhs=a[:], start=first, stop=last),
                                r=list(a.all) + list(ones.all), w=nps_.all, inc=last)
                            first = False
                    kb.op("dve", lambda e: e.reciprocal(out=rn[:], in_=nps_[:, :]), r=nps_.all, w=rn.all)
                    for blk in range(2 * KF):
                        wt, view = load_w(None, 2 * TCn, [(0, P)], cast=False,
                                          pre=c["FF"][blk].rearrange("p (k n) -> p k n", n=P))
                        ps = nps()
                        for kk in range(2 * TCn):
                            kb.op("pe", lambda e, kk=kk: e.matmul(ps[:, :], lhsT=view[:, kk, :], rhs=hfv[:, kk, :],
                                                                  start=(kk == 0), stop=(kk == 2 * TCn - 1)),
                                  r=list(wt.all) + list(HF.all), w=ps.all, inc=(kk == 2 * TCn - 1))
                        sel[0] ^= 1
                        h = ho[sel[0]]
                        kb.op("dve", lambda e, h=h: e.tensor_tensor(out=h[:], in0=ps[:, :], in1=rn[:], op=ALU.mult),
                              r=list(ps.all) + list(rn.all), w=h.all)
                        if blk < KF:
                            kb.op("pool", lambda e, h=h: e.tensor_tensor(out=h[:], in0=h[:], in1=skp[:, o * CH + ct * 512:o * CH + (ct + 1) * 512],
                                                                         op=ALU.add), r=list(h.all) + list(skp.all), w=h.all)
                        kb.dma("sp", c["HS"][o, blk, :, ct * 512:(ct + 1) * 512], h[:], h, r=h.all, w=[SCR])

    def hyena_conv(L, tok0):
        c = CL[L]
        TCn, KF = c["TC"], c["KF"]
        with kb.scope() as es2:
            Vt = kb.tile([P, TCn * 512], BF16, es=es2)
            Z1 = kb.tile([P, TCn * 512], BF16, es=es2)
            Y = kb.tile([P, 2 * KF * 512], BF16, es=es2)
            Hr = [kb.tile([P, 512], F32, es=es2) for _ in range(2)]
            Hi = [kb.tile([P, 512], F32, es=es2) for _ in range(2)]
            t1 = kb.tile([P, 512], F32, es=es2)
            t2 = kb.tile([P, 512], F32, es=es2)
            xm = [kb.tile([P, 512], BF16, es=es2) for _ in range(2)]
            zb = kb.tile([P, 512], BF16, es=es2)
            tsb = kb.tile([P, 512], BF16, es=es2)
            vv = Vt[:].rearrange("p (k c) -> p k c", c=512)
            z1v = Z1[:].rearrange("p (k c) -> p k c", c=512)
            yv = Y[:].rearrange("p (k c) -> p k c", c=512)
            sel = [0]
            for ct in range(2):
                c0 = ct * 512
                kb.dma("sp", vv, VTm[0, tok0:tok0 + L, c0:c0 + 512].rearrange("(k p) c -> p k c", p=P), Vt, r=[SCR], w=Vt.all)
                for o in range(2):
                    src, srcv = (Vt, vv) if o == 0 else (Z1, z1v)
                    for m in range(KF):
                        wt, view = load_w(None, TCn, [(0, 256)], cast=False, pre=c["Ff"][m].rearrange("p (k n) -> p k n", n=256))
                        pr, pi = nps(), nps()
                        for part, ps in ((0, pr), (1, pi)):
                            for kk in range(TCn):
                                kb.op("pe", lambda e, ps=ps, kk=kk, part=part: e.matmul(
                                    ps[:, :], lhsT=view[:, kk, part * P:(part + 1) * P], rhs=srcv[:, kk, :],
                                    start=(kk == 0), stop=(kk == TCn - 1)), r=list(wt.all) + list(src.all), w=ps.all, inc=(kk == TCn - 1))
                        sel[0] ^= 1
                        hr, hi = Hr[sel[0]], Hi[sel[0]]
                        kb.dma("sp", hr[:], c["HS"][o, m, :, c0:c0 + 512], hr, r=[SCR], w=hr.all)
                        kb.dma("sp", hi[:], c["HS"][o, KF + m, :, c0:c0 + 512], hi, r=[SCR], w=hi.all)
                        kb.op("dve", lambda e: e.tensor_tensor(out=t1[:], in0=pr[:, :], in1=hr[:], op=ALU.mult), r=list(pr.all) + list(hr.all), w=t1.all)
                        kb.op("dve", lambda e: e.tensor_tensor(out=t2[:], in0=pi[:, :], in1=hi[:], op=ALU.mult), r=list(pi.all) + list(hi.all), w=t2.all)
                        kb.op("pool", lambda e, m=m: e.tensor_tensor(out=yv[:, m, :], in0=t1[:], in1=t2[:], op=ALU.subtract),
                              r=list(t1.all) + list(t2.all), w=Y.all)
                        kb.op("dve", lambda e: e.tensor_tensor(out=t1[:], in0=pr[:, :], in1=hi[:], op=ALU.mult), r=list(pr.all) + list(hi.all), w=t1.all)
                        kb.op("dve", lambda e: e.tensor_tensor(out=t2[:], in0=pi[:, :], in1=hr[:], op=ALU.mult), r=list(pi.all) + list(hr.all), w=t2.all)
                        kb.op("pool", lambda e, m=m: e.tensor_tensor(out=yv[:, KF + m, :], in0=t1[:], in1=t2[:], op=ALU.add),
                              r=list(t1.all) + list(t2.all), w=Y.all)
                    for tc in range(TCn):
                        wt, view = load_w(None, 2 * KF, [(0, P)], cast=False, pre=c["G"][tc].rearrange("p (k n) -> p k n", n=P))
                        ps = nps()
                        for kk in range(2 * KF):
                            kb.op("pe", lambda e, kk=kk: e.matmul(ps[:, :], lhsT=view[:, kk, :], rhs=yv[:, kk, :],
                                                                  start=(kk == 0), stop=(kk == 2 * KF - 1)),
                                  r=list(wt.all) + list(Y.all), w=ps.all, inc=(kk == 2 * KF - 1))
                        sel[0] ^= 1
                        x = xm[sel[0]]
                        kb.dma("sp", x[:], VTm[1 + o, tok0 + tc * P:tok0 + (tc + 1) * P, c0:c0 + 512], x, r=[SCR], w=x.all)
                        if o == 0:
                            kb.op("dve", lambda e, tc=tc: e.tensor_tensor(out=z1v[:, tc, :], in0=ps[:, :], in1=x[:], op=ALU.mult),
                                  r=list(ps.all) + list(x.all), w=Z1.all)
                        else:
                            kb.op("dve", lambda e: e.tensor_tensor(out=zb[:], in0=ps[:, :], in1=x[:], op=ALU.mult),
                                  r=list(ps.all) + list(x.all), w=zb.all)
                            psb = npsb()
                            pv = psb[:, 0:512].rearrange("p (a b) -> p a b", b=P)
                            for cc in range(4):
                                kb.op("pe", lambda e, cc=cc: e.transpose(out=pv[:, cc, :], in_=zb[:, cc * P:(cc + 1) * P], identity=ident[:]),
                                      r=list(zb.all) + list(ident.all), w=psb.all, inc=(cc == 3))
                            kb.op("act", lambda e: e.copy(out=tsb[:], in_=psb[:, 0:512]), r=psb.all, w=tsb.all)
                            kb.dma("sp", ZTs[c0:c0 + 512, tok0 + tc * P:tok0 + (tc + 1) * P].rearrange("(a p) t -> p a t", p=P),
                                   tsb[:].rearrange("p (a b) -> p a b", b=P), tsb, r=tsb.all, w=[SCR])

    def gate_chain(es2, T, q_sb, k_of_dir, lf, dr, row0, tok0, n, U=P):
        nu = n // U
        EXd = EX if U == P else EX64
        smk = smask if U == P else smask64
        cs, d, e1, ob, ctot, nct, ex = T["cs"], T["d"], T["e1"], T["ob"], T["ctot"], T["nct"], T["ex"]
        kb.op("dve", lambda e: e.tensor_tensor_scan(out=cs[:, 0:n], data0=smk[:, 0:n], data1=lf[:, 0:n], initial=0.0,
                                                    op0=ALU.mult, op1=ALU.add), r=list(lf.all) + list(smk.all), w=cs.all)
        csv = cs[:, 0:n].rearrange("p (u t) -> p u t", t=U)
        kb.op("dve", lambda e: e.tensor_copy(out=ctot[:, 0:nu], in_=csv[:, :, U - 1]), r=cs.all, w=ctot.all)
        kb.op("dve", lambda e: e.tensor_scalar(out=nct[:, 0:nu], in0=ctot[:, 0:nu], scalar1=-1.0, scalar2=None, op0=ALU.mult),
              r=ctot.all, w=nct.all)
        kb.op("act", lambda e: e.activation(out=ex[:, 0:nu], in_=ctot[:, 0:nu], func=AF.Exp), r=ctot.all, w=ex.all)
        u0 = tok0 // U
        kb.dma("sp", EXd[dr, row0:row0 + P, u0:u0 + nu], ex[:, 0:nu], ex, r=ex.all, w=[SCR])

        def emit(dst, src_mul, make_exp):
            make_exp()
            kb.op("dve", lambda e: e.tensor_tensor(out=ob[:, 0:n], in0=src_mul[:, 0:n], in1=e1[:, 0:n], op=ALU.mult),
                  r=list(src_mul.all) + list(e1.all), w=ob.all)
            kb.dma("sp", dst[dr, row0:row0 + P, tok0:tok0 + n], ob[:, 0:n], ob, r=ob.all, w=[SCR])

        def full_exp(src, scale):
            return lambda: kb.op("act", lambda e: e.activation(out=e1[:, 0:n], in_=src[:, 0:n], func=AF.Exp, scale=scale),
                                 r=src.all, w=e1.all)

        def unit_exp(src, scale, bias_t):
            def f():
                for u in range(nu):
                    kb.op("act", lambda e, u=u: e.activation(out=e1[:, u * U:(u + 1) * U], in_=src[:, u * U:(u + 1) * U], func=AF.Exp,
                                                             scale=scale, bias=bias_t[:, u:u + 1]),
                          r=list(src.all) + list(bias_t.all), w=e1.all)
            return f
        if dr == 0:
            emit(QT, q_sb, full_exp(cs, 1.0))
            emit(KTs, k_of_dir, full_exp(cs, -1.0))
            emit(KHs, k_of_dir, unit_exp(cs, -1.0, ctot))
        else:
            kb.op("dve", lambda e: e.tensor_tensor(out=d[:, 0:n], in0=lf[:, 0:n], in1=cs[:, 0:n], op=ALU.subtract),
                  r=list(lf.all) + list(cs.all), w=d.all)
            emit(QT, q_sb, unit_exp(d, 1.0, ctot))
            emit(KTs, k_of_dir, unit_exp(d, -1.0, nct))
            emit(KHs, k_of_dir, full_exp(d, -1.0))

    def chain_tiles(es2):
        T = {}
        for nm in ("cs", "d", "e1"):
            T[nm] = kb.tile([P, 1024], F32, es=es2)
        T["ob"] = kb.tile([P, 1024], BF16, es=es2)
        for nm in ("ctot", "nct", "ex"):
            T[nm] = kb.tile([P, 16], F32, es=es2)
        return T

    def hgrn_pass_a(i, si, mv, mav, lbt):
        tok0, n, col = SUPER[si]
        with kb.scope() as es2:
            hT = kb.tile([P, KC * 1024], BF16, es=es2)
            with kb.scope() as es3:
                norm_mod(es3, si, mv, mav, 0, 0, hT)
            hv = hT[:, 0:KC * n].rearrange("p (k t) -> p k t", k=KC)
            T = chain_tiles(es2)
            q_sb = kb.tile([P, 1024], F32, es=es2)
            fv = kb.tile([P, 1024], F32, es=es2)
            lf = kb.tile([P, 1024], F32, es=es2)
            k_sb = kb.tile([P, 1024], F32, es=es2)
            ob2 = kb.tile([P, 1024], BF16, es=es2)
            tsb = kb.tile([P, 1024], BF16, es=es2)

            def epi(tag, pst):
                kind, h = tag
                for ti, (t0, tn) in enumerate(ttiles_of(n)):
                    ps = pst[ti]
                    if kind == "q":
                        kb.op("act", lambda e, ps=ps, t0=t0, tn=tn: e.activation(out=q_sb[:, t0:t0 + tn], in_=ps[:, 0:tn], func=AF.Silu),
                              r=ps.all, w=q_sb.all)
                    elif kind == "i":
                        kb.op("act", lambda e, ps=ps, t0=t0, tn=tn: e.copy(out=ob2[:, t0:t0 + tn], in_=ps[:, 0:tn]), r=ps.all, w=ob2.all)
                    elif kind == "g":
                        kb.op("act", lambda e, ps=ps, t0=t0, tn=tn: e.activation(out=ob2[:, t0:t0 + tn], in_=ps[:, 0:tn], func=AF.Silu),
                              r=ps.all, w=ob2.all)
                    else:
                        kb.op("act", lambda e, ps=ps, t0=t0, tn=tn: e.activation(out=fv[:, t0:t0 + tn], in_=ps[:, 0:tn], func=AF.Sigmoid),
                              r=ps.all, w=fv.all)
                if kind == "i":
                    fm_to_tm(es2, ob2, n, VT[0], tok0, h * P, tsb)
                elif kind == "g":
                    kb.dma("sp", GATE[h * P:(h + 1) * P, tok0:tok0 + n], ob2[:, 0:n], ob2, r=ob2.all, w=[SCR])
                elif kind in ("f0", "f1"):
                    dr = 0 if kind == "f0" else 1
                    kb.op("dve", lambda e: e.tensor_scalar(out=fv[:, 0:n], in0=fv[:, 0:n], scalar1=lbt[:, (2 + dr) * KC + h:(2 + dr) * KC + h + 1],
                                                           scalar2=lbt[:, dr * KC + h:dr * KC + h + 1], op0=ALU.mult, op1=ALU.add),
                          r=list(fv.all) + list(lbt.all), w=fv.all)
                    kb.op("act", lambda e: e.activation(out=lf[:, 0:n], in_=fv[:, 0:n], func=AF.Ln), r=fv.all, w=lf.all)
                    kb.op("dve", lambda e: e.tensor_scalar(out=k_sb[:, 0:n], in0=fv[:, 0:n], scalar1=-1.0, scalar2=1.0, op0=ALU.mult, op1=ALU.add),
                          r=fv.all, w=k_sb.all)
                    gate_chain(es2, T, q_sb, k_sb, lf, dr, h * P, tok0, n, U=64)
            blocks = []
            for h in range(KC):
                blocks.append(dict(ranges=[(h * P, P), (3 * D + h * P, P), (4 * D + h * P, P)],
                                   chunks=[(0, P, ("q", h)), (P, P, ("f0", h)), (2 * P, P, ("f1", h))]))
            for h in range(0, KC, 4):
                blocks.append(dict(ranges=[(D + h * P, 512)], chunks=[(o * P, P, ("i", h + o)) for o in range(4)]))
            for h in range(0, KC, 4):
                blocks.append(dict(ranges=[(2 * D + h * P, 512)], chunks=[(o * P, P, ("g", h + o)) for o in range(4)]))
            linear(hv, hT.all, KC, hg_w_in[0], blocks, ttiles_of(n), epi)

    def gla_pass_a(i, si, mv, mav, wup):
        tok0, n, col = SUPER[si]
        with kb.scope() as es2:
            hT = kb.tile([P, KC * 1024], BF16, es=es2)
            with kb.scope() as es3:
                norm_mod(es3, si, mv, mav, 0, 0, hT)
            hv = hT[:, 0:KC * n].rearrange("p (k t) -> p k t", k=KC)
            T = chain_tiles(es2)
            q_sb = kb.tile([P, 1024], F32, es=es2)
            k_sb = kb.tile([P, 1024], F32, es=es2)
            lf = kb.tile([P, 1024], F32, es=es2)
            aT = [kb.tile([P, 1024], BF16, es=es2) for _ in range(2)]
            ob2 = kb.tile([P, 1024], BF16, es=es2)
            tsb = kb.tile([P, 1024], BF16, es=es2)

            def epi(tag, pst):
                kind, h = tag
                for ti, (t0, tn) in enumerate(ttiles_of(n)):
                    ps = pst[ti]
                    if kind == "a":
                        kb.op("act", lambda e, ps=ps, t0=t0, tn=tn: e.copy(out=aT[h][0:16, t0:t0 + tn], in_=ps[0:16, 0:tn]), r=ps.all, w=aT[h].all)
                    elif kind == "q":
                        kb.op("act", lambda e, ps=ps, t0=t0, tn=tn: e.activation(out=q_sb[:, t0:t0 + tn], in_=ps[:, 0:tn], func=AF.Identity, scale=1.0 / 16.0),
                              r=ps.all, w=q_sb.all)
                    elif kind == "k":
                        kb.op("act", lambda e, ps=ps, t0=t0, tn=tn: e.copy(out=k_sb[:, t0:t0 + tn], in_=ps[:, 0:tn]), r=ps.all, w=k_sb.all)
                    elif kind == "v":
                        kb.op("act", lambda e, ps=ps, t0=t0, tn=tn: e.copy(out=ob2[:, t0:t0 + tn], in_=ps[:, 0:tn]), r=ps.all, w=ob2.all)
                    elif kind == "g":
                        kb.op("act", lambda e, ps=ps, t0=t0, tn=tn: e.activation(out=ob2[:, t0:t0 + tn], in_=ps[:, 0:tn], func=AF.Silu),
                              r=ps.all, w=ob2.all)
                if kind == "v":
                    fm_to_tm(es2, ob2, n, VT[0], tok0, h * P, tsb)
                elif kind == "g":
                    kb.dma("sp", GATE[h * P:(h + 1) * P, tok0:tok0 + n], ob2[:, 0:n], ob2, r=ob2.all, w=[SCR])
                elif kind == "k":
                    for dr in range(2):
                        for ti, (t0, tn) in enumerate(ttiles_of(n)):
                            ps = nps()
                            kb.op("pe", lambda e, ps=ps, t0=t0, tn=tn: e.matmul(
                                ps[:, 0:tn], lhsT=wup[0:16, dr * 1024 + h * P:dr * 1024 + (h + 1) * P], rhs=aT[dr][0:16, t0:t0 + tn],
                                start=True, stop=True), r=list(wup.all) + list(aT[dr].all), w=ps.all)
                            kb.op("act", lambda e, ps=ps, t0=t0, tn=tn: e.activation(
                                out=lf[:, t0:t0 + tn], in_=ps[:, 0:tn], func=AF.Exp, scale=-1.0, bias=V("nbup", dr * 8 + h)),
                                r=ps.all, w=lf.all)
                        kb.op("act", lambda e: e.activation(out=lf[:, 0:n], in_=lf[:, 0:n], func=AF.Ln, bias=1.0, scale=1.0), r=lf.all, w=lf.all)
                        kb.op("dve", lambda e: e.tensor_scalar(out=lf[:, 0:n], in0=lf[:, 0:n], scalar1=-1.0 / 16.0, scalar2=None, op0=ALU.mult),
                              r=lf.all, w=lf.all)
                        gate_chain(es2, T, q_sb, k_sb, lf, dr, h * P, tok0, n)
            blocks = [dict(ranges=[(6144, 16), (6160, 16)], chunks=[(0, 16, ("a", 0)), (16, 16, ("a", 1))])]
            for h in range(8):
                blocks.append(dict(ranges=[(h * P, P), (1024 + h * P, P)], chunks=[(0, P, ("q", h)), (P, P, ("k", h))]))
            for h in range(0, KC, 4):
                blocks.append(dict(ranges=[(2048 + h * P, 512)], chunks=[(o * P, P, ("v", h + o)) for o in range(4)]))
            for h in range(0, KC, 4):
                blocks.append(dict(ranges=[(4096 + h * P, 512)], chunks=[(o * P, P, ("g", h + o)) for o in range(4)]))
            linear(hv, hT.all, KC, gla_w_in[0], blocks, ttiles_of(n), epi)

    def scan_core(H, DKC, DV, gname, U=P):
        NUu = TT // U
        EXd = EX if U == P else EX64
        with kb.scope() as es2:
            qt = kb.tile([P, DKC * TT], BF16, es=es2)
            kt = kb.tile([P, DKC * TT], BF16, es=es2)
            khu = [kb.tile([P, DKC * P], BF16, es=es2) for _ in range(2)]
            vt = kb.tile([P, NUu * DV], BF16, es=es2)
            oacc = kb.tile([P, NUu * DV], F32, NUu, es=es2)
            ext = kb.tile([P, DKC * NUu], F32, es=es2)
            S = [kb.tile([P, DV], F32, es=es2) for _ in range(DKC)]
            Sb = [kb.tile([P, DV], BF16, es=es2) for _ in range(DKC)]
            asb = [kb.tile([P, P], BF16, es=es2) for _ in range(2)]
            khs = [kb.tile([P, DKC * P], BF16, es=es2) for _ in range(2)]
            sq = kb.tile([P, DV], F32, es=es2)
            ssq = kb.tile([P, 1], F32, es=es2)
            on = kb.tile([P, DV], BF16, es=es2)
            gt = [kb.tile([P, P], BF16, es=es2) for _ in range(2)]
            osb = [kb.tile([P, P], BF16, es=es2) for _ in range(2)]
            sel = [0]
            qv = qt[:].rearrange("p (k t) -> p k t", k=DKC)
            kv = kt[:].rearrange("p (k t) -> p k t", k=DKC)
            vv = vt[:].rearrange("p (u v) -> p u v", v=DV)
            ov = oacc[:].rearrange("p (u v) -> p u v", v=DV)
            exv = ext[:].rearrange("p (k u) -> p k u", k=DKC)
            for h in range(H // 2):
                r0 = h * DKC * P
                kb.dma("sp", vv[0:U], VTm[0, :, h * DV:(h + 1) * DV].rearrange("(u p) v -> p u v", p=U), vt, r=[SCR], w=vt.all)
                for dr in range(2):
                    for (t, vw, src) in ((qt, qv, FMm["q"]), (kt, kv, FMm["k"])):
                        kb.dma("sp", vw, src[dr, r0:r0 + DKC * P, :].rearrange("(k p) t -> p k t", p=P), t, r=[SCR], w=t.all)
                    kb.dma("sp", exv, EXm[dr, r0:r0 + DKC * P, 0:NUu].rearrange("(k p) u -> p k u", p=P), ext, r=[SCR], w=ext.all)
                    for k in range(DKC):
                        kb.op("dve", lambda e, k=k: e.memset(S[k][:], 0.0), w=S[k].all)
                        kb.op("pool", lambda e, k=k: e.memset(Sb[k][:], 0.0), w=Sb[k].all)
                    nc_ = TCX // U
                    order = list(range(NUu)) if dr == 0 else list(range(nc_ - 1, -1, -1)) + list(range(NUu - 1, nc_ - 1, -1))
                    for u in order:
                        ts = slice(u * U, (u + 1) * U)
                        sel[0] ^= 1
                        a, khb = asb[sel[0]], khs[sel[0]]
                        pa = nps()
                        for k in range(DKC):
                            kb.op("pe", lambda e, k=k: e.matmul(pa[0:U, 0:U], lhsT=kv[:, k, ts], rhs=qv[:, k, ts], start=(k == 0), stop=(k == DKC - 1)),
                                  r=list(kt.all) + list(qt.all), w=pa.all, inc=(k == DKC - 1))
                        kb.op("dve", lambda e: e.tensor_tensor(out=a[0:U, 0:U], in0=pa[0:U, 0:U], in1=tri[0:U, dr * P:dr * P + U], op=ALU.mult),
                              r=list(pa.all) + list(tri.all), w=a.all)
                        po = nps()
                        kb.op("pe", lambda e: e.matmul(po[0:U, 0:DV], lhsT=a[0:U, 0:U], rhs=vv[0:U, u, :], start=True, stop=False),
                              r=list(a.all) + list(vt.all), w=po.all, inc=False)
                        for k in range(DKC):
                            kb.op("pe", lambda e, k=k: e.matmul(po[0:U, 0:DV], lhsT=qv[:, k, ts], rhs=Sb[k][:], start=False, stop=(k == DKC - 1)),
                                  r=list(qt.all) + list(Sb[k].all), w=po.all, inc=(k == DKC - 1))
                        if dr == 0:
                            kb.op("act", lambda e: e.copy(out=ov[0:U, u, :], in_=po[0:U, 0:DV]), r=po.all, w=[oacc.parts[u]])
                        else:
                            kb.op("dve", lambda e: e.tensor_tensor(out=ov[0:U, u, :], in0=po[0:U, 0:DV], in1=ov[0:U, u, :], op=ALU.add),
                                  r=list(po.all) + [oacc.parts[u]], w=[oacc.parts[u]])
                        kht = khu[sel[0]]
                        khv = kht[:, 0:DKC * U].rearrange("p (k t) -> p k t", k=DKC)
                        kb.dma("sp", khv, FMm["h"][dr, r0:r0 + DKC * P, u * U:(u + 1) * U].rearrange("(k p) t -> p k t", p=P), kht, r=[SCR], w=kht.all)
                        psb = npsb()
                        pv = psb[:, 0:DKC * P].rearrange("p (k d) -> p k d", d=P)
                        for k in range(DKC):
                            kb.op("pe", lambda e, k=k: e.transpose(out=pv[0:U, k, :], in_=khv[:, k, :], identity=ident[:]),
                                  r=list(kht.all) + list(ident.all), w=psb.all, inc=(k == DKC - 1))
                        kb.op("act", lambda e: e.copy(out=khb[0:U, :], in_=psb[0:U, 0:DKC * P]), r=psb.all, w=khb.all)
                        for k in range(DKC):
                            pd = nps()
                            kb.op("pe", lambda e, k=k, pd=pd: e.matmul(pd[:, 0:DV], lhsT=khb[0:U, k * P:(k + 1) * P], rhs=vv[0:U, u, :], start=True, stop=True),
                                  r=list(khb.all) + list(vt.all), w=pd.all)
                            kb.op("dve", lambda e, k=k, pd=pd: e.scalar_tensor_tensor(
                                out=S[k][:], in0=S[k][:], scalar=exv[:, k, u:u + 1], in1=pd[:, 0:DV], op0=ALU.mult, op1=ALU.add),
                                r=list(S[k].all) + list(pd.all) + list(ext.all), w=S[k].all)
                            kb.op("act", lambda e, k=k: e.copy(out=Sb[k][:], in_=S[k][:]), r=S[k].all, w=Sb[k].all)
                for u in range(NUu):
                    kb.op("act", lambda e, u=u: e.activation(out=sq[0:U, :], in_=ov[0:U, u, :], func=AF.Square),
                          r=[oacc.parts[u]], w=list(sq.all))
                    kb.op("dve", lambda e: e.reduce_sum(out=ssq[0:U, :], in_=sq[0:U, :], axis=mybir.AxisListType.X), r=sq.all, w=ssq.all)
                    kb.op("act", lambda e: e.activation(out=ssq[0:U, :], in_=ssq[0:U, :], func=AF.Sqrt, bias=V("eps")[0:U, :], scale=1.0 / DV), r=ssq.all, w=ssq.all)
                    kb.op("dve", lambda e: e.reciprocal(out=ssq[0:U, :], in_=ssq[0:U, :]), r=ssq.all, w=ssq.all)
                    kb.op("dve", lambda e, u=u: e.tensor_scalar(out=on[0:U, :], in0=ov[0:U, u, :], scalar1=ssq[0:U, 0:1], scalar2=None, op0=ALU.mult),
                          r=[oacc.parts[u]] + list(ssq.all), w=on.all)
                    for cb in range(DV // P):
                        ch = (h * DV) // P + cb
                        sel[0] ^= 1
                        g, ob = gt[sel[0]], osb[sel[0]]
                        kb.dma("sp", g[:, 0:U], GATEm[ch * P:(ch + 1) * P, u * U:(u + 1) * U], g, r=[SCR], w=g.all)
                        psb = npsb()
                        kb.op("pe", lambda e, cb=cb: e.transpose(out=psb[:, 0:U], in_=on[0:U, cb * P:(cb + 1) * P], identity=ident[0:U, 0:U]),
                              r=list(on.all) + list(ident.all), w=psb.all)
                        kb.op("dve", lambda e, ch=ch, g=g, ob=ob: e.scalar_tensor_tensor(
                            out=ob[:, 0:U], in0=psb[:, 0:U], scalar=V(gname, ch), in1=g[:, 0:U], op0=ALU.mult, op1=ALU.mult),
                            r=list(psb.all) + list(g.all), w=ob.all)
                        kb.dma("sp", ZTs[ch * P:(ch + 1) * P, u * U:(u + 1) * U], ob[:, 0:U], ob, r=ob.all, w=[SCR])

    with kb.scope() as es2:
        cp = [kb.tile([P, 4352], F32, es=es2) for _ in range(2)]
        for r in range(KC):
            t = cp[r % 2]
            kb.dma("sp", t[:], XIN[r * P:(r + 1) * P, :], t, w=t.all)
            kb.dma("sp", XR[r * P:(r + 1) * P, :], t[:], t, r=t.all, w=XRb)
    kb.barrier()

    def loc_scan(R, U):
        NUu = TT // U
        EXd = EX if U == P else EX64
        e_sp, e_act, e_pool = (parCH, parsCH, pargCH) if R == CH else (par512, pars512, parg512)
        kb.dma("sp", FMm["q"][:, 0:R, :], QT[:, bass.ds(e_sp, R), :], dh[0], r=[SCR], w=[SCR])
        kb.dma("act", FMm["k"][:, 0:R, :], KTs[:, bass.ds(e_act, R), :], dh[1], r=[SCR], w=[SCR])
        kb.dma("pool", FMm["h"][:, 0:R, :], KHs[:, bass.ds(e_pool, R), :], dh[2], r=[SCR], w=[SCR])
        kb.dma("sp", EXm[:, 0:R, 0:NUu], EXd[:, bass.ds(e_sp, R), :], dh[0], r=[SCR], w=[SCR])
        kb.dma("sp", VTm[0, 0:TT // 2, :], VT[0, 0:TT // 2, :][:, bass.ds(parCH, CH)], dh[0], r=[SCR], w=[SCR])
        kb.dma("act", VTm[0, TT // 2:TT, :], VT[0, TT // 2:TT, :][:, bass.ds(parsCH, CH)], dh[1], r=[SCR], w=[SCR])
        kb.dma("pool", GATEm, GATE[bass.ds(pargCH, CH), :], dh[2], r=[SCR], w=[SCR])
        kb.barrier()

    def gather_zt():
        kb.allgather([(ZTs[c_ * P:(c_ + 1) * P, :], ZTg[c_ * 2 * P:(c_ + 1) * 2 * P, :]) for c_ in range(CH // P)], GROUPS)

    for i in range(nlayers):
        kind, j = i % 3, i // 3
        last = i == DEPTH - 1
        sis = list(range(len(SUPER)))
        if last and kind == 0:
            sis = sis[1:]
        mv, mav = pass_mod(i)
        if kind == 0:
            for si in sis:
                hyena_pass_a(i, j, si, mv, mav)
            kb.barrier()
            for w_, (qq, ee) in enumerate((("sp", parCH), ("act", parsCH), ("pool", pargCH))):
                for r0_, r1_ in ((0, TT // 2), (TT // 2, TT)):
                    kb.dma(qq, VTm[w_, r0_:r1_, :], VT[w_, r0_:r1_, :][:, bass.ds(ee, CH)], dh[w_], r=[SCR], w=[SCR])
            kb.barrier()
            hyena_filters(j, TL)
            kb.barrier()
            hyena_conv(TL, TCX)
            kb.barrier()
            if 0 in sis:
                hyena_filters(j, TCX)
                kb.barrier()
                hyena_conv(TCX, 0)
            kb.allgather([(ZTs[c_ * P:(c_ + 1) * P, :], ZTg[c_ * 2 * P:(c_ + 1) * 2 * P, :]) for c_ in range(CH // P)], GROUPS)
            Wout = hy_w_out[j]
        elif kind == 1:
            with kb.scope() as esl:
                lbt = kb.tile([P, 4 * KC], F32, es=esl)
                ee = kb.tile([P, 2 * 4 * KC], F32, es=esl)
                sm = kb.tile([P, 2 * KC], F32, es=esl)
                kb.op("act", lambda e: e.activation(out=ee[:], in_=V("lbl", 0, 2 * 4 * KC), func=AF.Exp), r=vecs.all, w=ee.all)
                ev = ee[:].rearrange("p (d l k) -> p d l k", d=2, l=4)
                smv = sm[:].rearrange("p (d k) -> p d k", d=2)
                kb.op("dve", lambda e: e.tensor_tensor(out=smv, in0=ev[:, :, 0, :], in1=ev[:, :, 1, :], op=ALU.add), r=ee.all, w=sm.all)
                kb.op("dve", lambda e: e.tensor_tensor(out=smv, in0=smv, in1=ev[:, :, 2, :], op=ALU.add), r=list(ee.all) + list(sm.all), w=sm.all)
                kb.op("dve", lambda e: e.tensor_tensor(out=smv, in0=smv, in1=ev[:, :, 3, :], op=ALU.add), r=list(ee.all) + list(sm.all), w=sm.all)
                kb.op("dve", lambda e: e.reciprocal(out=sm[:], in_=sm[:]), r=sm.all, w=sm.all)
                lv = lbt[:].rearrange("p (a d k) -> p a d k", a=2, d=2)
                kb.op("dve", lambda e: e.tensor_tensor(out=lv[:, 0], in0=ev[:, :, 1, :], in1=smv, op=ALU.mult), r=list(ee.all) + list(sm.all), w=lbt.all)
                kb.op("dve", lambda e: e.tensor_scalar(out=lv[:, 1], in0=lv[:, 0], scalar1=-1.0, scalar2=1.0, op0=ALU.mult, op1=ALU.add),
                      r=lbt.all, w=lbt.all)
                for si in sis:
                    hgrn_pass_a(i, si, mv, mav, lbt)
            kb.barrier()
            loc_scan(CH, 64)
            scan_core(16, 1, 128, "hgg", U=64)
            gather_zt()
            Wout = hg_w_out[0]
        else:
            with kb.scope() as esl:
                wup = kb.tile([P, 2048], BF16, es=esl)
                kb.dma("pool", wup[0:16, :].rearrange("p (d n) -> p d n", d=2), gla_w_up[0].rearrange("d p n -> p d n"), wup, w=wup.all)
                for si in sis:
                    gla_pass_a(i, si, mv, mav, wup)
            kb.barrier()
            loc_scan(512, P)
            scan_core(4, 2, 512, "glg")
            gather_zt()
            Wout = gla_w_out[0]
        ctx_used_later = any((ii != DEPTH - 1) or (ii % 3 != 0) for ii in range(i + 1, DEPTH))
        has_ctx = (0 in sis) and ctx_used_later
        kb.dma("act", ZTl[:, TCX:TLOC], ZTg[:, TCX:TT][:, bass.ds(parsTLH, TLH)], dh[1], r=[SCR], w=[SCR])
        kb.dma("pool", XRl[:, TCX:TLOC], XR[:, TCX:TT][:, bass.ds(pargTLH, TLH)], dh[2], r=XRb + [SCR], w=XRlb + [SCR])
        if has_ctx:
            kb.dma("sp", ZTl[:, 0:TCX], ZTg[:, 0:TCX], dh[0], r=[SCR], w=[SCR])
            kb.dma("sp", XRl[:, 0:TCX], XR[:, 0:TCX], dh[0], r=XRb + [SCR], w=XRlb + [SCR])
        kb.barrier()
        SPC.update(SUPER=SUPERL, XR=XRl, XRb=XRlb, ZT=ZTl)
        for si in ([0, 1, 2] if has_ctx else [1, 2]):
            pass_out_ffn(i, si, mv, mav, Wout, hy=True)
        SPC.update(SUPER=SUPER, XR=XR, XRb=XRb, ZT=ZTg)
        kb.dma("sp", XRs, XRl[:, TCX:TLOC], dh[0], r=XRlb + [SCR], w=[SCR])
        kb.allgather([(XRs[c_ * P:(c_ + 1) * P, :], XRg[c_ * 2 * P:(c_ + 1) * 2 * P, :]) for c_ in range(KC)], GROUPS)
        for s_ in range(2):
            kb.dma("sp", XR[:, TCX + s_ * TLH:TCX + (s_ + 1) * TLH].rearrange("(c p) t -> c p t", p=P),
                   XRg.rearrange("(c s p) t -> s c p t", s=2, p=P)[s_], dh[0], r=[SCR], w=XRb + [SCR])
        if has_ctx:
            kb.dma("sp", XR[:, 0:TCX], XRl[:, 0:TCX], dh[0], r=XRlb + [SCR], w=XRb + [SCR])
        kb.barrier()

    with kb.scope() as es2:
        xs = kb.tile([P, KC * 256], F32, es=es2)
        sq = kb.tile([P, KC * 256], BF16, es=es2)
        rstd = kb.tile([P, 256], F32, es=es2)
        ot = kb.tile([P, KC * 256], F32, es=es2)
        xv = xs[:].rearrange("p (k t) -> p k t", t=256)
        sv = sq[:].rearrange("p (k t) -> p k t", t=256)
        otv = ot[:].rearrange("p (k t) -> p k t", t=256)
        for s0 in range(0, TL, 256):
            kb.dma("sp", xv, XR[:, TCX + s0:TCX + s0 + 256].rearrange("(k p) t -> p k t", p=P), xs, r=XRb, w=xs.all)
            kb.op("act", lambda e: e.activation(out=sq[:], in_=xs[:], func=AF.Square), r=xs.all, w=sq.all)
            ps = nps()
            for kc in range(KC):
                kb.op("pe", lambda e, kc=kc: e.matmul(ps[:, 0:256], lhsT=ones[:, 0:P], rhs=sv[:, kc, :], start=(kc == 0), stop=(kc == KC - 1)),
                      r=list(sq.all) + list(ones.all), w=ps.all, inc=(kc == KC - 1))
            kb.op("act", lambda e: e.activation(out=rstd[:], in_=ps[:, 0:256], func=AF.Sqrt, bias=V("eps"), scale=1.0 / D), r=ps.all, w=rstd.all)
            kb.op("dve", lambda e: e.reciprocal(out=rstd[:], in_=rstd[:]), r=rstd.all, w=rstd.all)
            for kc in range(KC):
                kb.op("dve", lambda e, kc=kc: e.scalar_tensor_tensor(out=otv[:, kc, :], in0=xv[:, kc, :], scalar=V("fing", kc), in1=rstd[:],
                                                                     op0=ALU.mult, op1=ALU.mult), r=list(xs.all) + list(rstd.all), w=ot.all)
            kb.dma("sp", OUT[:, s0:s0 + 256].rearrange("(k p) t -> p k t", p=P), otv, ot, r=ot.all, w=[SCR])
    kb.barrier()
    es.close()
    return nc


VOFF = {}
NV = 0


def _voff():
    global NV
    o = 0
    for name, n in (("eps", 1), ("bmod", DEPTH * 96), ("n1g", DEPTH * KC), ("n2g", DEPTH * KC), ("fing", KC),
                    ("cw", 2 * 3 * 48), ("ffreq", 4), ("fbias", 4), ("lbl", 2 * 4 * KC), ("hgg", KC), ("glg", KC), ("nbup", 16)):
        VOFF[name] = o
        o += n
    NV = o


_voff()
nlayers = DEPTH


def fm(v):
    v = np.asarray(v, np.float32).reshape(-1, P)
    return v.T


def pack_vecs(inp, par=0):
    vecs = np.zeros((P, NV), np.float32)

    def put(name, arr, off=0):
        arr = np.asarray(arr, np.float32)
        vecs[:arr.shape[0], VOFF[name] + off:VOFF[name] + off + arr.shape[1]] = arr
    put("eps", np.full((P, 1), EPS, np.float32))
    for i in range(DEPTH):
        put("bmod", fm(inp["b_mod"][i]), i * 96)
        put("n1g", fm(inp["norm1_g"][i]), i * KC)
        put("n2g", fm(inp["norm2_g"][i]), i * KC)
    put("fing", fm(inp["final_g"]))
    for j in range(2):
        for tap in range(3):
            put("cw", fm(inp["hy_conv_w"][j, tap]), (j * 3 + tap) * 48)
        for q in range(2):
            put("ffreq", inp["hy_ffreq"][j, q].reshape(64, 1), j * 2 + q)
        put("fbias", inp["hy_fb1"][j].reshape(64, 1), j * 2 + 0)
        put("fbias", inp["hy_fb2"][j].reshape(64, 1), j * 2 + 1)
    for d in range(2):
        for l in range(4):
            put("lbl", fm(inp["hg_lb_logits"][d, l]), (d * 4 + l) * KC)
    put("hgg", np.roll(fm(inp["hg_onorm_g"][0]), -8 * par, axis=1))
    put("glg", np.roll(fm(inp["gla_onorm_g"][0]), -8 * par, axis=1))
    for d in range(2):
        put("nbup", -fm(inp["gla_b_up"][0, d]), d * 8)
    return vecs


_CONST = {}


def consts():
    if _CONST:
        return _CONST
    bf = ml_dtypes.bfloat16
    c = {}
    c["c_ident"] = np.eye(P, dtype=np.float32).astype(bf)
    on = np.ones((P, 2 * P), np.float32)
    on[P - 1, P:] = 0.0
    c["c_ones"] = on.astype(bf)
    s = np.arange(P)[:, None]
    t = np.arange(P)[None, :]
    c["c_tri"] = np.concatenate([(s <= t), (s >= t)], axis=1).astype(np.float32)
    sm = np.ones((P, 1024), np.float32)
    sm[:, ::P] = 0.0
    c["c_smask"] = sm
    sm2 = np.ones((P, 1024), np.float32)
    sm2[:, ::64] = 0.0
    c["c_smask64"] = sm2
    delt = np.abs(np.linspace(DMIN, DMAX, D, dtype=np.float32))
    c["c_negd"] = np.broadcast_to(-delt[None, :], (P, D)).astype(np.float32).copy()
    for L in (TL, TCX):
        TCn, KF = L // P, L // P + 1
        N = 2 * L
        pos = np.arange(L, dtype=np.float32)
        tt = pos / max(L - 1, 1)
        bands = np.arange(1, 17, dtype=np.float32)
        ang = (2.0 * math.pi / L) * pos[:, None] * bands[None, :]
        z = np.concatenate([tt[:, None], np.cos(ang), -np.sin(ang)], axis=-1).astype(np.float32)
        c["c_zT%d" % L] = np.ascontiguousarray(z.T)
        c["c_tn%d" % L] = np.ascontiguousarray(tt.reshape(TCn, P).T)
        kpad = KF * P
        kk = np.arange(kpad, dtype=np.float64)
        valid = (kk <= L).astype(np.float64)
        tpos = np.arange(L, dtype=np.float64)
        ph = 2 * math.pi * ((tpos[:, None] * kk[None, :]) % N) / N
        Fre = np.cos(ph) * valid
        Fim = -np.sin(ph) * valid
        Ff = np.stack([Fre.reshape(TCn, P, KF, P), Fim.reshape(TCn, P, KF, P)], axis=3)
        c["c_Ff%d" % L] = np.ascontiguousarray(Ff.transpose(2, 1, 0, 3, 4).reshape(KF, P, TCn * 256)).astype(bf)
        phb = 2 * math.pi * (((tpos[:, None] + 1) * kk[None, :]) % N) / N
        rowv = (tpos < L - 1).astype(np.float64)[:, None]
        Bre = np.cos(phb) * valid * rowv
        Bim = np.sin(phb) * valid * rowv
        Kre = np.concatenate([Fre, Bre], axis=0)
        Kim = np.concatenate([Fim, Bim], axis=0)
        FFm = np.concatenate([Kre, Kim], axis=1)
        FFb = FFm.reshape(2 * TCn, P, 2 * KF, P).transpose(2, 1, 0, 3).reshape(2 * KF, P, 2 * TCn * P)
        c["c_FF%d" % L] = np.ascontiguousarray(FFb).astype(bf)
        wk = np.where((kk == 0) | (kk == L), 1.0, 2.0) * valid / N
        phi = 2 * math.pi * ((kk[:, None] * tpos[None, :]) % N) / N
        Gre = np.cos(phi) * wk[:, None]
        Gim = -np.sin(phi) * wk[:, None]
        Gm = np.concatenate([Gre, Gim], axis=0)
        Gb = Gm.reshape(2 * KF, P, TCn, P).transpose(2, 1, 0, 3).reshape(TCn, P, 2 * KF * P)
        c["c_G%d" % L] = np.ascontiguousarray(Gb).astype(bf)
    _CONST.update(c)
    return _CONST


def make_in_maps(inp):
    cst = consts()
    vecs = [pack_vecs(inp, 0), pack_vecs(inp, 1)]
    B = inp["x"].shape[0]
    shared = {k: np.ascontiguousarray(inp[k], dtype=np.float32) for k in
              ("w_mod", "w_ffn_in", "w_ffn_out", "hy_w_in", "hy_w_out", "hy_fw1", "hy_fw2", "hy_fwout",
               "hg_w_in", "hg_w_out", "gla_w_in", "gla_w_up", "gla_w_out")}
    shared["fskip"] = np.ascontiguousarray(np.broadcast_to(inp["hy_fskip"].reshape(2, 1, 2 * D), (2, P, 2 * D)), dtype=np.float32)
    shared.update(cst)
    in_maps = []
    for core in range(8):
        b = (core // 2) % B
        m = dict(shared)
        m["vecs"] = vecs[core % 2]
        m["xin"] = np.ascontiguousarray(np.concatenate([inp["ctx"][b].T, inp["x"][b].T], axis=1), dtype=np.float32)
        sc = np.stack([fm(inp["c"][b]), fm(inp["c_ctx"])], axis=2)
        m["scin"] = np.ascontiguousarray(sc.reshape(P, KC * 2), dtype=np.float32)
        in_maps.append(m)
    return in_maps


def kernel(**inp):
    inp = {k: np.asarray(v) for k, v in inp.items()}
    nc = build()
    in_maps = make_in_maps(inp)
    B = inp["x"].shape[0]
    res = run_bass_kernel_spmd(nc, in_maps, core_ids=list(range(8)))
    out = np.stack([np.asarray(res.results[2 * b]["out"]).T for b in range(B)], axis=0)
    return np.ascontiguousarray(out, dtype=np.float32)
```
